# Optimizing a Trainium2 kernel written in Bass

```python
import math
import jax, jax.numpy as jnp
from jax import lax
import numpy as np

D_MODEL = 1024
BATCH = 16
SEQ = 2048
DEPTH = 1

NSA_HEADS = 8
NSA_KV_HEADS = 2
NSA_GROUP = NSA_HEADS // NSA_KV_HEADS
HEAD_DIM = 64
CMP_BLOCK = 32
CMP_STRIDE = 16
CMP_HIDDEN = 256
SEL_BLOCK = 64
SEL_TOPK = 8
WINDOW = 256
WIN_QBLOCK = 128
FORCE_SCORE = 1.0e4
RET_HEADS = 4
RET_QK_DIM = 64
RET_V_DIM = 128
RET_CHUNK = 128
D_FF = 4 * D_MODEL
ROPE_THETA = 10000.0
EPS = 1e-6
NEG_INF = -1.0e30

NSA_Q_DIM = NSA_HEADS * HEAD_DIM
NSA_KV_DIM = NSA_KV_HEADS * HEAD_DIM
NSA_GATE_DIM = 3 * NSA_HEADS
RET_QK_WIDTH = RET_HEADS * RET_QK_DIM
RET_V_WIDTH = RET_HEADS * RET_V_DIM
MIX_WIDTH = NSA_Q_DIM + RET_V_WIDTH
IN_SIZES = [NSA_Q_DIM, NSA_KV_DIM, NSA_KV_DIM, NSA_KV_DIM, NSA_KV_DIM, NSA_KV_DIM, NSA_KV_DIM,
            NSA_GATE_DIM, RET_QK_WIDTH, RET_QK_WIDTH, RET_V_WIDTH, RET_V_WIDTH]
IN_WIDTH = sum(IN_SIZES)

kernel_name = "hybrid_nsa_retention_sqrelu"


def rms_norm(x, w):
    xf = x.astype(jnp.float32)
    y = xf * lax.rsqrt(jnp.mean(xf * xf, axis=-1, keepdims=True) + EPS)
    return (y * w.astype(jnp.float32)).astype(x.dtype)


def rope(x, pos):
    half = x.shape[-1] // 2
    inv = ROPE_THETA ** (-jnp.arange(half, dtype=jnp.float32) / half)
    ang = pos.astype(jnp.float32)[:, None] * inv[None, :]
    cos, sin = jnp.cos(ang), jnp.sin(ang)
    xf = x.astype(jnp.float32)
    x1, x2 = xf[..., :half], xf[..., half:]
    return jnp.concatenate([x1 * cos - x2 * sin, x2 * cos + x1 * sin], axis=-1).astype(x.dtype)


def masked_softmax(s, mask):
    s = jnp.where(mask, s.astype(jnp.float32), NEG_INF)
    p = jax.nn.softmax(s, axis=-1)
    return jnp.where(mask, p, 0.0)


def compress(blocks, w1, w2):
    b, g, n, l, d = blocks.shape
    return jax.nn.silu(blocks.reshape(b, g, n, l * d) @ w1) @ w2


def nsa_mixer(q, kc, vc, ks, vs, kw, vw, gates, q_norm_w, k_norm_w,
              cmp_pe_k, cmp_pe_v, cmp_wk1, cmp_wk2, cmp_wv1, cmp_wv2):
    B, S, _ = q.shape
    G, R, dh = NSA_KV_HEADS, NSA_GROUP, HEAD_DIM
    pos = jnp.arange(S)
    scale = dh ** -0.5
    q = q.reshape(B, S, G, R, dh).transpose(0, 2, 3, 1, 4)
    q = rope(rms_norm(q, q_norm_w), pos)

    def prep_k(k, w):
        k = k.reshape(B, S, G, dh).transpose(0, 2, 1, 3)
        return rope(rms_norm(k, w), pos)

    def prep_v(v):
        return v.reshape(B, S, G, dh).transpose(0, 2, 1, 3)

    kc, ks, kw = prep_k(kc, k_norm_w[0]), prep_k(ks, k_norm_w[1]), prep_k(kw, k_norm_w[2])
    vc, vs, vw = prep_v(vc), prep_v(vs), prep_v(vw)

    n_cmp = (S - CMP_BLOCK) // CMP_STRIDE + 1
    c_start = jnp.arange(n_cmp) * CMP_STRIDE
    cidx = c_start[:, None] + jnp.arange(CMP_BLOCK)[None, :]
    k_cmp = compress(kc[:, :, cidx] + cmp_pe_k, cmp_wk1, cmp_wk2)
    v_cmp = compress(vc[:, :, cidx] + cmp_pe_v, cmp_wv1, cmp_wv2)
    s_cmp = jnp.einsum('bgrqd,bgcd->bgrqc', q, k_cmp) * scale
    cmask = (c_start + CMP_BLOCK - 1)[None, :] <= pos[:, None]
    p_cmp = masked_softmax(s_cmp, cmask)
    o_cmp = jnp.einsum('bgrqc,bgcd->bgrqd', p_cmp.astype(v_cmp.dtype), v_cmp)

    n_blk = S // SEL_BLOCK
    jb = jnp.arange(n_blk)
    overlap = ((c_start[:, None] < (jb[None, :] + 1) * SEL_BLOCK)
               & (c_start[:, None] + CMP_BLOCK > jb[None, :] * SEL_BLOCK)).astype(jnp.float32)
    p_slc = jnp.einsum('bgrqc,cj->bgqj', p_cmp, overlap)
    cur = pos // SEL_BLOCK
    forced = (jb[None, :] == 0) | (jb[None, :] == cur[:, None]) | (jb[None, :] == cur[:, None] - 1)
    valid = jb[None, :] <= cur[:, None]
    score = jnp.where(forced, FORCE_SCORE, jnp.where(valid, p_slc, -1.0))
    n_sel = min(SEL_TOPK, n_blk)
    _, sel_idx = lax.top_k(score, n_sel)

    k_blocks = ks.reshape(B, G, n_blk, SEL_BLOCK, dh)
    v_blocks = vs.reshape(B, G, n_blk, SEL_BLOCK, dh)
    nq = S // SEL_BLOCK
    q_b = jnp.moveaxis(q.reshape(B, G, R, nq, SEL_BLOCK, dh), 3, 0)
    idx_b = jnp.moveaxis(sel_idx.reshape(B, G, nq, SEL_BLOCK, n_sel), 2, 0)
    t_b = pos.reshape(nq, SEL_BLOCK)
    bi = jnp.arange(B)[:, None, None, None]
    gi = jnp.arange(G)[None, :, None, None]
    n_keys = n_sel * SEL_BLOCK

    def sel_block(args):
        qc, ic, tc = args
        ksel = k_blocks[bi, gi, ic]
        vsel = v_blocks[bi, gi, ic]
        s = jnp.einsum('bgrqd,bgqnkd->bgrqnk', qc, ksel) * scale
        kpos = ic[..., None] * SEL_BLOCK + jnp.arange(SEL_BLOCK)
        mask = (kpos <= tc[None, None, :, None, None])[:, :, None]
        p = masked_softmax(s.reshape(B, G, R, SEL_BLOCK, n_keys),
                           mask.reshape(B, G, 1, SEL_BLOCK, n_keys))
        return jnp.einsum('bgrqk,bgqkd->bgrqd', p.astype(vsel.dtype),
                          vsel.reshape(B, G, SEL_BLOCK, n_keys, dh))

    o_sel = lax.map(sel_block, (q_b, idx_b, t_b))
    o_sel = jnp.moveaxis(o_sel, 0, 3).reshape(B, G, R, S, dh)

    nw = S // WIN_QBLOCK
    n_wk = WIN_QBLOCK + WINDOW
    widx = jnp.arange(nw)[:, None] * WIN_QBLOCK + jnp.arange(n_wk)[None, :]
    pad = ((0, 0), (0, 0), (WINDOW, 0), (0, 0))
    kwin = jnp.pad(kw, pad)[:, :, widx]
    vwin = jnp.pad(vw, pad)[:, :, widx]
    kpos = widx - WINDOW
    qw = q.reshape(B, G, R, nw, WIN_QBLOCK, dh)
    s_w = jnp.einsum('bgrcqd,bgckd->bgrcqk', qw, kwin) * scale
    tq = pos.reshape(nw, WIN_QBLOCK)
    diff = tq[:, :, None] - kpos[:, None, :]
    wmask = (diff >= 0) & (diff < WINDOW) & (kpos[:, None, :] >= 0)
    p_w = masked_softmax(s_w, wmask)
    o_win = jnp.einsum('bgrcqk,bgckd->bgrcqd', p_w.astype(vwin.dtype), vwin).reshape(B, G, R, S, dh)

    g = jax.nn.sigmoid(gates.astype(jnp.float32)).reshape(B, S, 3, G, R)
    g = g.transpose(2, 0, 3, 4, 1)[..., None].astype(q.dtype)
    o = g[0] * o_cmp + g[1] * o_sel + g[2] * o_win
    return o.transpose(0, 3, 1, 2, 4).reshape(B, S, NSA_Q_DIM)


def retention_mixer(q, k, v, gate, ret_norm_w):
    B, S, _ = q.shape
    H, dk, dv, C = RET_HEADS, RET_QK_DIM, RET_V_DIM, RET_CHUNK
    out_dtype = q.dtype
    pos = jnp.arange(S)
    q = rope(q.reshape(B, S, H, dk).transpose(0, 2, 1, 3), pos).astype(jnp.float32)
    k = (rope(k.reshape(B, S, H, dk).transpose(0, 2, 1, 3), pos).astype(jnp.float32)) * (dk ** -0.5)
    v = v.reshape(B, S, H, dv).transpose(0, 2, 1, 3).astype(jnp.float32)

    log_gamma = jnp.log(1.0 - 2.0 ** (-5.0 - jnp.arange(H, dtype=jnp.float32)))
    i = jnp.arange(C)
    d_int = i[:, None] - i[None, :]
    dmat = jnp.where(d_int >= 0,
                     jnp.exp(log_gamma[:, None, None] * jnp.maximum(d_int, 0).astype(jnp.float32)), 0.0)
    xi = jnp.exp(log_gamma[:, None] * (i + 1).astype(jnp.float32))[..., None]
    zeta = jnp.exp(log_gamma[:, None] * (C - 1 - i).astype(jnp.float32))[..., None]
    gamma_c = jnp.exp(log_gamma * C)[:, None, None]

    nc = S // C
    to_chunks = lambda t: jnp.moveaxis(t.reshape(B, H, nc, C, t.shape[-1]), 2, 0)

    def step(state, inp):
        qc, kc, vc = inp
        inner = jnp.einsum('bhqd,bhkd->bhqk', qc, kc) * dmat
        y = (jnp.einsum('bhqk,bhkv->bhqv', inner, vc)
             + jnp.einsum('bhqd,bhdv->bhqv', qc, state) * xi)
        state = gamma_c * state + jnp.einsum('bhkd,bhkv->bhdv', kc * zeta, vc)
        return state, y

    state0 = jnp.zeros((B, H, dk, dv), jnp.float32)
    _, y = lax.scan(step, state0, (to_chunks(q), to_chunks(k), to_chunks(v)))
    y = jnp.moveaxis(y, 0, 2).reshape(B, H, S, dv)
    mu = jnp.mean(y, axis=-1, keepdims=True)
    var = jnp.mean(jnp.square(y - mu), axis=-1, keepdims=True)
    y = (y - mu) * lax.rsqrt(var + EPS) * ret_norm_w.astype(jnp.float32)[None, :, None, :]
    y = y.transpose(0, 2, 1, 3).reshape(B, S, RET_V_WIDTH)
    return (jax.nn.silu(gate.astype(jnp.float32)) * y).astype(out_dtype)


def setup_inputs(seed: int = 0) -> dict:
    key = jax.random.key(seed)
    ks = jax.random.split(key, 16)
    nrm = lambda k, shape, s: jax.random.normal(k, shape, jnp.float32) * s
    L = DEPTH
    return {
        "x": nrm(ks[0], (BATCH, SEQ, D_MODEL), 1.0),
        "ln1_w": 1.0 + nrm(ks[1], (L, D_MODEL), 0.02),
        "w_in": nrm(ks[2], (L, D_MODEL, IN_WIDTH), D_MODEL ** -0.5),
        "q_norm_w": 1.0 + nrm(ks[3], (L, HEAD_DIM), 0.02),
        "k_norm_w": 1.0 + nrm(ks[4], (L, 3, HEAD_DIM), 0.02),
        "cmp_pe_k": nrm(ks[5], (L, CMP_BLOCK, HEAD_DIM), 0.02),
        "cmp_pe_v": nrm(ks[6], (L, CMP_BLOCK, HEAD_DIM), 0.02),
        "cmp_wk1": nrm(ks[7], (L, CMP_BLOCK * HEAD_DIM, CMP_HIDDEN), (CMP_BLOCK * HEAD_DIM) ** -0.5),
        "cmp_wk2": nrm(ks[8], (L, CMP_HIDDEN, HEAD_DIM), CMP_HIDDEN ** -0.5),
        "cmp_wv1": nrm(ks[9], (L, CMP_BLOCK * HEAD_DIM, CMP_HIDDEN), (CMP_BLOCK * HEAD_DIM) ** -0.5),
        "cmp_wv2": nrm(ks[10], (L, CMP_HIDDEN, HEAD_DIM), CMP_HIDDEN ** -0.5),
        "ret_norm_w": 1.0 + nrm(ks[11], (L, RET_HEADS, RET_V_DIM), 0.02),
        "w_out": nrm(ks[12], (L, MIX_WIDTH, D_MODEL), MIX_WIDTH ** -0.5),
        "ln2_w": 1.0 + nrm(ks[13], (L, D_MODEL), 0.02),
        "w_up": nrm(ks[14], (L, D_MODEL, D_FF), D_MODEL ** -0.5),
        "w_down": nrm(ks[15], (L, D_FF, D_MODEL), D_FF ** -0.5),
    }


def reference(x, ln1_w, w_in, q_norm_w, k_norm_w, cmp_pe_k, cmp_pe_v, cmp_wk1, cmp_wk2,
              cmp_wv1, cmp_wv2, ret_norm_w, w_out, ln2_w, w_up, w_down):
    split_points = [int(p) for p in np.cumsum(IN_SIZES)[:-1]]
    h = x
    for l in range(DEPTH):
        xn = rms_norm(h, ln1_w[l])
        proj = xn @ w_in[l]
        (nq_, nkc, nvc, nks, nvs, nkw, nvw, ngate,
         rq, rk, rv, rg) = jnp.split(proj, split_points, axis=-1)
        o_nsa = nsa_mixer(nq_, nkc, nvc, nks, nvs, nkw, nvw, ngate, q_norm_w[l], k_norm_w[l],
                          cmp_pe_k[l], cmp_pe_v[l], cmp_wk1[l], cmp_wk2[l], cmp_wv1[l], cmp_wv2[l])
        o_ret = retention_mixer(rq, rk, rv, rg, ret_norm_w[l])
        mix = jnp.concatenate([o_nsa, o_ret], axis=-1)
        h = h + mix @ w_out[l]
        hn = rms_norm(h, ln2_w[l])
        h = h + jnp.square(jax.nn.relu(hn @ w_up[l])) @ w_down[l]
    return h
```

```python
import bisect
import math
import os
VAR = os.environ.get('KVAR', '')
from contextlib import ExitStack

import numpy as np
import concourse.bass as bass
import concourse.mybir as mybir
from concourse.bass_utils import run_bass_kernel_spmd

F32 = mybir.dt.float32
BF16 = mybir.dt.bfloat16
ALU = mybir.AluOpType
AF = mybir.ActivationFunctionType
AX = mybir.AxisListType

NEG = -30000.0
S = 2048
D = 1024
CH = 512
NCH_FULL = S // CH
EPS = 1e-6


class Tr:
    __slots__ = ("w", "r", "name", "excl")

    def __init__(self, name="", excl=False):
        self.w = None
        self.r = []
        self.name = name
        self.excl = excl


class V:
    __slots__ = ("ap", "trs")

    def __init__(self, ap, trs):
        self.ap = ap
        self.trs = tuple(trs)

    def __getitem__(self, k):
        return V(self.ap[k], self.trs)

    def re(self, pattern_, **kw):
        return V(self.ap.rearrange(pattern_, **kw), self.trs)

    def bc(self, shape):
        return V(self.ap.broadcast_to(list(shape)), self.trs)

    def un(self, axis):
        return V(self.ap.unsqueeze(axis), self.trs)

    def bitcast(self, dt):
        return V(self.ap.bitcast(dt), self.trs)


class Eng:
    def __init__(self, name, h, sem, is_pe=False):
        self.name = name
        self.h = h
        self.sem = sem
        self.count = 0
        self.n = 0
        self.last = None
        self.tick_idx = []
        self.tick_val = []
        self.seen = {}
        self.is_pe = is_pe


class FW:
    def __init__(self, nc, es, n_dma_sems=24):
        self.nc = nc
        self.es = es
        mk = lambda nm: es.enter_context(nc.semaphore(nm))
        self.E = {
            "pe": Eng("pe", nc.tensor, mk("s_pe"), True),
            "act": Eng("act", nc.scalar, mk("s_act")),
            "dve": Eng("dve", nc.vector, mk("s_dve")),
            "pool": Eng("pool", nc.gpsimd, mk("s_pool")),
            "sp": Eng("sp", nc.sync, mk("s_sp")),
        }
        self.dsems = [[mk(f"s_dma{i}"), 0] for i in range(n_dma_sems)]
        half_ = n_dma_sems // 2
        self.dpool = {"sp": list(range(0, half_)), "pool": list(range(half_, n_dma_sems))}
        self.dnext = {"sp": 0, "pool": 0}
        self.nwaits = 0

    def sbv(self, name, shape, dt=F32):
        t = self.es.enter_context(self.nc.sbuf_tensor("sb_" + name, list(shape), dt))
        return V(t[:], (Tr(name),))

    def psv(self, name, shape, dt=F32):
        t = self.es.enter_context(self.nc.psum_tensor(name, list(shape), dt))
        return V(t[:], (Tr(name, excl=True),))

    def _ticket(self, X, idx):
        k = bisect.bisect_left(X.tick_idx, idx)
        if k < len(X.tick_idx):
            return X.tick_val[k]
        assert X.last is not None and X.n - 1 >= idx
        X.count += 1
        X.last.then_inc(X.sem, 1)
        X.tick_idx.append(X.n - 1)
        X.tick_val.append(X.count)
        return X.count

    def _wait(self, E, ref):
        if ref[0] == "dma":
            _, si, val = ref
            key = ("d", si)
            if E.seen.get(key, 0) >= val:
                return
            E.h.wait_ge(self.dsems[si][0], val)
            E.seen[key] = val
        else:
            _, xn, idx = ref
            X = self.E[xn]
            tk = self._ticket(X, idx)
            if E.seen.get(xn, 0) >= tk:
                return
            E.h.wait_ge(X.sem, tk)
            E.seen[xn] = tk
        self.nwaits += 1

    def _deps(self, E, outs, ins):
        en = E.name
        for v in ins:
            for tr in v.trs:
                w = tr.w
                if w is not None:
                    if not (w[0] == "e" and w[1] == en and E.is_pe):
                        self._wait(E, w)
                if tr.excl:
                    for r in tr.r:
                        if r[0] == "e" and r[1] == en:
                            continue
                        self._wait(E, r)
        for v in outs:
            for tr in v.trs:
                w = tr.w
                if w is not None and not (w[0] == "e" and w[1] == en and E.is_pe):
                    self._wait(E, w)
                for r in tr.r:
                    if r[0] == "e" and r[1] == en and E.is_pe:
                        continue
                    self._wait(E, r)

    def _record(self, ref, outs, ins):
        for v in ins:
            for tr in v.trs:
                tr.r.append(ref)
                if len(tr.r) > 32:
                    best = {}
                    for r in tr.r:
                        key = (r[0], r[1])
                        if key not in best or r[2] > best[key][2]:
                            best[key] = r
                    tr.r = list(best.values())
        for v in outs:
            for tr in v.trs:
                tr.w = ref
                tr.r = []

    def op(self, en, outs, ins, fn):
        E = self.E[en]
        self._deps(E, outs, ins)
        inst = fn(E.h)
        E.last = inst
        idx = E.n
        E.n += 1
        self._record(("e", en, idx), outs, ins)
        return inst

    def dma(self, qn, out, in_, **kw):
        E = self.E[qn]
        if E.n > 0 and (not E.tick_idx or E.tick_idx[-1] < E.n - 1):
            self._ticket(E, E.n - 1)
        self._deps(E, [out], [in_])
        pl = self.dpool[qn]
        si = pl[self.dnext[qn] % len(pl)]
        self.dnext[qn] += 1
        sem, val = self.dsems[si]
        if val > 0:
            self._wait(E, ("dma", si, val))
        inst = E.h.dma_start(out=out.ap, in_=in_.ap, **kw)
        inst.then_inc(sem, 16)
        self.dsems[si][1] = val + 16
        E.last = inst
        E.n += 1
        E.tick_idx.append(E.n - 1)
        E.tick_val.append(E.count)
        self._record(("dma", si, val + 16), [out], [in_])
        return inst

    def finish(self):
        E = self.E["sp"]
        for si, (sem, val) in enumerate(self.dsems):
            if val > 0:
                self._wait(E, ("dma", si, val))


def _consts():
    c = {}
    c["c_ident"] = np.eye(128, dtype=np.float32)
    R = np.zeros((64, 64), np.float32)
    for m in range(32):
        R[m, m + 32] = -1.0
        R[m + 32, m] = 1.0
    rt = np.zeros((128, 128), np.float32)
    rt[:64, :64] = R.T
    rt[64:, 64:] = R.T
    c["c_rt"] = rt
    on = np.zeros((128, 128), np.float32)
    on[:64, :64] = 1.0 / 64
    on[64:, 64:] = 1.0 / 64
    c["c_ones"] = on
    half = 32
    inv = (10000.0 ** (-np.arange(half, dtype=np.float32) / half)).astype(np.float32)
    ang = np.arange(S, dtype=np.float32)[:, None] * inv[None, :]
    cos = np.cos(ang).astype(np.float32).T
    sin = np.sin(ang).astype(np.float32).T
    c["c_cos"] = np.concatenate([cos, cos, cos, cos], 0)
    c["c_sin"] = np.concatenate([sin, sin, sin, sin], 0)
    cc = np.arange(128)[:, None]
    tt = np.arange(S)[None, :]
    c["c_cmpmask"] = np.where((16 * cc + 31 <= tt) & (cc < 127), 0.0, NEG).astype(np.float32)
    k = np.arange(128)[:, None]
    q = np.arange(128)[None, :]
    c["c_trile"] = np.where(k <= q, 0.0, NEG).astype(np.float32)
    c["c_trigt"] = np.where(k > q, 0.0, NEG).astype(np.float32)
    E = np.zeros((32, S), np.float32)
    for j in range(32):
        E[j, 64 * j:64 * j + 64] = 1.0
    c["c_E"] = E
    cs = np.arange(127) * 16
    jb = np.arange(32)
    ov = ((cs[:, None] < (jb[None, :] + 1) * 64) & (cs[:, None] + 32 > jb[None, :] * 64)).astype(np.float32)
    ovl = np.zeros((128, 33), np.float32)
    ovl[:127, :32] = ov
    ovl[:127, 32] = 1.0
    c["c_ovl"] = ovl
    t = np.arange(S)
    cur = t // 64
    forced = (jb[None, :] == 0) | (jb[None, :] == cur[:, None]) | (jb[None, :] == cur[:, None] - 1)
    valid = jb[None, :] <= cur[:, None]
    bias = np.where(forced, 1.0e4, np.where(valid, 0.0, -1.0)).astype(np.float32)
    c["c_selbias"] = np.ascontiguousarray(bias.reshape(16, 128, 32).transpose(1, 0, 2))
    H = 4
    lg = np.log(1.0 - 2.0 ** (-5.0 - np.arange(H, dtype=np.float64)))
    i = np.arange(128, dtype=np.float64)
    dm = np.zeros((128, H, 128), np.float64)
    for h in range(H):
        dm[:, h, :] = (i[None, :] >= i[:, None]) * np.exp(-lg[h] * (i[:, None] + 1.0)) / 8.0
    c["c_dmatp"] = dm.astype(np.float32)
    c["c_zeta8"] = (np.exp(lg[None, :] * (127.0 - i[:, None])) / 8.0).astype(np.float32)
    xi = np.exp(lg[:, None] * (i[None, :] + 1.0))
    xi2 = np.zeros((128, 2, 128), np.float64)
    for j in range(2):
        xi2[:64, j, :] = xi[2 * j][None, :]
        xi2[64:, j, :] = xi[2 * j + 1][None, :]
    c["c_xi"] = xi2.astype(np.float32)
    c["_gamma_c"] = [float(np.exp(lg[h] * 128.0)) for h in range(H)]
    return c


CONST_SHAPES = None


def build(NB=2, NCH=NCH_FULL, consts=None, tab_dt=F32, dbg=(), stage=99):
    nc = bass.Bass("TRN2", target_bir_lowering=False)
    gamma_c = consts["_gamma_c"]

    def din(name, shape):
        return V(nc.dram_tensor(name, list(shape), F32, kind="ExternalInput").ap(), ())

    x_d = din("x", [NB, S, D])
    w_in_d = din("w_in", [D, 2840])
    w_out_d = din("w_out", [D, D])
    w_up_d = din("w_up", [D, 4096])
    w_down_d = din("w_down", [4096, D])
    wk1_d = din("cmp_wk1", [2048, 256])
    wv1_d = din("cmp_wv1", [2048, 256])
    wk2_d = din("cmp_wk2", [256, 64])
    wv2_d = din("cmp_wv2", [256, 64])
    lnw1_d = din("ln1_t", [128, 8])
    lnw2_d = din("ln2_t", [128, 8])
    qkw_d = din("qkw_t", [128, 4])
    pek_d = din("pek_t", [128, 16])
    pev_d = din("pev_t", [128, 16])
    retw_d = din("retw", [1, 512])
    cd = {k: din(k, v.shape) for k, v in consts.items() if not k.startswith("_")}
    out_d = V(nc.dram_tensor("out", [NB, S, D], F32, kind="ExternalOutput").ap(), ())

    with ExitStack() as es:
        fw = FW(nc, es)
        sbv, op, dma = fw.sbv, fw.op, fw.dma

        ident = sbv("ident", [128, 128], BF16)
        rt = sbv("rt", [128, 128], BF16)
        onesb = sbv("onesb", [128, 128], BF16)
        cos = sbv("cos", [128, S], tab_dt)
        sin = sbv("sin", [128, S], tab_dt)
        cmpmask = sbv("cmpmask", [128, S], BF16)
        trile = sbv("trile", [128, 128], BF16)
        trigt = sbv("trigt", [128, 128], BF16)
        ovl = sbv("ovl", [128, 33], BF16)
        selbias = sbv("selbias", [128, 16, 32], F32)
        dmatp = sbv("dmatp", [128, 4, 128], F32)
        zeta8 = sbv("zeta8", [128, 4], F32)
        xit = sbv("xit", [128, 2, 128], F32)
        lnw1 = sbv("lnw1", [128, 8], F32)
        lnw2 = sbv("lnw2", [128, 8], F32)
        qkw = sbv("qkw", [128, 4], F32)
        pek = sbv("pek", [128, 16], BF16)
        pev = sbv("pev", [128, 16], BF16)
        retw = sbv("retw", [128, 512], F32)
        wg = sbv("wg", [128, 8, 24], BF16)
        wk2 = sbv("wk2", [128, 2, 64], BF16)
        wv2 = sbv("wv2", [128, 2, 64], BF16)
        epsb = sbv("epsb", [128, 1], F32)
        biasH = sbv("biasH", [128, 4], F32)

        for dst, key in ((ident, "c_ident"), (rt, "c_rt"), (onesb, "c_ones"), (cmpmask, "c_cmpmask"),
                         (trile, "c_trile"), (trigt, "c_trigt"), (ovl, "c_ovl"),
                         (cos, "c_cos"), (sin, "c_sin")):
            dma("pool", dst, cd[key])
        for dst, src in ((selbias, cd["c_selbias"]), (dmatp, cd["c_dmatp"]), (zeta8, cd["c_zeta8"]),
                         (xit, cd["c_xi"]), (lnw1, lnw1_d), (lnw2, lnw2_d), (qkw, qkw_d)):
            dma("sp", dst, src)
        dma("pool", pek, pek_d)
        dma("pool", pev, pev_d)
        dma("sp", retw, V(retw_d.ap.broadcast_to([128, 512]), ()))
        dma("pool", wg, V(w_in_d.ap[:, 1280:1304].rearrange("(kc p) n -> p kc n", p=128), ()))
        dma("pool", wk2, V(wk2_d.ap.rearrange("(mh p) d -> p mh d", p=128), ()))
        dma("pool", wv2, V(wv2_d.ap.rearrange("(mh p) d -> p mh d", p=128), ()))
        op("dve", [epsb], [], lambda h: h.memset(epsb.ap, EPS))


        def scratch(name, src_v, rows, cols, piece_rows):
            t = nc.dram_tensor("scr_" + name, [rows, cols], BF16, kind="Internal").ap()
            trs = []
            for r0 in range(0, rows, piece_rows):
                tr = Tr(f"scr_{name}_{r0}")
                dma("pool", V(t[r0:r0 + piece_rows, :], (tr,)), V(src_v.ap[r0:r0 + piece_rows, :], ()))
                trs.append(tr)
            return V(t, trs)

        def conv_block(name, src_ap, rows, cols, nsplit=2):
            t = nc.dram_tensor("scr_" + name, [rows, cols], BF16, kind="Internal").ap()
            trs = []
            pr = rows // nsplit
            for r0 in range(0, rows, pr):
                tr = Tr(f"scr_{name}_{r0}")
                dma("pool", V(t[r0:r0 + pr, :], (tr,)), V(src_ap[r0:r0 + pr, :], ()))
                trs.append(tr)
            return V(t, trs)

        scr = {}
        scr["S0"] = conv_block("S0", w_in_d.ap[:, 0:512], D, 512)
        scr["S5"] = conv_block("S5", w_in_d.ap[:, 2328:2840], D, 512)
        s1_t = nc.dram_tensor("scr_s1", [D, 512], BF16, kind="Internal").ap()
        s1_trs = []
        for j, c0_ in enumerate([512, 512, 576, 576, 640, 640, 704, 704]):
            tr = Tr(f"scr_s1_{j}")
            dma("pool", V(s1_t[:, j * 64:(j + 1) * 64], (tr,)), V(w_in_d.ap[:, c0_:c0_ + 64], ()))
            s1_trs.append(tr)
        scr["S1"] = V(s1_t, s1_trs)
        scr["S2"] = conv_block("S2", w_in_d.ap[:, 768:1280], D, 512)
        scr["S3"] = conv_block("S3", w_in_d.ap[:, 1304:1816], D, 512)
        scr["S4"] = conv_block("S4", w_in_d.ap[:, 1816:2328], D, 512)
        scr["CK"] = conv_block("CK", wk1_d.ap, 2048, 256)
        scr["CV"] = conv_block("CV", wv1_d.ap, 2048, 256)
        scr["O0"] = conv_block("O0", w_out_d.ap[:, 0:512], D, 512)
        scr["O1"] = conv_block("O1", w_out_d.ap[:, 512:1024], D, 512)
        for fc in range(8):
            scr[("U", fc)] = conv_block(f"U{fc}", w_up_d.ap[:, fc * 512:(fc + 1) * 512], D, 512)
            scr[("D", fc)] = conv_block(f"D{fc}", w_down_d.ap[fc * 512:(fc + 1) * 512, :], 512, D)

        ksT = sbv("ksT", [128, 2, S], BF16)
        kwT = sbv("kwT", [64, 2, S], BF16)
        vs_aug = sbv("vs_aug", [128, 16, 2, 66], BF16)
        vw_aug = sbv("vw_aug", [128, 16, 2, 66], BF16)
        kcw = sbv("kcw", [128, 2, 544], BF16)
        vcw = sbv("vcw", [128, 2, 544], BF16)
        kcmpT = sbv("kcmpT", [64, 2, 128], BF16)
        vcmpT = sbv("vcmpT", [64, 2, 128], BF16)
        vcmp_aug = sbv("vcmp_aug", [128, 2, 66], BF16)
        state = sbv("state", [64, 4, 128], F32)
        state_bf = sbv("state_bf", [64, 4, 128], BF16)
        op("dve", [ksT], [], lambda h: h.memset(ksT.ap[64:128, :, :], 0.0))
        for g_ in range(2):
            dma("pool", ksT[64:96, g_, :], cd["c_E"])
        op("dve", [vs_aug], [], lambda h: h.memset(vs_aug.ap, 1.0))
        op("dve", [vw_aug], [], lambda h: h.memset(vw_aug.ap, 1.0))
        op("dve", [vcmp_aug], [], lambda h: h.memset(vcmp_aug.ap, 1.0))
        op("dve", [kcmpT], [], lambda h: h.memset(kcmpT.ap, 0.0))
        op("dve", [vcmpT], [], lambda h: h.memset(vcmpT.ap, 0.0))

        def tiled(v, slicer):
            trs = [Tr(f"t{t}") for t in range(4)]
            tiles = [V(slicer(v.ap, t), (trs[t],)) for t in range(4)]
            return V(v.ap, trs), tiles

        xh, xh_t = tiled(sbv("xh", [128, 4, D], F32), lambda ap, t: ap[:, t, :])
        tokb, tokb_t = tiled(sbv("tokb", [128, 4, D], BF16), lambda ap, t: ap[:, t, :])
        T8, T8_t = [], []
        for i in range(2):
            a_, t_l = tiled(sbv(f"T8_{i}", [128, 8, CH], BF16), lambda ap, t: ap[:, :, t * 128:(t + 1) * 128])
            T8.append(a_)
            T8_t.append(t_l)
        QT = sbv("QT", [128, 8, CH], BF16)
        op("dve", [QT], [], lambda h: h.memset(QT.ap[64:128, :, :], 0.0))
        rqxT = sbv("rqxT", [64, 4, CH], BF16)
        rkT = sbv("rkT", [64, 4, CH], BF16)
        kz = sbv("kz", [128, 4, 64], BF16)
        rv = sbv("rv", [128, 4, 512], BF16)
        rgs = sbv("rgs", [128, 4, 512], BF16)
        gates = sbv("gates", [128, 4, 24], F32)
        o_nsa = sbv("o_nsa", [128, 4, 512], F32)
        ss4 = sbv("ss4", [128, 4], F32)
        rstd4 = sbv("rstd4", [128, 4], F32)
        sqb = [sbv(f"sqb{i}", [128, CH], BF16) for i in range(2)]
        rstd_t = [sbv(f"rstd_t{i}", [128, CH], F32) for i in range(2)]
        qnb = [sbv(f"qnb{i}", [128, CH], BF16) for i in range(3)]
        big4k = fw.es.enter_context(nc.sbuf_tensor("sb_big4k", [128, 2 * CH], F32))
        _tra, _trb = Tr("t1a"), Tr("t1b")
        t1 = [V(big4k[:, 0:CH], (_tra,)), V(big4k[:, CH:2 * CH], (_trb,))]
        xs = V(big4k[:], (_tra, _trb))
        uptmp = [sbv(f"uptmp{i}", [128, CH], BF16) for i in range(2)]
        t2 = [sbv(f"t2_{i}", [128, CH], F32) for i in range(2)]
        PT = [sbv(f"PT{i}", [128, CH], BF16) for i in range(6)]
        HT = sbv("HT", [128, 2, 64], BF16)
        score = sbv("score", [128, 4, 32], F32)
        sctmp = sbv("sctmp", [128, 4, 32], F32)
        m8 = sbv("m8", [128, 4, 8], F32)
        thr = sbv("thr", [128, 4], F32)
        mneg = sbv("mneg", [128, 4, 96], BF16)
        op("dve", [mneg], [], lambda h: h.memset(mneg.ap, 0.0))
        rd = sbv("rd", [128, 4], F32)
        ff = sbv("ff", [128, 4], F32)
        otmp = sbv("otmp", [128, 4, 64], F32)
        innb = [sbv(f"innb{i}", [128, 128], BF16) for i in range(4)]
        st1 = sbv("st1", [128, 32], F32)
        st2 = sbv("st2", [128, 4], F32)
        stm = sbv("stm", [128, 4], F32)
        stv = sbv("stv", [128, 4], F32)
        actT = [sbv(f"actT{i}", [128, 4, CH], BF16) for i in range(2)]
        relu_t = [sbv(f"relu_t{i}", [128, CH], F32) for i in range(1)]
        WS = [sbv(f"WS{i}", [128, 4096], BF16) for i in range(4)]

        banks = [fw.psv(f"pb{i}", [128, 512], F32) for i in range(8)]
        rrA = [0]
        rrB = [0]

        def psA():
            b = banks[rrA[0] % 4]
            rrA[0] += 1
            return b

        def psB():
            b = banks[5 + rrB[0] % 3]
            rrB[0] += 1
            return b

        rrP = [0]

        def psP():
            b = banks[rrP[0] % 6]
            rrP[0] += 1
            return b

        rrC = [0]

        def psC():
            return banks[7]

        rr = {}

        def rot(name, lst):
            i = rr.get(name, 0)
            rr[name] = i + 1
            return lst[i % len(lst)]

        def wsrc_cols(wd, a, b):
            return V(wd.ap[:, a:b].rearrange("(kc p) n -> p kc n", p=128), wd.trs)

        items = []

        def add_item(parts):
            items.append(parts)
            return len(items) - 1

        witems = {}

        def kc_view(v):
            return V(v.ap.rearrange("(kc p) n -> p kc n", p=128), v.trs)

        for b in range(NB):
            for ci in range(NCH):
                d = {}
                for key in ("S0", "S5", "S1", "S2", "S3", "S4"):
                    d[key] = add_item([(lambda s: s.re("p (kc n) -> p kc n", kc=8), kc_view(scr[key]))])
                for key in ("CK", "CV"):
                    d[key] = add_item([(lambda s: s.re("p (i m) -> p i m", i=16),
                                        V(scr[key].ap.rearrange("(i p) m -> p i m", p=128), scr[key].trs))])
                for key in ("O0", "O1"):
                    d[key] = add_item([(lambda s: s.re("p (kc n) -> p kc n", kc=8), kc_view(scr[key]))])
                for fc in range(8):
                    d[("U", fc)] = add_item([(lambda s: s.re("p (kc n) -> p kc n", kc=8), kc_view(scr[("U", fc)]))])
                    d[("D", fc)] = add_item([(lambda s: s.re("p (fs n) -> p fs n", fs=4),
                                              V(scr[("D", fc)].ap.rearrange("(fs p) n -> p fs n", p=128),
                                                scr[("D", fc)].trs))])
                witems[(b, ci)] = d
        issued = [0]
        LOOK = 2

        def wget(idx):
            while issued[0] < len(items) and issued[0] <= idx + LOOK:
                k = issued[0]
                slot = WS[k % 4]
                for viewfn, src in items[k]:
                    dma("sp", viewfn(slot), src)
                issued[0] += 1
            return WS[idx % 4]

        def dump(name, v, shape):
            if name in dbg:
                dd = V(nc.dram_tensor("dbg_" + name, list(shape), F32, kind="ExternalOutput").ap(), ())
                dma("pool", dd, v)

        def mm(out, lhsT, rhs, start=True, stop=True):
            return op("pe", [out], [lhsT, rhs],
                      lambda h: h.matmul(out.ap, lhsT.ap, rhs.ap, start=start, stop=stop, skip_group_check=True))

        def transpose(out, in_, idn):
            return op("pe", [out], [in_, idn], lambda h: h.transpose(out.ap, in_.ap, idn.ap))

        def act(out, in_, func, bias=None, scale=1.0, accum=None):
            outs = [out] + ([accum] if accum is not None else [])
            ins = [in_] + ([bias] if isinstance(bias, V) else []) + ([scale] if isinstance(scale, V) else [])
            kw = {}
            if bias is not None:
                kw["bias"] = bias.ap if isinstance(bias, V) else bias
            if accum is not None:
                kw["accum_out"] = accum.ap
            sc = scale.ap if isinstance(scale, V) else scale
            return op("act", outs, ins, lambda h: h.activation(out.ap, in_.ap, func, scale=sc, **kw))

        def tt(out, a, b, alu, eng="dve"):
            return op(eng, [out], [a, b], lambda h: h.tensor_tensor(out.ap, a.ap, b.ap, alu))

        def ts(out, a, s1, s2, op0, op1=None):
            ins = [a] + [s for s in (s1, s2) if isinstance(s, V)]
            a1 = s1.ap if isinstance(s1, V) else s1
            a2 = s2.ap if isinstance(s2, V) else s2
            if op1 is None:
                return op("dve", [out], ins, lambda h: h.tensor_scalar(out.ap, a.ap, a1, a2, op0))
            return op("dve", [out], ins, lambda h: h.tensor_scalar(out.ap, a.ap, a1, a2, op0, op1))

        def stt(out, a, sc, b, op0, op1):
            ins = [a, b] + ([sc] if isinstance(sc, V) else [])
            s_ = sc.ap if isinstance(sc, V) else sc
            return op("dve", [out], ins, lambda h: h.scalar_tensor_tensor(out.ap, a.ap, s_, b.ap, op0, op1))

        def vcopy(out, in_):
            return op("dve", [out], [in_], lambda h: h.tensor_copy(out.ap, in_.ap))

        def recip(out, in_):
            return op("dve", [out], [in_], lambda h: h.reciprocal(out.ap, in_.ap))

        def rsum(out, in_):
            return op("dve", [out], [in_], lambda h: h.reduce_sum(out.ap, in_.ap, AX.X))

        def ln_tile(t_, src=None):
            if src is None:
                src = xh_t[t_]
            op("dve", [ss4], [], lambda h: h.memset(ss4.ap[:, t_:t_ + 1], 0.0))
            act(tokb_t[t_], src, AF.Square, accum=ss4[:, t_:t_ + 1])
            act(rstd4[:, t_:t_ + 1], ss4[:, t_:t_ + 1], AF.Sqrt, bias=epsb, scale=1.0 / D)
            recip(rstd4[:, t_:t_ + 1], rstd4[:, t_:t_ + 1])
            ts(tokb_t[t_], src, rstd4[:, t_:t_ + 1], None, ALU.mult)

        def T_tile(di, t_, lnw=None):
            dst_t = T8_t[di][t_]
            for half in range(2):
                pb = psA()
                pbb = pb.bitcast(BF16)
                for j in range(4):
                    kc = half * 4 + j
                    transpose(pbb[:, j * 128:(j + 1) * 128], tokb_t[t_][:, kc * 128:(kc + 1) * 128], ident)
                src = pbb[:, 0:512].re("p (j q) -> p j q", j=4)
                dst = dst_t[:, half * 4:(half + 1) * 4, :]
                if lnw is None:
                    act(dst, src, AF.Copy)
                else:
                    tt(dst, src, lnw[:, half * 4:(half + 1) * 4].un(2).bc([128, 4, 128]), ALU.mult)

        def rmsnorm_to_T(lnw, di):
            for t_ in range(4):
                ln_tile(t_)
            for t_ in range(4):
                T_tile(di, t_, lnw)

        def proj_fm(slot3, c0, M, xT):
            pb = psA()
            for kc in range(8):
                mm(pb[0:M, :], slot3[:, kc, c0:c0 + M], xT[:, kc, :], start=(kc == 0), stop=(kc == 7))
            return pb

        def norm_rope(pb, M, wcol, t0, outs, post=None):
            qn = rot("qnb", qnb)
            if wcol is not None:
                sq = rot("sqb", sqb)
                act(sq[0:M, :], pb[0:M, :], AF.Square)
                pm = psA()
                mm(pm[0:M, :], onesb[0:M, 0:M], sq[0:M, :])
                rs = rot("rstd_t", rstd_t)
                act(rs[0:M, :], pm[0:M, :], AF.Sqrt, bias=epsb[0:M, :])
                recip(rs[0:M, :], rs[0:M, :])
                stt(qn[0:M, :], pb[0:M, :], qkw[0:M, wcol:wcol + 1], rs[0:M, :], ALU.mult, ALU.mult)
            else:
                act(qn[0:M, :], pb[0:M, :], AF.Copy)
            pr = psA()
            mm(pr[0:M, :], rt[0:M, 0:M], qn[0:M, :])
            a = rot("t1", t1)
            b_ = rot("t2", t2)
            tt(a[0:M, :], qn[0:M, :], cos[0:M, t0:t0 + CH], ALU.mult)
            tt(b_[0:M, :], pr[0:M, :], sin[0:M, t0:t0 + CH], ALU.mult)
            if post is not None:
                tt(a[0:M, :], a[0:M, :], b_[0:M, :], ALU.add)
                for dst, r0, r1 in outs:
                    tt(dst, a[r0:r1, :].re("p (s q) -> p s q", s=4), post, ALU.mult)
            else:
                for dst, r0, r1 in outs:
                    tt(dst, a[r0:r1, :], b_[r0:r1, :], ALU.add)

        def setup_bias():
            for kv, (wd, pe) in enumerate(((wk1_d, pek), (wv1_d, pev))):
                slot = WS[kv]
                s3 = slot.re("p (i m) -> p i m", i=16)
                dma("pool", s3, V(wd.ap.rearrange("(i p) m -> p i m", p=128), ()))
                pb = psA()
                for mh in range(2):
                    for i in range(16):
                        mm(pb[:, mh:mh + 1], s3[:, i, mh * 128:(mh + 1) * 128], pe[:, i:i + 1],
                           start=(i == 0), stop=(i == 15))
                vcopy(biasH[:, kv * 2:kv * 2 + 2], pb[:, 0:2])

        setup_bias()

        prefetched = [False]
        for b in range(NB):
            op("dve", [state], [], lambda h: h.memset(state.ap, 0.0))
            op("dve", [state_bf], [], lambda h: h.memset(state_bf.ap, 0.0))
            op("dve", [kcw], [], lambda h: h.memset(kcw.ap, 0.0))
            op("dve", [vcw], [], lambda h: h.memset(vcw.ap, 0.0))
            for ci in range(NCH):
                t0 = ci * CH
                wi = witems[(b, ci)]
                def chunk_body(b=b, ci=ci, t0=t0, wi=wi):
                    dma("sp", xh, V(x_d.ap[b, t0:t0 + CH, :].rearrange("(t p) d -> p t d", p=128), ()))
                    xT = T8[0]
                    if stage < 0.5:
                        return
                    if not prefetched[0]:
                        rmsnorm_to_T(lnw1, 0)
                    if stage < 1.01:
                        return

                    def vc_g(g):
                        s1_ = wget(wi["S1"]).re("p (kc n) -> p kc n", kc=8)
                        if True:
                            pb = psC()
                            for kc in range(8):
                                mm(pb, s1_[:, kc, 256 + g * 128:256 + (g + 1) * 128], xT[:, kc, :],
                                   start=(kc == 0), stop=(kc == 7))
                            act(vcw[0:64, g, 16:528], pb[0:64, :], AF.Copy)
                            act(vcw[64:128, g, 15:527], pb[64:128, :], AF.Copy)

                    def kw_tile(t_):
                        s2_ = wget(wi["S2"]).re("p (kc n) -> p kc n", kc=8)
                        if True:
                            tile_i = ci * 4 + t_
                            pb = psC()
                            for kc in range(8):
                                mm(pb[:, 0:128], xT[:, kc, t_ * 128:(t_ + 1) * 128], s2_[:, kc, 128:256],
                                   start=(kc == 0), stop=(kc == 7))
                            for kc in range(8):
                                mm(pb[:, 128:256], xT[:, kc, t_ * 128:(t_ + 1) * 128], s2_[:, kc, 384:512],
                                   start=(kc == 0), stop=(kc == 7))
                            for kc in range(8):
                                mm(pb[:, 256:280], xT[:, kc, t_ * 128:(t_ + 1) * 128], wg[:, kc, :],
                                   start=(kc == 0), stop=(kc == 7))
                            act(vs_aug[:, tile_i, :, 0:64], pb[:, 0:128].re("p (g d) -> p g d", g=2), AF.Copy)
                            act(vw_aug[:, tile_i, :, 0:64], pb[:, 128:256].re("p (g d) -> p g d", g=2), AF.Copy)
                            act(gates[:, t_, :], pb[:, 256:280], AF.Sigmoid)

                    if ci > 0:
                        for g in range(2):
                            vcopy(kcw[0:64, g, 0:16], kcw[0:64, g, 512:528])
                            vcopy(kcw[64:128, g, 0:15], kcw[64:128, g, 512:527])
                            vcopy(vcw[0:64, g, 0:16], vcw[0:64, g, 512:528])
                            vcopy(vcw[64:128, g, 0:15], vcw[64:128, g, 512:527])
                    tasks = []
                    for j in range(4):
                        tasks.append(dict(key="S0", c0=j * 128, M=128, wcol=0,
                                          outs=[(QT[0:64, 2 * j, :], 0, 64, False), (QT[0:64, 2 * j + 1, :], 64, 128, True)]))
                    for g in range(2):
                        tasks.append(dict(key="S1", c0=g * 128, M=128, wcol=1,
                                          outs=[(kcw[0:64, g, 16:528], 0, 64, False),
                                                (kcw[64:128, g, 15:527], 64, 128, False)]))
                    tasks.append(dict(key="S2", c0=0, M=128, wcol=2,
                                      outs=[(ksT[0:64, 0, t0:t0 + CH], 0, 64, False), (ksT[0:64, 1, t0:t0 + CH], 64, 128, True)]))
                    tasks.append(dict(key="S2", c0=256, M=128, wcol=3,
                                      outs=[(kwT[:, 0, t0:t0 + CH], 0, 64, False), (kwT[:, 1, t0:t0 + CH], 64, 128, True)]))
                    for j in range(2):
                        tasks.append(dict(key="S3", c0=j * 128, M=128, wcol=None,
                                          outs=[(rqxT[:, 2 * j, :], 0, 64, False), (rqxT[:, 2 * j + 1, :], 64, 128, True)],
                                          post=xit[:, j, :].un(1).bc([128, 4, 128])))
                    for j in range(2):
                        tasks.append(dict(key="S3", c0=256 + j * 128, M=128, wcol=None,
                                          outs=[(rkT[:, 2 * j, :], 0, 64, False), (rkT[:, 2 * j + 1, :], 64, 128, True)]))
                    nt = len(tasks)
                    fillers = {8: [lambda: kw_tile(0)], 9: [lambda: kw_tile(1)], 10: [lambda: kw_tile(2)],
                               11: [lambda: kw_tile(3)]}

                    def rv_tile(t_):
                        s4_ = wget(wi["S4"]).re("p (kc n) -> p kc n", kc=8)
                        pb = psC()
                        for kc in range(8):
                            mm(pb, xT[:, kc, t_ * 128:(t_ + 1) * 128], s4_[:, kc, :], start=(kc == 0), stop=(kc == 7))
                        act(rv[:, t_, :], pb, AF.Copy)

                    def rg_tile(t_):
                        s5_ = wget(wi["S5"]).re("p (kc n) -> p kc n", kc=8)
                        pb = psC()
                        for kc in range(8):
                            mm(pb, xT[:, kc, t_ * 128:(t_ + 1) * 128], s5_[:, kc, :], start=(kc == 0), stop=(kc == 7))
                        act(rgs[:, t_, :], pb, AF.Silu)

                    for t_ in range(4):
                        fillers[1 + t_] = [lambda t_=t_: rg_tile(t_)]
                    fillers[6] = [lambda: vc_g(0)]
                    fillers[7] = [lambda: vc_g(1)]
                    fillers[12] = [lambda: rv_tile(0), lambda: rv_tile(1)]
                    fillers[13] = [lambda: rv_tile(2), lambda: rv_tile(3)]
                    stt_ = [dict() for _ in tasks]
                    for k in range(nt + 2):
                        if k < nt:
                            T = tasks[k]
                            M = T["M"]
                            slot3 = wget(wi[T["key"]]).re("p (kc n) -> p kc n", kc=8)
                            pb = banks[k % 3]
                            for kc in range(8):
                                mm(pb[0:M, :], slot3[:, kc, T["c0"]:T["c0"] + M], xT[:, kc, :],
                                   start=(kc == 0), stop=(kc == 7))
                            stt_[k]["pb"] = pb
                            if T["wcol"] is not None:
                                sq = sqb[k % 2]
                                act(sq[0:M, :], pb[0:M, :], AF.Square)
                                stt_[k]["sq"] = sq
                            if "after" in T:
                                T["after"]()
                        j = k - 1
                        if 0 <= j < nt:
                            T = tasks[j]
                            M = T["M"]
                            pb = stt_[j]["pb"]
                            qn = qnb[j % 3]
                            if T["wcol"] is not None:
                                pm = banks[3 + j % 2]
                                mm(pm[0:M, :], onesb[0:M, 0:M], stt_[j]["sq"][0:M, :])
                                rs = rstd_t[j % 2]
                                act(rs[0:M, :], pm[0:M, :], AF.Ln, bias=epsb[0:M, :])
                                act(rs[0:M, :], rs[0:M, :], AF.Exp, scale=-0.5)
                                wc = T["wcol"]
                                stt(qn[0:M, :], pb[0:M, :], qkw[0:M, wc:wc + 1], rs[0:M, :], ALU.mult, ALU.mult)
                            else:
                                act(qn[0:M, :], pb[0:M, :], AF.Copy)
                            stt_[j]["qn"] = qn
                        for f_ in fillers.get(k, []):
                            f_()
                        i_ = k - 2
                        if 0 <= i_ < nt:
                            T = tasks[i_]
                            M = T["M"]
                            qn = stt_[i_]["qn"]
                            pr = banks[5 + i_ % 2]
                            mm(pr[0:M, :], rt[0:M, 0:M], qn[0:M, :])
                            a = t1[i_ % 2]
                            b_ = t2[i_ % 2]
                            tt(a[0:M, :], qn[0:M, :], cos[0:M, t0:t0 + CH], ALU.mult)
                            tt(b_[0:M, :], pr[0:M, :], sin[0:M, t0:t0 + CH], ALU.mult)
                            if T.get("post") is not None:
                                tt(a[0:M, :], a[0:M, :], b_[0:M, :], ALU.add)
                            for dst, r0, r1, via in T["outs"]:
                                tgt = uptmp[i_ % 2][r0:r1, :] if via else dst
                                if T.get("post") is not None:
                                    tt(tgt.re("p (s q) -> p s q", s=4), a[r0:r1, :].re("p (s q) -> p s q", s=4),
                                       T["post"][r0:r1], ALU.mult)
                                else:
                                    tt(tgt, a[r0:r1, :], b_[r0:r1, :], ALU.add)
                                if via:
                                    dma("sp", dst, tgt)
                    if stage < 1.5:
                        return
                    if stage < 3:
                        return
                    c0 = 32 * ci - 1
                    for kv, (key, srcw, dstT) in enumerate((("CK", kcw, kcmpT), ("CV", vcw, vcmpT))):
                        w1 = wget(wi[key]).re("p (i m) -> p i m", i=16)
                        w2 = wk2 if kv == 0 else wv2
                        for mh in range(2):
                            pb = psA()
                            for i in range(16):
                                mm(pb[:, 0:64].re("p (g c) -> p g c", g=2), w1[:, i, mh * 128:(mh + 1) * 128],
                                   srcw[:, :, 2 * i:2 * i + 512:16], start=(i == 0), stop=(i == 15))
                            act(HT[:, mh, :], pb[:, 0:64], AF.Silu, bias=biasH[:, kv * 2 + mh:kv * 2 + mh + 1])
                        for g in range(2):
                            pb = psA()
                            for mh in range(2):
                                mm(pb[0:64, 0:32], w2[:, mh, :], HT[:, mh, g * 32:(g + 1) * 32],
                                   start=(mh == 0), stop=(mh == 1))
                            if ci == 0:
                                vcopy(dstT[:, g, 0:31], pb[0:64, 1:32])
                            else:
                                vcopy(dstT[:, g, c0:c0 + 32], pb[0:64, 0:32])
                    for g in range(2):
                        pb = psA()
                        pbb = pb.bitcast(BF16)
                        transpose(pbb[:, 0:64], vcmpT[:, g, :], ident[0:64, 0:64])
                        vcopy(vcmp_aug[:, g, 0:64], pbb[:, 0:64])

                    if stage < 4:
                        return
                    ret_pieces = []
                    for s_ in range(4):
                        ts_ = slice(s_ * 128, (s_ + 1) * 128)
                        cell = {}

                        def p0(s_=s_, ts_=ts_, cell=cell):
                            pb = psA()
                            pbb = pb.bitcast(BF16)
                            for hr in range(4):
                                transpose(pbb[:, hr * 64:(hr + 1) * 64], rkT[:, hr, ts_], ident[0:64, 0:64])
                            tt(kz, pbb[:, 0:256].re("p (h d) -> p h d", h=4), zeta8.un(2).bc([128, 4, 64]), ALU.mult)

                        def p1(s_=s_, ts_=ts_, cell=cell):
                            for hr in range(4):
                                pi = psA()
                                mm(pi[:, 0:128], rkT[:, hr, ts_], rqxT[:, hr, ts_])
                                tt(innb[hr], pi[:, 0:128], dmatp[:, hr, :], ALU.mult)

                        def p2(s_=s_, ts_=ts_, cell=cell):
                            py = banks[4]
                            for hr in range(4):
                                mm(py[:, hr * 128:(hr + 1) * 128], innb[hr], rv[:, s_, hr * 128:(hr + 1) * 128],
                                   start=(hr == 0), stop=False)
                                mm(py[:, hr * 128:(hr + 1) * 128], rqxT[:, hr, ts_], state_bf[:, hr, :],
                                   start=False, stop=(hr == 3))
                            cell["py"] = py

                        def p2s(s_=s_, cell=cell):
                            py = cell["py"]
                            ysb_s = t2[s_ % 2].re("p (h v) -> p h v", h=4)
                            ysq = rot("t1", t1).re("p (h v) -> p h v", h=4)
                            cell["ysb"] = ysb_s
                            for hr in range(4):
                                act(ysb_s[:, hr, :], py[:, hr * 128:(hr + 1) * 128], AF.Copy,
                                    accum=st1[:, s_ * 8 + hr:s_ * 8 + hr + 1])
                            for hr in range(4):
                                act(ysq[:, hr, :], py[:, hr * 128:(hr + 1) * 128], AF.Square,
                                    accum=st1[:, s_ * 8 + 4 + hr:s_ * 8 + 5 + hr])

                        def p2b(s_=s_, cell=cell):
                            sm = st1[:, s_ * 8:s_ * 8 + 4]
                            sq_ = st1[:, s_ * 8 + 4:s_ * 8 + 8]
                            ts(stm, sm, 1.0 / 128, None, ALU.mult)
                            tt(stv, stm, stm, ALU.mult)
                            stt(stv, sq_, 1.0 / 128, stv, ALU.mult, ALU.subtract)

                        def p3(s_=s_, ts_=ts_, cell=cell):
                            pst = psA()
                            for hr in range(4):
                                mm(pst[0:64, hr * 128:(hr + 1) * 128], kz[:, hr, :], rv[:, s_, hr * 128:(hr + 1) * 128],
                                   start=(hr == 0), stop=(hr == 3))
                            for hr in range(4):
                                stt(state[:, hr, :], state[:, hr, :], gamma_c[hr], pst[0:64, hr * 128:(hr + 1) * 128],
                                    ALU.mult, ALU.add)

                        def p3b(s_=s_, cell=cell):
                            act(stv, stv, AF.Sqrt, bias=epsb)
                            act(state_bf, state, AF.Copy)

                        def p4(s_=s_, cell=cell):
                            recip(stv, stv)
                            stt(st2, stm, -1.0, stv, ALU.mult, ALU.mult)

                        def p4b(s_=s_, cell=cell):
                            ysb_s = cell["ysb"]
                            for hr in range(4):
                                act(ysb_s[:, hr, :], ysb_s[:, hr, :], AF.Identity, bias=st2[:, hr:hr + 1],
                                    scale=stv[:, hr:hr + 1])

                        def p5(s_=s_, cell=cell):
                            ysb_s = cell["ysb"]
                            tt(ysb_s, ysb_s, retw.re("p (h v) -> p h v", h=4), ALU.mult)
                            tt(tokb_t[s_][:, 512:1024].re("p (h v) -> p h v", h=4), ysb_s,
                               rgs[:, s_, :].re("p (h v) -> p h v", h=4), ALU.mult)

                        ret_pieces += [p0, p1, p2, p2s, p2b, p3, p3b, p4, p4b, p5]
                    op("dve", [st1], [], lambda h: h.memset(st1.ap, 0.0))
                    ret_pos = [0]

                    def next_piece():
                        if ret_pos[0] < len(ret_pieces):
                            ret_pieces[ret_pos[0]]()
                            ret_pos[0] += 1

                    def evac_branch(ob, hq, br, first):
                        o3 = ob[:, 0:260].re("p (q e) -> p q e", q=4)
                        ts(rd, o3[:, :, 64], 1e-30, None, ALU.max)
                        recip(rd, rd)
                        tt(ff, rd, gates[:, :, br * 8 + hq], ALU.mult)
                        dst = o_nsa[:, :, hq * 64:(hq + 1) * 64]
                        if first:
                            tt(dst, o3[:, :, 0:64], ff.un(2).bc([128, 4, 64]), ALU.mult)
                        else:
                            tt(otmp, o3[:, :, 0:64], ff.un(2).bc([128, 4, 64]), ALU.mult)
                            tt(dst, dst, otmp, ALU.add)

                    pipe = {"pend": [], "n": 0}

                    def pipe_push(sc_fn, pv_fn, depth=4):
                        sc_fn()
                        pipe["pend"].append(pv_fn)
                        while len(pipe["pend"]) > depth:
                            pipe["pend"].pop(0)()
                        pipe["n"] += 1
                        if pipe["n"] % 3 == 0:
                            next_piece()

                    def pipe_flush():
                        while pipe["pend"]:
                            pipe["pend"].pop(0)()

                    def run_pipe(steps, depth=4, final=None):
                        for i_s, (sc_fn, pv_fn) in enumerate(steps):
                            if final is not None and i_s == len(steps) - 1:
                                def pv_last(pv_fn=pv_fn):
                                    pv_fn()
                                    final()
                                pipe_push(sc_fn, pv_last, depth)
                            else:
                                pipe_push(sc_fn, pv_fn, depth)
                        if VAR == 'flush':
                            pipe_flush()

                    for g in range(2):
                        steps = []
                        for r in range(4):
                            hq = g * 4 + r
                            cell = {}

                            def sc_fn(hq=hq, cell=cell):
                                ps = psA()
                                mm(ps, kcmpT[:, g, :], QT[0:64, hq, :], start=True, stop=False)
                                mm(ps, ident, cmpmask[:, t0:t0 + CH], start=False, stop=True)
                                pt = rot("PT", PT)
                                act(pt, ps, AF.Exp, scale=0.125)
                                cell["pt"] = pt

                            def pv_fn(hq=hq, r=r, cell=cell):
                                pt = cell["pt"]
                                oc = psB()
                                oc3 = oc[:, 0:260].re("p (q e) -> p q e", q=4)
                                for qs in range(4):
                                    mm(oc3[:, qs, :], pt[:, qs * 128:(qs + 1) * 128], vcmp_aug[:, g, 0:65],
                                       start=(qs == 0), stop=(qs == 3))
                                psl = psA()
                                psl3 = psl[:, 0:132].re("p (q e) -> p q e", q=4)
                                for qs in range(4):
                                    mm(psl3[:, qs, :], pt[:, qs * 128:(qs + 1) * 128], ovl, start=(qs == 0),
                                       stop=(qs == 3))
                                evac_branch(oc, hq, 0, True)
                                tt(sctmp, psl3[:, :, 0:32], rd.un(2).bc([128, 4, 32]), ALU.mult)
                                if r == 0:
                                    tt(score, sctmp, selbias[:, ci * 4:ci * 4 + 4, :], ALU.add)
                                else:
                                    tt(score, score, sctmp, ALU.add)
                                next_piece()

                            steps.append((sc_fn, pv_fn))
                        run_pipe(steps)
                        pipe_flush()
                        for qs in range(4):
                            op("dve", [m8], [score], lambda h, qs=qs: h.max(m8.ap[:, qs, :], score.ap[:, qs, :]))
                        ts(thr, m8[:, :, 7], -0.5, None, ALU.max)
                        tt(sctmp, score, thr.un(2).bc([128, 4, 32]), ALU.is_ge)
                        ts(mneg[:, :, 64:96], sctmp, -NEG, NEG, ALU.mult, ALU.add)
                        def emit_mask(g=g):
                            pb = psA()
                            pbb = pb.bitcast(BF16)
                            for qs in range(4):
                                transpose(pbb[0:96, qs * 128:(qs + 1) * 128], mneg[:, qs, :], ident)
                            for r_ in range(4):
                                act(QT[64:96, g * 4 + r_, :], pbb[64:96, 0:512], AF.Copy)

                        for r in range(4):
                            hq = g * 4 + r
                            Qs = QT[:, hq, :]
                            Q = QT[0:64, hq, :]
                            if r == 3:
                                emit_mask()
                            owb = psB()
                            ow3 = owb[:, 0:260].re("p (q e) -> p q e", q=4)
                            ow_state = {"first": True}
                            steps = []
                            for kt in range(max(0, 4 * ci - 2), 4 * ci + 4):
                                cell = {}

                                def sc_fn(kt=kt, cell=cell):
                                    dg = kt - 4 * ci
                                    qa, qb = max(0, dg), min(3, dg + 2)
                                    ps = psA()
                                    mm(ps[:, qa * 128:(qb + 1) * 128], kwT[:, g, kt * 128:(kt + 1) * 128],
                                       Q[:, qa * 128:(qb + 1) * 128], start=True, stop=False)
                                    masks = []
                                    for qs in range(qa, qb + 1):
                                        dd = qs - dg
                                        if dd == 0:
                                            masks.append((qs, trile))
                                        elif dd == 2:
                                            masks.append((qs, trigt))
                                    for mi, (qs, mk_) in enumerate(masks):
                                        mm(ps[:, qs * 128:(qs + 1) * 128], ident, mk_, start=False,
                                           stop=(mi == len(masks) - 1))
                                    pt = rot("PT", PT)
                                    act(pt[:, qa * 128:(qb + 1) * 128], ps[:, qa * 128:(qb + 1) * 128], AF.Exp,
                                        scale=0.125)
                                    cell["pt"] = pt
                                    cell["qa"], cell["qb"] = qa, qb

                                def pv_fn(kt=kt, cell=cell, ow3=ow3, ow_state=ow_state, g=g):
                                    pt = cell["pt"]
                                    for qs in range(cell["qa"], cell["qb"] + 1):
                                        mm(ow3[:, qs, :], pt[:, qs * 128:(qs + 1) * 128], vw_aug[:, kt, g, 0:65],
                                           start=ow_state["first"], stop=(kt == 4 * ci + 3))
                                        ow_state["first"] = False

                                steps.append((sc_fn, pv_fn))
                            def fin_w(owb=owb, hq=hq):
                                evac_branch(owb, hq, 2, False)
                                next_piece()
                            run_pipe(steps, final=fin_w)
                            if VAR == 'flushw':
                                pipe_flush()

                        for r in range(4):
                            hq = g * 4 + r
                            Qs = QT[:, hq, :]
                            Q = QT[0:64, hq, :]
                            osb = psB()
                            os3 = osb[:, 0:260].re("p (q e) -> p q e", q=4)
                            nkt = 4 * ci + 4
                            steps = []
                            for kt in range(nkt):
                                cell = {}

                                def sc_fn(kt=kt, cell=cell):
                                    ps = psA()
                                    dg = kt - 4 * ci
                                    mm(ps, ksT[:, g, kt * 128:(kt + 1) * 128], Qs, start=True, stop=(dg < 0))
                                    qlo = max(0, dg)
                                    if dg >= 0:
                                        mm(ps[:, dg * 128:(dg + 1) * 128], ident, trile, start=False, stop=True)
                                    pt = rot("PT", PT)
                                    act(pt[:, qlo * 128:512], ps[:, qlo * 128:512], AF.Exp, scale=0.125)
                                    cell["pt"] = pt
                                    cell["qlo"] = qlo

                                def pv_fn(kt=kt, cell=cell, os3=os3, g=g, nkt=nkt):
                                    pt = cell["pt"]
                                    for qs in range(cell["qlo"], 4):
                                        mm(os3[:, qs, :], pt[:, qs * 128:(qs + 1) * 128], vs_aug[:, kt, g, 0:65],
                                           start=(kt == 0 and qs == 0), stop=(kt == nkt - 1))

                                steps.append((sc_fn, pv_fn))
                            def fin_s(osb=osb, hq=hq):
                                evac_branch(osb, hq, 1, False)
                                next_piece()
                            run_pipe(steps, final=fin_s)
                            if VAR == 'flushs':
                                pipe_flush()
                    pipe_flush()
                    for t_ in range(4):
                        act(tokb_t[t_][:, 0:512], o_nsa[:, t_, :], AF.Copy)

                    pipe_flush()
                    while ret_pos[0] < len(ret_pieces):
                        next_piece()
                    wo_slots = [wget(wi["O0"]).re("p (kc n) -> p kc n", kc=8), wget(wi["O1"]).re("p (kc n) -> p kc n", kc=8)]
                    for t_ in range(4):
                        T_tile(0, t_)
                    for t_ in range(4):
                        for nh in range(2):
                            pb = psB()
                            for kc in range(8):
                                mm(pb, T8_t[0][t_][:, kc, :], wo_slots[nh][:, kc, :], start=(kc == 0), stop=(kc == 7))
                            dst = xh_t[t_][:, nh * 512:(nh + 1) * 512]
                            tt(dst, dst, pb, ALU.add)
                        ln_tile(t_)
                        if t_ >= 1:
                            T_tile(1, t_ - 1, lnw2)
                    T_tile(1, 3, lnw2)

                    if b == 0 and ci == NCH - 1:
                        dump("mix", tokb, [128, 4, D])
                        dump("gates", gates, [128, 4, 24])
                    if stage < 6:
                        return
                    hT = T8[1]
                    def ffn_up(fc):
                        U = wget(wi[("U", fc)]).re("p (kc n) -> p kc n", kc=8)
                        at = actT[fc % 2]
                        for fs in range(4):
                            pb = psA()
                            for kc in range(8):
                                mm(pb, U[:, kc, fs * 128:(fs + 1) * 128], hT[:, kc, :], start=(kc == 0), stop=(kc == 7))
                            rl = rot("relu_t", relu_t)
                            act(rl, pb, AF.Relu)
                            act(at[:, fs, :], rl, AF.Square)

                    def ffn_down(fc):
                        Dn = wget(wi[("D", fc)]).re("p (fs n) -> p fs n", fs=4)
                        at = actT[fc % 2]
                        for t_ in range(4):
                            for nh in range(2):
                                pd = psB()
                                for fs in range(4):
                                    mm(pd, at[:, fs, t_ * 128:(t_ + 1) * 128], Dn[:, fs, nh * 512:(nh + 1) * 512],
                                       start=(fs == 0), stop=(fs == 3))
                                dst = xh_t[t_][:, nh * 512:(nh + 1) * 512]
                                tt(dst, dst, pd, ALU.add)

                    nxt = None
                    if ci + 1 < NCH:
                        nxt = (b, ci + 1)
                    elif b + 1 < NB:
                        nxt = (b + 1, 0)

                    def pfA(t_):
                        nb_, nci = nxt
                        r0 = nci * CH + t_ * 128
                        dma("sp", xs, V(x_d.ap[nb_, r0:r0 + 128, :], ()))

                    def pfB(t_):
                        ln_tile(t_, src=xs)

                    def pfC(t_):
                        T_tile(0, t_, lnw1)

                    ffn_up(0)
                    for fc in range(8):
                        if fc + 1 < 8:
                            wget(wi[("D", fc)])
                            ffn_up(fc + 1)
                        if nxt is not None and stage >= 99:
                            if 2 <= fc <= 5:
                                pfC(fc - 2)
                            if 1 <= fc <= 4:
                                pfB(fc - 1)
                            if fc <= 3:
                                pfA(fc)
                        ffn_down(fc)
                    prefetched[0] = nxt is not None and stage >= 99

                chunk_body()
                dma("sp", V(out_d.ap[b, t0:t0 + CH, :].rearrange("(t p) d -> p t d", p=128), ()), xh)
        fw.finish()
        build.stats = {e: fw.E[e].n for e in fw.E}
        build.stats["waits"] = fw.nwaits
    return nc


_CONSTS = None


def _host_inputs(inputs, NB, cores):
    global _CONSTS
    if _CONSTS is None:
        _CONSTS = _consts()
    c = _CONSTS
    f = lambda a: np.ascontiguousarray(np.asarray(a, dtype=np.float32))
    shared = {
        "w_in": f(inputs["w_in"][0]), "w_out": f(inputs["w_out"][0]), "w_up": f(inputs["w_up"][0]),
        "w_down": f(inputs["w_down"][0]),
        "cmp_wk1": f(inputs["cmp_wk1"][0]), "cmp_wv1": f(inputs["cmp_wv1"][0]),
        "cmp_wk2": f(inputs["cmp_wk2"][0]), "cmp_wv2": f(inputs["cmp_wv2"][0]),
        "ln1_t": f(np.asarray(inputs["ln1_w"][0]).reshape(8, 128).T),
        "ln2_t": f(np.asarray(inputs["ln2_w"][0]).reshape(8, 128).T),
        "pek_t": f(np.asarray(inputs["cmp_pe_k"][0]).reshape(16, 128).T),
        "pev_t": f(np.asarray(inputs["cmp_pe_v"][0]).reshape(16, 128).T),
        "retw": f(np.asarray(inputs["ret_norm_w"][0]).reshape(1, 512)),
    }
    qn = np.asarray(inputs["q_norm_w"][0], dtype=np.float32)
    kn = np.asarray(inputs["k_norm_w"][0], dtype=np.float32)
    qkw = np.stack([qn, kn[0], kn[1], kn[2]], axis=1)
    shared["qkw_t"] = f(np.concatenate([qkw, qkw], 0))
    for k, v in c.items():
        if not k.startswith("_"):
            shared[k] = v
    x = np.asarray(inputs["x"], dtype=np.float32)
    maps = []
    for ci in range(cores):
        m = dict(shared)
        m["x"] = np.ascontiguousarray(x[ci * NB:(ci + 1) * NB])
        maps.append(m)
    return maps


def kernel(**inputs):
    global _CONSTS
    if _CONSTS is None:
        _CONSTS = _consts()
    n = 8
    NB = 2
    nc = build(NB=NB, NCH=NCH_FULL, consts=_CONSTS)
    maps = _host_inputs(inputs, NB, n)
    res = run_bass_kernel_spmd(nc, maps, core_ids=list(range(n)))
    out = np.concatenate([r["out"] for r in res.results], axis=0)
    return out.astype(np.float32)
```

```python
import bisect
import math
import os
VAR = os.environ.get('KVAR', '')
from contextlib import ExitStack

import numpy as np
import concourse.bass as bass
import concourse.mybir as mybir
from concourse.bass_utils import run_bass_kernel_spmd

F32 = mybir.dt.float32
BF16 = mybir.dt.bfloat16
ALU = mybir.AluOpType
AF = mybir.ActivationFunctionType
AX = mybir.AxisListType

NEG = -30000.0
S = 2048
D = 1024
CH = 512
NCH_FULL = S // CH
EPS = 1e-6


class Tr:
    __slots__ = ("w", "r", "name", "excl")

    def __init__(self, name="", excl=False):
        self.w = None
        self.r = []
        self.name = name
        self.excl = excl


class V:
    __slots__ = ("ap", "trs")

    def __init__(self, ap, trs):
        self.ap = ap
        self.trs = tuple(trs)

    def __getitem__(self, k):
        return V(self.ap[k], self.trs)

    def re(self, pattern_, **kw):
        return V(self.ap.rearrange(pattern_, **kw), self.trs)

    def bc(self, shape):
        return V(self.ap.broadcast_to(list(shape)), self.trs)

    def un(self, axis):
        return V(self.ap.unsqueeze(axis), self.trs)

    def bitcast(self, dt):
        return V(self.ap.bitcast(dt), self.trs)


class Eng:
    def __init__(self, name, h, sem, is_pe=False):
        self.name = name
        self.h = h
        self.sem = sem
        self.count = 0
        self.n = 0
        self.last = None
        self.tick_idx = []
        self.tick_val = []
        self.seen = {}
        self.is_pe = is_pe


class FW:
    def __init__(self, nc, es, n_dma_sems=24):
        self.nc = nc
        self.es = es
        mk = lambda nm: es.enter_context(nc.semaphore(nm))
        self.E = {
            "pe": Eng("pe", nc.tensor, mk("s_pe"), True),
            "act": Eng("act", nc.scalar, mk("s_act")),
            "dve": Eng("dve", nc.vector, mk("s_dve")),
            "pool": Eng("pool", nc.gpsimd, mk("s_pool")),
            "sp": Eng("sp", nc.sync, mk("s_sp")),
        }
        self.dsems = [[mk(f"s_dma{i}"), 0] for i in range(n_dma_sems)]
        half_ = n_dma_sems // 2
        self.dpool = {"sp": list(range(0, half_)), "pool": list(range(half_, n_dma_sems))}
        self.dnext = {"sp": 0, "pool": 0}
        self.nwaits = 0

    def sbv(self, name, shape, dt=F32):
        t = self.es.enter_context(self.nc.sbuf_tensor("sb_" + name, list(shape), dt))
        return V(t[:], (Tr(name),))

    def psv(self, name, shape, dt=F32):
        t = self.es.enter_context(self.nc.psum_tensor(name, list(shape), dt))
        return V(t[:], (Tr(name, excl=True),))

    def _ticket(self, X, idx):
        k = bisect.bisect_left(X.tick_idx, idx)
        if k < len(X.tick_idx):
            return X.tick_val[k]
        assert X.last is not None and X.n - 1 >= idx
        X.count += 1
        X.last.then_inc(X.sem, 1)
        X.tick_idx.append(X.n - 1)
        X.tick_val.append(X.count)
        return X.count

    def _wait(self, E, ref):
        if ref[0] == "dma":
            _, si, val = ref
            key = ("d", si)
            if E.seen.get(key, 0) >= val:
                return
            E.h.wait_ge(self.dsems[si][0], val)
            E.seen[key] = val
        else:
            _, xn, idx = ref
            X = self.E[xn]
            tk = self._ticket(X, idx)
            if E.seen.get(xn, 0) >= tk:
                return
            E.h.wait_ge(X.sem, tk)
            E.seen[xn] = tk
        self.nwaits += 1

    def _deps(self, E, outs, ins):
        en = E.name
        for v in ins:
            for tr in v.trs:
                w = tr.w
                if w is not None:
                    if not (w[0] == "e" and w[1] == en and E.is_pe):
                        self._wait(E, w)
                if tr.excl:
                    for r in tr.r:
                        if r[0] == "e" and r[1] == en:
                            continue
                        self._wait(E, r)
        for v in outs:
            for tr in v.trs:
                w = tr.w
                if w is not None and not (w[0] == "e" and w[1] == en and E.is_pe):
                    self._wait(E, w)
                for r in tr.r:
                    if r[0] == "e" and r[1] == en and E.is_pe:
                        continue
                    self._wait(E, r)

    def _record(self, ref, outs, ins):
        for v in ins:
            for tr in v.trs:
                tr.r.append(ref)
                if len(tr.r) > 32:
                    best = {}
                    for r in tr.r:
                        key = (r[0], r[1])
                        if key not in best or r[2] > best[key][2]:
                            best[key] = r
                    tr.r = list(best.values())
        for v in outs:
            for tr in v.trs:
                tr.w = ref
                tr.r = []

    def op(self, en, outs, ins, fn):
        E = self.E[en]
        self._deps(E, outs, ins)
        inst = fn(E.h)
        E.last = inst
        idx = E.n
        E.n += 1
        self._record(("e", en, idx), outs, ins)
        return inst

    def dma(self, qn, out, in_, **kw):
        E = self.E[qn]
        if E.n > 0 and (not E.tick_idx or E.tick_idx[-1] < E.n - 1):
            self._ticket(E, E.n - 1)
        self._deps(E, [out], [in_])
        pl = self.dpool[qn]
        si = pl[self.dnext[qn] % len(pl)]
        self.dnext[qn] += 1
        sem, val = self.dsems[si]
        if val > 0:
            self._wait(E, ("dma", si, val))
        inst = E.h.dma_start(out=out.ap, in_=in_.ap, **kw)
        inst.then_inc(sem, 16)
        self.dsems[si][1] = val + 16
        E.last = inst
        E.n += 1
        E.tick_idx.append(E.n - 1)
        E.tick_val.append(E.count)
        self._record(("dma", si, val + 16), [out], [in_])
        return inst

    def finish(self):
        E = self.E["sp"]
        for si, (sem, val) in enumerate(self.dsems):
            if val > 0:
                self._wait(E, ("dma", si, val))


def _consts():
    c = {}
    c["c_ident"] = np.eye(128, dtype=np.float32)
    R = np.zeros((64, 64), np.float32)
    for m in range(32):
        R[m, m + 32] = -1.0
        R[m + 32, m] = 1.0
    rt = np.zeros((128, 128), np.float32)
    rt[:64, :64] = R.T
    rt[64:, 64:] = R.T
    c["c_rt"] = rt
    on = np.zeros((128, 128), np.float32)
    on[:64, :64] = 1.0 / 64
    on[64:, 64:] = 1.0 / 64
    c["c_ones"] = on
    half = 32
    inv = (10000.0 ** (-np.arange(half, dtype=np.float32) / half)).astype(np.float32)
    ang = np.arange(S, dtype=np.float32)[:, None] * inv[None, :]
    cos = np.cos(ang).astype(np.float32).T
    sin = np.sin(ang).astype(np.float32).T
    c["c_cos"] = np.concatenate([cos, cos, cos, cos], 0)
    c["c_sin"] = np.concatenate([sin, sin, sin, sin], 0)
    cc = np.arange(128)[:, None]
    tt = np.arange(S)[None, :]
    c["c_cmpmask"] = np.where((16 * cc + 31 <= tt) & (cc < 127), 0.0, NEG).astype(np.float32)
    k = np.arange(128)[:, None]
    q = np.arange(128)[None, :]
    c["c_trile"] = np.where(k <= q, 0.0, NEG).astype(np.float32)
    c["c_trigt"] = np.where(k > q, 0.0, NEG).astype(np.float32)
    E = np.zeros((32, S), np.float32)
    for j in range(32):
        E[j, 64 * j:64 * j + 64] = 1.0
    c["c_E"] = E
    cs = np.arange(127) * 16
    jb = np.arange(32)
    ov = ((cs[:, None] < (jb[None, :] + 1) * 64) & (cs[:, None] + 32 > jb[None, :] * 64)).astype(np.float32)
    ovl = np.zeros((128, 33), np.float32)
    ovl[:127, :32] = ov
    ovl[:127, 32] = 1.0
    c["c_ovl"] = ovl
    t = np.arange(S)
    cur = t // 64
    forced = (jb[None, :] == 0) | (jb[None, :] == cur[:, None]) | (jb[None, :] == cur[:, None] - 1)
    valid = jb[None, :] <= cur[:, None]
    bias = np.where(forced, 1.0e4, np.where(valid, 0.0, -1.0)).astype(np.float32)
    c["c_selbias"] = np.ascontiguousarray(bias.reshape(16, 128, 32).transpose(1, 0, 2))
    H = 4
    lg = np.log(1.0 - 2.0 ** (-5.0 - np.arange(H, dtype=np.float64)))
    i = np.arange(128, dtype=np.float64)
    dm = np.zeros((128, H, 128), np.float64)
    for h in range(H):
        dm[:, h, :] = (i[None, :] >= i[:, None]) * np.exp(-lg[h] * (i[:, None] + 1.0)) / 8.0
    c["c_dmatp"] = dm.astype(np.float32)
    c["c_zeta8"] = (np.exp(lg[None, :] * (127.0 - i[:, None])) / 8.0).astype(np.float32)
    xi = np.exp(lg[:, None] * (i[None, :] + 1.0))
    xi2 = np.zeros((128, 2, 128), np.float64)
    for j in range(2):
        xi2[:64, j, :] = xi[2 * j][None, :]
        xi2[64:, j, :] = xi[2 * j + 1][None, :]
    c["c_xi"] = xi2.astype(np.float32)
    c["_gamma_c"] = [float(np.exp(lg[h] * 128.0)) for h in range(H)]
    return c


CONST_SHAPES = None


def build(NB=2, NCH=NCH_FULL, consts=None, tab_dt=F32, dbg=(), stage=99):
    nc = bass.Bass("TRN2", target_bir_lowering=False)
    gamma_c = consts["_gamma_c"]

    def din(name, shape):
        return V(nc.dram_tensor(name, list(shape), F32, kind="ExternalInput").ap(), ())

    x_d = din("x", [NB, S, D])
    w_in_d = din("w_in", [D, 2840])
    w_out_d = din("w_out", [D, D])
    w_up_d = din("w_up", [D, 4096])
    w_down_d = din("w_down", [4096, D])
    wk1_d = din("cmp_wk1", [2048, 256])
    wv1_d = din("cmp_wv1", [2048, 256])
    wk2_d = din("cmp_wk2", [256, 64])
    wv2_d = din("cmp_wv2", [256, 64])
    lnw1_d = din("ln1_t", [128, 8])
    lnw2_d = din("ln2_t", [128, 8])
    qkw_d = din("qkw_t", [128, 4])
    pek_d = din("pek_t", [128, 16])
    pev_d = din("pev_t", [128, 16])
    retw_d = din("retw", [1, 512])
    cd = {k: din(k, v.shape) for k, v in consts.items() if not k.startswith("_")}
    out_d = V(nc.dram_tensor("out", [NB, S, D], F32, kind="ExternalOutput").ap(), ())

    with ExitStack() as es:
        fw = FW(nc, es)
        sbv, op, dma = fw.sbv, fw.op, fw.dma

        ident = sbv("ident", [128, 128], BF16)
        rt = sbv("rt", [128, 128], BF16)
        onesb = sbv("onesb", [128, 128], BF16)
        cos = sbv("cos", [128, S], tab_dt)
        sin = sbv("sin", [128, S], tab_dt)
        cmpmask = sbv("cmpmask", [128, S], BF16)
        trile = sbv("trile", [128, 128], BF16)
        trigt = sbv("trigt", [128, 128], BF16)
        ovl = sbv("ovl", [128, 33], BF16)
        selbias = sbv("selbias", [128, 16, 32], F32)
        dmatp = sbv("dmatp", [128, 4, 128], F32)
        zeta8 = sbv("zeta8", [128, 4], F32)
        xit = sbv("xit", [128, 2, 128], F32)
        lnw1 = sbv("lnw1", [128, 8], F32)
        lnw2 = sbv("lnw2", [128, 8], F32)
        qkw = sbv("qkw", [128, 4], F32)
        pek = sbv("pek", [128, 16], BF16)
        pev = sbv("pev", [128, 16], BF16)
        retw = sbv("retw", [128, 512], F32)
        wg = sbv("wg", [128, 8, 24], BF16)
        wk2 = sbv("wk2", [128, 2, 64], BF16)
        wv2 = sbv("wv2", [128, 2, 64], BF16)
        epsb = sbv("epsb", [128, 1], F32)
        biasH = sbv("biasH", [128, 4], F32)

        for dst, key in ((ident, "c_ident"), (rt, "c_rt"), (onesb, "c_ones"), (cmpmask, "c_cmpmask"),
                         (trile, "c_trile"), (trigt, "c_trigt"), (ovl, "c_ovl"),
                         (cos, "c_cos"), (sin, "c_sin")):
            dma("pool", dst, cd[key])
        for dst, src in ((selbias, cd["c_selbias"]), (dmatp, cd["c_dmatp"]), (zeta8, cd["c_zeta8"]),
                         (xit, cd["c_xi"]), (lnw1, lnw1_d), (lnw2, lnw2_d), (qkw, qkw_d)):
            dma("sp", dst, src)
        dma("pool", pek, pek_d)
        dma("pool", pev, pev_d)
        dma("sp", retw, V(retw_d.ap.broadcast_to([128, 512]), ()))
        dma("pool", wg, V(w_in_d.ap[:, 1280:1304].rearrange("(kc p) n -> p kc n", p=128), ()))
        dma("pool", wk2, V(wk2_d.ap.rearrange("(mh p) d -> p mh d", p=128), ()))
        dma("pool", wv2, V(wv2_d.ap.rearrange("(mh p) d -> p mh d", p=128), ()))
        op("dve", [epsb], [], lambda h: h.memset(epsb.ap, EPS))


        def scratch(name, src_v, rows, cols, piece_rows):
            t = nc.dram_tensor("scr_" + name, [rows, cols], BF16, kind="Internal").ap()
            trs = []
            for r0 in range(0, rows, piece_rows):
                tr = Tr(f"scr_{name}_{r0}")
                dma("pool", V(t[r0:r0 + piece_rows, :], (tr,)), V(src_v.ap[r0:r0 + piece_rows, :], ()))
                trs.append(tr)
            return V(t, trs)

        def conv_block(name, src_ap, rows, cols, nsplit=2):
            t = nc.dram_tensor("scr_" + name, [rows, cols], BF16, kind="Internal").ap()
            trs = []
            pr = rows // nsplit
            for r0 in range(0, rows, pr):
                tr = Tr(f"scr_{name}_{r0}")
                dma("pool", V(t[r0:r0 + pr, :], (tr,)), V(src_ap[r0:r0 + pr, :], ()))
                trs.append(tr)
            return V(t, trs)

        scr = {}
        scr["S0"] = conv_block("S0", w_in_d.ap[:, 0:512], D, 512)
        scr["S5"] = conv_block("S5", w_in_d.ap[:, 2328:2840], D, 512)
        s1_t = nc.dram_tensor("scr_s1", [D, 512], BF16, kind="Internal").ap()
        s1_trs = []
        for j, c0_ in enumerate([512, 512, 576, 576, 640, 640, 704, 704]):
            tr = Tr(f"scr_s1_{j}")
            dma("pool", V(s1_t[:, j * 64:(j + 1) * 64], (tr,)), V(w_in_d.ap[:, c0_:c0_ + 64], ()))
            s1_trs.append(tr)
        scr["S1"] = V(s1_t, s1_trs)
        scr["S2"] = conv_block("S2", w_in_d.ap[:, 768:1280], D, 512)
        scr["S3"] = conv_block("S3", w_in_d.ap[:, 1304:1816], D, 512)
        scr["S4"] = conv_block("S4", w_in_d.ap[:, 1816:2328], D, 512)
        scr["CK"] = conv_block("CK", wk1_d.ap, 2048, 256)
        scr["CV"] = conv_block("CV", wv1_d.ap, 2048, 256)
        scr["O0"] = conv_block("O0", w_out_d.ap[:, 0:512], D, 512)
        scr["O1"] = conv_block("O1", w_out_d.ap[:, 512:1024], D, 512)
        for fc in range(8):
            scr[("U", fc)] = conv_block(f"U{fc}", w_up_d.ap[:, fc * 512:(fc + 1) * 512], D, 512)
            scr[("D", fc)] = conv_block(f"D{fc}", w_down_d.ap[fc * 512:(fc + 1) * 512, :], 512, D)

        ksT = sbv("ksT", [128, 2, S], BF16)
        kwT = sbv("kwT", [64, 2, S], BF16)
        vs_aug = sbv("vs_aug", [128, 16, 2, 66], BF16)
        vw_aug = sbv("vw_aug", [128, 16, 2, 66], BF16)
        kcw = sbv("kcw", [128, 2, 544], BF16)
        vcw = sbv("vcw", [128, 2, 544], BF16)
        kcmpT = sbv("kcmpT", [64, 2, 128], BF16)
        vcmpT = sbv("vcmpT", [64, 2, 128], BF16)
        vcmp_aug = sbv("vcmp_aug", [128, 2, 66], BF16)
        state = sbv("state", [64, 4, 128], F32)
        state_bf = sbv("state_bf", [64, 4, 128], BF16)
        op("dve", [ksT], [], lambda h: h.memset(ksT.ap[64:128, :, :], 0.0))
        for g_ in range(2):
            dma("pool", ksT[64:96, g_, :], cd["c_E"])
        op("dve", [vs_aug], [], lambda h: h.memset(vs_aug.ap, 1.0))
        op("dve", [vw_aug], [], lambda h: h.memset(vw_aug.ap, 1.0))
        op("dve", [vcmp_aug], [], lambda h: h.memset(vcmp_aug.ap, 1.0))
        op("dve", [kcmpT], [], lambda h: h.memset(kcmpT.ap, 0.0))
        op("dve", [vcmpT], [], lambda h: h.memset(vcmpT.ap, 0.0))

        def tiled(v, slicer):
            trs = [Tr(f"t{t}") for t in range(4)]
            tiles = [V(slicer(v.ap, t), (trs[t],)) for t in range(4)]
            return V(v.ap, trs), tiles

        xh, xh_t = tiled(sbv("xh", [128, 4, D], F32), lambda ap, t: ap[:, t, :])
        tokb, tokb_t = tiled(sbv("tokb", [128, 4, D], BF16), lambda ap, t: ap[:, t, :])
        T8, T8_t = [], []
        for i in range(2):
            a_, t_l = tiled(sbv(f"T8_{i}", [128, 8, CH], BF16), lambda ap, t: ap[:, :, t * 128:(t + 1) * 128])
            T8.append(a_)
            T8_t.append(t_l)
        QT = sbv("QT", [128, 8, CH], BF16)
        op("dve", [QT], [], lambda h: h.memset(QT.ap[64:128, :, :], 0.0))
        rqxT = sbv("rqxT", [64, 4, CH], BF16)
        rkT = sbv("rkT", [64, 4, CH], BF16)
        kz = sbv("kz", [128, 4, 64], BF16)
        rv = sbv("rv", [128, 4, 512], BF16)
        rgs = sbv("rgs", [128, 4, 512], BF16)
        gates = sbv("gates", [128, 4, 24], F32)
        o_nsa = sbv("o_nsa", [128, 4, 512], F32)
        ss4 = sbv("ss4", [128, 4], F32)
        rstd4 = sbv("rstd4", [128, 4], F32)
        sqb = [sbv(f"sqb{i}", [128, CH], BF16) for i in range(2)]
        rstd_t = [sbv(f"rstd_t{i}", [128, CH], F32) for i in range(2)]
        qnb = [sbv(f"qnb{i}", [128, CH], BF16) for i in range(3)]
        big4k = fw.es.enter_context(nc.sbuf_tensor("sb_big4k", [128, 2 * CH], F32))
        _tra, _trb = Tr("t1a"), Tr("t1b")
        t1 = [V(big4k[:, 0:CH], (_tra,)), V(big4k[:, CH:2 * CH], (_trb,))]
        xs = V(big4k[:], (_tra, _trb))
        uptmp = [sbv(f"uptmp{i}", [128, CH], BF16) for i in range(2)]
        t2 = [sbv(f"t2_{i}", [128, CH], F32) for i in range(2)]
        PT = [sbv(f"PT{i}", [128, CH], BF16) for i in range(7)]
        HT = sbv("HT", [128, 2, 64], BF16)
        score = sbv("score", [128, 4, 32], F32)
        sctmp = sbv("sctmp", [128, 4, 32], F32)
        m8 = sbv("m8", [128, 4, 8], F32)
        thr = sbv("thr", [128, 4], F32)
        mneg = sbv("mneg", [128, 4, 96], BF16)
        op("dve", [mneg], [], lambda h: h.memset(mneg.ap, 0.0))
        rd = sbv("rd", [128, 4], F32)
        ff = sbv("ff", [128, 4], F32)
        otmp = sbv("otmp", [128, 4, 64], F32)
        innb = [sbv(f"innb{i}", [128, 128], BF16) for i in range(4)]
        st1 = sbv("st1", [128, 32], F32)
        st2 = sbv("st2", [128, 4], F32)
        stm = sbv("stm", [128, 4], F32)
        stv = sbv("stv", [128, 4], F32)
        actT = [sbv(f"actT{i}", [128, 4, CH], BF16) for i in range(2)]
        relu_t = [sbv(f"relu_t{i}", [128, CH], F32) for i in range(1)]
        WS = [sbv(f"WS{i}", [128, 4096], BF16) for i in range(4)]

        banks = [fw.psv(f"pb{i}", [128, 512], F32) for i in range(8)]
        rrA = [0]
        rrB = [0]

        def psA():
            b = banks[rrA[0] % 4]
            rrA[0] += 1
            return b

        def psB():
            b = banks[5 + rrB[0] % 3]
            rrB[0] += 1
            return b

        rrP = [0]

        def psP():
            b = banks[rrP[0] % 6]
            rrP[0] += 1
            return b

        rrC = [0]

        def psC():
            return banks[7]

        rr = {}

        def rot(name, lst):
            i = rr.get(name, 0)
            rr[name] = i + 1
            return lst[i % len(lst)]

        def wsrc_cols(wd, a, b):
            return V(wd.ap[:, a:b].rearrange("(kc p) n -> p kc n", p=128), wd.trs)

        items = []

        def add_item(parts):
            items.append(parts)
            return len(items) - 1

        witems = {}

        def kc_view(v):
            return V(v.ap.rearrange("(kc p) n -> p kc n", p=128), v.trs)

        for b in range(NB):
            for ci in range(NCH):
                d = {}
                for key in ("S0", "S5", "S1", "S2", "S3", "S4"):
                    d[key] = add_item([(lambda s: s.re("p (kc n) -> p kc n", kc=8), kc_view(scr[key]))])
                for key in ("CK", "CV"):
                    d[key] = add_item([(lambda s: s.re("p (i m) -> p i m", i=16),
                                        V(scr[key].ap.rearrange("(i p) m -> p i m", p=128), scr[key].trs))])
                for key in ("O0", "O1"):
                    d[key] = add_item([(lambda s: s.re("p (kc n) -> p kc n", kc=8), kc_view(scr[key]))])
                for fc in range(8):
                    d[("U", fc)] = add_item([(lambda s: s.re("p (kc n) -> p kc n", kc=8), kc_view(scr[("U", fc)]))])
                    d[("D", fc)] = add_item([(lambda s: s.re("p (fs n) -> p fs n", fs=4),
                                              V(scr[("D", fc)].ap.rearrange("(fs p) n -> p fs n", p=128),
                                                scr[("D", fc)].trs))])
                witems[(b, ci)] = d
        issued = [0]
        LOOK = 2

        def wget(idx):
            while issued[0] < len(items) and issued[0] <= idx + LOOK:
                k = issued[0]
                slot = WS[k % 4]
                for viewfn, src in items[k]:
                    dma("sp", viewfn(slot), src)
                issued[0] += 1
            return WS[idx % 4]

        def dump(name, v, shape):
            if name in dbg:
                dd = V(nc.dram_tensor("dbg_" + name, list(shape), F32, kind="ExternalOutput").ap(), ())
                dma("pool", dd, v)

        def mm(out, lhsT, rhs, start=True, stop=True):
            return op("pe", [out], [lhsT, rhs],
                      lambda h: h.matmul(out.ap, lhsT.ap, rhs.ap, start=start, stop=stop, skip_group_check=True))

        def transpose(out, in_, idn):
            return op("pe", [out], [in_, idn], lambda h: h.transpose(out.ap, in_.ap, idn.ap))

        def act(out, in_, func, bias=None, scale=1.0, accum=None):
            outs = [out] + ([accum] if accum is not None else [])
            ins = [in_] + ([bias] if isinstance(bias, V) else []) + ([scale] if isinstance(scale, V) else [])
            kw = {}
            if bias is not None:
                kw["bias"] = bias.ap if isinstance(bias, V) else bias
            if accum is not None:
                kw["accum_out"] = accum.ap
            sc = scale.ap if isinstance(scale, V) else scale
            return op("act", outs, ins, lambda h: h.activation(out.ap, in_.ap, func, scale=sc, **kw))

        def tt(out, a, b, alu, eng="dve"):
            return op(eng, [out], [a, b], lambda h: h.tensor_tensor(out.ap, a.ap, b.ap, alu))

        def ts(out, a, s1, s2, op0, op1=None):
            ins = [a] + [s for s in (s1, s2) if isinstance(s, V)]
            a1 = s1.ap if isinstance(s1, V) else s1
            a2 = s2.ap if isinstance(s2, V) else s2
            if op1 is None:
                return op("dve", [out], ins, lambda h: h.tensor_scalar(out.ap, a.ap, a1, a2, op0))
            return op("dve", [out], ins, lambda h: h.tensor_scalar(out.ap, a.ap, a1, a2, op0, op1))

        def stt(out, a, sc, b, op0, op1):
            ins = [a, b] + ([sc] if isinstance(sc, V) else [])
            s_ = sc.ap if isinstance(sc, V) else sc
            return op("dve", [out], ins, lambda h: h.scalar_tensor_tensor(out.ap, a.ap, s_, b.ap, op0, op1))

        def vcopy(out, in_):
            return op("dve", [out], [in_], lambda h: h.tensor_copy(out.ap, in_.ap))

        def recip(out, in_):
            return op("dve", [out], [in_], lambda h: h.reciprocal(out.ap, in_.ap))

        def rsum(out, in_):
            return op("dve", [out], [in_], lambda h: h.reduce_sum(out.ap, in_.ap, AX.X))

        def ln_tile(t_, src=None):
            if src is None:
                src = xh_t[t_]
            op("dve", [ss4], [], lambda h: h.memset(ss4.ap[:, t_:t_ + 1], 0.0))
            act(tokb_t[t_], src, AF.Square, accum=ss4[:, t_:t_ + 1])
            act(rstd4[:, t_:t_ + 1], ss4[:, t_:t_ + 1], AF.Sqrt, bias=epsb, scale=1.0 / D)
            recip(rstd4[:, t_:t_ + 1], rstd4[:, t_:t_ + 1])
            ts(tokb_t[t_], src, rstd4[:, t_:t_ + 1], None, ALU.mult)

        def T_tile(di, t_, lnw=None):
            dst_t = T8_t[di][t_]
            for half in range(2):
                pb = psA()
                pbb = pb.bitcast(BF16)
                for j in range(4):
                    kc = half * 4 + j
                    transpose(pbb[:, j * 128:(j + 1) * 128], tokb_t[t_][:, kc * 128:(kc + 1) * 128], ident)
                src = pbb[:, 0:512].re("p (j q) -> p j q", j=4)
                dst = dst_t[:, half * 4:(half + 1) * 4, :]
                if lnw is None:
                    act(dst, src, AF.Copy)
                else:
                    tt(dst, src, lnw[:, half * 4:(half + 1) * 4].un(2).bc([128, 4, 128]), ALU.mult)

        def rmsnorm_to_T(lnw, di):
            for t_ in range(4):
                ln_tile(t_)
            for t_ in range(4):
                T_tile(di, t_, lnw)

        def proj_fm(slot3, c0, M, xT):
            pb = psA()
            for kc in range(8):
                mm(pb[0:M, :], slot3[:, kc, c0:c0 + M], xT[:, kc, :], start=(kc == 0), stop=(kc == 7))
            return pb

        def norm_rope(pb, M, wcol, t0, outs, post=None):
            qn = rot("qnb", qnb)
            if wcol is not None:
                sq = rot("sqb", sqb)
                act(sq[0:M, :], pb[0:M, :], AF.Square)
                pm = psA()
                mm(pm[0:M, :], onesb[0:M, 0:M], sq[0:M, :])
                rs = rot("rstd_t", rstd_t)
                act(rs[0:M, :], pm[0:M, :], AF.Sqrt, bias=epsb[0:M, :])
                recip(rs[0:M, :], rs[0:M, :])
                stt(qn[0:M, :], pb[0:M, :], qkw[0:M, wcol:wcol + 1], rs[0:M, :], ALU.mult, ALU.mult)
            else:
                act(qn[0:M, :], pb[0:M, :], AF.Copy)
            pr = psA()
            mm(pr[0:M, :], rt[0:M, 0:M], qn[0:M, :])
            a = rot("t1", t1)
            b_ = rot("t2", t2)
            tt(a[0:M, :], qn[0:M, :], cos[0:M, t0:t0 + CH], ALU.mult)
            tt(b_[0:M, :], pr[0:M, :], sin[0:M, t0:t0 + CH], ALU.mult)
            if post is not None:
                tt(a[0:M, :], a[0:M, :], b_[0:M, :], ALU.add)
                for dst, r0, r1 in outs:
                    tt(dst, a[r0:r1, :].re("p (s q) -> p s q", s=4), post, ALU.mult)
            else:
                for dst, r0, r1 in outs:
                    tt(dst, a[r0:r1, :], b_[r0:r1, :], ALU.add)

        def setup_bias():
            for kv, (wd, pe) in enumerate(((wk1_d, pek), (wv1_d, pev))):
                slot = WS[kv]
                s3 = slot.re("p (i m) -> p i m", i=16)
                dma("pool", s3, V(wd.ap.rearrange("(i p) m -> p i m", p=128), ()))
                pb = psA()
                for mh in range(2):
                    for i in range(16):
                        mm(pb[:, mh:mh + 1], s3[:, i, mh * 128:(mh + 1) * 128], pe[:, i:i + 1],
                           start=(i == 0), stop=(i == 15))
                vcopy(biasH[:, kv * 2:kv * 2 + 2], pb[:, 0:2])

        setup_bias()

        prefetched = [False]
        for b in range(NB):
            op("dve", [state], [], lambda h: h.memset(state.ap, 0.0))
            op("dve", [state_bf], [], lambda h: h.memset(state_bf.ap, 0.0))
            op("dve", [kcw], [], lambda h: h.memset(kcw.ap, 0.0))
            op("dve", [vcw], [], lambda h: h.memset(vcw.ap, 0.0))
            for ci in range(NCH):
                t0 = ci * CH
                wi = witems[(b, ci)]
                def chunk_body(b=b, ci=ci, t0=t0, wi=wi):
                    dma("sp", xh, V(x_d.ap[b, t0:t0 + CH, :].rearrange("(t p) d -> p t d", p=128), ()))
                    xT = T8[0]
                    if stage < 0.5:
                        return
                    if not prefetched[0]:
                        rmsnorm_to_T(lnw1, 0)
                    if stage < 1.01:
                        return

                    def vc_g(g):
                        s1_ = wget(wi["S1"]).re("p (kc n) -> p kc n", kc=8)
                        if True:
                            pb = psC()
                            for kc in range(8):
                                mm(pb, s1_[:, kc, 256 + g * 128:256 + (g + 1) * 128], xT[:, kc, :],
                                   start=(kc == 0), stop=(kc == 7))
                            act(vcw[0:64, g, 16:528], pb[0:64, :], AF.Copy)
                            act(vcw[64:128, g, 15:527], pb[64:128, :], AF.Copy)

                    def kw_tile(t_):
                        s2_ = wget(wi["S2"]).re("p (kc n) -> p kc n", kc=8)
                        if True:
                            tile_i = ci * 4 + t_
                            pb = psC()
                            for kc in range(8):
                                mm(pb[:, 0:128], xT[:, kc, t_ * 128:(t_ + 1) * 128], s2_[:, kc, 128:256],
                                   start=(kc == 0), stop=(kc == 7))
                            for kc in range(8):
                                mm(pb[:, 128:256], xT[:, kc, t_ * 128:(t_ + 1) * 128], s2_[:, kc, 384:512],
                                   start=(kc == 0), stop=(kc == 7))
                            for kc in range(8):
                                mm(pb[:, 256:280], xT[:, kc, t_ * 128:(t_ + 1) * 128], wg[:, kc, :],
                                   start=(kc == 0), stop=(kc == 7))
                            act(vs_aug[:, tile_i, :, 0:64], pb[:, 0:128].re("p (g d) -> p g d", g=2), AF.Copy)
                            act(vw_aug[:, tile_i, :, 0:64], pb[:, 128:256].re("p (g d) -> p g d", g=2), AF.Copy)
                            act(gates[:, t_, :], pb[:, 256:280], AF.Sigmoid)

                    if ci > 0:
                        for g in range(2):
                            vcopy(kcw[0:64, g, 0:16], kcw[0:64, g, 512:528])
                            vcopy(kcw[64:128, g, 0:15], kcw[64:128, g, 512:527])
                            vcopy(vcw[0:64, g, 0:16], vcw[0:64, g, 512:528])
                            vcopy(vcw[64:128, g, 0:15], vcw[64:128, g, 512:527])
                    tasks = []
                    for j in range(4):
                        tasks.append(dict(key="S0", c0=j * 128, M=128, wcol=0,
                                          outs=[(QT[0:64, 2 * j, :], 0, 64, False), (QT[0:64, 2 * j + 1, :], 64, 128, True)]))
                    for g in range(2):
                        tasks.append(dict(key="S1", c0=g * 128, M=128, wcol=1,
                                          outs=[(kcw[0:64, g, 16:528], 0, 64, False),
                                                (kcw[64:128, g, 15:527], 64, 128, False)]))
                    tasks.append(dict(key="S2", c0=0, M=128, wcol=2,
                                      outs=[(ksT[0:64, 0, t0:t0 + CH], 0, 64, False), (ksT[0:64, 1, t0:t0 + CH], 64, 128, True)]))
                    tasks.append(dict(key="S2", c0=256, M=128, wcol=3,
                                      outs=[(kwT[:, 0, t0:t0 + CH], 0, 64, False), (kwT[:, 1, t0:t0 + CH], 64, 128, True)]))
                    for j in range(2):
                        tasks.append(dict(key="S3", c0=j * 128, M=128, wcol=None,
                                          outs=[(rqxT[:, 2 * j, :], 0, 64, False), (rqxT[:, 2 * j + 1, :], 64, 128, True)],
                                          post=xit[:, j, :].un(1).bc([128, 4, 128])))
                    for j in range(2):
                        tasks.append(dict(key="S3", c0=256 + j * 128, M=128, wcol=None,
                                          outs=[(rkT[:, 2 * j, :], 0, 64, False), (rkT[:, 2 * j + 1, :], 64, 128, True)]))
                    nt = len(tasks)
                    fillers = {8: [lambda: kw_tile(0)], 9: [lambda: kw_tile(1)], 10: [lambda: kw_tile(2)],
                               11: [lambda: kw_tile(3)]}

                    def rv_tile(t_):
                        s4_ = wget(wi["S4"]).re("p (kc n) -> p kc n", kc=8)
                        pb = psC()
                        for kc in range(8):
                            mm(pb, xT[:, kc, t_ * 128:(t_ + 1) * 128], s4_[:, kc, :], start=(kc == 0), stop=(kc == 7))
                        act(rv[:, t_, :], pb, AF.Copy)

                    def rg_tile(t_):
                        s5_ = wget(wi["S5"]).re("p (kc n) -> p kc n", kc=8)
                        pb = psC()
                        for kc in range(8):
                            mm(pb, xT[:, kc, t_ * 128:(t_ + 1) * 128], s5_[:, kc, :], start=(kc == 0), stop=(kc == 7))
                        act(rgs[:, t_, :], pb, AF.Silu)

                    for t_ in range(4):
                        fillers[1 + t_] = [lambda t_=t_: rg_tile(t_)]
                    fillers[6] = [lambda: vc_g(0)]
                    fillers[7] = [lambda: vc_g(1)]
                    fillers[12] = [lambda: rv_tile(0), lambda: rv_tile(1)]
                    fillers[13] = [lambda: rv_tile(2), lambda: rv_tile(3)]
                    stt_ = [dict() for _ in tasks]
                    for k in range(nt + 2):
                        if k < nt:
                            T = tasks[k]
                            M = T["M"]
                            slot3 = wget(wi[T["key"]]).re("p (kc n) -> p kc n", kc=8)
                            pb = banks[k % 3]
                            for kc in range(8):
                                mm(pb[0:M, :], slot3[:, kc, T["c0"]:T["c0"] + M], xT[:, kc, :],
                                   start=(kc == 0), stop=(kc == 7))
                            stt_[k]["pb"] = pb
                            if T["wcol"] is not None:
                                sq = sqb[k % 2]
                                act(sq[0:M, :], pb[0:M, :], AF.Square)
                                stt_[k]["sq"] = sq
                            if "after" in T:
                                T["after"]()
                        j = k - 1
                        if 0 <= j < nt:
                            T = tasks[j]
                            M = T["M"]
                            pb = stt_[j]["pb"]
                            qn = qnb[j % 3]
                            if T["wcol"] is not None:
                                pm = banks[3 + j % 2]
                                mm(pm[0:M, :], onesb[0:M, 0:M], stt_[j]["sq"][0:M, :])
                                rs = rstd_t[j % 2]
                                act(rs[0:M, :], pm[0:M, :], AF.Ln, bias=epsb[0:M, :])
                                act(rs[0:M, :], rs[0:M, :], AF.Exp, scale=-0.5)
                                wc = T["wcol"]
                                stt(qn[0:M, :], pb[0:M, :], qkw[0:M, wc:wc + 1], rs[0:M, :], ALU.mult, ALU.mult)
                            else:
                                act(qn[0:M, :], pb[0:M, :], AF.Copy)
                            stt_[j]["qn"] = qn
                        for f_ in fillers.get(k, []):
                            f_()
                        i_ = k - 2
                        if 0 <= i_ < nt:
                            T = tasks[i_]
                            M = T["M"]
                            qn = stt_[i_]["qn"]
                            pr = banks[5 + i_ % 2]
                            mm(pr[0:M, :], rt[0:M, 0:M], qn[0:M, :])
                            a = t1[i_ % 2]
                            b_ = t2[i_ % 2]
                            tt(a[0:M, :], qn[0:M, :], cos[0:M, t0:t0 + CH], ALU.mult)
                            tt(b_[0:M, :], pr[0:M, :], sin[0:M, t0:t0 + CH], ALU.mult)
                            if T.get("post") is not None:
                                tt(a[0:M, :], a[0:M, :], b_[0:M, :], ALU.add)
                            for dst, r0, r1, via in T["outs"]:
                                tgt = uptmp[i_ % 2][r0:r1, :] if via else dst
                                if T.get("post") is not None:
                                    tt(tgt.re("p (s q) -> p s q", s=4), a[r0:r1, :].re("p (s q) -> p s q", s=4),
                                       T["post"][r0:r1], ALU.mult)
                                else:
                                    tt(tgt, a[r0:r1, :], b_[r0:r1, :], ALU.add)
                                if via:
                                    dma("sp", dst, tgt)
                    if stage < 1.5:
                        return
                    if stage < 3:
                        return
                    c0 = 32 * ci - 1
                    for kv, (key, srcw, dstT) in enumerate((("CK", kcw, kcmpT), ("CV", vcw, vcmpT))):
                        w1 = wget(wi[key]).re("p (i m) -> p i m", i=16)
                        w2 = wk2 if kv == 0 else wv2
                        for mh in range(2):
                            pb = psA()
                            for i in range(16):
                                mm(pb[:, 0:64].re("p (g c) -> p g c", g=2), w1[:, i, mh * 128:(mh + 1) * 128],
                                   srcw[:, :, 2 * i:2 * i + 512:16], start=(i == 0), stop=(i == 15))
                            act(HT[:, mh, :], pb[:, 0:64], AF.Silu, bias=biasH[:, kv * 2 + mh:kv * 2 + mh + 1])
                        for g in range(2):
                            pb = psA()
                            for mh in range(2):
                                mm(pb[0:64, 0:32], w2[:, mh, :], HT[:, mh, g * 32:(g + 1) * 32],
                                   start=(mh == 0), stop=(mh == 1))
                            if ci == 0:
                                vcopy(dstT[:, g, 0:31], pb[0:64, 1:32])
                            else:
                                vcopy(dstT[:, g, c0:c0 + 32], pb[0:64, 0:32])
                    for g in range(2):
                        pb = psA()
                        pbb = pb.bitcast(BF16)
                        transpose(pbb[:, 0:64], vcmpT[:, g, :], ident[0:64, 0:64])
                        vcopy(vcmp_aug[:, g, 0:64], pbb[:, 0:64])

                    if stage < 4:
                        return
                    ret_pieces = []
                    for s_ in range(4):
                        ts_ = slice(s_ * 128, (s_ + 1) * 128)
                        cell = {}

                        def p0(s_=s_, ts_=ts_, cell=cell):
                            pb = psA()
                            pbb = pb.bitcast(BF16)
                            for hr in range(4):
                                transpose(pbb[:, hr * 64:(hr + 1) * 64], rkT[:, hr, ts_], ident[0:64, 0:64])
                            tt(kz, pbb[:, 0:256].re("p (h d) -> p h d", h=4), zeta8.un(2).bc([128, 4, 64]), ALU.mult)

                        def p1(s_=s_, ts_=ts_, cell=cell):
                            for hr in range(4):
                                pi = psA()
                                mm(pi[:, 0:128], rkT[:, hr, ts_], rqxT[:, hr, ts_])
                                tt(innb[hr], pi[:, 0:128], dmatp[:, hr, :], ALU.mult)

                        def p2(s_=s_, ts_=ts_, cell=cell):
                            py = banks[4]
                            for hr in range(4):
                                mm(py[:, hr * 128:(hr + 1) * 128], innb[hr], rv[:, s_, hr * 128:(hr + 1) * 128],
                                   start=(hr == 0), stop=False)
                                mm(py[:, hr * 128:(hr + 1) * 128], rqxT[:, hr, ts_], state_bf[:, hr, :],
                                   start=False, stop=(hr == 3))
                            cell["py"] = py

                        def p2s(s_=s_, cell=cell):
                            py = cell["py"]
                            ysb_s = t2[s_ % 2].re("p (h v) -> p h v", h=4)
                            ysq = rot("t1", t1).re("p (h v) -> p h v", h=4)
                            cell["ysb"] = ysb_s
                            for hr in range(4):
                                act(ysb_s[:, hr, :], py[:, hr * 128:(hr + 1) * 128], AF.Copy,
                                    accum=st1[:, s_ * 8 + hr:s_ * 8 + hr + 1])
                            for hr in range(4):
                                act(ysq[:, hr, :], py[:, hr * 128:(hr + 1) * 128], AF.Square,
                                    accum=st1[:, s_ * 8 + 4 + hr:s_ * 8 + 5 + hr])

                        def p2b(s_=s_, cell=cell):
                            sm = st1[:, s_ * 8:s_ * 8 + 4]
                            sq_ = st1[:, s_ * 8 + 4:s_ * 8 + 8]
                            ts(stm, sm, 1.0 / 128, None, ALU.mult)
                            tt(stv, stm, stm, ALU.mult)
                            stt(stv, sq_, 1.0 / 128, stv, ALU.mult, ALU.subtract)

                        def p3(s_=s_, ts_=ts_, cell=cell):
                            pst = psA()
                            for hr in range(4):
                                mm(pst[0:64, hr * 128:(hr + 1) * 128], kz[:, hr, :], rv[:, s_, hr * 128:(hr + 1) * 128],
                                   start=(hr == 0), stop=(hr == 3))
                            for hr in range(4):
                                stt(state[:, hr, :], state[:, hr, :], gamma_c[hr], pst[0:64, hr * 128:(hr + 1) * 128],
                                    ALU.mult, ALU.add)

                        def p3b(s_=s_, cell=cell):
                            act(stv, stv, AF.Sqrt, bias=epsb)
                            act(state_bf, state, AF.Copy)

                        def p4(s_=s_, cell=cell):
                            recip(stv, stv)
                            stt(st2, stm, -1.0, stv, ALU.mult, ALU.mult)

                        def p4b(s_=s_, cell=cell):
                            ysb_s = cell["ysb"]
                            for hr in range(4):
                                act(ysb_s[:, hr, :], ysb_s[:, hr, :], AF.Identity, bias=st2[:, hr:hr + 1],
                                    scale=stv[:, hr:hr + 1])

                        def p5(s_=s_, cell=cell):
                            ysb_s = cell["ysb"]
                            tt(ysb_s, ysb_s, retw.re("p (h v) -> p h v", h=4), ALU.mult)
                            tt(tokb_t[s_][:, 512:1024].re("p (h v) -> p h v", h=4), ysb_s,
                               rgs[:, s_, :].re("p (h v) -> p h v", h=4), ALU.mult)

                        ret_pieces += [p0, p1, p2, p2s, p2b, p3, p3b, p4, p4b, p5]
                    op("dve", [st1], [], lambda h: h.memset(st1.ap, 0.0))
                    ret_pos = [0]

                    def next_piece():
                        if ret_pos[0] < len(ret_pieces):
                            ret_pieces[ret_pos[0]]()
                            ret_pos[0] += 1

                    def evac_branch(ob, hq, br, first):
                        o3 = ob[:, 0:260].re("p (q e) -> p q e", q=4)
                        ts(rd, o3[:, :, 64], 1e-30, None, ALU.max)
                        recip(rd, rd)
                        tt(ff, rd, gates[:, :, br * 8 + hq], ALU.mult)
                        dst = o_nsa[:, :, hq * 64:(hq + 1) * 64]
                        if first:
                            tt(dst, o3[:, :, 0:64], ff.un(2).bc([128, 4, 64]), ALU.mult)
                        else:
                            tt(otmp, o3[:, :, 0:64], ff.un(2).bc([128, 4, 64]), ALU.mult)
                            tt(dst, dst, otmp, ALU.add)

                    pipe = {"pend": [], "n": 0}

                    def pipe_push(sc_fn, pv_fn, depth=5):
                        sc_fn()
                        pipe["pend"].append(pv_fn)
                        while len(pipe["pend"]) > depth:
                            pipe["pend"].pop(0)()
                        pipe["n"] += 1
                        if pipe["n"] % 2 == 0:
                            next_piece()

                    def pipe_flush():
                        while pipe["pend"]:
                            pipe["pend"].pop(0)()

                    def run_pipe(steps, depth=5, final=None):
                        for i_s, (sc_fn, pv_fn) in enumerate(steps):
                            if final is not None and i_s == len(steps) - 1:
                                def pv_last(pv_fn=pv_fn):
                                    pv_fn()
                                    final()
                                pipe_push(sc_fn, pv_last, depth)
                            else:
                                pipe_push(sc_fn, pv_fn, depth)
                        if VAR == 'flush':
                            pipe_flush()

                    for g in range(2):
                        steps = []
                        for r in range(4):
                            hq = g * 4 + r
                            cell = {}

                            def sc_fn(hq=hq, cell=cell):
                                ps = psA()
                                mm(ps, kcmpT[:, g, :], QT[0:64, hq, :], start=True, stop=False)
                                mm(ps, ident, cmpmask[:, t0:t0 + CH], start=False, stop=True)
                                pt = rot("PT", PT)
                                act(pt, ps, AF.Exp, scale=0.125)
                                cell["pt"] = pt

                            def pv_fn(hq=hq, r=r, cell=cell):
                                pt = cell["pt"]
                                oc = psB()
                                oc3 = oc[:, 0:260].re("p (q e) -> p q e", q=4)
                                for qs in range(4):
                                    mm(oc3[:, qs, :], pt[:, qs * 128:(qs + 1) * 128], vcmp_aug[:, g, 0:65],
                                       start=(qs == 0), stop=(qs == 3))
                                psl = psA()
                                psl3 = psl[:, 0:132].re("p (q e) -> p q e", q=4)
                                for qs in range(4):
                                    mm(psl3[:, qs, :], pt[:, qs * 128:(qs + 1) * 128], ovl, start=(qs == 0),
                                       stop=(qs == 3))
                                evac_branch(oc, hq, 0, True)
                                tt(sctmp, psl3[:, :, 0:32], rd.un(2).bc([128, 4, 32]), ALU.mult)
                                if r == 0:
                                    tt(score, sctmp, selbias[:, ci * 4:ci * 4 + 4, :], ALU.add)
                                else:
                                    tt(score, score, sctmp, ALU.add)
                                next_piece()

                            steps.append((sc_fn, pv_fn))
                        run_pipe(steps)
                        pipe_flush()
                        for qs in range(4):
                            op("dve", [m8], [score], lambda h, qs=qs: h.max(m8.ap[:, qs, :], score.ap[:, qs, :]))
                        ts(thr, m8[:, :, 7], -0.5, None, ALU.max)
                        tt(sctmp, score, thr.un(2).bc([128, 4, 32]), ALU.is_ge)
                        ts(mneg[:, :, 64:96], sctmp, -NEG, NEG, ALU.mult, ALU.add)
                        def emit_mask(g=g):
                            pb = psA()
                            pbb = pb.bitcast(BF16)
                            for qs in range(4):
                                transpose(pbb[0:96, qs * 128:(qs + 1) * 128], mneg[:, qs, :], ident)
                            for r_ in range(4):
                                act(QT[64:96, g * 4 + r_, :], pbb[64:96, 0:512], AF.Copy)

                        for r in range(4):
                            hq = g * 4 + r
                            Qs = QT[:, hq, :]
                            Q = QT[0:64, hq, :]
                            if r == 3:
                                emit_mask()
                            owb = psB()
                            ow3 = owb[:, 0:260].re("p (q e) -> p q e", q=4)
                            ow_state = {"first": True}
                            steps = []
                            for kt in range(max(0, 4 * ci - 2), 4 * ci + 4):
                                cell = {}

                                def sc_fn(kt=kt, cell=cell):
                                    dg = kt - 4 * ci
                                    qa, qb = max(0, dg), min(3, dg + 2)
                                    ps = psA()
                                    mm(ps[:, qa * 128:(qb + 1) * 128], kwT[:, g, kt * 128:(kt + 1) * 128],
                                       Q[:, qa * 128:(qb + 1) * 128], start=True, stop=False)
                                    masks = []
                                    for qs in range(qa, qb + 1):
                                        dd = qs - dg
                                        if dd == 0:
                                            masks.append((qs, trile))
                                        elif dd == 2:
                                            masks.append((qs, trigt))
                                    for mi, (qs, mk_) in enumerate(masks):
                                        mm(ps[:, qs * 128:(qs + 1) * 128], ident, mk_, start=False,
                                           stop=(mi == len(masks) - 1))
                                    pt = rot("PT", PT)
                                    act(pt[:, qa * 128:(qb + 1) * 128], ps[:, qa * 128:(qb + 1) * 128], AF.Exp,
                                        scale=0.125)
                                    cell["pt"] = pt
                                    cell["qa"], cell["qb"] = qa, qb

                                def pv_fn(kt=kt, cell=cell, ow3=ow3, ow_state=ow_state, g=g):
                                    pt = cell["pt"]
                                    for qs in range(cell["qa"], cell["qb"] + 1):
                                        mm(ow3[:, qs, :], pt[:, qs * 128:(qs + 1) * 128], vw_aug[:, kt, g, 0:65],
                                           start=ow_state["first"], stop=(kt == 4 * ci + 3))
                                        ow_state["first"] = False

                                steps.append((sc_fn, pv_fn))
                            def fin_w(owb=owb, hq=hq):
                                evac_branch(owb, hq, 2, False)
                                next_piece()
                            run_pipe(steps, final=fin_w)
                            if VAR == 'flushw':
                                pipe_flush()

                        for r in range(4):
                            hq = g * 4 + r
                            Qs = QT[:, hq, :]
                            Q = QT[0:64, hq, :]
                            osb = psB()
                            os3 = osb[:, 0:260].re("p (q e) -> p q e", q=4)
                            nkt = 4 * ci + 4
                            steps = []
                            for kt in range(nkt):
                                cell = {}

                                def sc_fn(kt=kt, cell=cell):
                                    ps = psA()
                                    dg = kt - 4 * ci
                                    mm(ps, ksT[:, g, kt * 128:(kt + 1) * 128], Qs, start=True, stop=(dg < 0))
                                    qlo = max(0, dg)
                                    if dg >= 0:
                                        mm(ps[:, dg * 128:(dg + 1) * 128], ident, trile, start=False, stop=True)
                                    pt = rot("PT", PT)
                                    act(pt[:, qlo * 128:512], ps[:, qlo * 128:512], AF.Exp, scale=0.125)
                                    cell["pt"] = pt
                                    cell["qlo"] = qlo

                                def pv_fn(kt=kt, cell=cell, os3=os3, g=g, nkt=nkt):
                                    pt = cell["pt"]
                                    for qs in range(cell["qlo"], 4):
                                        mm(os3[:, qs, :], pt[:, qs * 128:(qs + 1) * 128], vs_aug[:, kt, g, 0:65],
                                           start=(kt == 0 and qs == 0), stop=(kt == nkt - 1))

                                steps.append((sc_fn, pv_fn))
                            def fin_s(osb=osb, hq=hq):
                                evac_branch(osb, hq, 1, False)
                                next_piece()
                            run_pipe(steps, final=fin_s)
                            if VAR == 'flushs':
                                pipe_flush()
                    pipe_flush()
                    for t_ in range(4):
                        act(tokb_t[t_][:, 0:512], o_nsa[:, t_, :], AF.Copy)

                    pipe_flush()
                    while ret_pos[0] < len(ret_pieces):
                        next_piece()
                    wo_slots = [wget(wi["O0"]).re("p (kc n) -> p kc n", kc=8), wget(wi["O1"]).re("p (kc n) -> p kc n", kc=8)]
                    for t_ in range(4):
                        T_tile(0, t_)
                    for t_ in range(4):
                        for nh in range(2):
                            pb = psB()
                            for kc in range(8):
                                mm(pb, T8_t[0][t_][:, kc, :], wo_slots[nh][:, kc, :], start=(kc == 0), stop=(kc == 7))
                            dst = xh_t[t_][:, nh * 512:(nh + 1) * 512]
                            tt(dst, dst, pb, ALU.add)
                        ln_tile(t_)
                        if t_ >= 1:
                            T_tile(1, t_ - 1, lnw2)
                    T_tile(1, 3, lnw2)

                    if b == 0 and ci == NCH - 1:
                        dump("mix", tokb, [128, 4, D])
                        dump("gates", gates, [128, 4, 24])
                    if stage < 6:
                        return
                    hT = T8[1]
                    def ffn_up(fc):
                        U = wget(wi[("U", fc)]).re("p (kc n) -> p kc n", kc=8)
                        at = actT[fc % 2]
                        for fs in range(4):
                            pb = psA()
                            for kc in range(8):
                                mm(pb, U[:, kc, fs * 128:(fs + 1) * 128], hT[:, kc, :], start=(kc == 0), stop=(kc == 7))
                            rl = rot("relu_t", relu_t)
                            act(rl, pb, AF.Relu)
                            act(at[:, fs, :], rl, AF.Square)

                    def ffn_down(fc):
                        Dn = wget(wi[("D", fc)]).re("p (fs n) -> p fs n", fs=4)
                        at = actT[fc % 2]
                        for t_ in range(4):
                            for nh in range(2):
                                pd = psB()
                                for fs in range(4):
                                    mm(pd, at[:, fs, t_ * 128:(t_ + 1) * 128], Dn[:, fs, nh * 512:(nh + 1) * 512],
                                       start=(fs == 0), stop=(fs == 3))
                                dst = xh_t[t_][:, nh * 512:(nh + 1) * 512]
                                tt(dst, dst, pd, ALU.add)

                    nxt = None
                    if ci + 1 < NCH:
                        nxt = (b, ci + 1)
                    elif b + 1 < NB:
                        nxt = (b + 1, 0)

                    def pfA(t_):
                        nb_, nci = nxt
                        r0 = nci * CH + t_ * 128
                        dma("sp", xs, V(x_d.ap[nb_, r0:r0 + 128, :], ()))

                    def pfB(t_):
                        ln_tile(t_, src=xs)

                    def pfC(t_):
                        T_tile(0, t_, lnw1)

                    ffn_up(0)
                    for fc in range(8):
                        if fc + 1 < 8:
                            wget(wi[("D", fc)])
                            ffn_up(fc + 1)
                        if nxt is not None and stage >= 99:
                            if 2 <= fc <= 5:
                                pfC(fc - 2)
                            if 1 <= fc <= 4:
                                pfB(fc - 1)
                            if fc <= 3:
                                pfA(fc)
                        ffn_down(fc)
                    prefetched[0] = nxt is not None and stage >= 99

                chunk_body()
                dma("sp", V(out_d.ap[b, t0:t0 + CH, :].rearrange("(t p) d -> p t d", p=128), ()), xh)
        fw.finish()
        build.stats = {e: fw.E[e].n for e in fw.E}
        build.stats["waits"] = fw.nwaits
    return nc


_CONSTS = None


def _host_inputs(inputs, NB, cores):
    global _CONSTS
    if _CONSTS is None:
        _CONSTS = _consts()
    c = _CONSTS
    f = lambda a: np.ascontiguousarray(np.asarray(a, dtype=np.float32))
    shared = {
        "w_in": f(inputs["w_in"][0]), "w_out": f(inputs["w_out"][0]), "w_up": f(inputs["w_up"][0]),
        "w_down": f(inputs["w_down"][0]),
        "cmp_wk1": f(inputs["cmp_wk1"][0]), "cmp_wv1": f(inputs["cmp_wv1"][0]),
        "cmp_wk2": f(inputs["cmp_wk2"][0]), "cmp_wv2": f(inputs["cmp_wv2"][0]),
        "ln1_t": f(np.asarray(inputs["ln1_w"][0]).reshape(8, 128).T),
        "ln2_t": f(np.asarray(inputs["ln2_w"][0]).reshape(8, 128).T),
        "pek_t": f(np.asarray(inputs["cmp_pe_k"][0]).reshape(16, 128).T),
        "pev_t": f(np.asarray(inputs["cmp_pe_v"][0]).reshape(16, 128).T),
        "retw": f(np.asarray(inputs["ret_norm_w"][0]).reshape(1, 512)),
    }
    qn = np.asarray(inputs["q_norm_w"][0], dtype=np.float32)
    kn = np.asarray(inputs["k_norm_w"][0], dtype=np.float32)
    qkw = np.stack([qn, kn[0], kn[1], kn[2]], axis=1)
    shared["qkw_t"] = f(np.concatenate([qkw, qkw], 0))
    for k, v in c.items():
        if not k.startswith("_"):
            shared[k] = v
    x = np.asarray(inputs["x"], dtype=np.float32)
    maps = []
    for ci in range(cores):
        m = dict(shared)
        m["x"] = np.ascontiguousarray(x[ci * NB:(ci + 1) * NB])
        maps.append(m)
    return maps


def kernel(**inputs):
    global _CONSTS
    if _CONSTS is None:
        _CONSTS = _consts()
    n = 8
    NB = 2
    nc = build(NB=NB, NCH=NCH_FULL, consts=_CONSTS)
    maps = _host_inputs(inputs, NB, n)
    res = run_bass_kernel_spmd(nc, maps, core_ids=list(range(n)))
    out = np.concatenate([r["out"] for r in res.results], axis=0)
    return out.astype(np.float32)
```

```python
import bisect
import math
import os
VAR = os.environ.get('KVAR', '')
from contextlib import ExitStack

import numpy as np
import concourse.bass as bass
import concourse.mybir as mybir
from concourse.bass_utils import run_bass_kernel_spmd

F32 = mybir.dt.float32
BF16 = mybir.dt.bfloat16
ALU = mybir.AluOpType
AF = mybir.ActivationFunctionType
AX = mybir.AxisListType

NEG = -30000.0
S = 2048
D = 1024
CH = 512
NCH_FULL = S // CH
EPS = 1e-6


class Tr:
    __slots__ = ("w", "r", "name", "excl")

    def __init__(self, name="", excl=False):
        self.w = None
        self.r = []
        self.name = name
        self.excl = excl


class V:
    __slots__ = ("ap", "trs")

    def __init__(self, ap, trs):
        self.ap = ap
        self.trs = tuple(trs)

    def __getitem__(self, k):
        return V(self.ap[k], self.trs)

    def re(self, pattern_, **kw):
        return V(self.ap.rearrange(pattern_, **kw), self.trs)

    def bc(self, shape):
        return V(self.ap.broadcast_to(list(shape)), self.trs)

    def un(self, axis):
        return V(self.ap.unsqueeze(axis), self.trs)

    def bitcast(self, dt):
        return V(self.ap.bitcast(dt), self.trs)


class Eng:
    def __init__(self, name, h, sem, is_pe=False):
        self.name = name
        self.h = h
        self.sem = sem
        self.count = 0
        self.n = 0
        self.last = None
        self.tick_idx = []
        self.tick_val = []
        self.seen = {}
        self.is_pe = is_pe


class FW:
    def __init__(self, nc, es, n_dma_sems=24):
        self.nc = nc
        self.es = es
        mk = lambda nm: es.enter_context(nc.semaphore(nm))
        self.E = {
            "pe": Eng("pe", nc.tensor, mk("s_pe"), True),
            "act": Eng("act", nc.scalar, mk("s_act")),
            "dve": Eng("dve", nc.vector, mk("s_dve")),
            "pool": Eng("pool", nc.gpsimd, mk("s_pool")),
            "sp": Eng("sp", nc.sync, mk("s_sp")),
        }
        self.dsems = [[mk(f"s_dma{i}"), 0] for i in range(n_dma_sems)]
        half_ = n_dma_sems // 2
        self.dpool = {"sp": list(range(0, half_)), "pool": list(range(half_, n_dma_sems))}
        self.dnext = {"sp": 0, "pool": 0}
        self.nwaits = 0

    def sbv(self, name, shape, dt=F32):
        t = self.es.enter_context(self.nc.sbuf_tensor("sb_" + name, list(shape), dt))
        return V(t[:], (Tr(name),))

    def psv(self, name, shape, dt=F32):
        t = self.es.enter_context(self.nc.psum_tensor(name, list(shape), dt))
        return V(t[:], (Tr(name, excl=True),))

    def _ticket(self, X, idx):
        k = bisect.bisect_left(X.tick_idx, idx)
        if k < len(X.tick_idx):
            return X.tick_val[k]
        assert X.last is not None and X.n - 1 >= idx
        X.count += 1
        X.last.then_inc(X.sem, 1)
        X.tick_idx.append(X.n - 1)
        X.tick_val.append(X.count)
        return X.count

    def _wait(self, E, ref):
        if ref[0] == "dma":
            _, si, val = ref
            key = ("d", si)
            if E.seen.get(key, 0) >= val:
                return
            E.h.wait_ge(self.dsems[si][0], val)
            E.seen[key] = val
        else:
            _, xn, idx = ref
            X = self.E[xn]
            tk = self._ticket(X, idx)
            if E.seen.get(xn, 0) >= tk:
                return
            E.h.wait_ge(X.sem, tk)
            E.seen[xn] = tk
        self.nwaits += 1

    def _deps(self, E, outs, ins):
        en = E.name
        for v in ins:
            for tr in v.trs:
                w = tr.w
                if w is not None:
                    if not (w[0] == "e" and w[1] == en and E.is_pe):
                        self._wait(E, w)
                if tr.excl:
                    for r in tr.r:
                        if r[0] == "e" and r[1] == en:
                            continue
                        self._wait(E, r)
        for v in outs:
            for tr in v.trs:
                w = tr.w
                if w is not None and not (w[0] == "e" and w[1] == en and E.is_pe):
                    self._wait(E, w)
                for r in tr.r:
                    if r[0] == "e" and r[1] == en and E.is_pe:
                        continue
                    self._wait(E, r)

    def _record(self, ref, outs, ins):
        for v in ins:
            for tr in v.trs:
                tr.r.append(ref)
                if len(tr.r) > 32:
                    best = {}
                    for r in tr.r:
                        key = (r[0], r[1])
                        if key not in best or r[2] > best[key][2]:
                            best[key] = r
                    tr.r = list(best.values())
        for v in outs:
            for tr in v.trs:
                tr.w = ref
                tr.r = []

    def op(self, en, outs, ins, fn):
        E = self.E[en]
        self._deps(E, outs, ins)
        inst = fn(E.h)
        E.last = inst
        idx = E.n
        E.n += 1
        self._record(("e", en, idx), outs, ins)
        return inst

    def dma(self, qn, out, in_, **kw):
        E = self.E[qn]
        if E.n > 0 and (not E.tick_idx or E.tick_idx[-1] < E.n - 1):
            self._ticket(E, E.n - 1)
        self._deps(E, [out], [in_])
        pl = self.dpool[qn]
        si = pl[self.dnext[qn] % len(pl)]
        self.dnext[qn] += 1
        sem, val = self.dsems[si]
        if val > 0:
            self._wait(E, ("dma", si, val))
        inst = E.h.dma_start(out=out.ap, in_=in_.ap, **kw)
        inst.then_inc(sem, 16)
        self.dsems[si][1] = val + 16
        E.last = inst
        E.n += 1
        E.tick_idx.append(E.n - 1)
        E.tick_val.append(E.count)
        self._record(("dma", si, val + 16), [out], [in_])
        return inst

    def finish(self):
        E = self.E["sp"]
        for si, (sem, val) in enumerate(self.dsems):
            if val > 0:
                self._wait(E, ("dma", si, val))


def _consts():
    c = {}
    c["c_ident"] = np.eye(128, dtype=np.float32)
    R = np.zeros((64, 64), np.float32)
    for m in range(32):
        R[m, m + 32] = -1.0
        R[m + 32, m] = 1.0
    rt = np.zeros((128, 128), np.float32)
    rt[:64, :64] = R.T
    rt[64:, 64:] = R.T
    c["c_rt"] = rt
    on = np.zeros((128, 128), np.float32)
    on[:64, :64] = 1.0 / 64
    on[64:, 64:] = 1.0 / 64
    c["c_ones"] = on
    half = 32
    inv = (10000.0 ** (-np.arange(half, dtype=np.float32) / half)).astype(np.float32)
    ang = np.arange(S, dtype=np.float32)[:, None] * inv[None, :]
    cos = np.cos(ang).astype(np.float32).T
    sin = np.sin(ang).astype(np.float32).T
    c["c_cos"] = np.concatenate([cos, cos, cos, cos], 0)
    c["c_sin"] = np.concatenate([sin, sin, sin, sin], 0)
    cc = np.arange(128)[:, None]
    tt = np.arange(S)[None, :]
    c["c_cmpmask"] = np.where((16 * cc + 31 <= tt) & (cc < 127), 0.0, NEG).astype(np.float32)
    k = np.arange(128)[:, None]
    q = np.arange(128)[None, :]
    c["c_trile"] = np.where(k <= q, 0.0, NEG).astype(np.float32)
    c["c_trigt"] = np.where(k > q, 0.0, NEG).astype(np.float32)
    E = np.zeros((32, S), np.float32)
    for j in range(32):
        E[j, 64 * j:64 * j + 64] = 1.0
    c["c_E"] = E
    cs = np.arange(127) * 16
    jb = np.arange(32)
    ov = ((cs[:, None] < (jb[None, :] + 1) * 64) & (cs[:, None] + 32 > jb[None, :] * 64)).astype(np.float32)
    ovl = np.zeros((128, 33), np.float32)
    ovl[:127, :32] = ov
    ovl[:127, 32] = 1.0
    c["c_ovl"] = ovl
    t = np.arange(S)
    cur = t // 64
    forced = (jb[None, :] == 0) | (jb[None, :] == cur[:, None]) | (jb[None, :] == cur[:, None] - 1)
    valid = jb[None, :] <= cur[:, None]
    bias = np.where(forced, 1.0e4, np.where(valid, 0.0, -1.0)).astype(np.float32)
    c["c_selbias"] = np.ascontiguousarray(bias.reshape(16, 128, 32).transpose(1, 0, 2))
    H = 4
    lg = np.log(1.0 - 2.0 ** (-5.0 - np.arange(H, dtype=np.float64)))
    i = np.arange(128, dtype=np.float64)
    dm = np.zeros((128, H, 128), np.float64)
    for h in range(H):
        dm[:, h, :] = (i[None, :] >= i[:, None]) * np.exp(-lg[h] * (i[:, None] + 1.0)) / 8.0
    c["c_dmatp"] = dm.astype(np.float32)
    c["c_zeta8"] = (np.exp(lg[None, :] * (127.0 - i[:, None])) / 8.0).astype(np.float32)
    xi = np.exp(lg[:, None] * (i[None, :] + 1.0))
    xi2 = np.zeros((128, 2, 128), np.float64)
    for j in range(2):
        xi2[:64, j, :] = xi[2 * j][None, :]
        xi2[64:, j, :] = xi[2 * j + 1][None, :]
    c["c_xi"] = xi2.astype(np.float32)
    c["_gamma_c"] = [float(np.exp(lg[h] * 128.0)) for h in range(H)]
    return c


CONST_SHAPES = None


def build(NB=2, NCH=NCH_FULL, consts=None, tab_dt=F32, dbg=(), stage=99):
    nc = bass.Bass("TRN2", target_bir_lowering=False)
    gamma_c = consts["_gamma_c"]

    def din(name, shape):
        return V(nc.dram_tensor(name, list(shape), F32, kind="ExternalInput").ap(), ())

    x_d = din("x", [NB, S, D])
    w_in_d = din("w_in", [D, 2840])
    w_out_d = din("w_out", [D, D])
    w_up_d = din("w_up", [D, 4096])
    w_down_d = din("w_down", [4096, D])
    wk1_d = din("cmp_wk1", [2048, 256])
    wv1_d = din("cmp_wv1", [2048, 256])
    wk2_d = din("cmp_wk2", [256, 64])
    wv2_d = din("cmp_wv2", [256, 64])
    lnw1_d = din("ln1_t", [128, 8])
    lnw2_d = din("ln2_t", [128, 8])
    qkw_d = din("qkw_t", [128, 4])
    pek_d = din("pek_t", [128, 16])
    pev_d = din("pev_t", [128, 16])
    retw_d = din("retw", [1, 512])
    cd = {k: din(k, v.shape) for k, v in consts.items() if not k.startswith("_")}
    out_d = V(nc.dram_tensor("out", [NB, S, D], F32, kind="ExternalOutput").ap(), ())

    with ExitStack() as es:
        fw = FW(nc, es)
        sbv, op, dma = fw.sbv, fw.op, fw.dma

        ident = sbv("ident", [128, 128], BF16)
        rt = sbv("rt", [128, 128], BF16)
        onesb = sbv("onesb", [128, 128], BF16)
        cos = sbv("cos", [128, S], tab_dt)
        sin = sbv("sin", [128, S], tab_dt)
        cmpmask = sbv("cmpmask", [128, S], BF16)
        trile = sbv("trile", [128, 128], BF16)
        trigt = sbv("trigt", [128, 128], BF16)
        ovl = sbv("ovl", [128, 33], BF16)
        selbias = sbv("selbias", [128, 16, 32], F32)
        dmatp = sbv("dmatp", [128, 4, 128], F32)
        zeta8 = sbv("zeta8", [128, 4], F32)
        xit = sbv("xit", [128, 2, 128], F32)
        lnw1 = sbv("lnw1", [128, 8], F32)
        lnw2 = sbv("lnw2", [128, 8], F32)
        qkw = sbv("qkw", [128, 4], F32)
        pek = sbv("pek", [128, 16], BF16)
        pev = sbv("pev", [128, 16], BF16)
        retw = sbv("retw", [128, 512], F32)
        wg = sbv("wg", [128, 8, 24], BF16)
        wk2 = sbv("wk2", [128, 2, 64], BF16)
        wv2 = sbv("wv2", [128, 2, 64], BF16)
        epsb = sbv("epsb", [128, 1], F32)
        biasH = sbv("biasH", [128, 4], F32)

        for dst, key in ((ident, "c_ident"), (rt, "c_rt"), (onesb, "c_ones"), (cmpmask, "c_cmpmask"),
                         (trile, "c_trile"), (trigt, "c_trigt"), (ovl, "c_ovl"),
                         (cos, "c_cos"), (sin, "c_sin")):
            dma("pool", dst, cd[key])
        for dst, src in ((selbias, cd["c_selbias"]), (dmatp, cd["c_dmatp"]), (zeta8, cd["c_zeta8"]),
                         (xit, cd["c_xi"]), (lnw1, lnw1_d), (lnw2, lnw2_d), (qkw, qkw_d)):
            dma("sp", dst, src)
        dma("pool", pek, pek_d)
        dma("pool", pev, pev_d)
        dma("sp", retw, V(retw_d.ap.broadcast_to([128, 512]), ()))
        dma("pool", wg, V(w_in_d.ap[:, 1280:1304].rearrange("(kc p) n -> p kc n", p=128), ()))
        dma("pool", wk2, V(wk2_d.ap.rearrange("(mh p) d -> p mh d", p=128), ()))
        dma("pool", wv2, V(wv2_d.ap.rearrange("(mh p) d -> p mh d", p=128), ()))
        op("dve", [epsb], [], lambda h: h.memset(epsb.ap, EPS))


        def scratch(name, src_v, rows, cols, piece_rows):
            t = nc.dram_tensor("scr_" + name, [rows, cols], BF16, kind="Internal").ap()
            trs = []
            for r0 in range(0, rows, piece_rows):
                tr = Tr(f"scr_{name}_{r0}")
                dma("pool", V(t[r0:r0 + piece_rows, :], (tr,)), V(src_v.ap[r0:r0 + piece_rows, :], ()))
                trs.append(tr)
            return V(t, trs)

        def conv_block(name, src_ap, rows, cols, nsplit=2):
            t = nc.dram_tensor("scr_" + name, [rows, cols], BF16, kind="Internal").ap()
            trs = []
            pr = rows // nsplit
            for r0 in range(0, rows, pr):
                tr = Tr(f"scr_{name}_{r0}")
                dma("pool", V(t[r0:r0 + pr, :], (tr,)), V(src_ap[r0:r0 + pr, :], ()))
                trs.append(tr)
            return V(t, trs)

        scr = {}
        scr["S0"] = conv_block("S0", w_in_d.ap[:, 0:512], D, 512)
        scr["S5"] = conv_block("S5", w_in_d.ap[:, 2328:2840], D, 512)
        s1_t = nc.dram_tensor("scr_s1", [D, 512], BF16, kind="Internal").ap()
        s1_trs = []
        for j, c0_ in enumerate([512, 512, 576, 576, 640, 640, 704, 704]):
            tr = Tr(f"scr_s1_{j}")
            dma("pool", V(s1_t[:, j * 64:(j + 1) * 64], (tr,)), V(w_in_d.ap[:, c0_:c0_ + 64], ()))
            s1_trs.append(tr)
        scr["S1"] = V(s1_t, s1_trs)
        scr["S2"] = conv_block("S2", w_in_d.ap[:, 768:1280], D, 512)
        scr["S3"] = conv_block("S3", w_in_d.ap[:, 1304:1816], D, 512)
        scr["S4"] = conv_block("S4", w_in_d.ap[:, 1816:2328], D, 512)
        scr["CK"] = conv_block("CK", wk1_d.ap, 2048, 256)
        scr["CV"] = conv_block("CV", wv1_d.ap, 2048, 256)
        scr["O0"] = conv_block("O0", w_out_d.ap[:, 0:512], D, 512)
        scr["O1"] = conv_block("O1", w_out_d.ap[:, 512:1024], D, 512)
        for fc in range(8):
            scr[("U", fc)] = conv_block(f"U{fc}", w_up_d.ap[:, fc * 512:(fc + 1) * 512], D, 512)
            scr[("D", fc)] = conv_block(f"D{fc}", w_down_d.ap[fc * 512:(fc + 1) * 512, :], 512, D)

        ksT = sbv("ksT", [128, 2, S], BF16)
        kwT = sbv("kwT", [64, 2, S], BF16)
        vs_aug = sbv("vs_aug", [128, 16, 2, 66], BF16)
        vw_aug = sbv("vw_aug", [128, 16, 2, 66], BF16)
        kcw = sbv("kcw", [128, 2, 544], BF16)
        vcw = sbv("vcw", [128, 2, 544], BF16)
        kcmpT = sbv("kcmpT", [64, 2, 128], BF16)
        vcmpT = sbv("vcmpT", [64, 2, 128], BF16)
        vcmp_aug = sbv("vcmp_aug", [128, 2, 66], BF16)
        state = sbv("state", [64, 4, 128], F32)
        state_bf = sbv("state_bf", [64, 4, 128], BF16)
        op("dve", [ksT], [], lambda h: h.memset(ksT.ap[64:128, :, :], 0.0))
        for g_ in range(2):
            dma("pool", ksT[64:96, g_, :], cd["c_E"])
        op("dve", [vs_aug], [], lambda h: h.memset(vs_aug.ap, 1.0))
        op("dve", [vw_aug], [], lambda h: h.memset(vw_aug.ap, 1.0))
        op("dve", [vcmp_aug], [], lambda h: h.memset(vcmp_aug.ap, 1.0))
        op("dve", [kcmpT], [], lambda h: h.memset(kcmpT.ap, 0.0))
        op("dve", [vcmpT], [], lambda h: h.memset(vcmpT.ap, 0.0))

        def tiled(v, slicer):
            trs = [Tr(f"t{t}") for t in range(4)]
            tiles = [V(slicer(v.ap, t), (trs[t],)) for t in range(4)]
            return V(v.ap, trs), tiles

        xh, xh_t = tiled(sbv("xh", [128, 4, D], F32), lambda ap, t: ap[:, t, :])
        tokb, tokb_t = tiled(sbv("tokb", [128, 4, D], BF16), lambda ap, t: ap[:, t, :])
        T8, T8_t = [], []
        for i in range(2):
            a_, t_l = tiled(sbv(f"T8_{i}", [128, 8, CH], BF16), lambda ap, t: ap[:, :, t * 128:(t + 1) * 128])
            T8.append(a_)
            T8_t.append(t_l)
        QT = sbv("QT", [128, 8, CH], BF16)
        op("dve", [QT], [], lambda h: h.memset(QT.ap[64:128, :, :], 0.0))
        rqxT = sbv("rqxT", [64, 4, CH], BF16)
        rkT = sbv("rkT", [64, 4, CH], BF16)
        kz = sbv("kz", [128, 4, 64], BF16)
        rv = sbv("rv", [128, 4, 512], BF16)
        rgs = sbv("rgs", [128, 4, 512], BF16)
        gates = sbv("gates", [128, 4, 24], F32)
        o_nsa = sbv("o_nsa", [128, 4, 512], F32)
        ss4 = sbv("ss4", [128, 4], F32)
        rstd4 = sbv("rstd4", [128, 4], F32)
        sqb = [sbv(f"sqb{i}", [128, CH], BF16) for i in range(2)]
        rstd_t = [sbv(f"rstd_t{i}", [128, CH], F32) for i in range(2)]
        qnb = [sbv(f"qnb{i}", [128, CH], BF16) for i in range(4)]
        big4k = fw.es.enter_context(nc.sbuf_tensor("sb_big4k", [128, 2 * CH], F32))
        _tra, _trb = Tr("t1a"), Tr("t1b")
        t1 = [V(big4k[:, 0:CH], (_tra,)), V(big4k[:, CH:2 * CH], (_trb,))]
        xs = V(big4k[:], (_tra, _trb))
        uptmp = [sbv(f"uptmp{i}", [128, CH], BF16) for i in range(2)]
        t2 = [sbv(f"t2_{i}", [128, CH], F32) for i in range(2)]
        PT = [sbv(f"PT{i}", [128, CH], BF16) for i in range(6)]
        HT = sbv("HT", [128, 2, 64], BF16)
        score = sbv("score", [128, 4, 32], F32)
        sctmp = sbv("sctmp", [128, 4, 32], F32)
        m8 = sbv("m8", [128, 4, 8], F32)
        thr = sbv("thr", [128, 4], F32)
        mneg = sbv("mneg", [128, 4, 96], BF16)
        op("dve", [mneg], [], lambda h: h.memset(mneg.ap, 0.0))
        rd = sbv("rd", [128, 4], F32)
        ff = sbv("ff", [128, 4], F32)
        otmp = sbv("otmp", [128, 4, 64], F32)
        innb = [sbv(f"innb{i}", [128, 128], BF16) for i in range(4)]
        st1 = sbv("st1", [128, 32], F32)
        st2 = sbv("st2", [128, 4], F32)
        stm = sbv("stm", [128, 4], F32)
        stv = sbv("stv", [128, 4], F32)
        actT = [sbv(f"actT{i}", [128, 4, CH], BF16) for i in range(2)]
        relu_t = [sbv(f"relu_t{i}", [128, CH], F32) for i in range(1)]
        WS = [sbv(f"WS{i}", [128, 4096], BF16) for i in range(4)]

        banks = [fw.psv(f"pb{i}", [128, 512], F32) for i in range(8)]
        rrA = [0]
        rrB = [0]

        def psA():
            b = banks[rrA[0] % 4]
            rrA[0] += 1
            return b

        def psB():
            b = banks[5 + rrB[0] % 3]
            rrB[0] += 1
            return b

        rrP = [0]

        def psP():
            b = banks[rrP[0] % 6]
            rrP[0] += 1
            return b

        rrC = [0]

        def psC():
            return banks[7]

        rr = {}

        def rot(name, lst):
            i = rr.get(name, 0)
            rr[name] = i + 1
            return lst[i % len(lst)]

        def wsrc_cols(wd, a, b):
            return V(wd.ap[:, a:b].rearrange("(kc p) n -> p kc n", p=128), wd.trs)

        items = []

        def add_item(parts):
            items.append(parts)
            return len(items) - 1

        witems = {}

        def kc_view(v):
            return V(v.ap.rearrange("(kc p) n -> p kc n", p=128), v.trs)

        for b in range(NB):
            for ci in range(NCH):
                d = {}
                for key in ("S0", "S5", "S1", "S2", "S3", "S4"):
                    d[key] = add_item([(lambda s: s.re("p (kc n) -> p kc n", kc=8), kc_view(scr[key]))])
                for key in ("CK", "CV"):
                    d[key] = add_item([(lambda s: s.re("p (i m) -> p i m", i=16),
                                        V(scr[key].ap.rearrange("(i p) m -> p i m", p=128), scr[key].trs))])
                for key in ("O0", "O1"):
                    d[key] = add_item([(lambda s: s.re("p (kc n) -> p kc n", kc=8), kc_view(scr[key]))])
                for fc in range(8):
                    d[("U", fc)] = add_item([(lambda s: s.re("p (kc n) -> p kc n", kc=8), kc_view(scr[("U", fc)]))])
                    d[("D", fc)] = add_item([(lambda s: s.re("p (fs n) -> p fs n", fs=4),
                                              V(scr[("D", fc)].ap.rearrange("(fs p) n -> p fs n", p=128),
                                                scr[("D", fc)].trs))])
                witems[(b, ci)] = d
        issued = [0]
        LOOK = 2

        def wget(idx):
            while issued[0] < len(items) and issued[0] <= idx + LOOK:
                k = issued[0]
                slot = WS[k % 4]
                for viewfn, src in items[k]:
                    dma("sp", viewfn(slot), src)
                issued[0] += 1
            return WS[idx % 4]

        def dump(name, v, shape):
            if name in dbg:
                dd = V(nc.dram_tensor("dbg_" + name, list(shape), F32, kind="ExternalOutput").ap(), ())
                dma("pool", dd, v)

        def mm(out, lhsT, rhs, start=True, stop=True):
            return op("pe", [out], [lhsT, rhs],
                      lambda h: h.matmul(out.ap, lhsT.ap, rhs.ap, start=start, stop=stop, skip_group_check=True))

        def transpose(out, in_, idn):
            return op("pe", [out], [in_, idn], lambda h: h.transpose(out.ap, in_.ap, idn.ap))

        def act(out, in_, func, bias=None, scale=1.0, accum=None):
            outs = [out] + ([accum] if accum is not None else [])
            ins = [in_] + ([bias] if isinstance(bias, V) else []) + ([scale] if isinstance(scale, V) else [])
            kw = {}
            if bias is not None:
                kw["bias"] = bias.ap if isinstance(bias, V) else bias
            if accum is not None:
                kw["accum_out"] = accum.ap
            sc = scale.ap if isinstance(scale, V) else scale
            return op("act", outs, ins, lambda h: h.activation(out.ap, in_.ap, func, scale=sc, **kw))

        def tt(out, a, b, alu, eng="dve"):
            return op(eng, [out], [a, b], lambda h: h.tensor_tensor(out.ap, a.ap, b.ap, alu))

        def ts(out, a, s1, s2, op0, op1=None):
            ins = [a] + [s for s in (s1, s2) if isinstance(s, V)]
            a1 = s1.ap if isinstance(s1, V) else s1
            a2 = s2.ap if isinstance(s2, V) else s2
            if op1 is None:
                return op("dve", [out], ins, lambda h: h.tensor_scalar(out.ap, a.ap, a1, a2, op0))
            return op("dve", [out], ins, lambda h: h.tensor_scalar(out.ap, a.ap, a1, a2, op0, op1))

        def stt(out, a, sc, b, op0, op1):
            ins = [a, b] + ([sc] if isinstance(sc, V) else [])
            s_ = sc.ap if isinstance(sc, V) else sc
            return op("dve", [out], ins, lambda h: h.scalar_tensor_tensor(out.ap, a.ap, s_, b.ap, op0, op1))

        def vcopy(out, in_):
            return op("dve", [out], [in_], lambda h: h.tensor_copy(out.ap, in_.ap))

        def recip(out, in_):
            return op("dve", [out], [in_], lambda h: h.reciprocal(out.ap, in_.ap))

        def rsum(out, in_):
            return op("dve", [out], [in_], lambda h: h.reduce_sum(out.ap, in_.ap, AX.X))

        def ln_tile(t_, src=None):
            if src is None:
                src = xh_t[t_]
            op("dve", [ss4], [], lambda h: h.memset(ss4.ap[:, t_:t_ + 1], 0.0))
            act(tokb_t[t_], src, AF.Square, accum=ss4[:, t_:t_ + 1])
            act(rstd4[:, t_:t_ + 1], ss4[:, t_:t_ + 1], AF.Sqrt, bias=epsb, scale=1.0 / D)
            recip(rstd4[:, t_:t_ + 1], rstd4[:, t_:t_ + 1])
            ts(tokb_t[t_], src, rstd4[:, t_:t_ + 1], None, ALU.mult)

        def T_tile(di, t_, lnw=None):
            dst_t = T8_t[di][t_]
            for half in range(2):
                pb = psA()
                pbb = pb.bitcast(BF16)
                for j in range(4):
                    kc = half * 4 + j
                    transpose(pbb[:, j * 128:(j + 1) * 128], tokb_t[t_][:, kc * 128:(kc + 1) * 128], ident)
                src = pbb[:, 0:512].re("p (j q) -> p j q", j=4)
                dst = dst_t[:, half * 4:(half + 1) * 4, :]
                if lnw is None:
                    act(dst, src, AF.Copy)
                else:
                    tt(dst, src, lnw[:, half * 4:(half + 1) * 4].un(2).bc([128, 4, 128]), ALU.mult)

        def rmsnorm_to_T(lnw, di):
            for t_ in range(4):
                ln_tile(t_)
            for t_ in range(4):
                T_tile(di, t_, lnw)

        def proj_fm(slot3, c0, M, xT):
            pb = psA()
            for kc in range(8):
                mm(pb[0:M, :], slot3[:, kc, c0:c0 + M], xT[:, kc, :], start=(kc == 0), stop=(kc == 7))
            return pb

        def norm_rope(pb, M, wcol, t0, outs, post=None):
            qn = rot("qnb", qnb)
            if wcol is not None:
                sq = rot("sqb", sqb)
                act(sq[0:M, :], pb[0:M, :], AF.Square)
                pm = psA()
                mm(pm[0:M, :], onesb[0:M, 0:M], sq[0:M, :])
                rs = rot("rstd_t", rstd_t)
                act(rs[0:M, :], pm[0:M, :], AF.Sqrt, bias=epsb[0:M, :])
                recip(rs[0:M, :], rs[0:M, :])
                stt(qn[0:M, :], pb[0:M, :], qkw[0:M, wcol:wcol + 1], rs[0:M, :], ALU.mult, ALU.mult)
            else:
                act(qn[0:M, :], pb[0:M, :], AF.Copy)
            pr = psA()
            mm(pr[0:M, :], rt[0:M, 0:M], qn[0:M, :])
            a = rot("t1", t1)
            b_ = rot("t2", t2)
            tt(a[0:M, :], qn[0:M, :], cos[0:M, t0:t0 + CH], ALU.mult)
            tt(b_[0:M, :], pr[0:M, :], sin[0:M, t0:t0 + CH], ALU.mult)
            if post is not None:
                tt(a[0:M, :], a[0:M, :], b_[0:M, :], ALU.add)
                for dst, r0, r1 in outs:
                    tt(dst, a[r0:r1, :].re("p (s q) -> p s q", s=4), post, ALU.mult)
            else:
                for dst, r0, r1 in outs:
                    tt(dst, a[r0:r1, :], b_[r0:r1, :], ALU.add)

        def setup_bias():
            for kv, (wd, pe) in enumerate(((wk1_d, pek), (wv1_d, pev))):
                slot = WS[kv]
                s3 = slot.re("p (i m) -> p i m", i=16)
                dma("pool", s3, V(wd.ap.rearrange("(i p) m -> p i m", p=128), ()))
                pb = psA()
                for mh in range(2):
                    for i in range(16):
                        mm(pb[:, mh:mh + 1], s3[:, i, mh * 128:(mh + 1) * 128], pe[:, i:i + 1],
                           start=(i == 0), stop=(i == 15))
                vcopy(biasH[:, kv * 2:kv * 2 + 2], pb[:, 0:2])

        setup_bias()

        prefetched = [False]
        for b in range(NB):
            op("dve", [state], [], lambda h: h.memset(state.ap, 0.0))
            op("dve", [state_bf], [], lambda h: h.memset(state_bf.ap, 0.0))
            op("dve", [kcw], [], lambda h: h.memset(kcw.ap, 0.0))
            op("dve", [vcw], [], lambda h: h.memset(vcw.ap, 0.0))
            for ci in range(NCH):
                t0 = ci * CH
                wi = witems[(b, ci)]
                def chunk_body(b=b, ci=ci, t0=t0, wi=wi):
                    dma("sp", xh, V(x_d.ap[b, t0:t0 + CH, :].rearrange("(t p) d -> p t d", p=128), ()))
                    xT = T8[0]
                    if stage < 0.5:
                        return
                    if not prefetched[0]:
                        rmsnorm_to_T(lnw1, 0)
                    if stage < 1.01:
                        return

                    def vc_g(g):
                        s1_ = wget(wi["S1"]).re("p (kc n) -> p kc n", kc=8)
                        if True:
                            pb = psC()
                            for kc in range(8):
                                mm(pb, s1_[:, kc, 256 + g * 128:256 + (g + 1) * 128], xT[:, kc, :],
                                   start=(kc == 0), stop=(kc == 7))
                            act(vcw[0:64, g, 16:528], pb[0:64, :], AF.Copy)
                            act(vcw[64:128, g, 15:527], pb[64:128, :], AF.Copy)

                    def kw_tile(t_):
                        s2_ = wget(wi["S2"]).re("p (kc n) -> p kc n", kc=8)
                        if True:
                            tile_i = ci * 4 + t_
                            pb = psC()
                            for kc in range(8):
                                mm(pb[:, 0:128], xT[:, kc, t_ * 128:(t_ + 1) * 128], s2_[:, kc, 128:256],
                                   start=(kc == 0), stop=(kc == 7))
                            for kc in range(8):
                                mm(pb[:, 128:256], xT[:, kc, t_ * 128:(t_ + 1) * 128], s2_[:, kc, 384:512],
                                   start=(kc == 0), stop=(kc == 7))
                            for kc in range(8):
                                mm(pb[:, 256:280], xT[:, kc, t_ * 128:(t_ + 1) * 128], wg[:, kc, :],
                                   start=(kc == 0), stop=(kc == 7))
                            act(vs_aug[:, tile_i, :, 0:64], pb[:, 0:128].re("p (g d) -> p g d", g=2), AF.Copy)
                            act(vw_aug[:, tile_i, :, 0:64], pb[:, 128:256].re("p (g d) -> p g d", g=2), AF.Copy)
                            act(gates[:, t_, :], pb[:, 256:280], AF.Sigmoid)

                    if ci > 0:
                        for g in range(2):
                            vcopy(kcw[0:64, g, 0:16], kcw[0:64, g, 512:528])
                            vcopy(kcw[64:128, g, 0:15], kcw[64:128, g, 512:527])
                            vcopy(vcw[0:64, g, 0:16], vcw[0:64, g, 512:528])
                            vcopy(vcw[64:128, g, 0:15], vcw[64:128, g, 512:527])
                    tasks = []
                    for j in range(4):
                        tasks.append(dict(key="S0", c0=j * 128, M=128, wcol=0,
                                          outs=[(QT[0:64, 2 * j, :], 0, 64, False), (QT[0:64, 2 * j + 1, :], 64, 128, True)]))
                    for g in range(2):
                        tasks.append(dict(key="S1", c0=g * 128, M=128, wcol=1,
                                          outs=[(kcw[0:64, g, 16:528], 0, 64, False),
                                                (kcw[64:128, g, 15:527], 64, 128, False)]))
                    tasks.append(dict(key="S2", c0=0, M=128, wcol=2,
                                      outs=[(ksT[0:64, 0, t0:t0 + CH], 0, 64, False), (ksT[0:64, 1, t0:t0 + CH], 64, 128, True)]))
                    tasks.append(dict(key="S2", c0=256, M=128, wcol=3,
                                      outs=[(kwT[:, 0, t0:t0 + CH], 0, 64, False), (kwT[:, 1, t0:t0 + CH], 64, 128, True)]))
                    for j in range(2):
                        tasks.append(dict(key="S3", c0=j * 128, M=128, wcol=None,
                                          outs=[(rqxT[:, 2 * j, :], 0, 64, False), (rqxT[:, 2 * j + 1, :], 64, 128, True)],
                                          post=xit[:, j, :].un(1).bc([128, 4, 128])))
                    for j in range(2):
                        tasks.append(dict(key="S3", c0=256 + j * 128, M=128, wcol=None,
                                          outs=[(rkT[:, 2 * j, :], 0, 64, False), (rkT[:, 2 * j + 1, :], 64, 128, True)]))
                    nt = len(tasks)
                    fillers = {8: [lambda: kw_tile(0)], 9: [lambda: kw_tile(1)], 10: [lambda: kw_tile(2)],
                               11: [lambda: kw_tile(3)]}

                    def rv_tile(t_):
                        s4_ = wget(wi["S4"]).re("p (kc n) -> p kc n", kc=8)
                        pb = psC()
                        for kc in range(8):
                            mm(pb, xT[:, kc, t_ * 128:(t_ + 1) * 128], s4_[:, kc, :], start=(kc == 0), stop=(kc == 7))
                        act(rv[:, t_, :], pb, AF.Copy)

                    def rg_tile(t_):
                        s5_ = wget(wi["S5"]).re("p (kc n) -> p kc n", kc=8)
                        pb = psC()
                        for kc in range(8):
                            mm(pb, xT[:, kc, t_ * 128:(t_ + 1) * 128], s5_[:, kc, :], start=(kc == 0), stop=(kc == 7))
                        act(rgs[:, t_, :], pb, AF.Silu)

                    for t_ in range(4):
                        fillers[1 + t_] = [lambda t_=t_: rg_tile(t_)]
                    fillers[6] = [lambda: vc_g(0)]
                    fillers[7] = [lambda: vc_g(1)]
                    fillers[12] = [lambda: rv_tile(0), lambda: rv_tile(1)]
                    fillers[13] = [lambda: rv_tile(2), lambda: rv_tile(3)]
                    stt_ = [dict() for _ in tasks]
                    for k in range(nt + 3):
                        if k < nt:
                            T = tasks[k]
                            M = T["M"]
                            slot3 = wget(wi[T["key"]]).re("p (kc n) -> p kc n", kc=8)
                            pb = banks[k % 3]
                            for kc in range(8):
                                mm(pb[0:M, :], slot3[:, kc, T["c0"]:T["c0"] + M], xT[:, kc, :],
                                   start=(kc == 0), stop=(kc == 7))
                            stt_[k]["pb"] = pb
                            if T["wcol"] is not None:
                                sq = sqb[k % 2]
                                act(sq[0:M, :], pb[0:M, :], AF.Square)
                                stt_[k]["sq"] = sq
                            if "after" in T:
                                T["after"]()
                        j = k - 1
                        if 0 <= j < nt:
                            T = tasks[j]
                            M = T["M"]
                            pb = stt_[j]["pb"]
                            qn = qnb[j % 4]
                            if T["wcol"] is not None:
                                pm = banks[3 + j % 2]
                                mm(pm[0:M, :], onesb[0:M, 0:M], stt_[j]["sq"][0:M, :])
                                rs = rstd_t[j % 2]
                                act(rs[0:M, :], pm[0:M, :], AF.Ln, bias=epsb[0:M, :])
                                act(rs[0:M, :], rs[0:M, :], AF.Exp, scale=-0.5)
                                wc = T["wcol"]
                                stt(qn[0:M, :], pb[0:M, :], qkw[0:M, wc:wc + 1], rs[0:M, :], ALU.mult, ALU.mult)
                            else:
                                act(qn[0:M, :], pb[0:M, :], AF.Copy)
                            stt_[j]["qn"] = qn
                        for f_ in fillers.get(k, []):
                            f_()
                        i_ = k - 3
                        if 0 <= i_ < nt:
                            T = tasks[i_]
                            M = T["M"]
                            qn = stt_[i_]["qn"]
                            pr = banks[5 + i_ % 2]
                            mm(pr[0:M, :], rt[0:M, 0:M], qn[0:M, :])
                            a = t1[i_ % 2]
                            b_ = t2[i_ % 2]
                            tt(a[0:M, :], qn[0:M, :], cos[0:M, t0:t0 + CH], ALU.mult)
                            tt(b_[0:M, :], pr[0:M, :], sin[0:M, t0:t0 + CH], ALU.mult)
                            if T.get("post") is not None:
                                tt(a[0:M, :], a[0:M, :], b_[0:M, :], ALU.add)
                            for dst, r0, r1, via in T["outs"]:
                                tgt = uptmp[i_ % 2][r0:r1, :] if via else dst
                                if T.get("post") is not None:
                                    tt(tgt.re("p (s q) -> p s q", s=4), a[r0:r1, :].re("p (s q) -> p s q", s=4),
                                       T["post"][r0:r1], ALU.mult)
                                else:
                                    tt(tgt, a[r0:r1, :], b_[r0:r1, :], ALU.add)
                                if via:
                                    dma("sp", dst, tgt)
                    if stage < 1.5:
                        return
                    if stage < 3:
                        return
                    c0 = 32 * ci - 1
                    for kv, (key, srcw, dstT) in enumerate((("CK", kcw, kcmpT), ("CV", vcw, vcmpT))):
                        w1 = wget(wi[key]).re("p (i m) -> p i m", i=16)
                        w2 = wk2 if kv == 0 else wv2
                        for mh in range(2):
                            pb = psA()
                            for i in range(16):
                                mm(pb[:, 0:64].re("p (g c) -> p g c", g=2), w1[:, i, mh * 128:(mh + 1) * 128],
                                   srcw[:, :, 2 * i:2 * i + 512:16], start=(i == 0), stop=(i == 15))
                            act(HT[:, mh, :], pb[:, 0:64], AF.Silu, bias=biasH[:, kv * 2 + mh:kv * 2 + mh + 1])
                        for g in range(2):
                            pb = psA()
                            for mh in range(2):
                                mm(pb[0:64, 0:32], w2[:, mh, :], HT[:, mh, g * 32:(g + 1) * 32],
                                   start=(mh == 0), stop=(mh == 1))
                            if ci == 0:
                                vcopy(dstT[:, g, 0:31], pb[0:64, 1:32])
                            else:
                                vcopy(dstT[:, g, c0:c0 + 32], pb[0:64, 0:32])
                    for g in range(2):
                        pb = psA()
                        pbb = pb.bitcast(BF16)
                        transpose(pbb[:, 0:64], vcmpT[:, g, :], ident[0:64, 0:64])
                        vcopy(vcmp_aug[:, g, 0:64], pbb[:, 0:64])

                    if stage < 4:
                        return
                    ret_pieces = []
                    for s_ in range(4):
                        ts_ = slice(s_ * 128, (s_ + 1) * 128)
                        cell = {}

                        def p0(s_=s_, ts_=ts_, cell=cell):
                            pb = psA()
                            pbb = pb.bitcast(BF16)
                            for hr in range(4):
                                transpose(pbb[:, hr * 64:(hr + 1) * 64], rkT[:, hr, ts_], ident[0:64, 0:64])
                            tt(kz, pbb[:, 0:256].re("p (h d) -> p h d", h=4), zeta8.un(2).bc([128, 4, 64]), ALU.mult)

                        def p1(s_=s_, ts_=ts_, cell=cell):
                            for hr in range(4):
                                pi = psA()
                                mm(pi[:, 0:128], rkT[:, hr, ts_], rqxT[:, hr, ts_])
                                tt(innb[hr], pi[:, 0:128], dmatp[:, hr, :], ALU.mult)

                        def p2(s_=s_, ts_=ts_, cell=cell):
                            py = banks[4]
                            for hr in range(4):
                                mm(py[:, hr * 128:(hr + 1) * 128], innb[hr], rv[:, s_, hr * 128:(hr + 1) * 128],
                                   start=(hr == 0), stop=False)
                                mm(py[:, hr * 128:(hr + 1) * 128], rqxT[:, hr, ts_], state_bf[:, hr, :],
                                   start=False, stop=(hr == 3))
                            cell["py"] = py

                        def p2s(s_=s_, cell=cell):
                            py = cell["py"]
                            ysb_s = t2[s_ % 2].re("p (h v) -> p h v", h=4)
                            ysq = rot("t1", t1).re("p (h v) -> p h v", h=4)
                            cell["ysb"] = ysb_s
                            for hr in range(4):
                                act(ysb_s[:, hr, :], py[:, hr * 128:(hr + 1) * 128], AF.Copy,
                                    accum=st1[:, s_ * 8 + hr:s_ * 8 + hr + 1])
                            for hr in range(4):
                                act(ysq[:, hr, :], py[:, hr * 128:(hr + 1) * 128], AF.Square,
                                    accum=st1[:, s_ * 8 + 4 + hr:s_ * 8 + 5 + hr])

                        def p2b(s_=s_, cell=cell):
                            sm = st1[:, s_ * 8:s_ * 8 + 4]
                            sq_ = st1[:, s_ * 8 + 4:s_ * 8 + 8]
                            ts(stm, sm, 1.0 / 128, None, ALU.mult)
                            tt(stv, stm, stm, ALU.mult)
                            stt(stv, sq_, 1.0 / 128, stv, ALU.mult, ALU.subtract)

                        def p3(s_=s_, ts_=ts_, cell=cell):
                            pst = psA()
                            for hr in range(4):
                                mm(pst[0:64, hr * 128:(hr + 1) * 128], kz[:, hr, :], rv[:, s_, hr * 128:(hr + 1) * 128],
                                   start=(hr == 0), stop=(hr == 3))
                            for hr in range(4):
                                stt(state[:, hr, :], state[:, hr, :], gamma_c[hr], pst[0:64, hr * 128:(hr + 1) * 128],
                                    ALU.mult, ALU.add)

                        def p3b(s_=s_, cell=cell):
                            act(stv, stv, AF.Sqrt, bias=epsb)
                            act(state_bf, state, AF.Copy)

                        def p4(s_=s_, cell=cell):
                            recip(stv, stv)
                            stt(st2, stm, -1.0, stv, ALU.mult, ALU.mult)

                        def p4b(s_=s_, cell=cell):
                            ysb_s = cell["ysb"]
                            for hr in range(4):
                                act(ysb_s[:, hr, :], ysb_s[:, hr, :], AF.Identity, bias=st2[:, hr:hr + 1],
                                    scale=stv[:, hr:hr + 1])

                        def p5(s_=s_, cell=cell):
                            ysb_s = cell["ysb"]
                            tt(ysb_s, ysb_s, retw.re("p (h v) -> p h v", h=4), ALU.mult)
                            tt(tokb_t[s_][:, 512:1024].re("p (h v) -> p h v", h=4), ysb_s,
                               rgs[:, s_, :].re("p (h v) -> p h v", h=4), ALU.mult)

                        ret_pieces += [p0, p1, p2, p2s, p2b, p3, p3b, p4, p4b, p5]
                    op("dve", [st1], [], lambda h: h.memset(st1.ap, 0.0))
                    ret_pos = [0]

                    def next_piece():
                        if ret_pos[0] < len(ret_pieces):
                            ret_pieces[ret_pos[0]]()
                            ret_pos[0] += 1

                    def evac_branch(ob, hq, br, first):
                        o3 = ob[:, 0:260].re("p (q e) -> p q e", q=4)
                        ts(rd, o3[:, :, 64], 1e-30, None, ALU.max)
                        recip(rd, rd)
                        tt(ff, rd, gates[:, :, br * 8 + hq], ALU.mult)
                        dst = o_nsa[:, :, hq * 64:(hq + 1) * 64]
                        if first:
                            tt(dst, o3[:, :, 0:64], ff.un(2).bc([128, 4, 64]), ALU.mult)
                        else:
                            tt(otmp, o3[:, :, 0:64], ff.un(2).bc([128, 4, 64]), ALU.mult)
                            tt(dst, dst, otmp, ALU.add)

                    pipe = {"pend": [], "n": 0}

                    def pipe_push(sc_fn, pv_fn, depth=5):
                        sc_fn()
                        pipe["pend"].append(pv_fn)
                        while len(pipe["pend"]) > depth:
                            pipe["pend"].pop(0)()
                        pipe["n"] += 1
                        if pipe["n"] % 2 == 0:
                            next_piece()

                    def pipe_flush():
                        while pipe["pend"]:
                            pipe["pend"].pop(0)()

                    def run_pipe(steps, depth=5, final=None):
                        for i_s, (sc_fn, pv_fn) in enumerate(steps):
                            if final is not None and i_s == len(steps) - 1:
                                def pv_last(pv_fn=pv_fn):
                                    pv_fn()
                                    final()
                                pipe_push(sc_fn, pv_last, depth)
                            else:
                                pipe_push(sc_fn, pv_fn, depth)
                        if VAR == 'flush':
                            pipe_flush()

                    for g in range(2):
                        steps = []
                        for r in range(4):
                            hq = g * 4 + r
                            cell = {}

                            def sc_fn(hq=hq, cell=cell):
                                ps = psA()
                                mm(ps, kcmpT[:, g, :], QT[0:64, hq, :], start=True, stop=False)
                                mm(ps, ident, cmpmask[:, t0:t0 + CH], start=False, stop=True)
                                pt = rot("PT", PT)
                                act(pt, ps, AF.Exp, scale=0.125)
                                cell["pt"] = pt

                            def pv_fn(hq=hq, r=r, cell=cell):
                                pt = cell["pt"]
                                oc = psB()
                                oc3 = oc[:, 0:260].re("p (q e) -> p q e", q=4)
                                for qs in range(4):
                                    mm(oc3[:, qs, :], pt[:, qs * 128:(qs + 1) * 128], vcmp_aug[:, g, 0:65],
                                       start=(qs == 0), stop=(qs == 3))
                                psl = psA()
                                psl3 = psl[:, 0:132].re("p (q e) -> p q e", q=4)
                                for qs in range(4):
                                    mm(psl3[:, qs, :], pt[:, qs * 128:(qs + 1) * 128], ovl, start=(qs == 0),
                                       stop=(qs == 3))
                                evac_branch(oc, hq, 0, True)
                                tt(sctmp, psl3[:, :, 0:32], rd.un(2).bc([128, 4, 32]), ALU.mult)
                                if r == 0:
                                    tt(score, sctmp, selbias[:, ci * 4:ci * 4 + 4, :], ALU.add)
                                else:
                                    tt(score, score, sctmp, ALU.add)
                                next_piece()

                            steps.append((sc_fn, pv_fn))
                        run_pipe(steps)
                        pipe_flush()
                        for qs in range(4):
                            op("dve", [m8], [score], lambda h, qs=qs: h.max(m8.ap[:, qs, :], score.ap[:, qs, :]))
                        ts(thr, m8[:, :, 7], -0.5, None, ALU.max)
                        tt(sctmp, score, thr.un(2).bc([128, 4, 32]), ALU.is_ge)
                        ts(mneg[:, :, 64:96], sctmp, -NEG, NEG, ALU.mult, ALU.add)
                        def emit_mask(g=g):
                            pb = psA()
                            pbb = pb.bitcast(BF16)
                            for qs in range(4):
                                transpose(pbb[0:96, qs * 128:(qs + 1) * 128], mneg[:, qs, :], ident)
                            for r_ in range(4):
                                act(QT[64:96, g * 4 + r_, :], pbb[64:96, 0:512], AF.Copy)

                        for r in range(4):
                            hq = g * 4 + r
                            Qs = QT[:, hq, :]
                            Q = QT[0:64, hq, :]
                            if r == 3:
                                emit_mask()
                            owb = psB()
                            ow3 = owb[:, 0:260].re("p (q e) -> p q e", q=4)
                            ow_state = {"first": True}
                            steps = []
                            for kt in range(max(0, 4 * ci - 2), 4 * ci + 4):
                                cell = {}

                                def sc_fn(kt=kt, cell=cell):
                                    dg = kt - 4 * ci
                                    qa, qb = max(0, dg), min(3, dg + 2)
                                    ps = psA()
                                    mm(ps[:, qa * 128:(qb + 1) * 128], kwT[:, g, kt * 128:(kt + 1) * 128],
                                       Q[:, qa * 128:(qb + 1) * 128], start=True, stop=False)
                                    masks = []
                                    for qs in range(qa, qb + 1):
                                        dd = qs - dg
                                        if dd == 0:
                                            masks.append((qs, trile))
                                        elif dd == 2:
                                            masks.append((qs, trigt))
                                    for mi, (qs, mk_) in enumerate(masks):
                                        mm(ps[:, qs * 128:(qs + 1) * 128], ident, mk_, start=False,
                                           stop=(mi == len(masks) - 1))
                                    pt = rot("PT", PT)
                                    act(pt[:, qa * 128:(qb + 1) * 128], ps[:, qa * 128:(qb + 1) * 128], AF.Exp,
                                        scale=0.125)
                                    cell["pt"] = pt
                                    cell["qa"], cell["qb"] = qa, qb

                                def pv_fn(kt=kt, cell=cell, ow3=ow3, ow_state=ow_state, g=g):
                                    pt = cell["pt"]
                                    for qs in range(cell["qa"], cell["qb"] + 1):
                                        mm(ow3[:, qs, :], pt[:, qs * 128:(qs + 1) * 128], vw_aug[:, kt, g, 0:65],
                                           start=ow_state["first"], stop=(kt == 4 * ci + 3))
                                        ow_state["first"] = False

                                steps.append((sc_fn, pv_fn))
                            def fin_w(owb=owb, hq=hq):
                                evac_branch(owb, hq, 2, False)
                                next_piece()
                            run_pipe(steps, final=fin_w)
                            if VAR == 'flushw':
                                pipe_flush()

                        for r in range(4):
                            hq = g * 4 + r
                            Qs = QT[:, hq, :]
                            Q = QT[0:64, hq, :]
                            osb = psB()
                            os3 = osb[:, 0:260].re("p (q e) -> p q e", q=4)
                            nkt = 4 * ci + 4
                            steps = []
                            for kt in range(nkt):
                                cell = {}

                                def sc_fn(kt=kt, cell=cell):
                                    ps = psA()
                                    dg = kt - 4 * ci
                                    mm(ps, ksT[:, g, kt * 128:(kt + 1) * 128], Qs, start=True, stop=(dg < 0))
                                    qlo = max(0, dg)
                                    if dg >= 0:
                                        mm(ps[:, dg * 128:(dg + 1) * 128], ident, trile, start=False, stop=True)
                                    pt = rot("PT", PT)
                                    act(pt[:, qlo * 128:512], ps[:, qlo * 128:512], AF.Exp, scale=0.125)
                                    cell["pt"] = pt
                                    cell["qlo"] = qlo

                                def pv_fn(kt=kt, cell=cell, os3=os3, g=g, nkt=nkt):
                                    pt = cell["pt"]
                                    for qs in range(cell["qlo"], 4):
                                        mm(os3[:, qs, :], pt[:, qs * 128:(qs + 1) * 128], vs_aug[:, kt, g, 0:65],
                                           start=(kt == 0 and qs == 0), stop=(kt == nkt - 1))

                                steps.append((sc_fn, pv_fn))
                            def fin_s(osb=osb, hq=hq):
                                evac_branch(osb, hq, 1, False)
                                next_piece()
                            run_pipe(steps, final=fin_s)
                            if VAR == 'flushs':
                                pipe_flush()
                    pipe_flush()
                    for t_ in range(4):
                        act(tokb_t[t_][:, 0:512], o_nsa[:, t_, :], AF.Copy)

                    pipe_flush()
                    while ret_pos[0] < len(ret_pieces):
                        next_piece()
                    wo_slots = [wget(wi["O0"]).re("p (kc n) -> p kc n", kc=8), wget(wi["O1"]).re("p (kc n) -> p kc n", kc=8)]
                    for t_ in range(4):
                        T_tile(0, t_)
                    for t_ in range(4):
                        for nh in range(2):
                            pb = psB()
                            for kc in range(8):
                                mm(pb, T8_t[0][t_][:, kc, :], wo_slots[nh][:, kc, :], start=(kc == 0), stop=(kc == 7))
                            dst = xh_t[t_][:, nh * 512:(nh + 1) * 512]
                            tt(dst, dst, pb, ALU.add)
                        ln_tile(t_)
                        if t_ >= 1:
                            T_tile(1, t_ - 1, lnw2)
                    T_tile(1, 3, lnw2)

                    if b == 0 and ci == NCH - 1:
                        dump("mix", tokb, [128, 4, D])
                        dump("gates", gates, [128, 4, 24])
                    if stage < 6:
                        return
                    hT = T8[1]
                    def ffn_up(fc):
                        U = wget(wi[("U", fc)]).re("p (kc n) -> p kc n", kc=8)
                        at = actT[fc % 2]
                        for fs in range(4):
                            pb = psA()
                            for kc in range(8):
                                mm(pb, U[:, kc, fs * 128:(fs + 1) * 128], hT[:, kc, :], start=(kc == 0), stop=(kc == 7))
                            rl = rot("relu_t", relu_t)
                            act(rl, pb, AF.Relu)
                            act(at[:, fs, :], rl, AF.Square)

                    def ffn_down(fc):
                        Dn = wget(wi[("D", fc)]).re("p (fs n) -> p fs n", fs=4)
                        at = actT[fc % 2]
                        for t_ in range(4):
                            for nh in range(2):
                                pd = psB()
                                for fs in range(4):
                                    mm(pd, at[:, fs, t_ * 128:(t_ + 1) * 128], Dn[:, fs, nh * 512:(nh + 1) * 512],
                                       start=(fs == 0), stop=(fs == 3))
                                dst = xh_t[t_][:, nh * 512:(nh + 1) * 512]
                                tt(dst, dst, pd, ALU.add)

                    nxt = None
                    if ci + 1 < NCH:
                        nxt = (b, ci + 1)
                    elif b + 1 < NB:
                        nxt = (b + 1, 0)

                    def pfA(t_):
                        nb_, nci = nxt
                        r0 = nci * CH + t_ * 128
                        dma("sp", xs, V(x_d.ap[nb_, r0:r0 + 128, :], ()))

                    def pfB(t_):
                        ln_tile(t_, src=xs)

                    def pfC(t_):
                        T_tile(0, t_, lnw1)

                    ffn_up(0)
                    for fc in range(8):
                        if fc + 1 < 8:
                            wget(wi[("D", fc)])
                            ffn_up(fc + 1)
                        if nxt is not None and stage >= 99:
                            if 2 <= fc <= 5:
                                pfC(fc - 2)
                            if 1 <= fc <= 4:
                                pfB(fc - 1)
                            if fc <= 3:
                                pfA(fc)
                        ffn_down(fc)
                    prefetched[0] = nxt is not None and stage >= 99

                chunk_body()
                dma("sp", V(out_d.ap[b, t0:t0 + CH, :].rearrange("(t p) d -> p t d", p=128), ()), xh)
        fw.finish()
        build.stats = {e: fw.E[e].n for e in fw.E}
        build.stats["waits"] = fw.nwaits
    return nc


_CONSTS = None


def _host_inputs(inputs, NB, cores):
    global _CONSTS
    if _CONSTS is None:
        _CONSTS = _consts()
    c = _CONSTS
    f = lambda a: np.ascontiguousarray(np.asarray(a, dtype=np.float32))
    shared = {
        "w_in": f(inputs["w_in"][0]), "w_out": f(inputs["w_out"][0]), "w_up": f(inputs["w_up"][0]),
        "w_down": f(inputs["w_down"][0]),
        "cmp_wk1": f(inputs["cmp_wk1"][0]), "cmp_wv1": f(inputs["cmp_wv1"][0]),
        "cmp_wk2": f(inputs["cmp_wk2"][0]), "cmp_wv2": f(inputs["cmp_wv2"][0]),
        "ln1_t": f(np.asarray(inputs["ln1_w"][0]).reshape(8, 128).T),
        "ln2_t": f(np.asarray(inputs["ln2_w"][0]).reshape(8, 128).T),
        "pek_t": f(np.asarray(inputs["cmp_pe_k"][0]).reshape(16, 128).T),
        "pev_t": f(np.asarray(inputs["cmp_pe_v"][0]).reshape(16, 128).T),
        "retw": f(np.asarray(inputs["ret_norm_w"][0]).reshape(1, 512)),
    }
    qn = np.asarray(inputs["q_norm_w"][0], dtype=np.float32)
    kn = np.asarray(inputs["k_norm_w"][0], dtype=np.float32)
    qkw = np.stack([qn, kn[0], kn[1], kn[2]], axis=1)
    shared["qkw_t"] = f(np.concatenate([qkw, qkw], 0))
    for k, v in c.items():
        if not k.startswith("_"):
            shared[k] = v
    x = np.asarray(inputs["x"], dtype=np.float32)
    maps = []
    for ci in range(cores):
        m = dict(shared)
        m["x"] = np.ascontiguousarray(x[ci * NB:(ci + 1) * NB])
        maps.append(m)
    return maps


def kernel(**inputs):
    global _CONSTS
    if _CONSTS is None:
        _CONSTS = _consts()
    n = 8
    NB = 2
    nc = build(NB=NB, NCH=NCH_FULL, consts=_CONSTS)
    maps = _host_inputs(inputs, NB, n)
    res = run_bass_kernel_spmd(nc, maps, core_ids=list(range(n)))
    out = np.concatenate([r["out"] for r in res.results], axis=0)
    return out.astype(np.float32)
```

```python
import bisect
import math
import os
VAR = os.environ.get('KVAR', '')
from contextlib import ExitStack

import numpy as np
import concourse.bass as bass
import concourse.mybir as mybir
from concourse.bass_utils import run_bass_kernel_spmd

F32 = mybir.dt.float32
BF16 = mybir.dt.bfloat16
ALU = mybir.AluOpType
AF = mybir.ActivationFunctionType
AX = mybir.AxisListType

NEG = -30000.0
S = 2048
D = 1024
CH = 512
NCH_FULL = S // CH
EPS = 1e-6


class Tr:
    __slots__ = ("w", "r", "name", "excl")

    def __init__(self, name="", excl=False):
        self.w = None
        self.r = []
        self.name = name
        self.excl = excl


class V:
    __slots__ = ("ap", "trs")

    def __init__(self, ap, trs):
        self.ap = ap
        self.trs = tuple(trs)

    def __getitem__(self, k):
        return V(self.ap[k], self.trs)

    def re(self, pattern_, **kw):
        return V(self.ap.rearrange(pattern_, **kw), self.trs)

    def bc(self, shape):
        return V(self.ap.broadcast_to(list(shape)), self.trs)

    def un(self, axis):
        return V(self.ap.unsqueeze(axis), self.trs)

    def bitcast(self, dt):
        return V(self.ap.bitcast(dt), self.trs)


class Eng:
    def __init__(self, name, h, sem, is_pe=False):
        self.name = name
        self.h = h
        self.sem = sem
        self.count = 0
        self.n = 0
        self.last = None
        self.tick_idx = []
        self.tick_val = []
        self.seen = {}
        self.is_pe = is_pe


class FW:
    def __init__(self, nc, es, n_dma_sems=24):
        self.nc = nc
        self.es = es
        mk = lambda nm: es.enter_context(nc.semaphore(nm))
        self.E = {
            "pe": Eng("pe", nc.tensor, mk("s_pe"), True),
            "act": Eng("act", nc.scalar, mk("s_act")),
            "dve": Eng("dve", nc.vector, mk("s_dve")),
            "pool": Eng("pool", nc.gpsimd, mk("s_pool")),
            "sp": Eng("sp", nc.sync, mk("s_sp")),
        }
        self.dsems = [[mk(f"s_dma{i}"), 0] for i in range(n_dma_sems)]
        half_ = n_dma_sems // 2
        self.dpool = {"sp": list(range(0, half_)), "pool": list(range(half_, n_dma_sems))}
        self.dnext = {"sp": 0, "pool": 0}
        self.nwaits = 0

    def sbv(self, name, shape, dt=F32):
        t = self.es.enter_context(self.nc.sbuf_tensor("sb_" + name, list(shape), dt))
        return V(t[:], (Tr(name),))

    def psv(self, name, shape, dt=F32):
        t = self.es.enter_context(self.nc.psum_tensor(name, list(shape), dt))
        return V(t[:], (Tr(name, excl=True),))

    def _ticket(self, X, idx):
        k = bisect.bisect_left(X.tick_idx, idx)
        if k < len(X.tick_idx):
            return X.tick_val[k]
        assert X.last is not None and X.n - 1 >= idx
        X.count += 1
        X.last.then_inc(X.sem, 1)
        X.tick_idx.append(X.n - 1)
        X.tick_val.append(X.count)
        return X.count

    def _wait(self, E, ref):
        if ref[0] == "dma":
            _, si, val = ref
            key = ("d", si)
            if E.seen.get(key, 0) >= val:
                return
            E.h.wait_ge(self.dsems[si][0], val)
            E.seen[key] = val
        else:
            _, xn, idx = ref
            X = self.E[xn]
            tk = self._ticket(X, idx)
            if E.seen.get(xn, 0) >= tk:
                return
            E.h.wait_ge(X.sem, tk)
            E.seen[xn] = tk
        self.nwaits += 1

    def _deps(self, E, outs, ins):
        en = E.name
        for v in ins:
            for tr in v.trs:
                w = tr.w
                if w is not None:
                    if not (w[0] == "e" and w[1] == en and E.is_pe):
                        self._wait(E, w)
                if tr.excl:
                    for r in tr.r:
                        if r[0] == "e" and r[1] == en:
                            continue
                        self._wait(E, r)
        for v in outs:
            for tr in v.trs:
                w = tr.w
                if w is not None and not (w[0] == "e" and w[1] == en and E.is_pe):
                    self._wait(E, w)
                for r in tr.r:
                    if r[0] == "e" and r[1] == en and E.is_pe:
                        continue
                    self._wait(E, r)

    def _record(self, ref, outs, ins):
        for v in ins:
            for tr in v.trs:
                tr.r.append(ref)
                if len(tr.r) > 32:
                    best = {}
                    for r in tr.r:
                        key = (r[0], r[1])
                        if key not in best or r[2] > best[key][2]:
                            best[key] = r
                    tr.r = list(best.values())
        for v in outs:
            for tr in v.trs:
                tr.w = ref
                tr.r = []

    def op(self, en, outs, ins, fn):
        E = self.E[en]
        self._deps(E, outs, ins)
        inst = fn(E.h)
        E.last = inst
        idx = E.n
        E.n += 1
        self._record(("e", en, idx), outs, ins)
        return inst

    def dma(self, qn, out, in_, **kw):
        E = self.E[qn]
        if E.n > 0 and (not E.tick_idx or E.tick_idx[-1] < E.n - 1):
            self._ticket(E, E.n - 1)
        self._deps(E, [out], [in_])
        pl = self.dpool[qn]
        si = pl[self.dnext[qn] % len(pl)]
        self.dnext[qn] += 1
        sem, val = self.dsems[si]
        if val > 0:
            self._wait(E, ("dma", si, val))
        inst = E.h.dma_start(out=out.ap, in_=in_.ap, **kw)
        inst.then_inc(sem, 16)
        self.dsems[si][1] = val + 16
        E.last = inst
        E.n += 1
        E.tick_idx.append(E.n - 1)
        E.tick_val.append(E.count)
        self._record(("dma", si, val + 16), [out], [in_])
        return inst

    def finish(self):
        E = self.E["sp"]
        for si, (sem, val) in enumerate(self.dsems):
            if val > 0:
                self._wait(E, ("dma", si, val))


def _consts():
    c = {}
    c["c_ident"] = np.eye(128, dtype=np.float32)
    R = np.zeros((64, 64), np.float32)
    for m in range(32):
        R[m, m + 32] = -1.0
        R[m + 32, m] = 1.0
    rt = np.zeros((128, 128), np.float32)
    rt[:64, :64] = R.T
    rt[64:, 64:] = R.T
    c["c_rt"] = rt
    on = np.zeros((128, 128), np.float32)
    on[:64, :64] = 1.0 / 64
    on[64:, 64:] = 1.0 / 64
    c["c_ones"] = on
    half = 32
    inv = (10000.0 ** (-np.arange(half, dtype=np.float32) / half)).astype(np.float32)
    ang = np.arange(S, dtype=np.float32)[:, None] * inv[None, :]
    cos = np.cos(ang).astype(np.float32).T
    sin = np.sin(ang).astype(np.float32).T
    c["c_cos"] = np.concatenate([cos, cos, cos, cos], 0)
    c["c_sin"] = np.concatenate([sin, sin, sin, sin], 0)
    cc = np.arange(128)[:, None]
    tt = np.arange(S)[None, :]
    c["c_cmpmask"] = np.where((16 * cc + 31 <= tt) & (cc < 127), 0.0, NEG).astype(np.float32)
    k = np.arange(128)[:, None]
    q = np.arange(128)[None, :]
    c["c_trile"] = np.where(k <= q, 0.0, NEG).astype(np.float32)
    c["c_trigt"] = np.where(k > q, 0.0, NEG).astype(np.float32)
    E = np.zeros((32, S), np.float32)
    for j in range(32):
        E[j, 64 * j:64 * j + 64] = 1.0
    c["c_E"] = E
    cs = np.arange(127) * 16
    jb = np.arange(32)
    ov = ((cs[:, None] < (jb[None, :] + 1) * 64) & (cs[:, None] + 32 > jb[None, :] * 64)).astype(np.float32)
    ovl = np.zeros((128, 33), np.float32)
    ovl[:127, :32] = ov
    ovl[:127, 32] = 1.0
    c["c_ovl"] = ovl
    t = np.arange(S)
    cur = t // 64
    forced = (jb[None, :] == 0) | (jb[None, :] == cur[:, None]) | (jb[None, :] == cur[:, None] - 1)
    valid = jb[None, :] <= cur[:, None]
    bias = np.where(forced, 1.0e4, np.where(valid, 0.0, -1.0)).astype(np.float32)
    c["c_selbias"] = np.ascontiguousarray(bias.reshape(16, 128, 32).transpose(1, 0, 2))
    H = 4
    lg = np.log(1.0 - 2.0 ** (-5.0 - np.arange(H, dtype=np.float64)))
    i = np.arange(128, dtype=np.float64)
    dm = np.zeros((128, H, 128), np.float64)
    for h in range(H):
        dm[:, h, :] = (i[None, :] >= i[:, None]) * np.exp(-lg[h] * (i[:, None] + 1.0)) / 8.0
    c["c_dmatp"] = dm.astype(np.float32)
    c["c_zeta8"] = (np.exp(lg[None, :] * (127.0 - i[:, None])) / 8.0).astype(np.float32)
    xi = np.exp(lg[:, None] * (i[None, :] + 1.0))
    xi2 = np.zeros((128, 2, 128), np.float64)
    for j in range(2):
        xi2[:64, j, :] = xi[2 * j][None, :]
        xi2[64:, j, :] = xi[2 * j + 1][None, :]
    c["c_xi"] = xi2.astype(np.float32)
    c["_gamma_c"] = [float(np.exp(lg[h] * 128.0)) for h in range(H)]
    return c


CONST_SHAPES = None


def build(NB=2, NCH=NCH_FULL, consts=None, tab_dt=F32, dbg=(), stage=99):
    nc = bass.Bass("TRN2", target_bir_lowering=False)
    gamma_c = consts["_gamma_c"]

    def din(name, shape):
        return V(nc.dram_tensor(name, list(shape), F32, kind="ExternalInput").ap(), ())

    x_d = din("x", [NB, S, D])
    w_in_d = din("w_in", [D, 2840])
    w_out_d = din("w_out", [D, D])
    w_up_d = din("w_up", [D, 4096])
    w_down_d = din("w_down", [4096, D])
    wk1_d = din("cmp_wk1", [2048, 256])
    wv1_d = din("cmp_wv1", [2048, 256])
    wk2_d = din("cmp_wk2", [256, 64])
    wv2_d = din("cmp_wv2", [256, 64])
    lnw1_d = din("ln1_t", [128, 8])
    lnw2_d = din("ln2_t", [128, 8])
    qkw_d = din("qkw_t", [128, 4])
    pek_d = din("pek_t", [128, 16])
    pev_d = din("pev_t", [128, 16])
    retw_d = din("retw", [1, 512])
    cd = {k: din(k, v.shape) for k, v in consts.items() if not k.startswith("_")}
    out_d = V(nc.dram_tensor("out", [NB, S, D], F32, kind="ExternalOutput").ap(), ())

    with ExitStack() as es:
        fw = FW(nc, es)
        sbv, op, dma = fw.sbv, fw.op, fw.dma

        ident = sbv("ident", [128, 128], BF16)
        rt = sbv("rt", [128, 128], BF16)
        onesb = sbv("onesb", [128, 128], BF16)
        cos = sbv("cos", [128, S], tab_dt)
        sin = sbv("sin", [128, S], tab_dt)
        cmpmask = sbv("cmpmask", [128, S], BF16)
        trile = sbv("trile", [128, 128], BF16)
        trigt = sbv("trigt", [128, 128], BF16)
        ovl = sbv("ovl", [128, 33], BF16)
        selbias = sbv("selbias", [128, 16, 32], F32)
        dmatp = sbv("dmatp", [128, 4, 128], F32)
        zeta8 = sbv("zeta8", [128, 4], F32)
        xit = sbv("xit", [128, 2, 128], F32)
        lnw1 = sbv("lnw1", [128, 8], F32)
        lnw2 = sbv("lnw2", [128, 8], F32)
        qkw = sbv("qkw", [128, 4], F32)
        pek = sbv("pek", [128, 16], BF16)
        pev = sbv("pev", [128, 16], BF16)
        retw = sbv("retw", [128, 512], F32)
        wg = sbv("wg", [128, 8, 24], BF16)
        wk2 = sbv("wk2", [128, 2, 64], BF16)
        wv2 = sbv("wv2", [128, 2, 64], BF16)
        epsb = sbv("epsb", [128, 1], F32)
        biasH = sbv("biasH", [128, 4], F32)

        for dst, key in ((ident, "c_ident"), (rt, "c_rt"), (onesb, "c_ones"), (cmpmask, "c_cmpmask"),
                         (trile, "c_trile"), (trigt, "c_trigt"), (ovl, "c_ovl"),
                         (cos, "c_cos"), (sin, "c_sin")):
            dma("pool", dst, cd[key])
        for dst, src in ((selbias, cd["c_selbias"]), (dmatp, cd["c_dmatp"]), (zeta8, cd["c_zeta8"]),
                         (xit, cd["c_xi"]), (lnw1, lnw1_d), (lnw2, lnw2_d), (qkw, qkw_d)):
            dma("sp", dst, src)
        dma("pool", pek, pek_d)
        dma("pool", pev, pev_d)
        dma("sp", retw, V(retw_d.ap.broadcast_to([128, 512]), ()))
        dma("pool", wg, V(w_in_d.ap[:, 1280:1304].rearrange("(kc p) n -> p kc n", p=128), ()))
        dma("pool", wk2, V(wk2_d.ap.rearrange("(mh p) d -> p mh d", p=128), ()))
        dma("pool", wv2, V(wv2_d.ap.rearrange("(mh p) d -> p mh d", p=128), ()))
        op("dve", [epsb], [], lambda h: h.memset(epsb.ap, EPS))


        def scratch(name, src_v, rows, cols, piece_rows):
            t = nc.dram_tensor("scr_" + name, [rows, cols], BF16, kind="Internal").ap()
            trs = []
            for r0 in range(0, rows, piece_rows):
                tr = Tr(f"scr_{name}_{r0}")
                dma("pool", V(t[r0:r0 + piece_rows, :], (tr,)), V(src_v.ap[r0:r0 + piece_rows, :], ()))
                trs.append(tr)
            return V(t, trs)

        def conv_block(name, src_ap, rows, cols, nsplit=2):
            t = nc.dram_tensor("scr_" + name, [rows, cols], BF16, kind="Internal").ap()
            trs = []
            pr = rows // nsplit
            for r0 in range(0, rows, pr):
                tr = Tr(f"scr_{name}_{r0}")
                dma("pool", V(t[r0:r0 + pr, :], (tr,)), V(src_ap[r0:r0 + pr, :], ()))
                trs.append(tr)
            return V(t, trs)

        scr = {}
        scr["S0"] = conv_block("S0", w_in_d.ap[:, 0:512], D, 512)
        scr["S5"] = conv_block("S5", w_in_d.ap[:, 2328:2840], D, 512)
        s1_t = nc.dram_tensor("scr_s1", [D, 512], BF16, kind="Internal").ap()
        s1_trs = []
        for j, c0_ in enumerate([512, 512, 576, 576, 640, 640, 704, 704]):
            tr = Tr(f"scr_s1_{j}")
            dma("pool", V(s1_t[:, j * 64:(j + 1) * 64], (tr,)), V(w_in_d.ap[:, c0_:c0_ + 64], ()))
            s1_trs.append(tr)
        scr["S1"] = V(s1_t, s1_trs)
        scr["S2"] = conv_block("S2", w_in_d.ap[:, 768:1280], D, 512)
        scr["S3"] = conv_block("S3", w_in_d.ap[:, 1304:1816], D, 512)
        scr["S4"] = conv_block("S4", w_in_d.ap[:, 1816:2328], D, 512)
        scr["CK"] = conv_block("CK", wk1_d.ap, 2048, 256)
        scr["CV"] = conv_block("CV", wv1_d.ap, 2048, 256)
        scr["O0"] = conv_block("O0", w_out_d.ap[:, 0:512], D, 512)
        scr["O1"] = conv_block("O1", w_out_d.ap[:, 512:1024], D, 512)
        for fc in range(8):
            scr[("U", fc)] = conv_block(f"U{fc}", w_up_d.ap[:, fc * 512:(fc + 1) * 512], D, 512)
            scr[("D", fc)] = conv_block(f"D{fc}", w_down_d.ap[fc * 512:(fc + 1) * 512, :], 512, D)

        ksT = sbv("ksT", [128, 2, S], BF16)
        kwT = sbv("kwT", [64, 2, S], BF16)
        vs_aug = sbv("vs_aug", [128, 16, 2, 66], BF16)
        vw_aug = sbv("vw_aug", [128, 16, 2, 66], BF16)
        kcw = sbv("kcw", [128, 2, 544], BF16)
        vcw = sbv("vcw", [128, 2, 544], BF16)
        kcmpT = sbv("kcmpT", [64, 2, 128], BF16)
        vcmpT = sbv("vcmpT", [64, 2, 128], BF16)
        vcmp_aug = sbv("vcmp_aug", [128, 2, 66], BF16)
        state = sbv("state", [64, 4, 128], F32)
        state_bf = sbv("state_bf", [64, 4, 128], BF16)
        op("dve", [ksT], [], lambda h: h.memset(ksT.ap[64:128, :, :], 0.0))
        for g_ in range(2):
            dma("pool", ksT[64:96, g_, :], cd["c_E"])
        op("dve", [vs_aug], [], lambda h: h.memset(vs_aug.ap, 1.0))
        op("dve", [vw_aug], [], lambda h: h.memset(vw_aug.ap, 1.0))
        op("dve", [vcmp_aug], [], lambda h: h.memset(vcmp_aug.ap, 1.0))
        op("dve", [kcmpT], [], lambda h: h.memset(kcmpT.ap, 0.0))
        op("dve", [vcmpT], [], lambda h: h.memset(vcmpT.ap, 0.0))

        def tiled(v, slicer):
            trs = [Tr(f"t{t}") for t in range(4)]
            tiles = [V(slicer(v.ap, t), (trs[t],)) for t in range(4)]
            return V(v.ap, trs), tiles

        xh, xh_t = tiled(sbv("xh", [128, 4, D], F32), lambda ap, t: ap[:, t, :])
        tokb, tokb_t = tiled(sbv("tokb", [128, 4, D], BF16), lambda ap, t: ap[:, t, :])
        T8, T8_t = [], []
        for i in range(2):
            a_, t_l = tiled(sbv(f"T8_{i}", [128, 8, CH], BF16), lambda ap, t: ap[:, :, t * 128:(t + 1) * 128])
            T8.append(a_)
            T8_t.append(t_l)
        QT = sbv("QT", [128, 8, CH], BF16)
        op("dve", [QT], [], lambda h: h.memset(QT.ap[64:128, :, :], 0.0))
        rqxT = sbv("rqxT", [64, 4, CH], BF16)
        rkT = sbv("rkT", [64, 4, CH], BF16)
        kz = sbv("kz", [128, 4, 64], BF16)
        rv = sbv("rv", [128, 4, 512], BF16)
        rgs = sbv("rgs", [128, 4, 512], BF16)
        gates = sbv("gates", [128, 4, 24], F32)
        o_nsa = sbv("o_nsa", [128, 4, 512], F32)
        ss4 = sbv("ss4", [128, 4], F32)
        rstd4 = sbv("rstd4", [128, 4], F32)
        sqb = [sbv(f"sqb{i}", [128, CH], BF16) for i in range(2)]
        rstd_t = [sbv(f"rstd_t{i}", [128, CH], F32) for i in range(2)]
        qnb = [sbv(f"qnb{i}", [128, CH], BF16) for i in range(3)]
        big4k = fw.es.enter_context(nc.sbuf_tensor("sb_big4k", [128, 2 * CH], F32))
        _tra, _trb = Tr("t1a"), Tr("t1b")
        t1 = [V(big4k[:, 0:CH], (_tra,)), V(big4k[:, CH:2 * CH], (_trb,))]
        xs = V(big4k[:], (_tra, _trb))
        uptmp = [sbv(f"uptmp{i}", [128, CH], BF16) for i in range(2)]
        t2 = [sbv(f"t2_{i}", [128, CH], F32) for i in range(2)]
        PT = [sbv(f"PT{i}", [128, CH], BF16) for i in range(7)]
        HT = sbv("HT", [128, 2, 64], BF16)
        score = sbv("score", [128, 4, 32], F32)
        sctmp = sbv("sctmp", [128, 4, 32], F32)
        m8 = sbv("m8", [128, 4, 8], F32)
        thr = sbv("thr", [128, 4], F32)
        mneg = sbv("mneg", [128, 4, 96], BF16)
        op("dve", [mneg], [], lambda h: h.memset(mneg.ap, 0.0))
        rd = sbv("rd", [128, 4], F32)
        ff = sbv("ff", [128, 4], F32)
        otmp = sbv("otmp", [128, 4, 64], F32)
        innb = [sbv(f"innb{i}", [128, 128], BF16) for i in range(4)]
        st1 = sbv("st1", [128, 32], F32)
        st2 = sbv("st2", [128, 4], F32)
        stm = sbv("stm", [128, 4], F32)
        stv = sbv("stv", [128, 4], F32)
        actT = [sbv(f"actT{i}", [128, 4, CH], BF16) for i in range(2)]
        relu_t = [sbv(f"relu_t{i}", [128, CH], F32) for i in range(1)]
        WS = [sbv(f"WS{i}", [128, 4096], BF16) for i in range(4)]

        banks = [fw.psv(f"pb{i}", [128, 512], F32) for i in range(8)]
        rrA = [0]
        rrB = [0]

        def psA():
            b = banks[rrA[0] % 4]
            rrA[0] += 1
            return b

        def psB():
            b = banks[5 + rrB[0] % 3]
            rrB[0] += 1
            return b

        rrP = [0]

        def psP():
            b = banks[rrP[0] % 6]
            rrP[0] += 1
            return b

        rrC = [0]

        def psC():
            return banks[7]

        rr = {}

        def rot(name, lst):
            i = rr.get(name, 0)
            rr[name] = i + 1
            return lst[i % len(lst)]

        def wsrc_cols(wd, a, b):
            return V(wd.ap[:, a:b].rearrange("(kc p) n -> p kc n", p=128), wd.trs)

        items = []

        def add_item(parts):
            items.append(parts)
            return len(items) - 1

        witems = {}

        def kc_view(v):
            return V(v.ap.rearrange("(kc p) n -> p kc n", p=128), v.trs)

        for b in range(NB):
            for ci in range(NCH):
                d = {}
                for key in ("S0", "S5", "S1", "S2", "S3", "S4"):
                    d[key] = add_item([(lambda s: s.re("p (kc n) -> p kc n", kc=8), kc_view(scr[key]))])
                for key in ("CK", "CV"):
                    d[key] = add_item([(lambda s: s.re("p (i m) -> p i m", i=16),
                                        V(scr[key].ap.rearrange("(i p) m -> p i m", p=128), scr[key].trs))])
                for key in ("O0", "O1"):
                    d[key] = add_item([(lambda s: s.re("p (kc n) -> p kc n", kc=8), kc_view(scr[key]))])
                for fc in range(8):
                    d[("U", fc)] = add_item([(lambda s: s.re("p (kc n) -> p kc n", kc=8), kc_view(scr[("U", fc)]))])
                    d[("D", fc)] = add_item([(lambda s: s.re("p (fs n) -> p fs n", fs=4),
                                              V(scr[("D", fc)].ap.rearrange("(fs p) n -> p fs n", p=128),
                                                scr[("D", fc)].trs))])
                witems[(b, ci)] = d
        issued = [0]
        LOOK = 2

        def wget(idx):
            while issued[0] < len(items) and issued[0] <= idx + LOOK:
                k = issued[0]
                slot = WS[k % 4]
                for viewfn, src in items[k]:
                    dma("sp", viewfn(slot), src)
                issued[0] += 1
            return WS[idx % 4]

        def dump(name, v, shape):
            if name in dbg:
                dd = V(nc.dram_tensor("dbg_" + name, list(shape), F32, kind="ExternalOutput").ap(), ())
                dma("pool", dd, v)

        def mm(out, lhsT, rhs, start=True, stop=True):
            return op("pe", [out], [lhsT, rhs],
                      lambda h: h.matmul(out.ap, lhsT.ap, rhs.ap, start=start, stop=stop, skip_group_check=True))

        def transpose(out, in_, idn):
            return op("pe", [out], [in_, idn], lambda h: h.transpose(out.ap, in_.ap, idn.ap))

        def act(out, in_, func, bias=None, scale=1.0, accum=None):
            outs = [out] + ([accum] if accum is not None else [])
            ins = [in_] + ([bias] if isinstance(bias, V) else []) + ([scale] if isinstance(scale, V) else [])
            kw = {}
            if bias is not None:
                kw["bias"] = bias.ap if isinstance(bias, V) else bias
            if accum is not None:
                kw["accum_out"] = accum.ap
            sc = scale.ap if isinstance(scale, V) else scale
            return op("act", outs, ins, lambda h: h.activation(out.ap, in_.ap, func, scale=sc, **kw))

        def tt(out, a, b, alu, eng="dve"):
            return op(eng, [out], [a, b], lambda h: h.tensor_tensor(out.ap, a.ap, b.ap, alu))

        def ts(out, a, s1, s2, op0, op1=None):
            ins = [a] + [s for s in (s1, s2) if isinstance(s, V)]
            a1 = s1.ap if isinstance(s1, V) else s1
            a2 = s2.ap if isinstance(s2, V) else s2
            if op1 is None:
                return op("dve", [out], ins, lambda h: h.tensor_scalar(out.ap, a.ap, a1, a2, op0))
            return op("dve", [out], ins, lambda h: h.tensor_scalar(out.ap, a.ap, a1, a2, op0, op1))

        def stt(out, a, sc, b, op0, op1):
            ins = [a, b] + ([sc] if isinstance(sc, V) else [])
            s_ = sc.ap if isinstance(sc, V) else sc
            return op("dve", [out], ins, lambda h: h.scalar_tensor_tensor(out.ap, a.ap, s_, b.ap, op0, op1))

        def vcopy(out, in_):
            return op("dve", [out], [in_], lambda h: h.tensor_copy(out.ap, in_.ap))

        def recip(out, in_):
            return op("dve", [out], [in_], lambda h: h.reciprocal(out.ap, in_.ap))

        def rsum(out, in_):
            return op("dve", [out], [in_], lambda h: h.reduce_sum(out.ap, in_.ap, AX.X))

        def ln_tile(t_, src=None):
            if src is None:
                src = xh_t[t_]
            op("dve", [ss4], [], lambda h: h.memset(ss4.ap[:, t_:t_ + 1], 0.0))
            act(tokb_t[t_], src, AF.Square, accum=ss4[:, t_:t_ + 1])
            act(rstd4[:, t_:t_ + 1], ss4[:, t_:t_ + 1], AF.Sqrt, bias=epsb, scale=1.0 / D)
            recip(rstd4[:, t_:t_ + 1], rstd4[:, t_:t_ + 1])
            ts(tokb_t[t_], src, rstd4[:, t_:t_ + 1], None, ALU.mult)

        def T_tile(di, t_, lnw=None):
            dst_t = T8_t[di][t_]
            for half in range(2):
                pb = psA()
                pbb = pb.bitcast(BF16)
                for j in range(4):
                    kc = half * 4 + j
                    transpose(pbb[:, j * 128:(j + 1) * 128], tokb_t[t_][:, kc * 128:(kc + 1) * 128], ident)
                src = pbb[:, 0:512].re("p (j q) -> p j q", j=4)
                dst = dst_t[:, half * 4:(half + 1) * 4, :]
                if lnw is None:
                    act(dst, src, AF.Copy)
                else:
                    tt(dst, src, lnw[:, half * 4:(half + 1) * 4].un(2).bc([128, 4, 128]), ALU.mult)

        def rmsnorm_to_T(lnw, di):
            for t_ in range(4):
                ln_tile(t_)
            for t_ in range(4):
                T_tile(di, t_, lnw)

        def proj_fm(slot3, c0, M, xT):
            pb = psA()
            for kc in range(8):
                mm(pb[0:M, :], slot3[:, kc, c0:c0 + M], xT[:, kc, :], start=(kc == 0), stop=(kc == 7))
            return pb

        def norm_rope(pb, M, wcol, t0, outs, post=None):
            qn = rot("qnb", qnb)
            if wcol is not None:
                sq = rot("sqb", sqb)
                act(sq[0:M, :], pb[0:M, :], AF.Square)
                pm = psA()
                mm(pm[0:M, :], onesb[0:M, 0:M], sq[0:M, :])
                rs = rot("rstd_t", rstd_t)
                act(rs[0:M, :], pm[0:M, :], AF.Sqrt, bias=epsb[0:M, :])
                recip(rs[0:M, :], rs[0:M, :])
                stt(qn[0:M, :], pb[0:M, :], qkw[0:M, wcol:wcol + 1], rs[0:M, :], ALU.mult, ALU.mult)
            else:
                act(qn[0:M, :], pb[0:M, :], AF.Copy)
            pr = psA()
            mm(pr[0:M, :], rt[0:M, 0:M], qn[0:M, :])
            a = rot("t1", t1)
            b_ = rot("t2", t2)
            tt(a[0:M, :], qn[0:M, :], cos[0:M, t0:t0 + CH], ALU.mult)
            tt(b_[0:M, :], pr[0:M, :], sin[0:M, t0:t0 + CH], ALU.mult)
            if post is not None:
                tt(a[0:M, :], a[0:M, :], b_[0:M, :], ALU.add)
                for dst, r0, r1 in outs:
                    tt(dst, a[r0:r1, :].re("p (s q) -> p s q", s=4), post, ALU.mult)
            else:
                for dst, r0, r1 in outs:
                    tt(dst, a[r0:r1, :], b_[r0:r1, :], ALU.add)

        def setup_bias():
            for kv, (wd, pe) in enumerate(((wk1_d, pek), (wv1_d, pev))):
                slot = WS[kv]
                s3 = slot.re("p (i m) -> p i m", i=16)
                dma("pool", s3, V(wd.ap.rearrange("(i p) m -> p i m", p=128), ()))
                pb = psA()
                for mh in range(2):
                    for i in range(16):
                        mm(pb[:, mh:mh + 1], s3[:, i, mh * 128:(mh + 1) * 128], pe[:, i:i + 1],
                           start=(i == 0), stop=(i == 15))
                vcopy(biasH[:, kv * 2:kv * 2 + 2], pb[:, 0:2])


        prefetched = [False]
        for b in range(NB):
            op("dve", [state], [], lambda h: h.memset(state.ap, 0.0))
            op("dve", [state_bf], [], lambda h: h.memset(state_bf.ap, 0.0))
            op("dve", [kcw], [], lambda h: h.memset(kcw.ap, 0.0))
            op("dve", [vcw], [], lambda h: h.memset(vcw.ap, 0.0))
            for ci in range(NCH):
                t0 = ci * CH
                wi = witems[(b, ci)]
                def chunk_body(b=b, ci=ci, t0=t0, wi=wi):
                    dma("sp", xh, V(x_d.ap[b, t0:t0 + CH, :].rearrange("(t p) d -> p t d", p=128), ()))
                    xT = T8[0]
                    if stage < 0.5:
                        return
                    if not prefetched[0]:
                        rmsnorm_to_T(lnw1, 0)
                    if stage < 1.01:
                        return

                    def vc_g(g):
                        s1_ = wget(wi["S1"]).re("p (kc n) -> p kc n", kc=8)
                        if True:
                            pb = psC()
                            for kc in range(8):
                                mm(pb, s1_[:, kc, 256 + g * 128:256 + (g + 1) * 128], xT[:, kc, :],
                                   start=(kc == 0), stop=(kc == 7))
                            act(vcw[0:64, g, 16:528], pb[0:64, :], AF.Copy)
                            act(vcw[64:128, g, 15:527], pb[64:128, :], AF.Copy)

                    def kw_tile(t_):
                        s2_ = wget(wi["S2"]).re("p (kc n) -> p kc n", kc=8)
                        if True:
                            tile_i = ci * 4 + t_
                            pb = psC()
                            for kc in range(8):
                                mm(pb[:, 0:128], xT[:, kc, t_ * 128:(t_ + 1) * 128], s2_[:, kc, 128:256],
                                   start=(kc == 0), stop=(kc == 7))
                            for kc in range(8):
                                mm(pb[:, 128:256], xT[:, kc, t_ * 128:(t_ + 1) * 128], s2_[:, kc, 384:512],
                                   start=(kc == 0), stop=(kc == 7))
                            for kc in range(8):
                                mm(pb[:, 256:280], xT[:, kc, t_ * 128:(t_ + 1) * 128], wg[:, kc, :],
                                   start=(kc == 0), stop=(kc == 7))
                            act(vs_aug[:, tile_i, :, 0:64], pb[:, 0:128].re("p (g d) -> p g d", g=2), AF.Copy)
                            act(vw_aug[:, tile_i, :, 0:64], pb[:, 128:256].re("p (g d) -> p g d", g=2), AF.Copy)
                            act(gates[:, t_, :], pb[:, 256:280], AF.Sigmoid)

                    if ci > 0:
                        for g in range(2):
                            vcopy(kcw[0:64, g, 0:16], kcw[0:64, g, 512:528])
                            vcopy(kcw[64:128, g, 0:15], kcw[64:128, g, 512:527])
                            vcopy(vcw[0:64, g, 0:16], vcw[0:64, g, 512:528])
                            vcopy(vcw[64:128, g, 0:15], vcw[64:128, g, 512:527])
                    tasks = []
                    for j in range(4):
                        tasks.append(dict(key="S0", c0=j * 128, M=128, wcol=0,
                                          outs=[(QT[0:64, 2 * j, :], 0, 64, False), (QT[0:64, 2 * j + 1, :], 64, 128, True)]))
                    for g in range(2):
                        tasks.append(dict(key="S1", c0=g * 128, M=128, wcol=1,
                                          outs=[(kcw[0:64, g, 16:528], 0, 64, False),
                                                (kcw[64:128, g, 15:527], 64, 128, False)]))
                    tasks.append(dict(key="S2", c0=0, M=128, wcol=2,
                                      outs=[(ksT[0:64, 0, t0:t0 + CH], 0, 64, False), (ksT[0:64, 1, t0:t0 + CH], 64, 128, True)]))
                    tasks.append(dict(key="S2", c0=256, M=128, wcol=3,
                                      outs=[(kwT[:, 0, t0:t0 + CH], 0, 64, False), (kwT[:, 1, t0:t0 + CH], 64, 128, True)]))
                    for j in range(2):
                        tasks.append(dict(key="S3", c0=j * 128, M=128, wcol=None,
                                          outs=[(rqxT[:, 2 * j, :], 0, 64, False), (rqxT[:, 2 * j + 1, :], 64, 128, True)],
                                          post=xit[:, j, :].un(1).bc([128, 4, 128])))
                    for j in range(2):
                        tasks.append(dict(key="S3", c0=256 + j * 128, M=128, wcol=None,
                                          outs=[(rkT[:, 2 * j, :], 0, 64, False), (rkT[:, 2 * j + 1, :], 64, 128, True)]))
                    nt = len(tasks)
                    fillers = {8: [lambda: kw_tile(0)], 9: [lambda: kw_tile(1)], 10: [lambda: kw_tile(2)],
                               11: [lambda: kw_tile(3)]}

                    def rv_tile(t_):
                        s4_ = wget(wi["S4"]).re("p (kc n) -> p kc n", kc=8)
                        pb = psC()
                        for kc in range(8):
                            mm(pb, xT[:, kc, t_ * 128:(t_ + 1) * 128], s4_[:, kc, :], start=(kc == 0), stop=(kc == 7))
                        act(rv[:, t_, :], pb, AF.Copy)

                    def rg_tile(t_):
                        s5_ = wget(wi["S5"]).re("p (kc n) -> p kc n", kc=8)
                        pb = psC()
                        for kc in range(8):
                            mm(pb, xT[:, kc, t_ * 128:(t_ + 1) * 128], s5_[:, kc, :], start=(kc == 0), stop=(kc == 7))
                        act(rgs[:, t_, :], pb, AF.Silu)

                    for t_ in range(4):
                        fillers[1 + t_] = [lambda t_=t_: rg_tile(t_)]
                    fillers[6] = [lambda: vc_g(0)]
                    fillers[7] = [lambda: vc_g(1)]
                    fillers[12] = [lambda: rv_tile(0), lambda: rv_tile(1)]
                    fillers[13] = [lambda: rv_tile(2), lambda: rv_tile(3)]
                    stt_ = [dict() for _ in tasks]
                    for k in range(nt + 2):
                        if k < nt:
                            T = tasks[k]
                            M = T["M"]
                            slot3 = wget(wi[T["key"]]).re("p (kc n) -> p kc n", kc=8)
                            pb = banks[k % 3]
                            for kc in range(8):
                                mm(pb[0:M, :], slot3[:, kc, T["c0"]:T["c0"] + M], xT[:, kc, :],
                                   start=(kc == 0), stop=(kc == 7))
                            stt_[k]["pb"] = pb
                            if T["wcol"] is not None:
                                sq = sqb[k % 2]
                                act(sq[0:M, :], pb[0:M, :], AF.Square)
                                stt_[k]["sq"] = sq
                            if "after" in T:
                                T["after"]()
                        j = k - 1
                        if 0 <= j < nt:
                            T = tasks[j]
                            M = T["M"]
                            pb = stt_[j]["pb"]
                            qn = qnb[j % 3]
                            if T["wcol"] is not None:
                                pm = banks[3 + j % 2]
                                mm(pm[0:M, :], onesb[0:M, 0:M], stt_[j]["sq"][0:M, :])
                                rs = rstd_t[j % 2]
                                act(rs[0:M, :], pm[0:M, :], AF.Ln, bias=epsb[0:M, :])
                                act(rs[0:M, :], rs[0:M, :], AF.Exp, scale=-0.5)
                                wc = T["wcol"]
                                stt(qn[0:M, :], pb[0:M, :], qkw[0:M, wc:wc + 1], rs[0:M, :], ALU.mult, ALU.mult)
                            else:
                                act(qn[0:M, :], pb[0:M, :], AF.Copy)
                            stt_[j]["qn"] = qn
                        for f_ in fillers.get(k, []):
                            f_()
                        i_ = k - 2
                        if 0 <= i_ < nt:
                            T = tasks[i_]
                            M = T["M"]
                            qn = stt_[i_]["qn"]
                            pr = banks[5 + i_ % 2]
                            mm(pr[0:M, :], rt[0:M, 0:M], qn[0:M, :])
                            a = t1[i_ % 2]
                            b_ = t2[i_ % 2]
                            tt(a[0:M, :], qn[0:M, :], cos[0:M, t0:t0 + CH], ALU.mult)
                            tt(b_[0:M, :], pr[0:M, :], sin[0:M, t0:t0 + CH], ALU.mult)
                            if T.get("post") is not None:
                                tt(a[0:M, :], a[0:M, :], b_[0:M, :], ALU.add)
                            for dst, r0, r1, via in T["outs"]:
                                tgt = uptmp[i_ % 2][r0:r1, :] if via else dst
                                if T.get("post") is not None:
                                    tt(tgt.re("p (s q) -> p s q", s=4), a[r0:r1, :].re("p (s q) -> p s q", s=4),
                                       T["post"][r0:r1], ALU.mult)
                                else:
                                    tt(tgt, a[r0:r1, :], b_[r0:r1, :], ALU.add)
                                if via:
                                    dma("sp", dst, tgt)
                    if stage < 1.5:
                        return
                    if stage < 3:
                        return
                    c0 = 32 * ci - 1
                    for kv, (key, srcw, dstT) in enumerate((("CK", kcw, kcmpT), ("CV", vcw, vcmpT))):
                        w1 = wget(wi[key]).re("p (i m) -> p i m", i=16)
                        w2 = wk2 if kv == 0 else wv2
                        if b == 0 and ci == 0:
                            pe_ = pek if kv == 0 else pev
                            pbz = psA()
                            for mh in range(2):
                                for i in range(16):
                                    mm(pbz[:, mh:mh + 1], w1[:, i, mh * 128:(mh + 1) * 128], pe_[:, i:i + 1],
                                       start=(i == 0), stop=(i == 15))
                            vcopy(biasH[:, kv * 2:kv * 2 + 2], pbz[:, 0:2])
                        for mh in range(2):
                            pb = psA()
                            for i in range(16):
                                mm(pb[:, 0:64].re("p (g c) -> p g c", g=2), w1[:, i, mh * 128:(mh + 1) * 128],
                                   srcw[:, :, 2 * i:2 * i + 512:16], start=(i == 0), stop=(i == 15))
                            act(HT[:, mh, :], pb[:, 0:64], AF.Silu, bias=biasH[:, kv * 2 + mh:kv * 2 + mh + 1])
                        for g in range(2):
                            pb = psA()
                            for mh in range(2):
                                mm(pb[0:64, 0:32], w2[:, mh, :], HT[:, mh, g * 32:(g + 1) * 32],
                                   start=(mh == 0), stop=(mh == 1))
                            if ci == 0:
                                vcopy(dstT[:, g, 0:31], pb[0:64, 1:32])
                            else:
                                vcopy(dstT[:, g, c0:c0 + 32], pb[0:64, 0:32])
                    for g in range(2):
                        pb = psA()
                        pbb = pb.bitcast(BF16)
                        transpose(pbb[:, 0:64], vcmpT[:, g, :], ident[0:64, 0:64])
                        vcopy(vcmp_aug[:, g, 0:64], pbb[:, 0:64])

                    if stage < 4:
                        return
                    ret_pieces = []
                    for s_ in range(4):
                        ts_ = slice(s_ * 128, (s_ + 1) * 128)
                        cell = {}

                        def p0(s_=s_, ts_=ts_, cell=cell):
                            pb = psA()
                            pbb = pb.bitcast(BF16)
                            for hr in range(4):
                                transpose(pbb[:, hr * 64:(hr + 1) * 64], rkT[:, hr, ts_], ident[0:64, 0:64])
                            tt(kz, pbb[:, 0:256].re("p (h d) -> p h d", h=4), zeta8.un(2).bc([128, 4, 64]), ALU.mult)

                        def p1(s_=s_, ts_=ts_, cell=cell):
                            for hr in range(4):
                                pi = psA()
                                mm(pi[:, 0:128], rkT[:, hr, ts_], rqxT[:, hr, ts_])
                                tt(innb[hr], pi[:, 0:128], dmatp[:, hr, :], ALU.mult)

                        def p2(s_=s_, ts_=ts_, cell=cell):
                            py = banks[4]
                            for hr in range(4):
                                mm(py[:, hr * 128:(hr + 1) * 128], innb[hr], rv[:, s_, hr * 128:(hr + 1) * 128],
                                   start=(hr == 0), stop=False)
                                mm(py[:, hr * 128:(hr + 1) * 128], rqxT[:, hr, ts_], state_bf[:, hr, :],
                                   start=False, stop=(hr == 3))
                            cell["py"] = py

                        def p2s(s_=s_, cell=cell):
                            py = cell["py"]
                            ysb_s = t2[s_ % 2].re("p (h v) -> p h v", h=4)
                            ysq = rot("t1", t1).re("p (h v) -> p h v", h=4)
                            cell["ysb"] = ysb_s
                            for hr in range(4):
                                act(ysb_s[:, hr, :], py[:, hr * 128:(hr + 1) * 128], AF.Copy,
                                    accum=st1[:, s_ * 8 + hr:s_ * 8 + hr + 1])
                            for hr in range(4):
                                act(ysq[:, hr, :], py[:, hr * 128:(hr + 1) * 128], AF.Square,
                                    accum=st1[:, s_ * 8 + 4 + hr:s_ * 8 + 5 + hr])

                        def p2b(s_=s_, cell=cell):
                            sm = st1[:, s_ * 8:s_ * 8 + 4]
                            sq_ = st1[:, s_ * 8 + 4:s_ * 8 + 8]
                            ts(stm, sm, 1.0 / 128, None, ALU.mult)
                            tt(stv, stm, stm, ALU.mult)
                            stt(stv, sq_, 1.0 / 128, stv, ALU.mult, ALU.subtract)

                        def p3(s_=s_, ts_=ts_, cell=cell):
                            pst = psA()
                            for hr in range(4):
                                mm(pst[0:64, hr * 128:(hr + 1) * 128], kz[:, hr, :], rv[:, s_, hr * 128:(hr + 1) * 128],
                                   start=(hr == 0), stop=(hr == 3))
                            for hr in range(4):
                                stt(state[:, hr, :], state[:, hr, :], gamma_c[hr], pst[0:64, hr * 128:(hr + 1) * 128],
                                    ALU.mult, ALU.add)

                        def p3b(s_=s_, cell=cell):
                            act(stv, stv, AF.Sqrt, bias=epsb)
                            act(state_bf, state, AF.Copy)

                        def p4(s_=s_, cell=cell):
                            recip(stv, stv)
                            stt(st2, stm, -1.0, stv, ALU.mult, ALU.mult)

                        def p4b(s_=s_, cell=cell):
                            ysb_s = cell["ysb"]
                            for hr in range(4):
                                act(ysb_s[:, hr, :], ysb_s[:, hr, :], AF.Identity, bias=st2[:, hr:hr + 1],
                                    scale=stv[:, hr:hr + 1])

                        def p5(s_=s_, cell=cell):
                            ysb_s = cell["ysb"]
                            tt(ysb_s, ysb_s, retw.re("p (h v) -> p h v", h=4), ALU.mult)
                            tt(tokb_t[s_][:, 512:1024].re("p (h v) -> p h v", h=4), ysb_s,
                               rgs[:, s_, :].re("p (h v) -> p h v", h=4), ALU.mult)

                        ret_pieces += [p0, p1, p2, p2s, p2b, p3, p3b, p4, p4b, p5]
                    op("dve", [st1], [], lambda h: h.memset(st1.ap, 0.0))
                    ret_pos = [0]

                    def next_piece():
                        if ret_pos[0] < len(ret_pieces):
                            ret_pieces[ret_pos[0]]()
                            ret_pos[0] += 1

                    def evac_branch(ob, hq, br, first):
                        o3 = ob[:, 0:260].re("p (q e) -> p q e", q=4)
                        ts(rd, o3[:, :, 64], 1e-30, None, ALU.max)
                        recip(rd, rd)
                        tt(ff, rd, gates[:, :, br * 8 + hq], ALU.mult)
                        dst = o_nsa[:, :, hq * 64:(hq + 1) * 64]
                        if first:
                            tt(dst, o3[:, :, 0:64], ff.un(2).bc([128, 4, 64]), ALU.mult)
                        else:
                            tt(otmp, o3[:, :, 0:64], ff.un(2).bc([128, 4, 64]), ALU.mult)
                            tt(dst, dst, otmp, ALU.add)

                    pipe = {"pend": [], "n": 0}

                    def pipe_push(sc_fn, pv_fn, depth=5):
                        sc_fn()
                        pipe["pend"].append(pv_fn)
                        while len(pipe["pend"]) > depth:
                            pipe["pend"].pop(0)()
                        pipe["n"] += 1
                        if pipe["n"] % 2 == 0:
                            next_piece()

                    def pipe_flush():
                        while pipe["pend"]:
                            pipe["pend"].pop(0)()

                    def run_pipe(steps, depth=5, final=None):
                        for i_s, (sc_fn, pv_fn) in enumerate(steps):
                            if final is not None and i_s == len(steps) - 1:
                                def pv_last(pv_fn=pv_fn):
                                    pv_fn()
                                    final()
                                pipe_push(sc_fn, pv_last, depth)
                            else:
                                pipe_push(sc_fn, pv_fn, depth)
                        if VAR == 'flush':
                            pipe_flush()

                    for g in range(2):
                        steps = []
                        for r in range(4):
                            hq = g * 4 + r
                            cell = {}

                            def sc_fn(hq=hq, cell=cell):
                                ps = psA()
                                mm(ps, kcmpT[:, g, :], QT[0:64, hq, :], start=True, stop=False)
                                mm(ps, ident, cmpmask[:, t0:t0 + CH], start=False, stop=True)
                                pt = rot("PT", PT)
                                act(pt, ps, AF.Exp, scale=0.125)
                                cell["pt"] = pt

                            def pv_fn(hq=hq, r=r, cell=cell):
                                pt = cell["pt"]
                                oc = psB()
                                oc3 = oc[:, 0:260].re("p (q e) -> p q e", q=4)
                                for qs in range(4):
                                    mm(oc3[:, qs, :], pt[:, qs * 128:(qs + 1) * 128], vcmp_aug[:, g, 0:65],
                                       start=(qs == 0), stop=(qs == 3))
                                psl = psA()
                                psl3 = psl[:, 0:132].re("p (q e) -> p q e", q=4)
                                for qs in range(4):
                                    mm(psl3[:, qs, :], pt[:, qs * 128:(qs + 1) * 128], ovl, start=(qs == 0),
                                       stop=(qs == 3))
                                evac_branch(oc, hq, 0, True)
                                tt(sctmp, psl3[:, :, 0:32], rd.un(2).bc([128, 4, 32]), ALU.mult)
                                if r == 0:
                                    tt(score, sctmp, selbias[:, ci * 4:ci * 4 + 4, :], ALU.add)
                                else:
                                    tt(score, score, sctmp, ALU.add)
                                next_piece()

                            steps.append((sc_fn, pv_fn))
                        run_pipe(steps)
                        pipe_flush()
                        for qs in range(4):
                            op("dve", [m8], [score], lambda h, qs=qs: h.max(m8.ap[:, qs, :], score.ap[:, qs, :]))
                        ts(thr, m8[:, :, 7], -0.5, None, ALU.max)
                        tt(sctmp, score, thr.un(2).bc([128, 4, 32]), ALU.is_ge)
                        ts(mneg[:, :, 64:96], sctmp, -NEG, NEG, ALU.mult, ALU.add)
                        def emit_mask(g=g):
                            pb = psA()
                            pbb = pb.bitcast(BF16)
                            for qs in range(4):
                                transpose(pbb[0:96, qs * 128:(qs + 1) * 128], mneg[:, qs, :], ident)
                            for r_ in range(4):
                                act(QT[64:96, g * 4 + r_, :], pbb[64:96, 0:512], AF.Copy)

                        for r in range(4):
                            hq = g * 4 + r
                            Qs = QT[:, hq, :]
                            Q = QT[0:64, hq, :]
                            if r == 3:
                                emit_mask()
                            owb = psB()
                            ow3 = owb[:, 0:260].re("p (q e) -> p q e", q=4)
                            ow_state = {"first": True}
                            steps = []
                            for kt in range(max(0, 4 * ci - 2), 4 * ci + 4):
                                cell = {}

                                def sc_fn(kt=kt, cell=cell):
                                    dg = kt - 4 * ci
                                    qa, qb = max(0, dg), min(3, dg + 2)
                                    ps = psA()
                                    mm(ps[:, qa * 128:(qb + 1) * 128], kwT[:, g, kt * 128:(kt + 1) * 128],
                                       Q[:, qa * 128:(qb + 1) * 128], start=True, stop=False)
                                    masks = []
                                    for qs in range(qa, qb + 1):
                                        dd = qs - dg
                                        if dd == 0:
                                            masks.append((qs, trile))
                                        elif dd == 2:
                                            masks.append((qs, trigt))
                                    for mi, (qs, mk_) in enumerate(masks):
                                        mm(ps[:, qs * 128:(qs + 1) * 128], ident, mk_, start=False,
                                           stop=(mi == len(masks) - 1))
                                    pt = rot("PT", PT)
                                    act(pt[:, qa * 128:(qb + 1) * 128], ps[:, qa * 128:(qb + 1) * 128], AF.Exp,
                                        scale=0.125)
                                    cell["pt"] = pt
                                    cell["qa"], cell["qb"] = qa, qb

                                def pv_fn(kt=kt, cell=cell, ow3=ow3, ow_state=ow_state, g=g):
                                    pt = cell["pt"]
                                    for qs in range(cell["qa"], cell["qb"] + 1):
                                        mm(ow3[:, qs, :], pt[:, qs * 128:(qs + 1) * 128], vw_aug[:, kt, g, 0:65],
                                           start=ow_state["first"], stop=(kt == 4 * ci + 3))
                                        ow_state["first"] = False

                                steps.append((sc_fn, pv_fn))
                            def fin_w(owb=owb, hq=hq):
                                evac_branch(owb, hq, 2, False)
                                next_piece()
                            run_pipe(steps, final=fin_w)
                            if VAR == 'flushw':
                                pipe_flush()

                        for r in range(4):
                            hq = g * 4 + r
                            Qs = QT[:, hq, :]
                            Q = QT[0:64, hq, :]
                            osb = psB()
                            os3 = osb[:, 0:260].re("p (q e) -> p q e", q=4)
                            nkt = 4 * ci + 4
                            steps = []
                            for kt in range(nkt):
                                cell = {}

                                def sc_fn(kt=kt, cell=cell):
                                    ps = psA()
                                    dg = kt - 4 * ci
                                    mm(ps, ksT[:, g, kt * 128:(kt + 1) * 128], Qs, start=True, stop=(dg < 0))
                                    qlo = max(0, dg)
                                    if dg >= 0:
                                        mm(ps[:, dg * 128:(dg + 1) * 128], ident, trile, start=False, stop=True)
                                    pt = rot("PT", PT)
                                    act(pt[:, qlo * 128:512], ps[:, qlo * 128:512], AF.Exp, scale=0.125)
                                    cell["pt"] = pt
                                    cell["qlo"] = qlo

                                def pv_fn(kt=kt, cell=cell, os3=os3, g=g, nkt=nkt):
                                    pt = cell["pt"]
                                    for qs in range(cell["qlo"], 4):
                                        mm(os3[:, qs, :], pt[:, qs * 128:(qs + 1) * 128], vs_aug[:, kt, g, 0:65],
                                           start=(kt == 0 and qs == 0), stop=(kt == nkt - 1))

                                steps.append((sc_fn, pv_fn))
                            def fin_s(osb=osb, hq=hq):
                                evac_branch(osb, hq, 1, False)
                                next_piece()
                            run_pipe(steps, final=fin_s)
                            if VAR == 'flushs':
                                pipe_flush()
                    pipe_flush()
                    for t_ in range(4):
                        act(tokb_t[t_][:, 0:512], o_nsa[:, t_, :], AF.Copy)

                    pipe_flush()
                    while ret_pos[0] < len(ret_pieces):
                        next_piece()
                    wo_slots = [wget(wi["O0"]).re("p (kc n) -> p kc n", kc=8), wget(wi["O1"]).re("p (kc n) -> p kc n", kc=8)]
                    for t_ in range(4):
                        T_tile(0, t_)
                    for t_ in range(4):
                        for nh in range(2):
                            pb = psB()
                            for kc in range(8):
                                mm(pb, T8_t[0][t_][:, kc, :], wo_slots[nh][:, kc, :], start=(kc == 0), stop=(kc == 7))
                            dst = xh_t[t_][:, nh * 512:(nh + 1) * 512]
                            tt(dst, dst, pb, ALU.add)
                        ln_tile(t_)
                        if t_ >= 1:
                            T_tile(1, t_ - 1, lnw2)
                    T_tile(1, 3, lnw2)

                    if b == 0 and ci == NCH - 1:
                        dump("mix", tokb, [128, 4, D])
                        dump("gates", gates, [128, 4, 24])
                    if stage < 6:
                        return
                    hT = T8[1]
                    def ffn_up(fc):
                        U = wget(wi[("U", fc)]).re("p (kc n) -> p kc n", kc=8)
                        at = actT[fc % 2]
                        for fs in range(4):
                            pb = psA()
                            for kc in range(8):
                                mm(pb, U[:, kc, fs * 128:(fs + 1) * 128], hT[:, kc, :], start=(kc == 0), stop=(kc == 7))
                            rl = rot("relu_t", relu_t)
                            act(rl, pb, AF.Relu)
                            act(at[:, fs, :], rl, AF.Square)

                    def ffn_down(fc):
                        Dn = wget(wi[("D", fc)]).re("p (fs n) -> p fs n", fs=4)
                        at = actT[fc % 2]
                        for t_ in range(4):
                            for nh in range(2):
                                pd = psB()
                                for fs in range(4):
                                    mm(pd, at[:, fs, t_ * 128:(t_ + 1) * 128], Dn[:, fs, nh * 512:(nh + 1) * 512],
                                       start=(fs == 0), stop=(fs == 3))
                                dst = xh_t[t_][:, nh * 512:(nh + 1) * 512]
                                tt(dst, dst, pd, ALU.add)

                    nxt = None
                    if ci + 1 < NCH:
                        nxt = (b, ci + 1)
                    elif b + 1 < NB:
                        nxt = (b + 1, 0)

                    def pfA(t_):
                        nb_, nci = nxt
                        r0 = nci * CH + t_ * 128
                        dma("sp", xs, V(x_d.ap[nb_, r0:r0 + 128, :], ()))

                    def pfB(t_):
                        ln_tile(t_, src=xs)

                    def pfC(t_):
                        T_tile(0, t_, lnw1)

                    ffn_up(0)
                    for fc in range(8):
                        if fc + 1 < 8:
                            wget(wi[("D", fc)])
                            ffn_up(fc + 1)
                        if nxt is not None and stage >= 99:
                            if 2 <= fc <= 5:
                                pfC(fc - 2)
                            if 1 <= fc <= 4:
                                pfB(fc - 1)
                            if fc <= 3:
                                pfA(fc)
                        ffn_down(fc)
                    prefetched[0] = nxt is not None and stage >= 99

                chunk_body()
                dma("sp", V(out_d.ap[b, t0:t0 + CH, :].rearrange("(t p) d -> p t d", p=128), ()), xh)
        fw.finish()
        build.stats = {e: fw.E[e].n for e in fw.E}
        build.stats["waits"] = fw.nwaits
    return nc


_CONSTS = None


def _host_inputs(inputs, NB, cores):
    global _CONSTS
    if _CONSTS is None:
        _CONSTS = _consts()
    c = _CONSTS
    f = lambda a: np.ascontiguousarray(np.asarray(a, dtype=np.float32))
    shared = {
        "w_in": f(inputs["w_in"][0]), "w_out": f(inputs["w_out"][0]), "w_up": f(inputs["w_up"][0]),
        "w_down": f(inputs["w_down"][0]),
        "cmp_wk1": f(inputs["cmp_wk1"][0]), "cmp_wv1": f(inputs["cmp_wv1"][0]),
        "cmp_wk2": f(inputs["cmp_wk2"][0]), "cmp_wv2": f(inputs["cmp_wv2"][0]),
        "ln1_t": f(np.asarray(inputs["ln1_w"][0]).reshape(8, 128).T),
        "ln2_t": f(np.asarray(inputs["ln2_w"][0]).reshape(8, 128).T),
        "pek_t": f(np.asarray(inputs["cmp_pe_k"][0]).reshape(16, 128).T),
        "pev_t": f(np.asarray(inputs["cmp_pe_v"][0]).reshape(16, 128).T),
        "retw": f(np.asarray(inputs["ret_norm_w"][0]).reshape(1, 512)),
    }
    qn = np.asarray(inputs["q_norm_w"][0], dtype=np.float32)
    kn = np.asarray(inputs["k_norm_w"][0], dtype=np.float32)
    qkw = np.stack([qn, kn[0], kn[1], kn[2]], axis=1)
    shared["qkw_t"] = f(np.concatenate([qkw, qkw], 0))
    for k, v in c.items():
        if not k.startswith("_"):
            shared[k] = v
    x = np.asarray(inputs["x"], dtype=np.float32)
    maps = []
    for ci in range(cores):
        m = dict(shared)
        m["x"] = np.ascontiguousarray(x[ci * NB:(ci + 1) * NB])
        maps.append(m)
    return maps


def kernel(**inputs):
    global _CONSTS
    if _CONSTS is None:
        _CONSTS = _consts()
    n = 8
    NB = 2
    nc = build(NB=NB, NCH=NCH_FULL, consts=_CONSTS)
    maps = _host_inputs(inputs, NB, n)
    res = run_bass_kernel_spmd(nc, maps, core_ids=list(range(n)))
    out = np.concatenate([r["out"] for r in res.results], axis=0)
    return out.astype(np.float32)
```

```python
import bisect
import math
import os
VAR = os.environ.get('KVAR', '')
from contextlib import ExitStack

import numpy as np
import concourse.bass as bass
import concourse.mybir as mybir
from concourse.bass_utils import run_bass_kernel_spmd

F32 = mybir.dt.float32
BF16 = mybir.dt.bfloat16
ALU = mybir.AluOpType
AF = mybir.ActivationFunctionType
AX = mybir.AxisListType

NEG = -30000.0
S = 2048
D = 1024
CH = 512
NCH_FULL = S // CH
EPS = 1e-6


class Tr:
    __slots__ = ("w", "r", "name", "excl")

    def __init__(self, name="", excl=False):
        self.w = None
        self.r = []
        self.name = name
        self.excl = excl


class V:
    __slots__ = ("ap", "trs")

    def __init__(self, ap, trs):
        self.ap = ap
        self.trs = tuple(trs)

    def __getitem__(self, k):
        return V(self.ap[k], self.trs)

    def re(self, pattern_, **kw):
        return V(self.ap.rearrange(pattern_, **kw), self.trs)

    def bc(self, shape):
        return V(self.ap.broadcast_to(list(shape)), self.trs)

    def un(self, axis):
        return V(self.ap.unsqueeze(axis), self.trs)

    def bitcast(self, dt):
        return V(self.ap.bitcast(dt), self.trs)


class Eng:
    def __init__(self, name, h, sem, is_pe=False):
        self.name = name
        self.h = h
        self.sem = sem
        self.count = 0
        self.n = 0
        self.last = None
        self.tick_idx = []
        self.tick_val = []
        self.seen = {}
        self.is_pe = is_pe


class FW:
    def __init__(self, nc, es, n_dma_sems=24):
        self.nc = nc
        self.es = es
        mk = lambda nm: es.enter_context(nc.semaphore(nm))
        self.E = {
            "pe": Eng("pe", nc.tensor, mk("s_pe"), True),
            "act": Eng("act", nc.scalar, mk("s_act")),
            "dve": Eng("dve", nc.vector, mk("s_dve")),
            "pool": Eng("pool", nc.gpsimd, mk("s_pool")),
            "sp": Eng("sp", nc.sync, mk("s_sp")),
        }
        self.dsems = [[mk(f"s_dma{i}"), 0] for i in range(n_dma_sems)]
        half_ = n_dma_sems // 2
        self.dpool = {"sp": list(range(0, half_)), "pool": list(range(half_, n_dma_sems))}
        self.dnext = {"sp": 0, "pool": 0}
        self.nwaits = 0

    def sbv(self, name, shape, dt=F32):
        t = self.es.enter_context(self.nc.sbuf_tensor("sb_" + name, list(shape), dt))
        return V(t[:], (Tr(name),))

    def psv(self, name, shape, dt=F32):
        t = self.es.enter_context(self.nc.psum_tensor(name, list(shape), dt))
        return V(t[:], (Tr(name, excl=True),))

    def _ticket(self, X, idx):
        k = bisect.bisect_left(X.tick_idx, idx)
        if k < len(X.tick_idx):
            return X.tick_val[k]
        assert X.last is not None and X.n - 1 >= idx
        X.count += 1
        X.last.then_inc(X.sem, 1)
        X.tick_idx.append(X.n - 1)
        X.tick_val.append(X.count)
        return X.count

    def _wait(self, E, ref):
        if ref[0] == "dma":
            _, si, val = ref
            key = ("d", si)
            if E.seen.get(key, 0) >= val:
                return
            E.h.wait_ge(self.dsems[si][0], val)
            E.seen[key] = val
        else:
            _, xn, idx = ref
            X = self.E[xn]
            tk = self._ticket(X, idx)
            if E.seen.get(xn, 0) >= tk:
                return
            E.h.wait_ge(X.sem, tk)
            E.seen[xn] = tk
        self.nwaits += 1

    def _deps(self, E, outs, ins):
        en = E.name
        for v in ins:
            for tr in v.trs:
                w = tr.w
                if w is not None:
                    if not (w[0] == "e" and w[1] == en and E.is_pe):
                        self._wait(E, w)
                if tr.excl:
                    for r in tr.r:
                        if r[0] == "e" and r[1] == en:
                            continue
                        self._wait(E, r)
        for v in outs:
            for tr in v.trs:
                w = tr.w
                if w is not None and not (w[0] == "e" and w[1] == en and E.is_pe):
                    self._wait(E, w)
                for r in tr.r:
                    if r[0] == "e" and r[1] == en and E.is_pe:
                        continue
                    self._wait(E, r)

    def _record(self, ref, outs, ins):
        for v in ins:
            for tr in v.trs:
                tr.r.append(ref)
                if len(tr.r) > 32:
                    best = {}
                    for r in tr.r:
                        key = (r[0], r[1])
                        if key not in best or r[2] > best[key][2]:
                            best[key] = r
                    tr.r = list(best.values())
        for v in outs:
            for tr in v.trs:
                tr.w = ref
                tr.r = []

    def op(self, en, outs, ins, fn):
        E = self.E[en]
        self._deps(E, outs, ins)
        inst = fn(E.h)
        E.last = inst
        idx = E.n
        E.n += 1
        self._record(("e", en, idx), outs, ins)
        return inst

    def dma(self, qn, out, in_, **kw):
        E = self.E[qn]
        if E.n > 0 and (not E.tick_idx or E.tick_idx[-1] < E.n - 1):
            self._ticket(E, E.n - 1)
        self._deps(E, [out], [in_])
        pl = self.dpool[qn]
        si = pl[self.dnext[qn] % len(pl)]
        self.dnext[qn] += 1
        sem, val = self.dsems[si]
        if val > 0:
            self._wait(E, ("dma", si, val))
        inst = E.h.dma_start(out=out.ap, in_=in_.ap, **kw)
        inst.then_inc(sem, 16)
        self.dsems[si][1] = val + 16
        E.last = inst
        E.n += 1
        E.tick_idx.append(E.n - 1)
        E.tick_val.append(E.count)
        self._record(("dma", si, val + 16), [out], [in_])
        return inst

    def finish(self):
        E = self.E["sp"]
        for si, (sem, val) in enumerate(self.dsems):
            if val > 0:
                self._wait(E, ("dma", si, val))


def _consts():
    c = {}
    c["c_ident"] = np.eye(128, dtype=np.float32)
    R = np.zeros((64, 64), np.float32)
    for m in range(32):
        R[m, m + 32] = -1.0
        R[m + 32, m] = 1.0
    rt = np.zeros((128, 128), np.float32)
    rt[:64, :64] = R.T
    rt[64:, 64:] = R.T
    c["c_rt"] = rt
    on = np.zeros((128, 128), np.float32)
    on[:64, :64] = 1.0 / 64
    on[64:, 64:] = 1.0 / 64
    c["c_ones"] = on
    half = 32
    inv = (10000.0 ** (-np.arange(half, dtype=np.float32) / half)).astype(np.float32)
    ang = np.arange(S, dtype=np.float32)[:, None] * inv[None, :]
    cos = np.cos(ang).astype(np.float32).T
    sin = np.sin(ang).astype(np.float32).T
    c["c_cos"] = np.concatenate([cos, cos, cos, cos], 0)
    c["c_sin"] = np.concatenate([sin, sin, sin, sin], 0)
    cc = np.arange(128)[:, None]
    tt = np.arange(S)[None, :]
    c["c_cmpmask"] = np.where((16 * cc + 31 <= tt) & (cc < 127), 0.0, NEG).astype(np.float32)
    k = np.arange(128)[:, None]
    q = np.arange(128)[None, :]
    c["c_trile"] = np.where(k <= q, 0.0, NEG).astype(np.float32)
    c["c_trigt"] = np.where(k > q, 0.0, NEG).astype(np.float32)
    E = np.zeros((32, S), np.float32)
    for j in range(32):
        E[j, 64 * j:64 * j + 64] = 1.0
    c["c_E"] = E
    cs = np.arange(127) * 16
    jb = np.arange(32)
    ov = ((cs[:, None] < (jb[None, :] + 1) * 64) & (cs[:, None] + 32 > jb[None, :] * 64)).astype(np.float32)
    ovl = np.zeros((128, 33), np.float32)
    ovl[:127, :32] = ov
    ovl[:127, 32] = 1.0
    c["c_ovl"] = ovl
    t = np.arange(S)
    cur = t // 64
    forced = (jb[None, :] == 0) | (jb[None, :] == cur[:, None]) | (jb[None, :] == cur[:, None] - 1)
    valid = jb[None, :] <= cur[:, None]
    bias = np.where(forced, 1.0e4, np.where(valid, 0.0, -1.0)).astype(np.float32)
    c["c_selbias"] = np.ascontiguousarray(bias.reshape(16, 128, 32).transpose(1, 0, 2))
    H = 4
    lg = np.log(1.0 - 2.0 ** (-5.0 - np.arange(H, dtype=np.float64)))
    i = np.arange(128, dtype=np.float64)
    dm = np.zeros((128, H, 128), np.float64)
    for h in range(H):
        dm[:, h, :] = (i[None, :] >= i[:, None]) * np.exp(-lg[h] * (i[:, None] + 1.0)) / 8.0
    c["c_dmatp"] = dm.astype(np.float32)
    c["c_zeta8"] = (np.exp(lg[None, :] * (127.0 - i[:, None])) / 8.0).astype(np.float32)
    xi = np.exp(lg[:, None] * (i[None, :] + 1.0))
    xi2 = np.zeros((128, 2, 128), np.float64)
    for j in range(2):
        xi2[:64, j, :] = xi[2 * j][None, :]
        xi2[64:, j, :] = xi[2 * j + 1][None, :]
    c["c_xi"] = xi2.astype(np.float32)
    c["_gamma_c"] = [float(np.exp(lg[h] * 128.0)) for h in range(H)]
    return c


CONST_SHAPES = None


def build(NB=2, NCH=NCH_FULL, consts=None, tab_dt=F32, dbg=(), stage=99):
    nc = bass.Bass("TRN2", target_bir_lowering=False)
    gamma_c = consts["_gamma_c"]

    def din(name, shape):
        return V(nc.dram_tensor(name, list(shape), F32, kind="ExternalInput").ap(), ())

    x_d = din("x", [NB, S, D])
    w_in_d = din("w_in", [D, 2840])
    w_out_d = din("w_out", [D, D])
    w_up_d = din("w_up", [D, 4096])
    w_down_d = din("w_down", [4096, D])
    wk1_d = din("cmp_wk1", [2048, 256])
    wv1_d = din("cmp_wv1", [2048, 256])
    wk2_d = din("cmp_wk2", [256, 64])
    wv2_d = din("cmp_wv2", [256, 64])
    lnw1_d = din("ln1_t", [128, 8])
    lnw2_d = din("ln2_t", [128, 8])
    qkw_d = din("qkw_t", [128, 4])
    pek_d = din("pek_t", [128, 16])
    pev_d = din("pev_t", [128, 16])
    retw_d = din("retw", [1, 512])
    cd = {k: din(k, v.shape) for k, v in consts.items() if not k.startswith("_")}
    out_d = V(nc.dram_tensor("out", [NB, S, D], F32, kind="ExternalOutput").ap(), ())

    with ExitStack() as es:
        fw = FW(nc, es)
        sbv, op, dma = fw.sbv, fw.op, fw.dma

        ident = sbv("ident", [128, 128], BF16)
        rt = sbv("rt", [128, 128], BF16)
        onesb = sbv("onesb", [128, 128], BF16)
        cos = sbv("cos", [128, S], tab_dt)
        sin = sbv("sin", [128, S], tab_dt)
        cmpmask = sbv("cmpmask", [128, S], BF16)
        trile = sbv("trile", [128, 128], BF16)
        trigt = sbv("trigt", [128, 128], BF16)
        ovl = sbv("ovl", [128, 33], BF16)
        selbias = sbv("selbias", [128, 16, 32], F32)
        dmatp = sbv("dmatp", [128, 4, 128], F32)
        zeta8 = sbv("zeta8", [128, 4], F32)
        xit = sbv("xit", [128, 2, 128], F32)
        lnw1 = sbv("lnw1", [128, 8], F32)
        lnw2 = sbv("lnw2", [128, 8], F32)
        qkw = sbv("qkw", [128, 4], F32)
        pek = sbv("pek", [128, 16], BF16)
        pev = sbv("pev", [128, 16], BF16)
        retw = sbv("retw", [128, 512], F32)
        wg = sbv("wg", [128, 8, 24], BF16)
        wk2 = sbv("wk2", [128, 2, 64], BF16)
        wv2 = sbv("wv2", [128, 2, 64], BF16)
        epsb = sbv("epsb", [128, 1], F32)
        biasH = sbv("biasH", [128, 4], F32)

        for dst, key in ((ident, "c_ident"), (rt, "c_rt"), (onesb, "c_ones"), (cmpmask, "c_cmpmask"),
                         (trile, "c_trile"), (trigt, "c_trigt"), (ovl, "c_ovl"),
                         (cos, "c_cos"), (sin, "c_sin")):
            dma("pool", dst, cd[key])
        for dst, src in ((selbias, cd["c_selbias"]), (dmatp, cd["c_dmatp"]), (zeta8, cd["c_zeta8"]),
                         (xit, cd["c_xi"]), (lnw1, lnw1_d), (lnw2, lnw2_d), (qkw, qkw_d)):
            dma("sp", dst, src)
        dma("pool", pek, pek_d)
        dma("pool", pev, pev_d)
        dma("sp", retw, V(retw_d.ap.broadcast_to([128, 512]), ()))
        dma("pool", wg, V(w_in_d.ap[:, 1280:1304].rearrange("(kc p) n -> p kc n", p=128), ()))
        dma("pool", wk2, V(wk2_d.ap.rearrange("(mh p) d -> p mh d", p=128), ()))
        dma("pool", wv2, V(wv2_d.ap.rearrange("(mh p) d -> p mh d", p=128), ()))
        op("dve", [epsb], [], lambda h: h.memset(epsb.ap, EPS))


        def scratch(name, src_v, rows, cols, piece_rows):
            t = nc.dram_tensor("scr_" + name, [rows, cols], BF16, kind="Internal").ap()
            trs = []
            for r0 in range(0, rows, piece_rows):
                tr = Tr(f"scr_{name}_{r0}")
                dma("pool", V(t[r0:r0 + piece_rows, :], (tr,)), V(src_v.ap[r0:r0 + piece_rows, :], ()))
                trs.append(tr)
            return V(t, trs)

        def conv_block(name, src_ap, rows, cols, nsplit=2, defer=None, gate=None):
            t = nc.dram_tensor("scr_" + name, [rows, cols], BF16, kind="Internal").ap()
            trs = []
            pr = rows // nsplit
            for r0 in range(0, rows, pr):
                tr = Tr(f"scr_{name}_{r0}")
                dst = V(t[r0:r0 + pr, :], (tr,))
                src_piece = src_ap[r0:r0 + pr, :]
                if defer is None:
                    dma("pool", dst, V(src_piece, ()))
                else:
                    defer.append((dst, src_piece))
                trs.append(tr)
            return V(t, trs)

        deferred_conv = []
        conv_gate = sbv("conv_gate", [128, 1], F32)

        scr = {}
        scr["S0"] = conv_block("S0", w_in_d.ap[:, 0:512], D, 512)
        scr["S5"] = conv_block("S5", w_in_d.ap[:, 2328:2840], D, 512)
        s1_t = nc.dram_tensor("scr_s1", [D, 512], BF16, kind="Internal").ap()
        s1_trs = []
        for j, c0_ in enumerate([512, 512, 576, 576, 640, 640, 704, 704]):
            tr = Tr(f"scr_s1_{j}")
            dma("pool", V(s1_t[:, j * 64:(j + 1) * 64], (tr,)), V(w_in_d.ap[:, c0_:c0_ + 64], ()))
            s1_trs.append(tr)
        scr["S1"] = V(s1_t, s1_trs)
        scr["S2"] = conv_block("S2", w_in_d.ap[:, 768:1280], D, 512)
        scr["S3"] = conv_block("S3", w_in_d.ap[:, 1304:1816], D, 512)
        scr["S4"] = conv_block("S4", w_in_d.ap[:, 1816:2328], D, 512)
        scr["CK"] = conv_block("CK", wk1_d.ap, 2048, 256)
        scr["CV"] = conv_block("CV", wv1_d.ap, 2048, 256)
        scr["O0"] = conv_block("O0", w_out_d.ap[:, 0:512], D, 512)
        scr["O1"] = conv_block("O1", w_out_d.ap[:, 512:1024], D, 512)
        for fc in range(8):
            scr[("U", fc)] = conv_block(f"U{fc}", w_up_d.ap[:, fc * 512:(fc + 1) * 512], D, 512, defer=deferred_conv)
            scr[("D", fc)] = conv_block(f"D{fc}", w_down_d.ap[fc * 512:(fc + 1) * 512, :], 512, D, defer=deferred_conv)

        ksT = sbv("ksT", [128, 2, S], BF16)
        kwT = sbv("kwT", [64, 2, S], BF16)
        vs_aug = sbv("vs_aug", [128, 16, 2, 66], BF16)
        vw_aug = sbv("vw_aug", [128, 16, 2, 66], BF16)
        kcw = sbv("kcw", [128, 2, 544], BF16)
        vcw = sbv("vcw", [128, 2, 544], BF16)
        kcmpT = sbv("kcmpT", [64, 2, 128], BF16)
        vcmpT = sbv("vcmpT", [64, 2, 128], BF16)
        vcmp_aug = sbv("vcmp_aug", [128, 2, 66], BF16)
        state = sbv("state", [64, 4, 128], F32)
        state_bf = sbv("state_bf", [64, 4, 128], BF16)
        op("dve", [ksT], [], lambda h: h.memset(ksT.ap[64:128, :, :], 0.0))
        for g_ in range(2):
            dma("pool", ksT[64:96, g_, :], cd["c_E"])
        op("dve", [vs_aug], [], lambda h: h.memset(vs_aug.ap, 1.0))
        op("dve", [vw_aug], [], lambda h: h.memset(vw_aug.ap, 1.0))
        op("dve", [vcmp_aug], [], lambda h: h.memset(vcmp_aug.ap, 1.0))
        op("dve", [kcmpT], [], lambda h: h.memset(kcmpT.ap, 0.0))
        op("dve", [vcmpT], [], lambda h: h.memset(vcmpT.ap, 0.0))

        def tiled(v, slicer):
            trs = [Tr(f"t{t}") for t in range(4)]
            tiles = [V(slicer(v.ap, t), (trs[t],)) for t in range(4)]
            return V(v.ap, trs), tiles

        xh, xh_t = tiled(sbv("xh", [128, 4, D], F32), lambda ap, t: ap[:, t, :])
        tokb, tokb_t = tiled(sbv("tokb", [128, 4, D], BF16), lambda ap, t: ap[:, t, :])
        T8, T8_t = [], []
        for i in range(2):
            a_, t_l = tiled(sbv(f"T8_{i}", [128, 8, CH], BF16), lambda ap, t: ap[:, :, t * 128:(t + 1) * 128])
            T8.append(a_)
            T8_t.append(t_l)
        QT = sbv("QT", [128, 8, CH], BF16)
        op("dve", [QT], [], lambda h: h.memset(QT.ap[64:128, :, :], 0.0))
        rqxT = sbv("rqxT", [64, 4, CH], BF16)
        rkT = sbv("rkT", [64, 4, CH], BF16)
        kz = sbv("kz", [128, 4, 64], BF16)
        rv = sbv("rv", [128, 4, 512], BF16)
        rgs = sbv("rgs", [128, 4, 512], BF16)
        gates = sbv("gates", [128, 4, 24], F32)
        o_nsa = sbv("o_nsa", [128, 4, 512], F32)
        ss4 = sbv("ss4", [128, 4], F32)
        rstd4 = sbv("rstd4", [128, 4], F32)
        sqb = [sbv(f"sqb{i}", [128, CH], BF16) for i in range(2)]
        rstd_t = [sbv(f"rstd_t{i}", [128, CH], F32) for i in range(2)]
        qnb = [sbv(f"qnb{i}", [128, CH], BF16) for i in range(3)]
        big4k = fw.es.enter_context(nc.sbuf_tensor("sb_big4k", [128, 2 * CH], F32))
        _tra, _trb = Tr("t1a"), Tr("t1b")
        t1 = [V(big4k[:, 0:CH], (_tra,)), V(big4k[:, CH:2 * CH], (_trb,))]
        xs = V(big4k[:], (_tra, _trb))
        uptmp = [sbv(f"uptmp{i}", [128, CH], BF16) for i in range(2)]
        t2 = [sbv(f"t2_{i}", [128, CH], F32) for i in range(2)]
        PT = [sbv(f"PT{i}", [128, CH], BF16) for i in range(7)]
        HT = sbv("HT", [128, 2, 64], BF16)
        score = sbv("score", [128, 4, 32], F32)
        sctmp = sbv("sctmp", [128, 4, 32], F32)
        m8 = sbv("m8", [128, 4, 8], F32)
        thr = sbv("thr", [128, 4], F32)
        mneg = sbv("mneg", [128, 4, 96], BF16)
        op("dve", [mneg], [], lambda h: h.memset(mneg.ap, 0.0))
        rd = sbv("rd", [128, 4], F32)
        ff = sbv("ff", [128, 4], F32)
        otmp = sbv("otmp", [128, 4, 64], F32)
        innb = [sbv(f"innb{i}", [128, 128], BF16) for i in range(4)]
        st1 = sbv("st1", [128, 32], F32)
        st2 = sbv("st2", [128, 4], F32)
        stm = sbv("stm", [128, 4], F32)
        stv = sbv("stv", [128, 4], F32)
        actT = [sbv(f"actT{i}", [128, 4, CH], BF16) for i in range(2)]
        relu_t = [sbv(f"relu_t{i}", [128, CH], F32) for i in range(1)]
        WS = [sbv(f"WS{i}", [128, 4096], BF16) for i in range(4)]

        banks = [fw.psv(f"pb{i}", [128, 512], F32) for i in range(8)]
        rrA = [0]
        rrB = [0]

        def psA():
            b = banks[rrA[0] % 4]
            rrA[0] += 1
            return b

        def psB():
            b = banks[5 + rrB[0] % 3]
            rrB[0] += 1
            return b

        rrP = [0]

        def psP():
            b = banks[rrP[0] % 6]
            rrP[0] += 1
            return b

        rrC = [0]

        def psC():
            return banks[7]

        rr = {}

        def rot(name, lst):
            i = rr.get(name, 0)
            rr[name] = i + 1
            return lst[i % len(lst)]

        def wsrc_cols(wd, a, b):
            return V(wd.ap[:, a:b].rearrange("(kc p) n -> p kc n", p=128), wd.trs)

        items = []

        def add_item(parts):
            items.append(parts)
            return len(items) - 1

        witems = {}

        def kc_view(v):
            return V(v.ap.rearrange("(kc p) n -> p kc n", p=128), v.trs)

        for b in range(NB):
            for ci in range(NCH):
                d = {}
                for key in ("S0", "S5", "S1", "S2", "S3", "S4"):
                    d[key] = add_item([(lambda s: s.re("p (kc n) -> p kc n", kc=8), kc_view(scr[key]))])
                for key in ("CK", "CV"):
                    d[key] = add_item([(lambda s: s.re("p (i m) -> p i m", i=16),
                                        V(scr[key].ap.rearrange("(i p) m -> p i m", p=128), scr[key].trs))])
                for key in ("O0", "O1"):
                    d[key] = add_item([(lambda s: s.re("p (kc n) -> p kc n", kc=8), kc_view(scr[key]))])
                for fc in range(8):
                    d[("U", fc)] = add_item([(lambda s: s.re("p (kc n) -> p kc n", kc=8), kc_view(scr[("U", fc)]))])
                    d[("D", fc)] = add_item([(lambda s: s.re("p (fs n) -> p fs n", fs=4),
                                              V(scr[("D", fc)].ap.rearrange("(fs p) n -> p fs n", p=128),
                                                scr[("D", fc)].trs))])
                witems[(b, ci)] = d
        issued = [0]
        LOOK = 2

        def wget(idx):
            while issued[0] < len(items) and issued[0] <= idx + LOOK:
                k = issued[0]
                slot = WS[k % 4]
                for viewfn, src in items[k]:
                    dma("sp", viewfn(slot), src)
                issued[0] += 1
            return WS[idx % 4]

        def dump(name, v, shape):
            if name in dbg:
                dd = V(nc.dram_tensor("dbg_" + name, list(shape), F32, kind="ExternalOutput").ap(), ())
                dma("pool", dd, v)

        def mm(out, lhsT, rhs, start=True, stop=True):
            return op("pe", [out], [lhsT, rhs],
                      lambda h: h.matmul(out.ap, lhsT.ap, rhs.ap, start=start, stop=stop, skip_group_check=True))

        def transpose(out, in_, idn):
            return op("pe", [out], [in_, idn], lambda h: h.transpose(out.ap, in_.ap, idn.ap))

        def act(out, in_, func, bias=None, scale=1.0, accum=None):
            outs = [out] + ([accum] if accum is not None else [])
            ins = [in_] + ([bias] if isinstance(bias, V) else []) + ([scale] if isinstance(scale, V) else [])
            kw = {}
            if bias is not None:
                kw["bias"] = bias.ap if isinstance(bias, V) else bias
            if accum is not None:
                kw["accum_out"] = accum.ap
            sc = scale.ap if isinstance(scale, V) else scale
            return op("act", outs, ins, lambda h: h.activation(out.ap, in_.ap, func, scale=sc, **kw))

        def tt(out, a, b, alu, eng="dve"):
            return op(eng, [out], [a, b], lambda h: h.tensor_tensor(out.ap, a.ap, b.ap, alu))

        def ts(out, a, s1, s2, op0, op1=None):
            ins = [a] + [s for s in (s1, s2) if isinstance(s, V)]
            a1 = s1.ap if isinstance(s1, V) else s1
            a2 = s2.ap if isinstance(s2, V) else s2
            if op1 is None:
                return op("dve", [out], ins, lambda h: h.tensor_scalar(out.ap, a.ap, a1, a2, op0))
            return op("dve", [out], ins, lambda h: h.tensor_scalar(out.ap, a.ap, a1, a2, op0, op1))

        def stt(out, a, sc, b, op0, op1):
            ins = [a, b] + ([sc] if isinstance(sc, V) else [])
            s_ = sc.ap if isinstance(sc, V) else sc
            return op("dve", [out], ins, lambda h: h.scalar_tensor_tensor(out.ap, a.ap, s_, b.ap, op0, op1))

        def vcopy(out, in_):
            return op("dve", [out], [in_], lambda h: h.tensor_copy(out.ap, in_.ap))

        def recip(out, in_):
            return op("dve", [out], [in_], lambda h: h.reciprocal(out.ap, in_.ap))

        def rsum(out, in_):
            return op("dve", [out], [in_], lambda h: h.reduce_sum(out.ap, in_.ap, AX.X))

        def ln_tile(t_, src=None):
            if src is None:
                src = xh_t[t_]
            op("dve", [ss4], [], lambda h: h.memset(ss4.ap[:, t_:t_ + 1], 0.0))
            act(tokb_t[t_], src, AF.Square, accum=ss4[:, t_:t_ + 1])
            act(rstd4[:, t_:t_ + 1], ss4[:, t_:t_ + 1], AF.Sqrt, bias=epsb, scale=1.0 / D)
            recip(rstd4[:, t_:t_ + 1], rstd4[:, t_:t_ + 1])
            ts(tokb_t[t_], src, rstd4[:, t_:t_ + 1], None, ALU.mult)

        def T_tile(di, t_, lnw=None):
            dst_t = T8_t[di][t_]
            for half in range(2):
                pb = psA()
                pbb = pb.bitcast(BF16)
                for j in range(4):
                    kc = half * 4 + j
                    transpose(pbb[:, j * 128:(j + 1) * 128], tokb_t[t_][:, kc * 128:(kc + 1) * 128], ident)
                src = pbb[:, 0:512].re("p (j q) -> p j q", j=4)
                dst = dst_t[:, half * 4:(half + 1) * 4, :]
                if lnw is None:
                    act(dst, src, AF.Copy)
                else:
                    tt(dst, src, lnw[:, half * 4:(half + 1) * 4].un(2).bc([128, 4, 128]), ALU.mult)

        def rmsnorm_to_T(lnw, di):
            for t_ in range(4):
                ln_tile(t_)
            for t_ in range(4):
                T_tile(di, t_, lnw)

        def proj_fm(slot3, c0, M, xT):
            pb = psA()
            for kc in range(8):
                mm(pb[0:M, :], slot3[:, kc, c0:c0 + M], xT[:, kc, :], start=(kc == 0), stop=(kc == 7))
            return pb

        def norm_rope(pb, M, wcol, t0, outs, post=None):
            qn = rot("qnb", qnb)
            if wcol is not None:
                sq = rot("sqb", sqb)
                act(sq[0:M, :], pb[0:M, :], AF.Square)
                pm = psA()
                mm(pm[0:M, :], onesb[0:M, 0:M], sq[0:M, :])
                rs = rot("rstd_t", rstd_t)
                act(rs[0:M, :], pm[0:M, :], AF.Sqrt, bias=epsb[0:M, :])
                recip(rs[0:M, :], rs[0:M, :])
                stt(qn[0:M, :], pb[0:M, :], qkw[0:M, wcol:wcol + 1], rs[0:M, :], ALU.mult, ALU.mult)
            else:
                act(qn[0:M, :], pb[0:M, :], AF.Copy)
            pr = psA()
            mm(pr[0:M, :], rt[0:M, 0:M], qn[0:M, :])
            a = rot("t1", t1)
            b_ = rot("t2", t2)
            tt(a[0:M, :], qn[0:M, :], cos[0:M, t0:t0 + CH], ALU.mult)
            tt(b_[0:M, :], pr[0:M, :], sin[0:M, t0:t0 + CH], ALU.mult)
            if post is not None:
                tt(a[0:M, :], a[0:M, :], b_[0:M, :], ALU.add)
                for dst, r0, r1 in outs:
                    tt(dst, a[r0:r1, :].re("p (s q) -> p s q", s=4), post, ALU.mult)
            else:
                for dst, r0, r1 in outs:
                    tt(dst, a[r0:r1, :], b_[r0:r1, :], ALU.add)

        def setup_bias():
            for kv, (wd, pe) in enumerate(((wk1_d, pek), (wv1_d, pev))):
                slot = WS[kv]
                s3 = slot.re("p (i m) -> p i m", i=16)
                dma("pool", s3, V(wd.ap.rearrange("(i p) m -> p i m", p=128), ()))
                pb = psA()
                for mh in range(2):
                    for i in range(16):
                        mm(pb[:, mh:mh + 1], s3[:, i, mh * 128:(mh + 1) * 128], pe[:, i:i + 1],
                           start=(i == 0), stop=(i == 15))
                vcopy(biasH[:, kv * 2:kv * 2 + 2], pb[:, 0:2])


        prefetched = [False]
        for b in range(NB):
            op("dve", [state], [], lambda h: h.memset(state.ap, 0.0))
            op("dve", [state_bf], [], lambda h: h.memset(state_bf.ap, 0.0))
            op("dve", [kcw], [], lambda h: h.memset(kcw.ap, 0.0))
            op("dve", [vcw], [], lambda h: h.memset(vcw.ap, 0.0))
            for ci in range(NCH):
                t0 = ci * CH
                wi = witems[(b, ci)]
                def chunk_body(b=b, ci=ci, t0=t0, wi=wi):
                    dma("sp", xh, V(x_d.ap[b, t0:t0 + CH, :].rearrange("(t p) d -> p t d", p=128), ()))
                    xT = T8[0]
                    if stage < 0.5:
                        return
                    if not prefetched[0]:
                        rmsnorm_to_T(lnw1, 0)
                    if stage < 1.01:
                        return

                    def vc_g(g):
                        s1_ = wget(wi["S1"]).re("p (kc n) -> p kc n", kc=8)
                        if True:
                            pb = psC()
                            for kc in range(8):
                                mm(pb, s1_[:, kc, 256 + g * 128:256 + (g + 1) * 128], xT[:, kc, :],
                                   start=(kc == 0), stop=(kc == 7))
                            act(vcw[0:64, g, 16:528], pb[0:64, :], AF.Copy)
                            act(vcw[64:128, g, 15:527], pb[64:128, :], AF.Copy)

                    def kw_tile(t_):
                        s2_ = wget(wi["S2"]).re("p (kc n) -> p kc n", kc=8)
                        if True:
                            tile_i = ci * 4 + t_
                            pb = psC()
                            for kc in range(8):
                                mm(pb[:, 0:128], xT[:, kc, t_ * 128:(t_ + 1) * 128], s2_[:, kc, 128:256],
                                   start=(kc == 0), stop=(kc == 7))
                            for kc in range(8):
                                mm(pb[:, 128:256], xT[:, kc, t_ * 128:(t_ + 1) * 128], s2_[:, kc, 384:512],
                                   start=(kc == 0), stop=(kc == 7))
                            for kc in range(8):
                                mm(pb[:, 256:280], xT[:, kc, t_ * 128:(t_ + 1) * 128], wg[:, kc, :],
                                   start=(kc == 0), stop=(kc == 7))
                            act(vs_aug[:, tile_i, :, 0:64], pb[:, 0:128].re("p (g d) -> p g d", g=2), AF.Copy)
                            act(vw_aug[:, tile_i, :, 0:64], pb[:, 128:256].re("p (g d) -> p g d", g=2), AF.Copy)
                            act(gates[:, t_, :], pb[:, 256:280], AF.Sigmoid)

                    if ci > 0:
                        for g in range(2):
                            vcopy(kcw[0:64, g, 0:16], kcw[0:64, g, 512:528])
                            vcopy(kcw[64:128, g, 0:15], kcw[64:128, g, 512:527])
                            vcopy(vcw[0:64, g, 0:16], vcw[0:64, g, 512:528])
                            vcopy(vcw[64:128, g, 0:15], vcw[64:128, g, 512:527])
                    tasks = []
                    for j in range(4):
                        tasks.append(dict(key="S0", c0=j * 128, M=128, wcol=0,
                                          outs=[(QT[0:64, 2 * j, :], 0, 64, False), (QT[0:64, 2 * j + 1, :], 64, 128, True)]))
                    for g in range(2):
                        tasks.append(dict(key="S1", c0=g * 128, M=128, wcol=1,
                                          outs=[(kcw[0:64, g, 16:528], 0, 64, False),
                                                (kcw[64:128, g, 15:527], 64, 128, False)]))
                    tasks.append(dict(key="S2", c0=0, M=128, wcol=2,
                                      outs=[(ksT[0:64, 0, t0:t0 + CH], 0, 64, False), (ksT[0:64, 1, t0:t0 + CH], 64, 128, True)]))
                    tasks.append(dict(key="S2", c0=256, M=128, wcol=3,
                                      outs=[(kwT[:, 0, t0:t0 + CH], 0, 64, False), (kwT[:, 1, t0:t0 + CH], 64, 128, True)]))
                    for j in range(2):
                        tasks.append(dict(key="S3", c0=j * 128, M=128, wcol=None,
                                          outs=[(rqxT[:, 2 * j, :], 0, 64, False), (rqxT[:, 2 * j + 1, :], 64, 128, True)],
                                          post=xit[:, j, :].un(1).bc([128, 4, 128])))
                    for j in range(2):
                        tasks.append(dict(key="S3", c0=256 + j * 128, M=128, wcol=None,
                                          outs=[(rkT[:, 2 * j, :], 0, 64, False), (rkT[:, 2 * j + 1, :], 64, 128, True)]))
                    nt = len(tasks)
                    fillers = {8: [lambda: kw_tile(0)], 9: [lambda: kw_tile(1)], 10: [lambda: kw_tile(2)],
                               11: [lambda: kw_tile(3)]}

                    def rv_tile(t_):
                        s4_ = wget(wi["S4"]).re("p (kc n) -> p kc n", kc=8)
                        pb = psC()
                        for kc in range(8):
                            mm(pb, xT[:, kc, t_ * 128:(t_ + 1) * 128], s4_[:, kc, :], start=(kc == 0), stop=(kc == 7))
                        act(rv[:, t_, :], pb, AF.Copy)

                    def rg_tile(t_):
                        s5_ = wget(wi["S5"]).re("p (kc n) -> p kc n", kc=8)
                        pb = psC()
                        for kc in range(8):
                            mm(pb, xT[:, kc, t_ * 128:(t_ + 1) * 128], s5_[:, kc, :], start=(kc == 0), stop=(kc == 7))
                        act(rgs[:, t_, :], pb, AF.Silu)

                    for t_ in range(4):
                        fillers[1 + t_] = [lambda t_=t_: rg_tile(t_)]
                    fillers[6] = [lambda: vc_g(0)]
                    fillers[7] = [lambda: vc_g(1)]
                    fillers[12] = [lambda: rv_tile(0), lambda: rv_tile(1)]
                    fillers[13] = [lambda: rv_tile(2), lambda: rv_tile(3)]
                    stt_ = [dict() for _ in tasks]
                    for k in range(nt + 2):
                        if k < nt:
                            T = tasks[k]
                            M = T["M"]
                            slot3 = wget(wi[T["key"]]).re("p (kc n) -> p kc n", kc=8)
                            pb = banks[k % 3]
                            for kc in range(8):
                                mm(pb[0:M, :], slot3[:, kc, T["c0"]:T["c0"] + M], xT[:, kc, :],
                                   start=(kc == 0), stop=(kc == 7))
                            stt_[k]["pb"] = pb
                            if T["wcol"] is not None:
                                sq = sqb[k % 2]
                                act(sq[0:M, :], pb[0:M, :], AF.Square)
                                stt_[k]["sq"] = sq
                            if "after" in T:
                                T["after"]()
                        j = k - 1
                        if 0 <= j < nt:
                            T = tasks[j]
                            M = T["M"]
                            pb = stt_[j]["pb"]
                            qn = qnb[j % 3]
                            if T["wcol"] is not None:
                                pm = banks[3 + j % 2]
                                mm(pm[0:M, :], onesb[0:M, 0:M], stt_[j]["sq"][0:M, :])
                                rs = rstd_t[j % 2]
                                act(rs[0:M, :], pm[0:M, :], AF.Ln, bias=epsb[0:M, :])
                                act(rs[0:M, :], rs[0:M, :], AF.Exp, scale=-0.5)
                                wc = T["wcol"]
                                stt(qn[0:M, :], pb[0:M, :], qkw[0:M, wc:wc + 1], rs[0:M, :], ALU.mult, ALU.mult)
                            else:
                                act(qn[0:M, :], pb[0:M, :], AF.Copy)
                            stt_[j]["qn"] = qn
                        for f_ in fillers.get(k, []):
                            f_()
                        i_ = k - 2
                        if 0 <= i_ < nt:
                            T = tasks[i_]
                            M = T["M"]
                            qn = stt_[i_]["qn"]
                            pr = banks[5 + i_ % 2]
                            mm(pr[0:M, :], rt[0:M, 0:M], qn[0:M, :])
                            a = t1[i_ % 2]
                            b_ = t2[i_ % 2]
                            tt(a[0:M, :], qn[0:M, :], cos[0:M, t0:t0 + CH], ALU.mult)
                            tt(b_[0:M, :], pr[0:M, :], sin[0:M, t0:t0 + CH], ALU.mult)
                            if T.get("post") is not None:
                                tt(a[0:M, :], a[0:M, :], b_[0:M, :], ALU.add)
                            for dst, r0, r1, via in T["outs"]:
                                tgt = uptmp[i_ % 2][r0:r1, :] if via else dst
                                if T.get("post") is not None:
                                    tt(tgt.re("p (s q) -> p s q", s=4), a[r0:r1, :].re("p (s q) -> p s q", s=4),
                                       T["post"][r0:r1], ALU.mult)
                                else:
                                    tt(tgt, a[r0:r1, :], b_[r0:r1, :], ALU.add)
                                if via:
                                    dma("sp", dst, tgt)
                    if stage < 1.5:
                        return
                    if b == 0 and ci == 0:
                        op("dve", [conv_gate], [], lambda h: h.memset(conv_gate.ap, 0.0))
                        for dst, src_piece in deferred_conv:
                            dma("pool", dst, V(src_piece, conv_gate.trs))
                    if stage < 3:
                        return
                    c0 = 32 * ci - 1
                    for kv, (key, srcw, dstT) in enumerate((("CK", kcw, kcmpT), ("CV", vcw, vcmpT))):
                        w1 = wget(wi[key]).re("p (i m) -> p i m", i=16)
                        w2 = wk2 if kv == 0 else wv2
                        if b == 0 and ci == 0:
                            pe_ = pek if kv == 0 else pev
                            pbz = psA()
                            for mh in range(2):
                                for i in range(16):
                                    mm(pbz[:, mh:mh + 1], w1[:, i, mh * 128:(mh + 1) * 128], pe_[:, i:i + 1],
                                       start=(i == 0), stop=(i == 15))
                            vcopy(biasH[:, kv * 2:kv * 2 + 2], pbz[:, 0:2])
                        for mh in range(2):
                            pb = psA()
                            for i in range(16):
                                mm(pb[:, 0:64].re("p (g c) -> p g c", g=2), w1[:, i, mh * 128:(mh + 1) * 128],
                                   srcw[:, :, 2 * i:2 * i + 512:16], start=(i == 0), stop=(i == 15))
                            act(HT[:, mh, :], pb[:, 0:64], AF.Silu, bias=biasH[:, kv * 2 + mh:kv * 2 + mh + 1])
                        for g in range(2):
                            pb = psA()
                            for mh in range(2):
                                mm(pb[0:64, 0:32], w2[:, mh, :], HT[:, mh, g * 32:(g + 1) * 32],
                                   start=(mh == 0), stop=(mh == 1))
                            if ci == 0:
                                vcopy(dstT[:, g, 0:31], pb[0:64, 1:32])
                            else:
                                vcopy(dstT[:, g, c0:c0 + 32], pb[0:64, 0:32])
                    for g in range(2):
                        pb = psA()
                        pbb = pb.bitcast(BF16)
                        transpose(pbb[:, 0:64], vcmpT[:, g, :], ident[0:64, 0:64])
                        vcopy(vcmp_aug[:, g, 0:64], pbb[:, 0:64])

                    if stage < 4:
                        return
                    ret_pieces = []
                    for s_ in range(4):
                        ts_ = slice(s_ * 128, (s_ + 1) * 128)
                        cell = {}

                        def p0(s_=s_, ts_=ts_, cell=cell):
                            pb = psA()
                            pbb = pb.bitcast(BF16)
                            for hr in range(4):
                                transpose(pbb[:, hr * 64:(hr + 1) * 64], rkT[:, hr, ts_], ident[0:64, 0:64])
                            tt(kz, pbb[:, 0:256].re("p (h d) -> p h d", h=4), zeta8.un(2).bc([128, 4, 64]), ALU.mult)

                        def p1(s_=s_, ts_=ts_, cell=cell):
                            for hr in range(4):
                                pi = psA()
                                mm(pi[:, 0:128], rkT[:, hr, ts_], rqxT[:, hr, ts_])
                                tt(innb[hr], pi[:, 0:128], dmatp[:, hr, :], ALU.mult)

                        def p2(s_=s_, ts_=ts_, cell=cell):
                            py = banks[4]
                            for hr in range(4):
                                mm(py[:, hr * 128:(hr + 1) * 128], innb[hr], rv[:, s_, hr * 128:(hr + 1) * 128],
                                   start=(hr == 0), stop=False)
                                mm(py[:, hr * 128:(hr + 1) * 128], rqxT[:, hr, ts_], state_bf[:, hr, :],
                                   start=False, stop=(hr == 3))
                            cell["py"] = py

                        def p2s(s_=s_, cell=cell):
                            py = cell["py"]
                            ysb_s = t2[s_ % 2].re("p (h v) -> p h v", h=4)
                            ysq = rot("t1", t1).re("p (h v) -> p h v", h=4)
                            cell["ysb"] = ysb_s
                            for hr in range(4):
                                act(ysb_s[:, hr, :], py[:, hr * 128:(hr + 1) * 128], AF.Copy,
                                    accum=st1[:, s_ * 8 + hr:s_ * 8 + hr + 1])
                            for hr in range(4):
                                act(ysq[:, hr, :], py[:, hr * 128:(hr + 1) * 128], AF.Square,
                                    accum=st1[:, s_ * 8 + 4 + hr:s_ * 8 + 5 + hr])

                        def p2b(s_=s_, cell=cell):
                            sm = st1[:, s_ * 8:s_ * 8 + 4]
                            sq_ = st1[:, s_ * 8 + 4:s_ * 8 + 8]
                            ts(stm, sm, 1.0 / 128, None, ALU.mult)
                            tt(stv, stm, stm, ALU.mult)
                            stt(stv, sq_, 1.0 / 128, stv, ALU.mult, ALU.subtract)

                        def p3(s_=s_, ts_=ts_, cell=cell):
                            pst = psA()
                            for hr in range(4):
                                mm(pst[0:64, hr * 128:(hr + 1) * 128], kz[:, hr, :], rv[:, s_, hr * 128:(hr + 1) * 128],
                                   start=(hr == 0), stop=(hr == 3))
                            for hr in range(4):
                                stt(state[:, hr, :], state[:, hr, :], gamma_c[hr], pst[0:64, hr * 128:(hr + 1) * 128],
                                    ALU.mult, ALU.add)

                        def p3b(s_=s_, cell=cell):
                            act(stv, stv, AF.Sqrt, bias=epsb)
                            act(state_bf, state, AF.Copy)

                        def p4(s_=s_, cell=cell):
                            recip(stv, stv)
                            stt(st2, stm, -1.0, stv, ALU.mult, ALU.mult)

                        def p4b(s_=s_, cell=cell):
                            ysb_s = cell["ysb"]
                            for hr in range(4):
                                act(ysb_s[:, hr, :], ysb_s[:, hr, :], AF.Identity, bias=st2[:, hr:hr + 1],
                                    scale=stv[:, hr:hr + 1])

                        def p5(s_=s_, cell=cell):
                            ysb_s = cell["ysb"]
                            tt(ysb_s, ysb_s, retw.re("p (h v) -> p h v", h=4), ALU.mult)
                            tt(tokb_t[s_][:, 512:1024].re("p (h v) -> p h v", h=4), ysb_s,
                               rgs[:, s_, :].re("p (h v) -> p h v", h=4), ALU.mult)

                        ret_pieces += [p0, p1, p2, p2s, p2b, p3, p3b, p4, p4b, p5]
                    op("dve", [st1], [], lambda h: h.memset(st1.ap, 0.0))
                    ret_pos = [0]

                    def next_piece():
                        if ret_pos[0] < len(ret_pieces):
                            ret_pieces[ret_pos[0]]()
                            ret_pos[0] += 1

                    def evac_branch(ob, hq, br, first):
                        o3 = ob[:, 0:260].re("p (q e) -> p q e", q=4)
                        ts(rd, o3[:, :, 64], 1e-30, None, ALU.max)
                        recip(rd, rd)
                        tt(ff, rd, gates[:, :, br * 8 + hq], ALU.mult)
                        dst = o_nsa[:, :, hq * 64:(hq + 1) * 64]
                        if first:
                            tt(dst, o3[:, :, 0:64], ff.un(2).bc([128, 4, 64]), ALU.mult)
                        else:
                            tt(otmp, o3[:, :, 0:64], ff.un(2).bc([128, 4, 64]), ALU.mult)
                            tt(dst, dst, otmp, ALU.add)

                    pipe = {"pend": [], "n": 0}

                    def pipe_push(sc_fn, pv_fn, depth=5):
                        sc_fn()
                        pipe["pend"].append(pv_fn)
                        while len(pipe["pend"]) > depth:
                            pipe["pend"].pop(0)()
                        pipe["n"] += 1
                        if pipe["n"] % 2 == 0:
                            next_piece()

                    def pipe_flush():
                        while pipe["pend"]:
                            pipe["pend"].pop(0)()

                    def run_pipe(steps, depth=5, final=None):
                        for i_s, (sc_fn, pv_fn) in enumerate(steps):
                            if final is not None and i_s == len(steps) - 1:
                                def pv_last(pv_fn=pv_fn):
                                    pv_fn()
                                    final()
                                pipe_push(sc_fn, pv_last, depth)
                            else:
                                pipe_push(sc_fn, pv_fn, depth)
                        if VAR == 'flush':
                            pipe_flush()

                    for g in range(2):
                        steps = []
                        for r in range(4):
                            hq = g * 4 + r
                            cell = {}

                            def sc_fn(hq=hq, cell=cell):
                                ps = psA()
                                mm(ps, kcmpT[:, g, :], QT[0:64, hq, :], start=True, stop=False)
                                mm(ps, ident, cmpmask[:, t0:t0 + CH], start=False, stop=True)
                                pt = rot("PT", PT)
                                act(pt, ps, AF.Exp, scale=0.125)
                                cell["pt"] = pt

                            def pv_fn(hq=hq, r=r, cell=cell):
                                pt = cell["pt"]
                                oc = psB()
                                oc3 = oc[:, 0:260].re("p (q e) -> p q e", q=4)
                                for qs in range(4):
                                    mm(oc3[:, qs, :], pt[:, qs * 128:(qs + 1) * 128], vcmp_aug[:, g, 0:65],
                                       start=(qs == 0), stop=(qs == 3))
                                psl = psA()
                                psl3 = psl[:, 0:132].re("p (q e) -> p q e", q=4)
                                for qs in range(4):
                                    mm(psl3[:, qs, :], pt[:, qs * 128:(qs + 1) * 128], ovl, start=(qs == 0),
                                       stop=(qs == 3))
                                evac_branch(oc, hq, 0, True)
                                tt(sctmp, psl3[:, :, 0:32], rd.un(2).bc([128, 4, 32]), ALU.mult)
                                if r == 0:
                                    tt(score, sctmp, selbias[:, ci * 4:ci * 4 + 4, :], ALU.add)
                                else:
                                    tt(score, score, sctmp, ALU.add)
                                next_piece()

                            steps.append((sc_fn, pv_fn))
                        run_pipe(steps)
                        pipe_flush()
                        for qs in range(4):
                            op("dve", [m8], [score], lambda h, qs=qs: h.max(m8.ap[:, qs, :], score.ap[:, qs, :]))
                        ts(thr, m8[:, :, 7], -0.5, None, ALU.max)
                        tt(sctmp, score, thr.un(2).bc([128, 4, 32]), ALU.is_ge)
                        ts(mneg[:, :, 64:96], sctmp, -NEG, NEG, ALU.mult, ALU.add)
                        def emit_mask(g=g):
                            pb = psA()
                            pbb = pb.bitcast(BF16)
                            for qs in range(4):
                                transpose(pbb[0:96, qs * 128:(qs + 1) * 128], mneg[:, qs, :], ident)
                            for r_ in range(4):
                                act(QT[64:96, g * 4 + r_, :], pbb[64:96, 0:512], AF.Copy)

                        for r in range(4):
                            hq = g * 4 + r
                            Qs = QT[:, hq, :]
                            Q = QT[0:64, hq, :]
                            if r == 3:
                                emit_mask()
                            owb = psB()
                            ow3 = owb[:, 0:260].re("p (q e) -> p q e", q=4)
                            ow_state = {"first": True}
                            steps = []
                            for kt in range(max(0, 4 * ci - 2), 4 * ci + 4):
                                cell = {}

                                def sc_fn(kt=kt, cell=cell):
                                    dg = kt - 4 * ci
                                    qa, qb = max(0, dg), min(3, dg + 2)
                                    ps = psA()
                                    mm(ps[:, qa * 128:(qb + 1) * 128], kwT[:, g, kt * 128:(kt + 1) * 128],
                                       Q[:, qa * 128:(qb + 1) * 128], start=True, stop=False)
                                    masks = []
                                    for qs in range(qa, qb + 1):
                                        dd = qs - dg
                                        if dd == 0:
                                            masks.append((qs, trile))
                                        elif dd == 2:
                                            masks.append((qs, trigt))
                                    for mi, (qs, mk_) in enumerate(masks):
                                        mm(ps[:, qs * 128:(qs + 1) * 128], ident, mk_, start=False,
                                           stop=(mi == len(masks) - 1))
                                    pt = rot("PT", PT)
                                    act(pt[:, qa * 128:(qb + 1) * 128], ps[:, qa * 128:(qb + 1) * 128], AF.Exp,
                                        scale=0.125)
                                    cell["pt"] = pt
                                    cell["qa"], cell["qb"] = qa, qb

                                def pv_fn(kt=kt, cell=cell, ow3=ow3, ow_state=ow_state, g=g):
                                    pt = cell["pt"]
                                    for qs in range(cell["qa"], cell["qb"] + 1):
                                        mm(ow3[:, qs, :], pt[:, qs * 128:(qs + 1) * 128], vw_aug[:, kt, g, 0:65],
                                           start=ow_state["first"], stop=(kt == 4 * ci + 3))
                                        ow_state["first"] = False

                                steps.append((sc_fn, pv_fn))
                            def fin_w(owb=owb, hq=hq):
                                evac_branch(owb, hq, 2, False)
                                next_piece()
                            run_pipe(steps, final=fin_w)
                            if VAR == 'flushw':
                                pipe_flush()

                        for r in range(4):
                            hq = g * 4 + r
                            Qs = QT[:, hq, :]
                            Q = QT[0:64, hq, :]
                            osb = psB()
                            os3 = osb[:, 0:260].re("p (q e) -> p q e", q=4)
                            nkt = 4 * ci + 4
                            steps = []
                            for kt in range(nkt):
                                cell = {}

                                def sc_fn(kt=kt, cell=cell):
                                    ps = psA()
                                    dg = kt - 4 * ci
                                    mm(ps, ksT[:, g, kt * 128:(kt + 1) * 128], Qs, start=True, stop=(dg < 0))
                                    qlo = max(0, dg)
                                    if dg >= 0:
                                        mm(ps[:, dg * 128:(dg + 1) * 128], ident, trile, start=False, stop=True)
                                    pt = rot("PT", PT)
                                    act(pt[:, qlo * 128:512], ps[:, qlo * 128:512], AF.Exp, scale=0.125)
                                    cell["pt"] = pt
                                    cell["qlo"] = qlo

                                def pv_fn(kt=kt, cell=cell, os3=os3, g=g, nkt=nkt):
                                    pt = cell["pt"]
                                    for qs in range(cell["qlo"], 4):
                                        mm(os3[:, qs, :], pt[:, qs * 128:(qs + 1) * 128], vs_aug[:, kt, g, 0:65],
                                           start=(kt == 0 and qs == 0), stop=(kt == nkt - 1))

                                steps.append((sc_fn, pv_fn))
                            def fin_s(osb=osb, hq=hq):
                                evac_branch(osb, hq, 1, False)
                                next_piece()
                            run_pipe(steps, final=fin_s)
                            if VAR == 'flushs':
                                pipe_flush()
                    pipe_flush()
                    for t_ in range(4):
                        act(tokb_t[t_][:, 0:512], o_nsa[:, t_, :], AF.Copy)

                    pipe_flush()
                    while ret_pos[0] < len(ret_pieces):
                        next_piece()
                    wo_slots = [wget(wi["O0"]).re("p (kc n) -> p kc n", kc=8), wget(wi["O1"]).re("p (kc n) -> p kc n", kc=8)]
                    for t_ in range(4):
                        T_tile(0, t_)
                    for t_ in range(4):
                        for nh in range(2):
                            pb = psB()
                            for kc in range(8):
                                mm(pb, T8_t[0][t_][:, kc, :], wo_slots[nh][:, kc, :], start=(kc == 0), stop=(kc == 7))
                            dst = xh_t[t_][:, nh * 512:(nh + 1) * 512]
                            tt(dst, dst, pb, ALU.add)
                        ln_tile(t_)
                        if t_ >= 1:
                            T_tile(1, t_ - 1, lnw2)
                    T_tile(1, 3, lnw2)

                    if b == 0 and ci == NCH - 1:
                        dump("mix", tokb, [128, 4, D])
                        dump("gates", gates, [128, 4, 24])
                    if stage < 6:
                        return
                    hT = T8[1]
                    def ffn_up(fc):
                        U = wget(wi[("U", fc)]).re("p (kc n) -> p kc n", kc=8)
                        at = actT[fc % 2]
                        for fs in range(4):
                            pb = psA()
                            for kc in range(8):
                                mm(pb, U[:, kc, fs * 128:(fs + 1) * 128], hT[:, kc, :], start=(kc == 0), stop=(kc == 7))
                            rl = rot("relu_t", relu_t)
                            act(rl, pb, AF.Relu)
                            act(at[:, fs, :], rl, AF.Square)

                    def ffn_down(fc):
                        Dn = wget(wi[("D", fc)]).re("p (fs n) -> p fs n", fs=4)
                        at = actT[fc % 2]
                        for t_ in range(4):
                            for nh in range(2):
                                pd = psB()
                                for fs in range(4):
                                    mm(pd, at[:, fs, t_ * 128:(t_ + 1) * 128], Dn[:, fs, nh * 512:(nh + 1) * 512],
                                       start=(fs == 0), stop=(fs == 3))
                                dst = xh_t[t_][:, nh * 512:(nh + 1) * 512]
                                tt(dst, dst, pd, ALU.add)

                    nxt = None
                    if ci + 1 < NCH:
                        nxt = (b, ci + 1)
                    elif b + 1 < NB:
                        nxt = (b + 1, 0)

                    def pfA(t_):
                        nb_, nci = nxt
                        r0 = nci * CH + t_ * 128
                        dma("sp", xs, V(x_d.ap[nb_, r0:r0 + 128, :], ()))

                    def pfB(t_):
                        ln_tile(t_, src=xs)

                    def pfC(t_):
                        T_tile(0, t_, lnw1)

                    ffn_up(0)
                    for fc in range(8):
                        if fc + 1 < 8:
                            wget(wi[("D", fc)])
                            ffn_up(fc + 1)
                        if nxt is not None and stage >= 99:
                            if 2 <= fc <= 5:
                                pfC(fc - 2)
                            if 1 <= fc <= 4:
                                pfB(fc - 1)
                            if fc <= 3:
                                pfA(fc)
                        ffn_down(fc)
                    prefetched[0] = nxt is not None and stage >= 99

                chunk_body()
                dma("sp", V(out_d.ap[b, t0:t0 + CH, :].rearrange("(t p) d -> p t d", p=128), ()), xh)
        fw.finish()
        build.stats = {e: fw.E[e].n for e in fw.E}
        build.stats["waits"] = fw.nwaits
    return nc


_CONSTS = None


def _host_inputs(inputs, NB, cores):
    global _CONSTS
    if _CONSTS is None:
        _CONSTS = _consts()
    c = _CONSTS
    f = lambda a: np.ascontiguousarray(np.asarray(a, dtype=np.float32))
    shared = {
        "w_in": f(inputs["w_in"][0]), "w_out": f(inputs["w_out"][0]), "w_up": f(inputs["w_up"][0]),
        "w_down": f(inputs["w_down"][0]),
        "cmp_wk1": f(inputs["cmp_wk1"][0]), "cmp_wv1": f(inputs["cmp_wv1"][0]),
        "cmp_wk2": f(inputs["cmp_wk2"][0]), "cmp_wv2": f(inputs["cmp_wv2"][0]),
        "ln1_t": f(np.asarray(inputs["ln1_w"][0]).reshape(8, 128).T),
        "ln2_t": f(np.asarray(inputs["ln2_w"][0]).reshape(8, 128).T),
        "pek_t": f(np.asarray(inputs["cmp_pe_k"][0]).reshape(16, 128).T),
        "pev_t": f(np.asarray(inputs["cmp_pe_v"][0]).reshape(16, 128).T),
        "retw": f(np.asarray(inputs["ret_norm_w"][0]).reshape(1, 512)),
    }
    qn = np.asarray(inputs["q_norm_w"][0], dtype=np.float32)
    kn = np.asarray(inputs["k_norm_w"][0], dtype=np.float32)
    qkw = np.stack([qn, kn[0], kn[1], kn[2]], axis=1)
    shared["qkw_t"] = f(np.concatenate([qkw, qkw], 0))
    for k, v in c.items():
        if not k.startswith("_"):
            shared[k] = v
    x = np.asarray(inputs["x"], dtype=np.float32)
    maps = []
    for ci in range(cores):
        m = dict(shared)
        m["x"] = np.ascontiguousarray(x[ci * NB:(ci + 1) * NB])
        maps.append(m)
    return maps


def kernel(**inputs):
    global _CONSTS
    if _CONSTS is None:
        _CONSTS = _consts()
    n = 8
    NB = 2
    nc = build(NB=NB, NCH=NCH_FULL, consts=_CONSTS)
    maps = _host_inputs(inputs, NB, n)
    res = run_bass_kernel_spmd(nc, maps, core_ids=list(range(n)))
    out = np.concatenate([r["out"] for r in res.results], axis=0)
    return out.astype(np.float32)
```

```python
import bisect
import math
import os
VAR = os.environ.get('KVAR', '')
from contextlib import ExitStack

import numpy as np
import concourse.bass as bass
import concourse.mybir as mybir
from concourse.bass_utils import run_bass_kernel_spmd

F32 = mybir.dt.float32
BF16 = mybir.dt.bfloat16
ALU = mybir.AluOpType
AF = mybir.ActivationFunctionType
AX = mybir.AxisListType

NEG = -30000.0
S = 2048
D = 1024
CH = 512
NCH_FULL = S // CH
EPS = 1e-6


class Tr:
    __slots__ = ("w", "r", "name", "excl")

    def __init__(self, name="", excl=False):
        self.w = None
        self.r = []
        self.name = name
        self.excl = excl


class V:
    __slots__ = ("ap", "trs")

    def __init__(self, ap, trs):
        self.ap = ap
        self.trs = tuple(trs)

    def __getitem__(self, k):
        return V(self.ap[k], self.trs)

    def re(self, pattern_, **kw):
        return V(self.ap.rearrange(pattern_, **kw), self.trs)

    def bc(self, shape):
        return V(self.ap.broadcast_to(list(shape)), self.trs)

    def un(self, axis):
        return V(self.ap.unsqueeze(axis), self.trs)

    def bitcast(self, dt):
        return V(self.ap.bitcast(dt), self.trs)


class Eng:
    def __init__(self, name, h, sem, is_pe=False):
        self.name = name
        self.h = h
        self.sem = sem
        self.count = 0
        self.n = 0
        self.last = None
        self.tick_idx = []
        self.tick_val = []
        self.seen = {}
        self.is_pe = is_pe


class FW:
    def __init__(self, nc, es, n_dma_sems=24):
        self.nc = nc
        self.es = es
        mk = lambda nm: es.enter_context(nc.semaphore(nm))
        self.E = {
            "pe": Eng("pe", nc.tensor, mk("s_pe"), True),
            "act": Eng("act", nc.scalar, mk("s_act")),
            "dve": Eng("dve", nc.vector, mk("s_dve")),
            "pool": Eng("pool", nc.gpsimd, mk("s_pool")),
            "sp": Eng("sp", nc.sync, mk("s_sp")),
        }
        self.dsems = [[mk(f"s_dma{i}"), 0] for i in range(n_dma_sems)]
        half_ = n_dma_sems // 2
        self.dpool = {"sp": list(range(0, half_)), "pool": list(range(half_, n_dma_sems))}
        self.dnext = {"sp": 0, "pool": 0}
        self.nwaits = 0

    def sbv(self, name, shape, dt=F32):
        t = self.es.enter_context(self.nc.sbuf_tensor("sb_" + name, list(shape), dt))
        return V(t[:], (Tr(name),))

    def psv(self, name, shape, dt=F32):
        t = self.es.enter_context(self.nc.psum_tensor(name, list(shape), dt))
        return V(t[:], (Tr(name, excl=True),))

    def _ticket(self, X, idx):
        k = bisect.bisect_left(X.tick_idx, idx)
        if k < len(X.tick_idx):
            return X.tick_val[k]
        assert X.last is not None and X.n - 1 >= idx
        X.count += 1
        X.last.then_inc(X.sem, 1)
        X.tick_idx.append(X.n - 1)
        X.tick_val.append(X.count)
        return X.count

    def _wait(self, E, ref):
        if ref[0] == "dma":
            _, si, val = ref
            key = ("d", si)
            if E.seen.get(key, 0) >= val:
                return
            E.h.wait_ge(self.dsems[si][0], val)
            E.seen[key] = val
        else:
            _, xn, idx = ref
            X = self.E[xn]
            tk = self._ticket(X, idx)
            if E.seen.get(xn, 0) >= tk:
                return
            E.h.wait_ge(X.sem, tk)
            E.seen[xn] = tk
        self.nwaits += 1

    def _deps(self, E, outs, ins):
        en = E.name
        for v in ins:
            for tr in v.trs:
                w = tr.w
                if w is not None:
                    if not (w[0] == "e" and w[1] == en and E.is_pe):
                        self._wait(E, w)
                if tr.excl:
                    for r in tr.r:
                        if r[0] == "e" and r[1] == en:
                            continue
                        self._wait(E, r)
        for v in outs:
            for tr in v.trs:
                w = tr.w
                if w is not None and not (w[0] == "e" and w[1] == en and E.is_pe):
                    self._wait(E, w)
                for r in tr.r:
                    if r[0] == "e" and r[1] == en and E.is_pe:
                        continue
                    self._wait(E, r)

    def _record(self, ref, outs, ins):
        for v in ins:
            for tr in v.trs:
                tr.r.append(ref)
                if len(tr.r) > 32:
                    best = {}
                    for r in tr.r:
                        key = (r[0], r[1])
                        if key not in best or r[2] > best[key][2]:
                            best[key] = r
                    tr.r = list(best.values())
        for v in outs:
            for tr in v.trs:
                tr.w = ref
                tr.r = []

    def op(self, en, outs, ins, fn):
        E = self.E[en]
        self._deps(E, outs, ins)
        inst = fn(E.h)
        E.last = inst
        idx = E.n
        E.n += 1
        self._record(("e", en, idx), outs, ins)
        return inst

    def dma(self, qn, out, in_, **kw):
        E = self.E[qn]
        if E.n > 0 and (not E.tick_idx or E.tick_idx[-1] < E.n - 1):
            self._ticket(E, E.n - 1)
        self._deps(E, [out], [in_])
        pl = self.dpool[qn]
        si = pl[self.dnext[qn] % len(pl)]
        self.dnext[qn] += 1
        sem, val = self.dsems[si]
        if val > 0:
            self._wait(E, ("dma", si, val))
        inst = E.h.dma_start(out=out.ap, in_=in_.ap, **kw)
        inst.then_inc(sem, 16)
        self.dsems[si][1] = val + 16
        E.last = inst
        E.n += 1
        E.tick_idx.append(E.n - 1)
        E.tick_val.append(E.count)
        self._record(("dma", si, val + 16), [out], [in_])
        return inst

    def finish(self):
        E = self.E["sp"]
        for si, (sem, val) in enumerate(self.dsems):
            if val > 0:
                self._wait(E, ("dma", si, val))


def _consts():
    c = {}
    c["c_ident"] = np.eye(128, dtype=np.float32)
    R = np.zeros((64, 64), np.float32)
    for m in range(32):
        R[m, m + 32] = -1.0
        R[m + 32, m] = 1.0
    rt = np.zeros((128, 128), np.float32)
    rt[:64, :64] = R.T
    rt[64:, 64:] = R.T
    c["c_rt"] = rt
    on = np.zeros((128, 128), np.float32)
    on[:64, :64] = 1.0 / 64
    on[64:, 64:] = 1.0 / 64
    c["c_ones"] = on
    half = 32
    inv = (10000.0 ** (-np.arange(half, dtype=np.float32) / half)).astype(np.float32)
    ang = np.arange(S, dtype=np.float32)[:, None] * inv[None, :]
    cos = np.cos(ang).astype(np.float32).T
    sin = np.sin(ang).astype(np.float32).T
    c["c_cos"] = np.concatenate([cos, cos, cos, cos], 0)
    c["c_sin"] = np.concatenate([sin, sin, sin, sin], 0)
    cc = np.arange(128)[:, None]
    tt = np.arange(S)[None, :]
    c["c_cmpmask"] = np.where((16 * cc + 31 <= tt) & (cc < 127), 0.0, NEG).astype(np.float32)
    k = np.arange(128)[:, None]
    q = np.arange(128)[None, :]
    c["c_trile"] = np.where(k <= q, 0.0, NEG).astype(np.float32)
    c["c_trigt"] = np.where(k > q, 0.0, NEG).astype(np.float32)
    E = np.zeros((32, S), np.float32)
    for j in range(32):
        E[j, 64 * j:64 * j + 64] = 1.0
    c["c_E"] = E
    cs = np.arange(127) * 16
    jb = np.arange(32)
    ov = ((cs[:, None] < (jb[None, :] + 1) * 64) & (cs[:, None] + 32 > jb[None, :] * 64)).astype(np.float32)
    ovl = np.zeros((128, 33), np.float32)
    ovl[:127, :32] = ov
    ovl[:127, 32] = 1.0
    c["c_ovl"] = ovl
    t = np.arange(S)
    cur = t // 64
    forced = (jb[None, :] == 0) | (jb[None, :] == cur[:, None]) | (jb[None, :] == cur[:, None] - 1)
    valid = jb[None, :] <= cur[:, None]
    bias = np.where(forced, 1.0e4, np.where(valid, 0.0, -1.0)).astype(np.float32)
    c["c_selbias"] = np.ascontiguousarray(bias.reshape(16, 128, 32).transpose(1, 0, 2))
    H = 4
    lg = np.log(1.0 - 2.0 ** (-5.0 - np.arange(H, dtype=np.float64)))
    i = np.arange(128, dtype=np.float64)
    dm = np.zeros((128, H, 128), np.float64)
    for h in range(H):
        dm[:, h, :] = (i[None, :] >= i[:, None]) * np.exp(-lg[h] * (i[:, None] + 1.0)) / 8.0
    c["c_dmatp"] = dm.astype(np.float32)
    c["c_zeta8"] = (np.exp(lg[None, :] * (127.0 - i[:, None])) / 8.0).astype(np.float32)
    xi = np.exp(lg[:, None] * (i[None, :] + 1.0))
    xi2 = np.zeros((128, 2, 128), np.float64)
    for j in range(2):
        xi2[:64, j, :] = xi[2 * j][None, :]
        xi2[64:, j, :] = xi[2 * j + 1][None, :]
    c["c_xi"] = xi2.astype(np.float32)
    c["_gamma_c"] = [float(np.exp(lg[h] * 128.0)) for h in range(H)]
    return c


CONST_SHAPES = None


def build(NB=2, NCH=NCH_FULL, consts=None, tab_dt=F32, dbg=(), stage=99):
    nc = bass.Bass("TRN2", target_bir_lowering=False)
    gamma_c = consts["_gamma_c"]

    def din(name, shape):
        return V(nc.dram_tensor(name, list(shape), F32, kind="ExternalInput").ap(), ())

    x_d = din("x", [NB, S, D])
    w_in_d = din("w_in", [D, 2840])
    w_out_d = din("w_out", [D, D])
    w_up_d = din("w_up", [D, 4096])
    w_down_d = din("w_down", [4096, D])
    wk1_d = din("cmp_wk1", [2048, 256])
    wv1_d = din("cmp_wv1", [2048, 256])
    wk2_d = din("cmp_wk2", [256, 64])
    wv2_d = din("cmp_wv2", [256, 64])
    lnw1_d = din("ln1_t", [128, 8])
    lnw2_d = din("ln2_t", [128, 8])
    qkw_d = din("qkw_t", [128, 4])
    pek_d = din("pek_t", [128, 16])
    pev_d = din("pev_t", [128, 16])
    retw_d = din("retw", [1, 512])
    cd = {k: din(k, v.shape) for k, v in consts.items() if not k.startswith("_")}
    out_d = V(nc.dram_tensor("out", [NB, S, D], F32, kind="ExternalOutput").ap(), ())

    with ExitStack() as es:
        fw = FW(nc, es)
        sbv, op, dma = fw.sbv, fw.op, fw.dma

        ident = sbv("ident", [128, 128], BF16)
        rt = sbv("rt", [128, 128], BF16)
        onesb = sbv("onesb", [128, 128], BF16)
        cos = sbv("cos", [128, S], tab_dt)
        sin = sbv("sin", [128, S], tab_dt)
        cmpmask = sbv("cmpmask", [128, S], BF16)
        trile = sbv("trile", [128, 128], BF16)
        trigt = sbv("trigt", [128, 128], BF16)
        ovl = sbv("ovl", [128, 33], BF16)
        selbias = sbv("selbias", [128, 16, 32], F32)
        dmatp = sbv("dmatp", [128, 4, 128], F32)
        zeta8 = sbv("zeta8", [128, 4], F32)
        xit = sbv("xit", [128, 2, 128], F32)
        lnw1 = sbv("lnw1", [128, 8], F32)
        lnw2 = sbv("lnw2", [128, 8], F32)
        qkw = sbv("qkw", [128, 4], F32)
        pek = sbv("pek", [128, 16], BF16)
        pev = sbv("pev", [128, 16], BF16)
        retw = sbv("retw", [128, 512], F32)
        wg = sbv("wg", [128, 8, 24], BF16)
        wk2 = sbv("wk2", [128, 2, 64], BF16)
        wv2 = sbv("wv2", [128, 2, 64], BF16)
        epsb = sbv("epsb", [128, 1], F32)
        biasH = sbv("biasH", [128, 4], F32)

        for dst, key in ((ident, "c_ident"), (rt, "c_rt"), (onesb, "c_ones"), (cmpmask, "c_cmpmask"),
                         (trile, "c_trile"), (trigt, "c_trigt"), (ovl, "c_ovl"),
                         (cos, "c_cos"), (sin, "c_sin")):
            dma("pool", dst, cd[key])
        for dst, src in ((selbias, cd["c_selbias"]), (dmatp, cd["c_dmatp"]), (zeta8, cd["c_zeta8"]),
                         (xit, cd["c_xi"]), (lnw1, lnw1_d), (lnw2, lnw2_d), (qkw, qkw_d)):
            dma("sp", dst, src)
        dma("pool", pek, pek_d)
        dma("pool", pev, pev_d)
        dma("sp", retw, V(retw_d.ap.broadcast_to([128, 512]), ()))
        dma("pool", wg, V(w_in_d.ap[:, 1280:1304].rearrange("(kc p) n -> p kc n", p=128), ()))
        dma("pool", wk2, V(wk2_d.ap.rearrange("(mh p) d -> p mh d", p=128), ()))
        dma("pool", wv2, V(wv2_d.ap.rearrange("(mh p) d -> p mh d", p=128), ()))
        op("dve", [epsb], [], lambda h: h.memset(epsb.ap, EPS))


        def scratch(name, src_v, rows, cols, piece_rows):
            t = nc.dram_tensor("scr_" + name, [rows, cols], BF16, kind="Internal").ap()
            trs = []
            for r0 in range(0, rows, piece_rows):
                tr = Tr(f"scr_{name}_{r0}")
                dma("pool", V(t[r0:r0 + piece_rows, :], (tr,)), V(src_v.ap[r0:r0 + piece_rows, :], ()))
                trs.append(tr)
            return V(t, trs)

        def conv_block(name, src_ap, rows, cols, nsplit=2, defer=None, gate=None):
            t = nc.dram_tensor("scr_" + name, [rows, cols], BF16, kind="Internal").ap()
            trs = []
            pr = rows // nsplit
            for r0 in range(0, rows, pr):
                tr = Tr(f"scr_{name}_{r0}")
                dst = V(t[r0:r0 + pr, :], (tr,))
                src_piece = src_ap[r0:r0 + pr, :]
                if defer is None:
                    dma("pool", dst, V(src_piece, ()))
                else:
                    defer.append((dst, src_piece))
                trs.append(tr)
            return V(t, trs)

        deferred_conv = []
        conv_gate = sbv("conv_gate", [128, 1], F32)

        scr = {}
        scr["S0"] = conv_block("S0", w_in_d.ap[:, 0:512], D, 512)
        scr["S5"] = conv_block("S5", w_in_d.ap[:, 2328:2840], D, 512)
        s1_t = nc.dram_tensor("scr_s1", [D, 512], BF16, kind="Internal").ap()
        s1_trs = []
        for j, c0_ in enumerate([512, 512, 576, 576, 640, 640, 704, 704]):
            tr = Tr(f"scr_s1_{j}")
            dma("pool", V(s1_t[:, j * 64:(j + 1) * 64], (tr,)), V(w_in_d.ap[:, c0_:c0_ + 64], ()))
            s1_trs.append(tr)
        scr["S1"] = V(s1_t, s1_trs)
        scr["S2"] = conv_block("S2", w_in_d.ap[:, 768:1280], D, 512)
        scr["S3"] = conv_block("S3", w_in_d.ap[:, 1304:1816], D, 512)
        scr["S4"] = conv_block("S4", w_in_d.ap[:, 1816:2328], D, 512)
        scr["CK"] = conv_block("CK", wk1_d.ap, 2048, 256)
        scr["CV"] = conv_block("CV", wv1_d.ap, 2048, 256)
        scr["O0"] = conv_block("O0", w_out_d.ap[:, 0:512], D, 512)
        scr["O1"] = conv_block("O1", w_out_d.ap[:, 512:1024], D, 512)
        for fc in range(8):
            scr[("U", fc)] = conv_block(f"U{fc}", w_up_d.ap[:, fc * 512:(fc + 1) * 512], D, 512, defer=deferred_conv)
            scr[("D", fc)] = conv_block(f"D{fc}", w_down_d.ap[fc * 512:(fc + 1) * 512, :], 512, D, defer=deferred_conv)

        ksT = sbv("ksT", [128, 2, S], BF16)
        kwT = sbv("kwT", [64, 2, S], BF16)
        vs_aug = sbv("vs_aug", [128, 16, 2, 66], BF16)
        vw_aug = sbv("vw_aug", [128, 16, 2, 66], BF16)
        kcw = sbv("kcw", [128, 2, 544], BF16)
        vcw = sbv("vcw", [128, 2, 544], BF16)
        kcmpT = sbv("kcmpT", [64, 2, 128], BF16)
        vcmpT = sbv("vcmpT", [64, 2, 128], BF16)
        vcmp_aug = sbv("vcmp_aug", [128, 2, 66], BF16)
        state = sbv("state", [64, 4, 128], F32)
        state_bf = sbv("state_bf", [64, 4, 128], BF16)
        op("dve", [ksT], [], lambda h: h.memset(ksT.ap[64:128, :, :], 0.0))
        for g_ in range(2):
            dma("pool", ksT[64:96, g_, :], cd["c_E"])
        op("dve", [vs_aug], [], lambda h: h.memset(vs_aug.ap, 1.0))
        op("dve", [vw_aug], [], lambda h: h.memset(vw_aug.ap, 1.0))
        op("dve", [vcmp_aug], [], lambda h: h.memset(vcmp_aug.ap, 1.0))
        op("dve", [kcmpT], [], lambda h: h.memset(kcmpT.ap, 0.0))
        op("dve", [vcmpT], [], lambda h: h.memset(vcmpT.ap, 0.0))

        def tiled(v, slicer):
            trs = [Tr(f"t{t}") for t in range(4)]
            tiles = [V(slicer(v.ap, t), (trs[t],)) for t in range(4)]
            return V(v.ap, trs), tiles

        xh, xh_t = tiled(sbv("xh", [128, 4, D], F32), lambda ap, t: ap[:, t, :])
        tokb, tokb_t = tiled(sbv("tokb", [128, 4, D], BF16), lambda ap, t: ap[:, t, :])
        T8, T8_t = [], []
        for i in range(2):
            a_, t_l = tiled(sbv(f"T8_{i}", [128, 8, CH], BF16), lambda ap, t: ap[:, :, t * 128:(t + 1) * 128])
            T8.append(a_)
            T8_t.append(t_l)
        QT = sbv("QT", [128, 8, CH], BF16)
        op("dve", [QT], [], lambda h: h.memset(QT.ap[64:128, :, :], 0.0))
        rqxT = sbv("rqxT", [64, 4, CH], BF16)
        rkT = sbv("rkT", [64, 4, CH], BF16)
        kz = sbv("kz", [128, 4, 64], BF16)
        rv = sbv("rv", [128, 4, 512], BF16)
        rgs = sbv("rgs", [128, 4, 512], BF16)
        gates = sbv("gates", [128, 4, 24], F32)
        o_nsa = sbv("o_nsa", [128, 4, 512], F32)
        ss4 = sbv("ss4", [128, 4], F32)
        rstd4 = sbv("rstd4", [128, 4], F32)
        sqb = [sbv(f"sqb{i}", [128, CH], BF16) for i in range(2)]
        rstd_t = [sbv(f"rstd_t{i}", [128, CH], F32) for i in range(2)]
        qnb = [sbv(f"qnb{i}", [128, CH], BF16) for i in range(3)]
        big4k = fw.es.enter_context(nc.sbuf_tensor("sb_big4k", [128, 2 * CH], F32))
        _tra, _trb = Tr("t1a"), Tr("t1b")
        t1 = [V(big4k[:, 0:CH], (_tra,)), V(big4k[:, CH:2 * CH], (_trb,))]
        xs = V(big4k[:], (_tra, _trb))
        uptmp = [sbv(f"uptmp{i}", [128, CH], BF16) for i in range(2)]
        t2 = [sbv(f"t2_{i}", [128, CH], F32) for i in range(2)]
        PT = [sbv(f"PT{i}", [128, CH], BF16) for i in range(7)]
        HT = sbv("HT", [128, 2, 64], BF16)
        score = sbv("score", [128, 4, 32], F32)
        sctmp = sbv("sctmp", [128, 4, 32], F32)
        m8 = sbv("m8", [128, 4, 8], F32)
        thr = sbv("thr", [128, 4], F32)
        mneg = sbv("mneg", [128, 4, 96], BF16)
        op("dve", [mneg], [], lambda h: h.memset(mneg.ap, 0.0))
        rd = sbv("rd", [128, 4], F32)
        ff = sbv("ff", [128, 4], F32)
        otmp = sbv("otmp", [128, 4, 64], F32)
        innb = [sbv(f"innb{i}", [128, 128], BF16) for i in range(4)]
        st1 = sbv("st1", [128, 32], F32)
        st2 = sbv("st2", [128, 4], F32)
        stm = sbv("stm", [128, 4], F32)
        stv = sbv("stv", [128, 4], F32)
        actT = [sbv(f"actT{i}", [128, 4, CH], BF16) for i in range(2)]
        relu_t = [sbv(f"relu_t{i}", [128, CH], F32) for i in range(1)]
        WS = [sbv(f"WS{i}", [128, 4096], BF16) for i in range(4)]

        banks = [fw.psv(f"pb{i}", [128, 512], F32) for i in range(8)]
        rrA = [0]
        rrB = [0]

        def psA():
            b = banks[rrA[0] % 4]
            rrA[0] += 1
            return b

        def psB():
            b = banks[5 + rrB[0] % 3]
            rrB[0] += 1
            return b

        rrP = [0]

        def psP():
            b = banks[rrP[0] % 6]
            rrP[0] += 1
            return b

        rrC = [0]

        def psC():
            return banks[7]

        rr = {}

        def rot(name, lst):
            i = rr.get(name, 0)
            rr[name] = i + 1
            return lst[i % len(lst)]

        def wsrc_cols(wd, a, b):
            return V(wd.ap[:, a:b].rearrange("(kc p) n -> p kc n", p=128), wd.trs)

        items = []

        def add_item(parts):
            items.append(parts)
            return len(items) - 1

        witems = {}

        def kc_view(v):
            return V(v.ap.rearrange("(kc p) n -> p kc n", p=128), v.trs)

        for b in range(NB):
            for ci in range(NCH):
                d = {}
                for key in ("S0", "S5", "S1", "S2", "S3", "S4"):
                    d[key] = add_item([(lambda s: s.re("p (kc n) -> p kc n", kc=8), kc_view(scr[key]))])
                for key in ("CK", "CV"):
                    d[key] = add_item([(lambda s: s.re("p (i m) -> p i m", i=16),
                                        V(scr[key].ap.rearrange("(i p) m -> p i m", p=128), scr[key].trs))])
                for key in ("O0", "O1"):
                    d[key] = add_item([(lambda s: s.re("p (kc n) -> p kc n", kc=8), kc_view(scr[key]))])
                for fc in range(8):
                    d[("U", fc)] = add_item([(lambda s: s.re("p (kc n) -> p kc n", kc=8), kc_view(scr[("U", fc)]))])
                    d[("D", fc)] = add_item([(lambda s: s.re("p (fs n) -> p fs n", fs=4),
                                              V(scr[("D", fc)].ap.rearrange("(fs p) n -> p fs n", p=128),
                                                scr[("D", fc)].trs))])
                witems[(b, ci)] = d
        issued = [0]
        LOOK = 2

        def wget(idx):
            while issued[0] < len(items) and issued[0] <= idx + LOOK:
                k = issued[0]
                slot = WS[k % 4]
                for viewfn, src in items[k]:
                    dma("sp", viewfn(slot), src)
                issued[0] += 1
            return WS[idx % 4]

        def dump(name, v, shape):
            if name in dbg:
                dd = V(nc.dram_tensor("dbg_" + name, list(shape), F32, kind="ExternalOutput").ap(), ())
                dma("pool", dd, v)

        def mm(out, lhsT, rhs, start=True, stop=True):
            return op("pe", [out], [lhsT, rhs],
                      lambda h: h.matmul(out.ap, lhsT.ap, rhs.ap, start=start, stop=stop, skip_group_check=True))

        def transpose(out, in_, idn):
            return op("pe", [out], [in_, idn], lambda h: h.transpose(out.ap, in_.ap, idn.ap))

        def act(out, in_, func, bias=None, scale=1.0, accum=None):
            outs = [out] + ([accum] if accum is not None else [])
            ins = [in_] + ([bias] if isinstance(bias, V) else []) + ([scale] if isinstance(scale, V) else [])
            kw = {}
            if bias is not None:
                kw["bias"] = bias.ap if isinstance(bias, V) else bias
            if accum is not None:
                kw["accum_out"] = accum.ap
            sc = scale.ap if isinstance(scale, V) else scale
            return op("act", outs, ins, lambda h: h.activation(out.ap, in_.ap, func, scale=sc, **kw))

        def tt(out, a, b, alu, eng="dve"):
            return op(eng, [out], [a, b], lambda h: h.tensor_tensor(out.ap, a.ap, b.ap, alu))

        def ts(out, a, s1, s2, op0, op1=None):
            ins = [a] + [s for s in (s1, s2) if isinstance(s, V)]
            a1 = s1.ap if isinstance(s1, V) else s1
            a2 = s2.ap if isinstance(s2, V) else s2
            if op1 is None:
                return op("dve", [out], ins, lambda h: h.tensor_scalar(out.ap, a.ap, a1, a2, op0))
            return op("dve", [out], ins, lambda h: h.tensor_scalar(out.ap, a.ap, a1, a2, op0, op1))

        def stt(out, a, sc, b, op0, op1):
            ins = [a, b] + ([sc] if isinstance(sc, V) else [])
            s_ = sc.ap if isinstance(sc, V) else sc
            return op("dve", [out], ins, lambda h: h.scalar_tensor_tensor(out.ap, a.ap, s_, b.ap, op0, op1))

        def vcopy(out, in_):
            return op("dve", [out], [in_], lambda h: h.tensor_copy(out.ap, in_.ap))

        def recip(out, in_):
            return op("dve", [out], [in_], lambda h: h.reciprocal(out.ap, in_.ap))

        def rsum(out, in_):
            return op("dve", [out], [in_], lambda h: h.reduce_sum(out.ap, in_.ap, AX.X))

        def ln_tile(t_, src=None):
            if src is None:
                src = xh_t[t_]
            op("dve", [ss4], [], lambda h: h.memset(ss4.ap[:, t_:t_ + 1], 0.0))
            act(tokb_t[t_], src, AF.Square, accum=ss4[:, t_:t_ + 1])
            act(rstd4[:, t_:t_ + 1], ss4[:, t_:t_ + 1], AF.Sqrt, bias=epsb, scale=1.0 / D)
            recip(rstd4[:, t_:t_ + 1], rstd4[:, t_:t_ + 1])
            ts(tokb_t[t_], src, rstd4[:, t_:t_ + 1], None, ALU.mult)

        def T_tile(di, t_, lnw=None):
            dst_t = T8_t[di][t_]
            for half in range(2):
                pb = psA()
                pbb = pb.bitcast(BF16)
                for j in range(4):
                    kc = half * 4 + j
                    transpose(pbb[:, j * 128:(j + 1) * 128], tokb_t[t_][:, kc * 128:(kc + 1) * 128], ident)
                src = pbb[:, 0:512].re("p (j q) -> p j q", j=4)
                dst = dst_t[:, half * 4:(half + 1) * 4, :]
                if lnw is None:
                    act(dst, src, AF.Copy)
                else:
                    tt(dst, src, lnw[:, half * 4:(half + 1) * 4].un(2).bc([128, 4, 128]), ALU.mult)

        def rmsnorm_to_T(lnw, di):
            for t_ in range(4):
                ln_tile(t_)
            for t_ in range(4):
                T_tile(di, t_, lnw)

        def proj_fm(slot3, c0, M, xT):
            pb = psA()
            for kc in range(8):
                mm(pb[0:M, :], slot3[:, kc, c0:c0 + M], xT[:, kc, :], start=(kc == 0), stop=(kc == 7))
            return pb

        def norm_rope(pb, M, wcol, t0, outs, post=None):
            qn = rot("qnb", qnb)
            if wcol is not None:
                sq = rot("sqb", sqb)
                act(sq[0:M, :], pb[0:M, :], AF.Square)
                pm = psA()
                mm(pm[0:M, :], onesb[0:M, 0:M], sq[0:M, :])
                rs = rot("rstd_t", rstd_t)
                act(rs[0:M, :], pm[0:M, :], AF.Sqrt, bias=epsb[0:M, :])
                recip(rs[0:M, :], rs[0:M, :])
                stt(qn[0:M, :], pb[0:M, :], qkw[0:M, wcol:wcol + 1], rs[0:M, :], ALU.mult, ALU.mult)
            else:
                act(qn[0:M, :], pb[0:M, :], AF.Copy)
            pr = psA()
            mm(pr[0:M, :], rt[0:M, 0:M], qn[0:M, :])
            a = rot("t1", t1)
            b_ = rot("t2", t2)
            tt(a[0:M, :], qn[0:M, :], cos[0:M, t0:t0 + CH], ALU.mult)
            tt(b_[0:M, :], pr[0:M, :], sin[0:M, t0:t0 + CH], ALU.mult)
            if post is not None:
                tt(a[0:M, :], a[0:M, :], b_[0:M, :], ALU.add)
                for dst, r0, r1 in outs:
                    tt(dst, a[r0:r1, :].re("p (s q) -> p s q", s=4), post, ALU.mult)
            else:
                for dst, r0, r1 in outs:
                    tt(dst, a[r0:r1, :], b_[r0:r1, :], ALU.add)

        def setup_bias():
            for kv, (wd, pe) in enumerate(((wk1_d, pek), (wv1_d, pev))):
                slot = WS[kv]
                s3 = slot.re("p (i m) -> p i m", i=16)
                dma("pool", s3, V(wd.ap.rearrange("(i p) m -> p i m", p=128), ()))
                pb = psA()
                for mh in range(2):
                    for i in range(16):
                        mm(pb[:, mh:mh + 1], s3[:, i, mh * 128:(mh + 1) * 128], pe[:, i:i + 1],
                           start=(i == 0), stop=(i == 15))
                vcopy(biasH[:, kv * 2:kv * 2 + 2], pb[:, 0:2])


        prefetched = [False]
        for b in range(NB):
            op("dve", [state], [], lambda h: h.memset(state.ap, 0.0))
            op("dve", [state_bf], [], lambda h: h.memset(state_bf.ap, 0.0))
            op("dve", [kcw], [], lambda h: h.memset(kcw.ap, 0.0))
            op("dve", [vcw], [], lambda h: h.memset(vcw.ap, 0.0))
            for ci in range(NCH):
                t0 = ci * CH
                wi = witems[(b, ci)]
                def chunk_body(b=b, ci=ci, t0=t0, wi=wi):
                    dma("sp", xh, V(x_d.ap[b, t0:t0 + CH, :].rearrange("(t p) d -> p t d", p=128), ()))
                    xT = T8[0]
                    if stage < 0.5:
                        return
                    if not prefetched[0]:
                        rmsnorm_to_T(lnw1, 0)
                    if stage < 1.01:
                        return

                    def vc_g(g):
                        s1_ = wget(wi["S1"]).re("p (kc n) -> p kc n", kc=8)
                        if True:
                            pb = psC()
                            for kc in range(8):
                                mm(pb, s1_[:, kc, 256 + g * 128:256 + (g + 1) * 128], xT[:, kc, :],
                                   start=(kc == 0), stop=(kc == 7))
                            act(vcw[0:64, g, 16:528], pb[0:64, :], AF.Copy)
                            act(vcw[64:128, g, 15:527], pb[64:128, :], AF.Copy)

                    def kw_tile(t_):
                        s2_ = wget(wi["S2"]).re("p (kc n) -> p kc n", kc=8)
                        if True:
                            tile_i = ci * 4 + t_
                            pb = psC()
                            for kc in range(8):
                                mm(pb[:, 0:128], xT[:, kc, t_ * 128:(t_ + 1) * 128], s2_[:, kc, 128:256],
                                   start=(kc == 0), stop=(kc == 7))
                            for kc in range(8):
                                mm(pb[:, 128:256], xT[:, kc, t_ * 128:(t_ + 1) * 128], s2_[:, kc, 384:512],
                                   start=(kc == 0), stop=(kc == 7))
                            for kc in range(8):
                                mm(pb[:, 256:280], xT[:, kc, t_ * 128:(t_ + 1) * 128], wg[:, kc, :],
                                   start=(kc == 0), stop=(kc == 7))
                            act(vs_aug[:, tile_i, :, 0:64], pb[:, 0:128].re("p (g d) -> p g d", g=2), AF.Copy)
                            act(vw_aug[:, tile_i, :, 0:64], pb[:, 128:256].re("p (g d) -> p g d", g=2), AF.Copy)
                            act(gates[:, t_, :], pb[:, 256:280], AF.Sigmoid)

                    if ci > 0:
                        for g in range(2):
                            vcopy(kcw[0:64, g, 0:16], kcw[0:64, g, 512:528])
                            vcopy(kcw[64:128, g, 0:15], kcw[64:128, g, 512:527])
                            vcopy(vcw[0:64, g, 0:16], vcw[0:64, g, 512:528])
                            vcopy(vcw[64:128, g, 0:15], vcw[64:128, g, 512:527])
                    tasks = []
                    for j in range(4):
                        tasks.append(dict(key="S0", c0=j * 128, M=128, wcol=0,
                                          outs=[(QT[0:64, 2 * j, :], 0, 64, False), (QT[0:64, 2 * j + 1, :], 64, 128, True)]))
                    for g in range(2):
                        tasks.append(dict(key="S1", c0=g * 128, M=128, wcol=1,
                                          outs=[(kcw[0:64, g, 16:528], 0, 64, False),
                                                (kcw[64:128, g, 15:527], 64, 128, False)]))
                    tasks.append(dict(key="S2", c0=0, M=128, wcol=2,
                                      outs=[(ksT[0:64, 0, t0:t0 + CH], 0, 64, False), (ksT[0:64, 1, t0:t0 + CH], 64, 128, True)]))
                    tasks.append(dict(key="S2", c0=256, M=128, wcol=3,
                                      outs=[(kwT[:, 0, t0:t0 + CH], 0, 64, False), (kwT[:, 1, t0:t0 + CH], 64, 128, True)]))
                    for j in range(2):
                        tasks.append(dict(key="S3", c0=j * 128, M=128, wcol=None,
                                          outs=[(rqxT[:, 2 * j, :], 0, 64, False), (rqxT[:, 2 * j + 1, :], 64, 128, True)],
                                          post=xit[:, j, :].un(1).bc([128, 4, 128])))
                    for j in range(2):
                        tasks.append(dict(key="S3", c0=256 + j * 128, M=128, wcol=None,
                                          outs=[(rkT[:, 2 * j, :], 0, 64, False), (rkT[:, 2 * j + 1, :], 64, 128, True)]))
                    nt = len(tasks)
                    fillers = {8: [lambda: kw_tile(0)], 9: [lambda: kw_tile(1)], 10: [lambda: kw_tile(2)],
                               11: [lambda: kw_tile(3)]}

                    def rv_tile(t_):
                        s4_ = wget(wi["S4"]).re("p (kc n) -> p kc n", kc=8)
                        pb = psC()
                        for kc in range(8):
                            mm(pb, xT[:, kc, t_ * 128:(t_ + 1) * 128], s4_[:, kc, :], start=(kc == 0), stop=(kc == 7))
                        act(rv[:, t_, :], pb, AF.Copy)

                    def rg_tile(t_):
                        s5_ = wget(wi["S5"]).re("p (kc n) -> p kc n", kc=8)
                        pb = psC()
                        for kc in range(8):
                            mm(pb, xT[:, kc, t_ * 128:(t_ + 1) * 128], s5_[:, kc, :], start=(kc == 0), stop=(kc == 7))
                        act(rgs[:, t_, :], pb, AF.Silu)

                    for t_ in range(4):
                        fillers[1 + t_] = [lambda t_=t_: rg_tile(t_)]
                    fillers[6] = [lambda: vc_g(0)]
                    fillers[7] = [lambda: vc_g(1)]
                    fillers[12] = [lambda: rv_tile(0), lambda: rv_tile(1)]
                    fillers[13] = [lambda: rv_tile(2), lambda: rv_tile(3)]
                    stt_ = [dict() for _ in tasks]
                    for k in range(nt + 2):
                        if k < nt:
                            T = tasks[k]
                            M = T["M"]
                            slot3 = wget(wi[T["key"]]).re("p (kc n) -> p kc n", kc=8)
                            pb = banks[k % 3]
                            for kc in range(8):
                                mm(pb[0:M, :], slot3[:, kc, T["c0"]:T["c0"] + M], xT[:, kc, :],
                                   start=(kc == 0), stop=(kc == 7))
                            stt_[k]["pb"] = pb
                            if T["wcol"] is not None:
                                sq = sqb[k % 2]
                                act(sq[0:M, :], pb[0:M, :], AF.Square)
                                stt_[k]["sq"] = sq
                            if "after" in T:
                                T["after"]()
                        j = k - 1
                        if 0 <= j < nt:
                            T = tasks[j]
                            M = T["M"]
                            pb = stt_[j]["pb"]
                            qn = qnb[j % 3]
                            if T["wcol"] is not None:
                                pm = banks[3 + j % 2]
                                mm(pm[0:M, :], onesb[0:M, 0:M], stt_[j]["sq"][0:M, :])
                                rs = rstd_t[j % 2]
                                act(rs[0:M, :], pm[0:M, :], AF.Ln, bias=epsb[0:M, :])
                                act(rs[0:M, :], rs[0:M, :], AF.Exp, scale=-0.5)
                                wc = T["wcol"]
                                stt(qn[0:M, :], pb[0:M, :], qkw[0:M, wc:wc + 1], rs[0:M, :], ALU.mult, ALU.mult)
                            else:
                                act(qn[0:M, :], pb[0:M, :], AF.Copy)
                            stt_[j]["qn"] = qn
                        for f_ in fillers.get(k, []):
                            f_()
                        i_ = k - 2
                        if 0 <= i_ < nt:
                            T = tasks[i_]
                            M = T["M"]
                            qn = stt_[i_]["qn"]
                            pr = banks[5 + i_ % 2]
                            mm(pr[0:M, :], rt[0:M, 0:M], qn[0:M, :])
                            a = t1[i_ % 2]
                            b_ = t2[i_ % 2]
                            tt(a[0:M, :], qn[0:M, :], cos[0:M, t0:t0 + CH], ALU.mult)
                            tt(b_[0:M, :], pr[0:M, :], sin[0:M, t0:t0 + CH], ALU.mult)
                            if T.get("post") is not None:
                                tt(a[0:M, :], a[0:M, :], b_[0:M, :], ALU.add)
                            for dst, r0, r1, via in T["outs"]:
                                tgt = uptmp[i_ % 2][r0:r1, :] if via else dst
                                if T.get("post") is not None:
                                    tt(tgt.re("p (s q) -> p s q", s=4), a[r0:r1, :].re("p (s q) -> p s q", s=4),
                                       T["post"][r0:r1], ALU.mult)
                                else:
                                    tt(tgt, a[r0:r1, :], b_[r0:r1, :], ALU.add)
                                if via:
                                    dma("sp", dst, tgt)
                    if stage < 1.5:
                        return
                    if b == 0 and ci == 0:
                        op("dve", [conv_gate], [], lambda h: h.memset(conv_gate.ap, 0.0))
                        for dst, src_piece in deferred_conv:
                            dma("pool", dst, V(src_piece, conv_gate.trs))
                    if stage < 3:
                        return
                    c0 = 32 * ci - 1
                    for kv, (key, srcw, dstT) in enumerate((("CK", kcw, kcmpT), ("CV", vcw, vcmpT))):
                        w1 = wget(wi[key]).re("p (i m) -> p i m", i=16)
                        w2 = wk2 if kv == 0 else wv2
                        if b == 0 and ci == 0:
                            pe_ = pek if kv == 0 else pev
                            pbz = psA()
                            for mh in range(2):
                                for i in range(16):
                                    mm(pbz[:, mh:mh + 1], w1[:, i, mh * 128:(mh + 1) * 128], pe_[:, i:i + 1],
                                       start=(i == 0), stop=(i == 15))
                            vcopy(biasH[:, kv * 2:kv * 2 + 2], pbz[:, 0:2])
                        for mh in range(2):
                            pb = psA()
                            for i in range(16):
                                mm(pb[:, 0:64].re("p (g c) -> p g c", g=2), w1[:, i, mh * 128:(mh + 1) * 128],
                                   srcw[:, :, 2 * i:2 * i + 512:16], start=(i == 0), stop=(i == 15))
                            act(HT[:, mh, :], pb[:, 0:64], AF.Silu, bias=biasH[:, kv * 2 + mh:kv * 2 + mh + 1])
                        for g in range(2):
                            pb = psA()
                            for mh in range(2):
                                mm(pb[0:64, 0:32], w2[:, mh, :], HT[:, mh, g * 32:(g + 1) * 32],
                                   start=(mh == 0), stop=(mh == 1))
                            if ci == 0:
                                vcopy(dstT[:, g, 0:31], pb[0:64, 1:32])
                            else:
                                vcopy(dstT[:, g, c0:c0 + 32], pb[0:64, 0:32])
                    for g in range(2):
                        pb = psA()
                        pbb = pb.bitcast(BF16)
                        transpose(pbb[:, 0:64], vcmpT[:, g, :], ident[0:64, 0:64])
                        vcopy(vcmp_aug[:, g, 0:64], pbb[:, 0:64])

                    if stage < 4:
                        return
                    ret_pieces = []
                    for s_ in range(4):
                        ts_ = slice(s_ * 128, (s_ + 1) * 128)
                        cell = {}

                        def p0(s_=s_, ts_=ts_, cell=cell):
                            pb = psA()
                            pbb = pb.bitcast(BF16)
                            for hr in range(4):
                                transpose(pbb[:, hr * 64:(hr + 1) * 64], rkT[:, hr, ts_], ident[0:64, 0:64])
                            tt(kz, pbb[:, 0:256].re("p (h d) -> p h d", h=4), zeta8.un(2).bc([128, 4, 64]), ALU.mult)

                        def p1(s_=s_, ts_=ts_, cell=cell):
                            for hr in range(4):
                                pi = psA()
                                mm(pi[:, 0:128], rkT[:, hr, ts_], rqxT[:, hr, ts_])
                                tt(innb[hr], pi[:, 0:128], dmatp[:, hr, :], ALU.mult)

                        def p2(s_=s_, ts_=ts_, cell=cell):
                            py = banks[4]
                            for hr in range(4):
                                mm(py[:, hr * 128:(hr + 1) * 128], innb[hr], rv[:, s_, hr * 128:(hr + 1) * 128],
                                   start=(hr == 0), stop=False)
                                mm(py[:, hr * 128:(hr + 1) * 128], rqxT[:, hr, ts_], state_bf[:, hr, :],
                                   start=False, stop=(hr == 3))
                            cell["py"] = py

                        def p2s(s_=s_, cell=cell):
                            py = cell["py"]
                            ysb_s = t2[s_ % 2].re("p (h v) -> p h v", h=4)
                            ysq = rot("t1", t1).re("p (h v) -> p h v", h=4)
                            cell["ysb"] = ysb_s
                            for hr in range(4):
                                act(ysb_s[:, hr, :], py[:, hr * 128:(hr + 1) * 128], AF.Copy,
                                    accum=st1[:, s_ * 8 + hr:s_ * 8 + hr + 1])
                            for hr in range(4):
                                act(ysq[:, hr, :], py[:, hr * 128:(hr + 1) * 128], AF.Square,
                                    accum=st1[:, s_ * 8 + 4 + hr:s_ * 8 + 5 + hr])

                        def p2b(s_=s_, cell=cell):
                            sm = st1[:, s_ * 8:s_ * 8 + 4]
                            sq_ = st1[:, s_ * 8 + 4:s_ * 8 + 8]
                            ts(stm, sm, 1.0 / 128, None, ALU.mult)
                            tt(stv, stm, stm, ALU.mult)
                            stt(stv, sq_, 1.0 / 128, stv, ALU.mult, ALU.subtract)

                        def p3(s_=s_, ts_=ts_, cell=cell):
                            pst = psA()
                            for hr in range(4):
                                mm(pst[0:64, hr * 128:(hr + 1) * 128], kz[:, hr, :], rv[:, s_, hr * 128:(hr + 1) * 128],
                                   start=(hr == 0), stop=(hr == 3))
                            for hr in range(4):
                                stt(state[:, hr, :], state[:, hr, :], gamma_c[hr], pst[0:64, hr * 128:(hr + 1) * 128],
                                    ALU.mult, ALU.add)

                        def p3b(s_=s_, cell=cell):
                            act(stv, stv, AF.Sqrt, bias=epsb)
                            act(state_bf, state, AF.Copy)

                        def p4(s_=s_, cell=cell):
                            recip(stv, stv)
                            stt(st2, stm, -1.0, stv, ALU.mult, ALU.mult)

                        def p4b(s_=s_, cell=cell):
                            ysb_s = cell["ysb"]
                            for hr in range(4):
                                act(ysb_s[:, hr, :], ysb_s[:, hr, :], AF.Identity, bias=st2[:, hr:hr + 1],
                                    scale=stv[:, hr:hr + 1])

                        def p5(s_=s_, cell=cell):
                            ysb_s = cell["ysb"]
                            tt(ysb_s, ysb_s, retw.re("p (h v) -> p h v", h=4), ALU.mult)
                            tt(tokb_t[s_][:, 512:1024].re("p (h v) -> p h v", h=4), ysb_s,
                               rgs[:, s_, :].re("p (h v) -> p h v", h=4), ALU.mult)

                        ret_pieces += [p0, p1, p2, p2s, p2b, p3, p3b, p4, p4b, p5]
                    op("dve", [st1], [], lambda h: h.memset(st1.ap, 0.0))
                    ret_pos = [0]

                    def next_piece():
                        if ret_pos[0] < len(ret_pieces):
                            ret_pieces[ret_pos[0]]()
                            ret_pos[0] += 1

                    def evac_branch(ob, hq, br, first):
                        o3 = ob[:, 0:260].re("p (q e) -> p q e", q=4)
                        ts(rd, o3[:, :, 64], 1e-30, None, ALU.max)
                        recip(rd, rd)
                        tt(ff, rd, gates[:, :, br * 8 + hq], ALU.mult)
                        dst = o_nsa[:, :, hq * 64:(hq + 1) * 64]
                        if first:
                            tt(dst, o3[:, :, 0:64], ff.un(2).bc([128, 4, 64]), ALU.mult)
                        else:
                            tt(otmp, o3[:, :, 0:64], ff.un(2).bc([128, 4, 64]), ALU.mult)
                            tt(dst, dst, otmp, ALU.add)

                    pipe = {"pend": [], "n": 0}

                    def pipe_push(sc_fn, pv_fn, depth=6):
                        sc_fn()
                        pipe["pend"].append(pv_fn)
                        while len(pipe["pend"]) > depth:
                            pipe["pend"].pop(0)()
                        pipe["n"] += 1
                        if pipe["n"] % 2 == 0:
                            next_piece()

                    def pipe_flush():
                        while pipe["pend"]:
                            pipe["pend"].pop(0)()

                    def run_pipe(steps, depth=6, final=None):
                        for i_s, (sc_fn, pv_fn) in enumerate(steps):
                            if final is not None and i_s == len(steps) - 1:
                                def pv_last(pv_fn=pv_fn):
                                    pv_fn()
                                    final()
                                pipe_push(sc_fn, pv_last, depth)
                            else:
                                pipe_push(sc_fn, pv_fn, depth)
                        if VAR == 'flush':
                            pipe_flush()

                    for g in range(2):
                        steps = []
                        for r in range(4):
                            hq = g * 4 + r
                            cell = {}

                            def sc_fn(hq=hq, cell=cell):
                                ps = psA()
                                mm(ps, kcmpT[:, g, :], QT[0:64, hq, :], start=True, stop=False)
                                mm(ps, ident, cmpmask[:, t0:t0 + CH], start=False, stop=True)
                                pt = rot("PT", PT)
                                act(pt, ps, AF.Exp, scale=0.125)
                                cell["pt"] = pt

                            def pv_fn(hq=hq, r=r, cell=cell):
                                pt = cell["pt"]
                                oc = psB()
                                oc3 = oc[:, 0:260].re("p (q e) -> p q e", q=4)
                                for qs in range(4):
                                    mm(oc3[:, qs, :], pt[:, qs * 128:(qs + 1) * 128], vcmp_aug[:, g, 0:65],
                                       start=(qs == 0), stop=(qs == 3))
                                psl = psA()
                                psl3 = psl[:, 0:132].re("p (q e) -> p q e", q=4)
                                for qs in range(4):
                                    mm(psl3[:, qs, :], pt[:, qs * 128:(qs + 1) * 128], ovl, start=(qs == 0),
                                       stop=(qs == 3))
                                evac_branch(oc, hq, 0, True)
                                tt(sctmp, psl3[:, :, 0:32], rd.un(2).bc([128, 4, 32]), ALU.mult)
                                if r == 0:
                                    tt(score, sctmp, selbias[:, ci * 4:ci * 4 + 4, :], ALU.add)
                                else:
                                    tt(score, score, sctmp, ALU.add)
                                next_piece()

                            steps.append((sc_fn, pv_fn))
                        run_pipe(steps)
                        pipe_flush()
                        for qs in range(4):
                            op("dve", [m8], [score], lambda h, qs=qs: h.max(m8.ap[:, qs, :], score.ap[:, qs, :]))
                        ts(thr, m8[:, :, 7], -0.5, None, ALU.max)
                        tt(sctmp, score, thr.un(2).bc([128, 4, 32]), ALU.is_ge)
                        ts(mneg[:, :, 64:96], sctmp, -NEG, NEG, ALU.mult, ALU.add)
                        def emit_mask(g=g):
                            pb = psA()
                            pbb = pb.bitcast(BF16)
                            for qs in range(4):
                                transpose(pbb[0:96, qs * 128:(qs + 1) * 128], mneg[:, qs, :], ident)
                            for r_ in range(4):
                                act(QT[64:96, g * 4 + r_, :], pbb[64:96, 0:512], AF.Copy)

                        for r in range(4):
                            hq = g * 4 + r
                            Qs = QT[:, hq, :]
                            Q = QT[0:64, hq, :]
                            if r == 3:
                                emit_mask()
                            owb = psB()
                            ow3 = owb[:, 0:260].re("p (q e) -> p q e", q=4)
                            ow_state = {"first": True}
                            steps = []
                            for kt in range(max(0, 4 * ci - 2), 4 * ci + 4):
                                cell = {}

                                def sc_fn(kt=kt, cell=cell):
                                    dg = kt - 4 * ci
                                    qa, qb = max(0, dg), min(3, dg + 2)
                                    ps = psA()
                                    mm(ps[:, qa * 128:(qb + 1) * 128], kwT[:, g, kt * 128:(kt + 1) * 128],
                                       Q[:, qa * 128:(qb + 1) * 128], start=True, stop=False)
                                    masks = []
                                    for qs in range(qa, qb + 1):
                                        dd = qs - dg
                                        if dd == 0:
                                            masks.append((qs, trile))
                                        elif dd == 2:
                                            masks.append((qs, trigt))
                                    for mi, (qs, mk_) in enumerate(masks):
                                        mm(ps[:, qs * 128:(qs + 1) * 128], ident, mk_, start=False,
                                           stop=(mi == len(masks) - 1))
                                    pt = rot("PT", PT)
                                    act(pt[:, qa * 128:(qb + 1) * 128], ps[:, qa * 128:(qb + 1) * 128], AF.Exp,
                                        scale=0.125)
                                    cell["pt"] = pt
                                    cell["qa"], cell["qb"] = qa, qb

                                def pv_fn(kt=kt, cell=cell, ow3=ow3, ow_state=ow_state, g=g):
                                    pt = cell["pt"]
                                    for qs in range(cell["qa"], cell["qb"] + 1):
                                        mm(ow3[:, qs, :], pt[:, qs * 128:(qs + 1) * 128], vw_aug[:, kt, g, 0:65],
                                           start=ow_state["first"], stop=(kt == 4 * ci + 3))
                                        ow_state["first"] = False

                                steps.append((sc_fn, pv_fn))
                            def fin_w(owb=owb, hq=hq):
                                evac_branch(owb, hq, 2, False)
                                next_piece()
                            run_pipe(steps, final=fin_w)
                            if VAR == 'flushw':
                                pipe_flush()

                        for r in range(4):
                            hq = g * 4 + r
                            Qs = QT[:, hq, :]
                            Q = QT[0:64, hq, :]
                            osb = psB()
                            os3 = osb[:, 0:260].re("p (q e) -> p q e", q=4)
                            nkt = 4 * ci + 4
                            steps = []
                            for kt in range(nkt):
                                cell = {}

                                def sc_fn(kt=kt, cell=cell):
                                    ps = psA()
                                    dg = kt - 4 * ci
                                    mm(ps, ksT[:, g, kt * 128:(kt + 1) * 128], Qs, start=True, stop=(dg < 0))
                                    qlo = max(0, dg)
                                    if dg >= 0:
                                        mm(ps[:, dg * 128:(dg + 1) * 128], ident, trile, start=False, stop=True)
                                    pt = rot("PT", PT)
                                    act(pt[:, qlo * 128:512], ps[:, qlo * 128:512], AF.Exp, scale=0.125)
                                    cell["pt"] = pt
                                    cell["qlo"] = qlo

                                def pv_fn(kt=kt, cell=cell, os3=os3, g=g, nkt=nkt):
                                    pt = cell["pt"]
                                    for qs in range(cell["qlo"], 4):
                                        mm(os3[:, qs, :], pt[:, qs * 128:(qs + 1) * 128], vs_aug[:, kt, g, 0:65],
                                           start=(kt == 0 and qs == 0), stop=(kt == nkt - 1))

                                steps.append((sc_fn, pv_fn))
                            def fin_s(osb=osb, hq=hq):
                                evac_branch(osb, hq, 1, False)
                                next_piece()
                            run_pipe(steps, final=fin_s)
                            if VAR == 'flushs':
                                pipe_flush()
                    pipe_flush()
                    for t_ in range(4):
                        act(tokb_t[t_][:, 0:512], o_nsa[:, t_, :], AF.Copy)

                    pipe_flush()
                    while ret_pos[0] < len(ret_pieces):
                        next_piece()
                    wo_slots = [wget(wi["O0"]).re("p (kc n) -> p kc n", kc=8), wget(wi["O1"]).re("p (kc n) -> p kc n", kc=8)]
                    for t_ in range(4):
                        T_tile(0, t_)
                    for t_ in range(4):
                        for nh in range(2):
                            pb = psB()
                            for kc in range(8):
                                mm(pb, T8_t[0][t_][:, kc, :], wo_slots[nh][:, kc, :], start=(kc == 0), stop=(kc == 7))
                            dst = xh_t[t_][:, nh * 512:(nh + 1) * 512]
                            tt(dst, dst, pb, ALU.add)
                        ln_tile(t_)
                        if t_ >= 1:
                            T_tile(1, t_ - 1, lnw2)
                    T_tile(1, 3, lnw2)

                    if b == 0 and ci == NCH - 1:
                        dump("mix", tokb, [128, 4, D])
                        dump("gates", gates, [128, 4, 24])
                    if stage < 6:
                        return
                    hT = T8[1]
                    def ffn_up(fc):
                        U = wget(wi[("U", fc)]).re("p (kc n) -> p kc n", kc=8)
                        at = actT[fc % 2]
                        for fs in range(4):
                            pb = psA()
                            for kc in range(8):
                                mm(pb, U[:, kc, fs * 128:(fs + 1) * 128], hT[:, kc, :], start=(kc == 0), stop=(kc == 7))
                            rl = rot("relu_t", relu_t)
                            act(rl, pb, AF.Relu)
                            act(at[:, fs, :], rl, AF.Square)

                    def ffn_down(fc):
                        Dn = wget(wi[("D", fc)]).re("p (fs n) -> p fs n", fs=4)
                        at = actT[fc % 2]
                        for t_ in range(4):
                            for nh in range(2):
                                pd = psB()
                                for fs in range(4):
                                    mm(pd, at[:, fs, t_ * 128:(t_ + 1) * 128], Dn[:, fs, nh * 512:(nh + 1) * 512],
                                       start=(fs == 0), stop=(fs == 3))
                                dst = xh_t[t_][:, nh * 512:(nh + 1) * 512]
                                tt(dst, dst, pd, ALU.add)

                    nxt = None
                    if ci + 1 < NCH:
                        nxt = (b, ci + 1)
                    elif b + 1 < NB:
                        nxt = (b + 1, 0)

                    def pfA(t_):
                        nb_, nci = nxt
                        r0 = nci * CH + t_ * 128
                        dma("sp", xs, V(x_d.ap[nb_, r0:r0 + 128, :], ()))

                    def pfB(t_):
                        ln_tile(t_, src=xs)

                    def pfC(t_):
                        T_tile(0, t_, lnw1)

                    ffn_up(0)
                    for fc in range(8):
                        if fc + 1 < 8:
                            wget(wi[("D", fc)])
                            ffn_up(fc + 1)
                        if nxt is not None and stage >= 99:
                            if 2 <= fc <= 5:
                                pfC(fc - 2)
                            if 1 <= fc <= 4:
                                pfB(fc - 1)
                            if fc <= 3:
                                pfA(fc)
                        ffn_down(fc)
                    prefetched[0] = nxt is not None and stage >= 99

                chunk_body()
                dma("sp", V(out_d.ap[b, t0:t0 + CH, :].rearrange("(t p) d -> p t d", p=128), ()), xh)
        fw.finish()
        build.stats = {e: fw.E[e].n for e in fw.E}
        build.stats["waits"] = fw.nwaits
    return nc


_CONSTS = None


def _host_inputs(inputs, NB, cores):
    global _CONSTS
    if _CONSTS is None:
        _CONSTS = _consts()
    c = _CONSTS
    f = lambda a: np.ascontiguousarray(np.asarray(a, dtype=np.float32))
    shared = {
        "w_in": f(inputs["w_in"][0]), "w_out": f(inputs["w_out"][0]), "w_up": f(inputs["w_up"][0]),
        "w_down": f(inputs["w_down"][0]),
        "cmp_wk1": f(inputs["cmp_wk1"][0]), "cmp_wv1": f(inputs["cmp_wv1"][0]),
        "cmp_wk2": f(inputs["cmp_wk2"][0]), "cmp_wv2": f(inputs["cmp_wv2"][0]),
        "ln1_t": f(np.asarray(inputs["ln1_w"][0]).reshape(8, 128).T),
        "ln2_t": f(np.asarray(inputs["ln2_w"][0]).reshape(8, 128).T),
        "pek_t": f(np.asarray(inputs["cmp_pe_k"][0]).reshape(16, 128).T),
        "pev_t": f(np.asarray(inputs["cmp_pe_v"][0]).reshape(16, 128).T),
        "retw": f(np.asarray(inputs["ret_norm_w"][0]).reshape(1, 512)),
    }
    qn = np.asarray(inputs["q_norm_w"][0], dtype=np.float32)
    kn = np.asarray(inputs["k_norm_w"][0], dtype=np.float32)
    qkw = np.stack([qn, kn[0], kn[1], kn[2]], axis=1)
    shared["qkw_t"] = f(np.concatenate([qkw, qkw], 0))
    for k, v in c.items():
        if not k.startswith("_"):
            shared[k] = v
    x = np.asarray(inputs["x"], dtype=np.float32)
    maps = []
    for ci in range(cores):
        m = dict(shared)
        m["x"] = np.ascontiguousarray(x[ci * NB:(ci + 1) * NB])
        maps.append(m)
    return maps


def kernel(**inputs):
    global _CONSTS
    if _CONSTS is None:
        _CONSTS = _consts()
    n = 8
    NB = 2
    nc = build(NB=NB, NCH=NCH_FULL, consts=_CONSTS)
    maps = _host_inputs(inputs, NB, n)
    res = run_bass_kernel_spmd(nc, maps, core_ids=list(range(n)))
    out = np.concatenate([r["out"] for r in res.results], axis=0)
    return out.astype(np.float32)
```

```python
import bisect
import math
import os
VAR = os.environ.get('KVAR', '')
from contextlib import ExitStack

import numpy as np
import concourse.bass as bass
import concourse.mybir as mybir
from concourse.bass_utils import run_bass_kernel_spmd

F32 = mybir.dt.float32
BF16 = mybir.dt.bfloat16
ALU = mybir.AluOpType
AF = mybir.ActivationFunctionType
AX = mybir.AxisListType

NEG = -30000.0
S = 2048
D = 1024
CH = 512
NCH_FULL = S // CH
EPS = 1e-6


class Tr:
    __slots__ = ("w", "r", "name", "excl")

    def __init__(self, name="", excl=False):
        self.w = None
        self.r = []
        self.name = name
        self.excl = excl


class V:
    __slots__ = ("ap", "trs")

    def __init__(self, ap, trs):
        self.ap = ap
        self.trs = tuple(trs)

    def __getitem__(self, k):
        return V(self.ap[k], self.trs)

    def re(self, pattern_, **kw):
        return V(self.ap.rearrange(pattern_, **kw), self.trs)

    def bc(self, shape):
        return V(self.ap.broadcast_to(list(shape)), self.trs)

    def un(self, axis):
        return V(self.ap.unsqueeze(axis), self.trs)

    def bitcast(self, dt):
        return V(self.ap.bitcast(dt), self.trs)


class Eng:
    def __init__(self, name, h, sem, is_pe=False):
        self.name = name
        self.h = h
        self.sem = sem
        self.count = 0
        self.n = 0
        self.last = None
        self.tick_idx = []
        self.tick_val = []
        self.seen = {}
        self.is_pe = is_pe


class FW:
    def __init__(self, nc, es, n_dma_sems=24):
        self.nc = nc
        self.es = es
        mk = lambda nm: es.enter_context(nc.semaphore(nm))
        self.E = {
            "pe": Eng("pe", nc.tensor, mk("s_pe"), True),
            "act": Eng("act", nc.scalar, mk("s_act")),
            "dve": Eng("dve", nc.vector, mk("s_dve")),
            "pool": Eng("pool", nc.gpsimd, mk("s_pool")),
            "sp": Eng("sp", nc.sync, mk("s_sp")),
        }
        self.dsems = [[mk(f"s_dma{i}"), 0] for i in range(n_dma_sems)]
        half_ = n_dma_sems // 2
        self.dpool = {"sp": list(range(0, half_)), "pool": list(range(half_, n_dma_sems))}
        self.dnext = {"sp": 0, "pool": 0}
        self.nwaits = 0

    def sbv(self, name, shape, dt=F32):
        t = self.es.enter_context(self.nc.sbuf_tensor("sb_" + name, list(shape), dt))
        return V(t[:], (Tr(name),))

    def psv(self, name, shape, dt=F32):
        t = self.es.enter_context(self.nc.psum_tensor(name, list(shape), dt))
        return V(t[:], (Tr(name, excl=True),))

    def _ticket(self, X, idx):
        k = bisect.bisect_left(X.tick_idx, idx)
        if k < len(X.tick_idx):
            return X.tick_val[k]
        assert X.last is not None and X.n - 1 >= idx
        X.count += 1
        X.last.then_inc(X.sem, 1)
        X.tick_idx.append(X.n - 1)
        X.tick_val.append(X.count)
        return X.count

    def _wait(self, E, ref):
        if ref[0] == "dma":
            _, si, val = ref
            key = ("d", si)
            if E.seen.get(key, 0) >= val:
                return
            E.h.wait_ge(self.dsems[si][0], val)
            E.seen[key] = val
        else:
            _, xn, idx = ref
            X = self.E[xn]
            tk = self._ticket(X, idx)
            if E.seen.get(xn, 0) >= tk:
                return
            E.h.wait_ge(X.sem, tk)
            E.seen[xn] = tk
        self.nwaits += 1

    def _deps(self, E, outs, ins):
        en = E.name
        for v in ins:
            for tr in v.trs:
                w = tr.w
                if w is not None:
                    if not (w[0] == "e" and w[1] == en and E.is_pe):
                        self._wait(E, w)
                if tr.excl:
                    for r in tr.r:
                        if r[0] == "e" and r[1] == en:
                            continue
                        self._wait(E, r)
        for v in outs:
            for tr in v.trs:
                w = tr.w
                if w is not None and not (w[0] == "e" and w[1] == en and E.is_pe):
                    self._wait(E, w)
                for r in tr.r:
                    if r[0] == "e" and r[1] == en and E.is_pe:
                        continue
                    self._wait(E, r)

    def _record(self, ref, outs, ins):
        for v in ins:
            for tr in v.trs:
                tr.r.append(ref)
                if len(tr.r) > 32:
                    best = {}
                    for r in tr.r:
                        key = (r[0], r[1])
                        if key not in best or r[2] > best[key][2]:
                            best[key] = r
                    tr.r = list(best.values())
        for v in outs:
            for tr in v.trs:
                tr.w = ref
                tr.r = []

    def op(self, en, outs, ins, fn):
        E = self.E[en]
        self._deps(E, outs, ins)
        inst = fn(E.h)
        E.last = inst
        idx = E.n
        E.n += 1
        self._record(("e", en, idx), outs, ins)
        return inst

    def dma(self, qn, out, in_, **kw):
        E = self.E[qn]
        if E.n > 0 and (not E.tick_idx or E.tick_idx[-1] < E.n - 1):
            self._ticket(E, E.n - 1)
        self._deps(E, [out], [in_])
        pl = self.dpool[qn]
        si = pl[self.dnext[qn] % len(pl)]
        self.dnext[qn] += 1
        sem, val = self.dsems[si]
        if val > 0:
            self._wait(E, ("dma", si, val))
        inst = E.h.dma_start(out=out.ap, in_=in_.ap, **kw)
        inst.then_inc(sem, 16)
        self.dsems[si][1] = val + 16
        E.last = inst
        E.n += 1
        E.tick_idx.append(E.n - 1)
        E.tick_val.append(E.count)
        self._record(("dma", si, val + 16), [out], [in_])
        return inst

    def finish(self):
        E = self.E["sp"]
        for si, (sem, val) in enumerate(self.dsems):
            if val > 0:
                self._wait(E, ("dma", si, val))


def _consts():
    c = {}
    c["c_ident"] = np.eye(128, dtype=np.float32)
    R = np.zeros((64, 64), np.float32)
    for m in range(32):
        R[m, m + 32] = -1.0
        R[m + 32, m] = 1.0
    rt = np.zeros((128, 128), np.float32)
    rt[:64, :64] = R.T
    rt[64:, 64:] = R.T
    c["c_rt"] = rt
    on = np.zeros((128, 128), np.float32)
    on[:64, :64] = 1.0 / 64
    on[64:, 64:] = 1.0 / 64
    c["c_ones"] = on
    half = 32
    inv = (10000.0 ** (-np.arange(half, dtype=np.float32) / half)).astype(np.float32)
    ang = np.arange(S, dtype=np.float32)[:, None] * inv[None, :]
    cos = np.cos(ang).astype(np.float32).T
    sin = np.sin(ang).astype(np.float32).T
    c["c_cos"] = np.concatenate([cos, cos, cos, cos], 0)
    c["c_sin"] = np.concatenate([sin, sin, sin, sin], 0)
    cc = np.arange(128)[:, None]
    tt = np.arange(S)[None, :]
    c["c_cmpmask"] = np.where((16 * cc + 31 <= tt) & (cc < 127), 0.0, NEG).astype(np.float32)
    k = np.arange(128)[:, None]
    q = np.arange(128)[None, :]
    c["c_trile"] = np.where(k <= q, 0.0, NEG).astype(np.float32)
    c["c_trigt"] = np.where(k > q, 0.0, NEG).astype(np.float32)
    E = np.zeros((32, S), np.float32)
    for j in range(32):
        E[j, 64 * j:64 * j + 64] = 1.0
    c["c_E"] = E
    cs = np.arange(127) * 16
    jb = np.arange(32)
    ov = ((cs[:, None] < (jb[None, :] + 1) * 64) & (cs[:, None] + 32 > jb[None, :] * 64)).astype(np.float32)
    ovl = np.zeros((128, 33), np.float32)
    ovl[:127, :32] = ov
    ovl[:127, 32] = 1.0
    c["c_ovl"] = ovl
    t = np.arange(S)
    cur = t // 64
    forced = (jb[None, :] == 0) | (jb[None, :] == cur[:, None]) | (jb[None, :] == cur[:, None] - 1)
    valid = jb[None, :] <= cur[:, None]
    bias = np.where(forced, 1.0e4, np.where(valid, 0.0, -1.0)).astype(np.float32)
    c["c_selbias"] = np.ascontiguousarray(bias.reshape(16, 128, 32).transpose(1, 0, 2))
    H = 4
    lg = np.log(1.0 - 2.0 ** (-5.0 - np.arange(H, dtype=np.float64)))
    i = np.arange(128, dtype=np.float64)
    dm = np.zeros((128, H, 128), np.float64)
    for h in range(H):
        dm[:, h, :] = (i[None, :] >= i[:, None]) * np.exp(-lg[h] * (i[:, None] + 1.0)) / 8.0
    c["c_dmatp"] = dm.astype(np.float32)
    c["c_zeta8"] = (np.exp(lg[None, :] * (127.0 - i[:, None])) / 8.0).astype(np.float32)
    xi = np.exp(lg[:, None] * (i[None, :] + 1.0))
    xi2 = np.zeros((128, 2, 128), np.float64)
    for j in range(2):
        xi2[:64, j, :] = xi[2 * j][None, :]
        xi2[64:, j, :] = xi[2 * j + 1][None, :]
    c["c_xi"] = xi2.astype(np.float32)
    c["_gamma_c"] = [float(np.exp(lg[h] * 128.0)) for h in range(H)]
    return c


CONST_SHAPES = None


def build(NB=2, NCH=NCH_FULL, consts=None, tab_dt=F32, dbg=(), stage=99):
    nc = bass.Bass("TRN2", target_bir_lowering=False)
    gamma_c = consts["_gamma_c"]

    def din(name, shape):
        return V(nc.dram_tensor(name, list(shape), F32, kind="ExternalInput").ap(), ())

    x_d = din("x", [NB, S, D])
    w_in_d = din("w_in", [D, 2840])
    w_out_d = din("w_out", [D, D])
    w_up_d = din("w_up", [D, 4096])
    w_down_d = din("w_down", [4096, D])
    wk1_d = din("cmp_wk1", [2048, 256])
    wv1_d = din("cmp_wv1", [2048, 256])
    wk2_d = din("cmp_wk2", [256, 64])
    wv2_d = din("cmp_wv2", [256, 64])
    lnw1_d = din("ln1_t", [128, 8])
    lnw2_d = din("ln2_t", [128, 8])
    qkw_d = din("qkw_t", [128, 4])
    pek_d = din("pek_t", [128, 16])
    pev_d = din("pev_t", [128, 16])
    retw_d = din("retw", [1, 512])
    cd = {k: din(k, v.shape) for k, v in consts.items() if not k.startswith("_")}
    out_d = V(nc.dram_tensor("out", [NB, S, D], F32, kind="ExternalOutput").ap(), ())

    with ExitStack() as es:
        fw = FW(nc, es)
        sbv, op, dma = fw.sbv, fw.op, fw.dma

        ident = sbv("ident", [128, 128], BF16)
        rt = sbv("rt", [128, 128], BF16)
        onesb = sbv("onesb", [128, 128], BF16)
        cos = sbv("cos", [128, S], tab_dt)
        sin = sbv("sin", [128, S], tab_dt)
        cmpmask = sbv("cmpmask", [128, S], BF16)
        trile = sbv("trile", [128, 128], BF16)
        trigt = sbv("trigt", [128, 128], BF16)
        ovl = sbv("ovl", [128, 33], BF16)
        selbias = sbv("selbias", [128, 16, 32], F32)
        dmatp = sbv("dmatp", [128, 4, 128], F32)
        zeta8 = sbv("zeta8", [128, 4], F32)
        xit = sbv("xit", [128, 2, 128], F32)
        lnw1 = sbv("lnw1", [128, 8], F32)
        lnw2 = sbv("lnw2", [128, 8], F32)
        qkw = sbv("qkw", [128, 4], F32)
        pek = sbv("pek", [128, 16], BF16)
        pev = sbv("pev", [128, 16], BF16)
        retw = sbv("retw", [128, 512], F32)
        wg = sbv("wg", [128, 8, 24], BF16)
        wk2 = sbv("wk2", [128, 2, 64], BF16)
        wv2 = sbv("wv2", [128, 2, 64], BF16)
        epsb = sbv("epsb", [128, 1], F32)
        biasH = sbv("biasH", [128, 4], F32)

        for dst, key in ((ident, "c_ident"), (rt, "c_rt"), (onesb, "c_ones"), (cmpmask, "c_cmpmask"),
                         (trile, "c_trile"), (trigt, "c_trigt"), (ovl, "c_ovl"),
                         (cos, "c_cos"), (sin, "c_sin")):
            dma("pool", dst, cd[key])
        sp_consts = [(lnw1, lnw1_d), (qkw, qkw_d), (lnw2, lnw2_d), (selbias, cd["c_selbias"]), (dmatp, cd["c_dmatp"]),
                     (zeta8, cd["c_zeta8"]), (xit, cd["c_xi"])]
        dma("pool", pek, pek_d)
        dma("pool", pev, pev_d)
        sp_consts.append((retw, V(retw_d.ap.broadcast_to([128, 512]), ())))
        dma("pool", wg, V(w_in_d.ap[:, 1280:1304].rearrange("(kc p) n -> p kc n", p=128), ()))
        dma("pool", wk2, V(wk2_d.ap.rearrange("(mh p) d -> p mh d", p=128), ()))
        dma("pool", wv2, V(wv2_d.ap.rearrange("(mh p) d -> p mh d", p=128), ()))
        op("dve", [epsb], [], lambda h: h.memset(epsb.ap, EPS))


        def scratch(name, src_v, rows, cols, piece_rows):
            t = nc.dram_tensor("scr_" + name, [rows, cols], BF16, kind="Internal").ap()
            trs = []
            for r0 in range(0, rows, piece_rows):
                tr = Tr(f"scr_{name}_{r0}")
                dma("pool", V(t[r0:r0 + piece_rows, :], (tr,)), V(src_v.ap[r0:r0 + piece_rows, :], ()))
                trs.append(tr)
            return V(t, trs)

        def conv_block(name, src_ap, rows, cols, nsplit=2, defer=None, gate=None):
            t = nc.dram_tensor("scr_" + name, [rows, cols], BF16, kind="Internal").ap()
            trs = []
            pr = rows // nsplit
            for r0 in range(0, rows, pr):
                tr = Tr(f"scr_{name}_{r0}")
                dst = V(t[r0:r0 + pr, :], (tr,))
                src_piece = src_ap[r0:r0 + pr, :]
                if defer is None:
                    dma("pool", dst, V(src_piece, ()))
                else:
                    defer.append((dst, src_piece))
                trs.append(tr)
            return V(t, trs)

        deferred_conv = []
        conv_gate = sbv("conv_gate", [128, 1], F32)

        scr = {}
        scr["S0"] = conv_block("S0", w_in_d.ap[:, 0:512], D, 512)
        scr["S5"] = conv_block("S5", w_in_d.ap[:, 2328:2840], D, 512)
        s1_t = nc.dram_tensor("scr_s1", [D, 512], BF16, kind="Internal").ap()
        s1_trs = []
        for j, c0_ in enumerate([512, 512, 576, 576, 640, 640, 704, 704]):
            tr = Tr(f"scr_s1_{j}")
            dma("pool", V(s1_t[:, j * 64:(j + 1) * 64], (tr,)), V(w_in_d.ap[:, c0_:c0_ + 64], ()))
            s1_trs.append(tr)
        scr["S1"] = V(s1_t, s1_trs)
        scr["S2"] = conv_block("S2", w_in_d.ap[:, 768:1280], D, 512)
        scr["S3"] = conv_block("S3", w_in_d.ap[:, 1304:1816], D, 512)
        scr["S4"] = conv_block("S4", w_in_d.ap[:, 1816:2328], D, 512)
        scr["CK"] = conv_block("CK", wk1_d.ap, 2048, 256)
        scr["CV"] = conv_block("CV", wv1_d.ap, 2048, 256)
        scr["O0"] = conv_block("O0", w_out_d.ap[:, 0:512], D, 512, defer=deferred_conv)
        scr["O1"] = conv_block("O1", w_out_d.ap[:, 512:1024], D, 512, defer=deferred_conv)
        for fc in range(8):
            scr[("U", fc)] = conv_block(f"U{fc}", w_up_d.ap[:, fc * 512:(fc + 1) * 512], D, 512, defer=deferred_conv)
            scr[("D", fc)] = conv_block(f"D{fc}", w_down_d.ap[fc * 512:(fc + 1) * 512, :], 512, D, defer=deferred_conv)

        ksT = sbv("ksT", [128, 2, S], BF16)
        kwT = sbv("kwT", [64, 2, S], BF16)
        vs_aug = sbv("vs_aug", [128, 16, 2, 66], BF16)
        vw_aug = sbv("vw_aug", [128, 16, 2, 66], BF16)
        kcw = sbv("kcw", [128, 2, 544], BF16)
        vcw = sbv("vcw", [128, 2, 544], BF16)
        kcmpT = sbv("kcmpT", [64, 2, 128], BF16)
        vcmpT = sbv("vcmpT", [64, 2, 128], BF16)
        vcmp_aug = sbv("vcmp_aug", [128, 2, 66], BF16)
        state = sbv("state", [64, 4, 128], F32)
        state_bf = sbv("state_bf", [64, 4, 128], BF16)
        op("dve", [ksT], [], lambda h: h.memset(ksT.ap[64:128, :, :], 0.0))
        for g_ in range(2):
            dma("pool", ksT[64:96, g_, :], cd["c_E"])
        op("dve", [vs_aug], [], lambda h: h.memset(vs_aug.ap, 1.0))
        op("dve", [vw_aug], [], lambda h: h.memset(vw_aug.ap, 1.0))
        op("dve", [vcmp_aug], [], lambda h: h.memset(vcmp_aug.ap, 1.0))
        op("dve", [kcmpT], [], lambda h: h.memset(kcmpT.ap, 0.0))
        op("dve", [vcmpT], [], lambda h: h.memset(vcmpT.ap, 0.0))

        def tiled(v, slicer):
            trs = [Tr(f"t{t}") for t in range(4)]
            tiles = [V(slicer(v.ap, t), (trs[t],)) for t in range(4)]
            return V(v.ap, trs), tiles

        xh, xh_t = tiled(sbv("xh", [128, 4, D], F32), lambda ap, t: ap[:, t, :])
        tokb, tokb_t = tiled(sbv("tokb", [128, 4, D], BF16), lambda ap, t: ap[:, t, :])
        T8, T8_t = [], []
        for i in range(2):
            a_, t_l = tiled(sbv(f"T8_{i}", [128, 8, CH], BF16), lambda ap, t: ap[:, :, t * 128:(t + 1) * 128])
            T8.append(a_)
            T8_t.append(t_l)
        QT = sbv("QT", [128, 8, CH], BF16)
        op("dve", [QT], [], lambda h: h.memset(QT.ap[64:128, :, :], 0.0))
        rqxT = sbv("rqxT", [64, 4, CH], BF16)
        rkT = sbv("rkT", [64, 4, CH], BF16)
        kz = sbv("kz", [128, 4, 64], BF16)
        rv = sbv("rv", [128, 4, 512], BF16)
        rgs = sbv("rgs", [128, 4, 512], BF16)
        gates = sbv("gates", [128, 4, 24], F32)
        o_nsa = sbv("o_nsa", [128, 4, 512], F32)
        ss4 = sbv("ss4", [128, 4], F32)
        rstd4 = sbv("rstd4", [128, 4], F32)
        sqb = [sbv(f"sqb{i}", [128, CH], BF16) for i in range(2)]
        rstd_t = [sbv(f"rstd_t{i}", [128, CH], F32) for i in range(2)]
        qnb = [sbv(f"qnb{i}", [128, CH], BF16) for i in range(3)]
        big4k = fw.es.enter_context(nc.sbuf_tensor("sb_big4k", [128, 2 * CH], F32))
        _tra, _trb = Tr("t1a"), Tr("t1b")
        t1 = [V(big4k[:, 0:CH], (_tra,)), V(big4k[:, CH:2 * CH], (_trb,))]
        xs = V(big4k[:], (_tra, _trb))
        uptmp = [sbv(f"uptmp{i}", [128, CH], BF16) for i in range(2)]
        t2 = [sbv(f"t2_{i}", [128, CH], F32) for i in range(2)]
        PT = [sbv(f"PT{i}", [128, CH], BF16) for i in range(7)]
        HT = sbv("HT", [128, 2, 64], BF16)
        score = sbv("score", [128, 4, 32], F32)
        sctmp = sbv("sctmp", [128, 4, 32], F32)
        m8 = sbv("m8", [128, 4, 8], F32)
        thr = sbv("thr", [128, 4], F32)
        mneg = sbv("mneg", [128, 4, 96], BF16)
        op("dve", [mneg], [], lambda h: h.memset(mneg.ap, 0.0))
        rd = sbv("rd", [128, 4], F32)
        ff = sbv("ff", [128, 4], F32)
        otmp = sbv("otmp", [128, 4, 64], F32)
        innb = [sbv(f"innb{i}", [128, 128], BF16) for i in range(4)]
        st1 = sbv("st1", [128, 32], F32)
        st2 = sbv("st2", [128, 4], F32)
        stm = sbv("stm", [128, 4], F32)
        stv = sbv("stv", [128, 4], F32)
        actT = [sbv(f"actT{i}", [128, 4, CH], BF16) for i in range(2)]
        relu_t = [sbv(f"relu_t{i}", [128, CH], F32) for i in range(1)]
        WS = [sbv(f"WS{i}", [128, 4096], BF16) for i in range(4)]

        banks = [fw.psv(f"pb{i}", [128, 512], F32) for i in range(8)]
        rrA = [0]
        rrB = [0]

        def psA():
            b = banks[rrA[0] % 4]
            rrA[0] += 1
            return b

        def psB():
            b = banks[5 + rrB[0] % 3]
            rrB[0] += 1
            return b

        rrP = [0]

        def psP():
            b = banks[rrP[0] % 6]
            rrP[0] += 1
            return b

        rrC = [0]

        def psC():
            return banks[7]

        rr = {}

        def rot(name, lst):
            i = rr.get(name, 0)
            rr[name] = i + 1
            return lst[i % len(lst)]

        def wsrc_cols(wd, a, b):
            return V(wd.ap[:, a:b].rearrange("(kc p) n -> p kc n", p=128), wd.trs)

        items = []

        def add_item(parts):
            items.append(parts)
            return len(items) - 1

        witems = {}

        def kc_view(v):
            return V(v.ap.rearrange("(kc p) n -> p kc n", p=128), v.trs)

        for b in range(NB):
            for ci in range(NCH):
                d = {}
                for key in ("S0", "S5", "S1", "S2", "S3", "S4"):
                    d[key] = add_item([(lambda s: s.re("p (kc n) -> p kc n", kc=8), kc_view(scr[key]))])
                for key in ("CK", "CV"):
                    d[key] = add_item([(lambda s: s.re("p (i m) -> p i m", i=16),
                                        V(scr[key].ap.rearrange("(i p) m -> p i m", p=128), scr[key].trs))])
                for key in ("O0", "O1"):
                    d[key] = add_item([(lambda s: s.re("p (kc n) -> p kc n", kc=8), kc_view(scr[key]))])
                for fc in range(8):
                    d[("U", fc)] = add_item([(lambda s: s.re("p (kc n) -> p kc n", kc=8), kc_view(scr[("U", fc)]))])
                    d[("D", fc)] = add_item([(lambda s: s.re("p (fs n) -> p fs n", fs=4),
                                              V(scr[("D", fc)].ap.rearrange("(fs p) n -> p fs n", p=128),
                                                scr[("D", fc)].trs))])
                witems[(b, ci)] = d
        issued = [0]
        LOOK = 2

        def wget(idx):
            while issued[0] < len(items) and issued[0] <= idx + LOOK:
                k = issued[0]
                slot = WS[k % 4]
                for viewfn, src in items[k]:
                    dma("sp", viewfn(slot), src)
                issued[0] += 1
            return WS[idx % 4]

        def dump(name, v, shape):
            if name in dbg:
                dd = V(nc.dram_tensor("dbg_" + name, list(shape), F32, kind="ExternalOutput").ap(), ())
                dma("pool", dd, v)

        def mm(out, lhsT, rhs, start=True, stop=True):
            return op("pe", [out], [lhsT, rhs],
                      lambda h: h.matmul(out.ap, lhsT.ap, rhs.ap, start=start, stop=stop, skip_group_check=True))

        def transpose(out, in_, idn):
            return op("pe", [out], [in_, idn], lambda h: h.transpose(out.ap, in_.ap, idn.ap))

        def act(out, in_, func, bias=None, scale=1.0, accum=None):
            outs = [out] + ([accum] if accum is not None else [])
            ins = [in_] + ([bias] if isinstance(bias, V) else []) + ([scale] if isinstance(scale, V) else [])
            kw = {}
            if bias is not None:
                kw["bias"] = bias.ap if isinstance(bias, V) else bias
            if accum is not None:
                kw["accum_out"] = accum.ap
            sc = scale.ap if isinstance(scale, V) else scale
            return op("act", outs, ins, lambda h: h.activation(out.ap, in_.ap, func, scale=sc, **kw))

        def tt(out, a, b, alu, eng="dve"):
            return op(eng, [out], [a, b], lambda h: h.tensor_tensor(out.ap, a.ap, b.ap, alu))

        def ts(out, a, s1, s2, op0, op1=None):
            ins = [a] + [s for s in (s1, s2) if isinstance(s, V)]
            a1 = s1.ap if isinstance(s1, V) else s1
            a2 = s2.ap if isinstance(s2, V) else s2
            if op1 is None:
                return op("dve", [out], ins, lambda h: h.tensor_scalar(out.ap, a.ap, a1, a2, op0))
            return op("dve", [out], ins, lambda h: h.tensor_scalar(out.ap, a.ap, a1, a2, op0, op1))

        def stt(out, a, sc, b, op0, op1):
            ins = [a, b] + ([sc] if isinstance(sc, V) else [])
            s_ = sc.ap if isinstance(sc, V) else sc
            return op("dve", [out], ins, lambda h: h.scalar_tensor_tensor(out.ap, a.ap, s_, b.ap, op0, op1))

        def vcopy(out, in_):
            return op("dve", [out], [in_], lambda h: h.tensor_copy(out.ap, in_.ap))

        def recip(out, in_):
            return op("dve", [out], [in_], lambda h: h.reciprocal(out.ap, in_.ap))

        def rsum(out, in_):
            return op("dve", [out], [in_], lambda h: h.reduce_sum(out.ap, in_.ap, AX.X))

        def ln_tile(t_, src=None):
            if src is None:
                src = xh_t[t_]
            op("dve", [ss4], [], lambda h: h.memset(ss4.ap[:, t_:t_ + 1], 0.0))
            act(tokb_t[t_], src, AF.Square, accum=ss4[:, t_:t_ + 1])
            act(rstd4[:, t_:t_ + 1], ss4[:, t_:t_ + 1], AF.Sqrt, bias=epsb, scale=1.0 / D)
            recip(rstd4[:, t_:t_ + 1], rstd4[:, t_:t_ + 1])
            ts(tokb_t[t_], src, rstd4[:, t_:t_ + 1], None, ALU.mult)

        def T_tile(di, t_, lnw=None):
            dst_t = T8_t[di][t_]
            for half in range(2):
                pb = psA()
                pbb = pb.bitcast(BF16)
                for j in range(4):
                    kc = half * 4 + j
                    transpose(pbb[:, j * 128:(j + 1) * 128], tokb_t[t_][:, kc * 128:(kc + 1) * 128], ident)
                src = pbb[:, 0:512].re("p (j q) -> p j q", j=4)
                dst = dst_t[:, half * 4:(half + 1) * 4, :]
                if lnw is None:
                    act(dst, src, AF.Copy)
                else:
                    tt(dst, src, lnw[:, half * 4:(half + 1) * 4].un(2).bc([128, 4, 128]), ALU.mult)

        def rmsnorm_to_T(lnw, di):
            for t_ in range(4):
                ln_tile(t_)
            for t_ in range(4):
                T_tile(di, t_, lnw)

        def proj_fm(slot3, c0, M, xT):
            pb = psA()
            for kc in range(8):
                mm(pb[0:M, :], slot3[:, kc, c0:c0 + M], xT[:, kc, :], start=(kc == 0), stop=(kc == 7))
            return pb

        def norm_rope(pb, M, wcol, t0, outs, post=None):
            qn = rot("qnb", qnb)
            if wcol is not None:
                sq = rot("sqb", sqb)
                act(sq[0:M, :], pb[0:M, :], AF.Square)
                pm = psA()
                mm(pm[0:M, :], onesb[0:M, 0:M], sq[0:M, :])
                rs = rot("rstd_t", rstd_t)
                act(rs[0:M, :], pm[0:M, :], AF.Sqrt, bias=epsb[0:M, :])
                recip(rs[0:M, :], rs[0:M, :])
                stt(qn[0:M, :], pb[0:M, :], qkw[0:M, wcol:wcol + 1], rs[0:M, :], ALU.mult, ALU.mult)
            else:
                act(qn[0:M, :], pb[0:M, :], AF.Copy)
            pr = psA()
            mm(pr[0:M, :], rt[0:M, 0:M], qn[0:M, :])
            a = rot("t1", t1)
            b_ = rot("t2", t2)
            tt(a[0:M, :], qn[0:M, :], cos[0:M, t0:t0 + CH], ALU.mult)
            tt(b_[0:M, :], pr[0:M, :], sin[0:M, t0:t0 + CH], ALU.mult)
            if post is not None:
                tt(a[0:M, :], a[0:M, :], b_[0:M, :], ALU.add)
                for dst, r0, r1 in outs:
                    tt(dst, a[r0:r1, :].re("p (s q) -> p s q", s=4), post, ALU.mult)
            else:
                for dst, r0, r1 in outs:
                    tt(dst, a[r0:r1, :], b_[r0:r1, :], ALU.add)

        def setup_bias():
            for kv, (wd, pe) in enumerate(((wk1_d, pek), (wv1_d, pev))):
                slot = WS[kv]
                s3 = slot.re("p (i m) -> p i m", i=16)
                dma("pool", s3, V(wd.ap.rearrange("(i p) m -> p i m", p=128), ()))
                pb = psA()
                for mh in range(2):
                    for i in range(16):
                        mm(pb[:, mh:mh + 1], s3[:, i, mh * 128:(mh + 1) * 128], pe[:, i:i + 1],
                           start=(i == 0), stop=(i == 15))
                vcopy(biasH[:, kv * 2:kv * 2 + 2], pb[:, 0:2])


        prefetched = [False]
        for b in range(NB):
            op("dve", [state], [], lambda h: h.memset(state.ap, 0.0))
            op("dve", [state_bf], [], lambda h: h.memset(state_bf.ap, 0.0))
            op("dve", [kcw], [], lambda h: h.memset(kcw.ap, 0.0))
            op("dve", [vcw], [], lambda h: h.memset(vcw.ap, 0.0))
            for ci in range(NCH):
                t0 = ci * CH
                wi = witems[(b, ci)]
                def chunk_body(b=b, ci=ci, t0=t0, wi=wi):
                    dma("sp", xh, V(x_d.ap[b, t0:t0 + CH, :].rearrange("(t p) d -> p t d", p=128), ()))
                    if b == 0 and ci == 0:
                        for dst_c, src_c in sp_consts:
                            dma("sp", dst_c, src_c)
                    xT = T8[0]
                    if stage < 0.5:
                        return
                    if not prefetched[0]:
                        rmsnorm_to_T(lnw1, 0)
                    if stage < 1.01:
                        return

                    def vc_g(g):
                        s1_ = wget(wi["S1"]).re("p (kc n) -> p kc n", kc=8)
                        if True:
                            pb = psC()
                            for kc in range(8):
                                mm(pb, s1_[:, kc, 256 + g * 128:256 + (g + 1) * 128], xT[:, kc, :],
                                   start=(kc == 0), stop=(kc == 7))
                            act(vcw[0:64, g, 16:528], pb[0:64, :], AF.Copy)
                            act(vcw[64:128, g, 15:527], pb[64:128, :], AF.Copy)

                    def kw_tile(t_):
                        s2_ = wget(wi["S2"]).re("p (kc n) -> p kc n", kc=8)
                        if True:
                            tile_i = ci * 4 + t_
                            pb = psC()
                            for kc in range(8):
                                mm(pb[:, 0:128], xT[:, kc, t_ * 128:(t_ + 1) * 128], s2_[:, kc, 128:256],
                                   start=(kc == 0), stop=(kc == 7))
                            for kc in range(8):
                                mm(pb[:, 128:256], xT[:, kc, t_ * 128:(t_ + 1) * 128], s2_[:, kc, 384:512],
                                   start=(kc == 0), stop=(kc == 7))
                            for kc in range(8):
                                mm(pb[:, 256:280], xT[:, kc, t_ * 128:(t_ + 1) * 128], wg[:, kc, :],
                                   start=(kc == 0), stop=(kc == 7))
                            act(vs_aug[:, tile_i, :, 0:64], pb[:, 0:128].re("p (g d) -> p g d", g=2), AF.Copy)
                            act(vw_aug[:, tile_i, :, 0:64], pb[:, 128:256].re("p (g d) -> p g d", g=2), AF.Copy)
                            act(gates[:, t_, :], pb[:, 256:280], AF.Sigmoid)

                    if ci > 0:
                        for g in range(2):
                            vcopy(kcw[0:64, g, 0:16], kcw[0:64, g, 512:528])
                            vcopy(kcw[64:128, g, 0:15], kcw[64:128, g, 512:527])
                            vcopy(vcw[0:64, g, 0:16], vcw[0:64, g, 512:528])
                            vcopy(vcw[64:128, g, 0:15], vcw[64:128, g, 512:527])
                    tasks = []
                    for j in range(4):
                        tasks.append(dict(key="S0", c0=j * 128, M=128, wcol=0,
                                          outs=[(QT[0:64, 2 * j, :], 0, 64, False), (QT[0:64, 2 * j + 1, :], 64, 128, True)]))
                    for g in range(2):
                        tasks.append(dict(key="S1", c0=g * 128, M=128, wcol=1,
                                          outs=[(kcw[0:64, g, 16:528], 0, 64, False),
                                                (kcw[64:128, g, 15:527], 64, 128, False)]))
                    tasks.append(dict(key="S2", c0=0, M=128, wcol=2,
                                      outs=[(ksT[0:64, 0, t0:t0 + CH], 0, 64, False), (ksT[0:64, 1, t0:t0 + CH], 64, 128, True)]))
                    tasks.append(dict(key="S2", c0=256, M=128, wcol=3,
                                      outs=[(kwT[:, 0, t0:t0 + CH], 0, 64, False), (kwT[:, 1, t0:t0 + CH], 64, 128, True)]))
                    for j in range(2):
                        tasks.append(dict(key="S3", c0=j * 128, M=128, wcol=None,
                                          outs=[(rqxT[:, 2 * j, :], 0, 64, False), (rqxT[:, 2 * j + 1, :], 64, 128, True)],
                                          post=xit[:, j, :].un(1).bc([128, 4, 128])))
                    for j in range(2):
                        tasks.append(dict(key="S3", c0=256 + j * 128, M=128, wcol=None,
                                          outs=[(rkT[:, 2 * j, :], 0, 64, False), (rkT[:, 2 * j + 1, :], 64, 128, True)]))
                    nt = len(tasks)
                    fillers = {8: [lambda: kw_tile(0)], 9: [lambda: kw_tile(1)], 10: [lambda: kw_tile(2)],
                               11: [lambda: kw_tile(3)]}

                    def rv_tile(t_):
                        s4_ = wget(wi["S4"]).re("p (kc n) -> p kc n", kc=8)
                        pb = psC()
                        for kc in range(8):
                            mm(pb, xT[:, kc, t_ * 128:(t_ + 1) * 128], s4_[:, kc, :], start=(kc == 0), stop=(kc == 7))
                        act(rv[:, t_, :], pb, AF.Copy)

                    def rg_tile(t_):
                        s5_ = wget(wi["S5"]).re("p (kc n) -> p kc n", kc=8)
                        pb = psC()
                        for kc in range(8):
                            mm(pb, xT[:, kc, t_ * 128:(t_ + 1) * 128], s5_[:, kc, :], start=(kc == 0), stop=(kc == 7))
                        act(rgs[:, t_, :], pb, AF.Silu)

                    for t_ in range(4):
                        fillers[1 + t_] = [lambda t_=t_: rg_tile(t_)]
                    fillers[6] = [lambda: vc_g(0)]
                    fillers[7] = [lambda: vc_g(1)]
                    fillers[12] = [lambda: rv_tile(0), lambda: rv_tile(1)]
                    fillers[13] = [lambda: rv_tile(2), lambda: rv_tile(3)]
                    stt_ = [dict() for _ in tasks]
                    for k in range(nt + 2):
                        if k < nt:
                            T = tasks[k]
                            M = T["M"]
                            slot3 = wget(wi[T["key"]]).re("p (kc n) -> p kc n", kc=8)
                            pb = banks[k % 3]
                            for kc in range(8):
                                mm(pb[0:M, :], slot3[:, kc, T["c0"]:T["c0"] + M], xT[:, kc, :],
                                   start=(kc == 0), stop=(kc == 7))
                            stt_[k]["pb"] = pb
                            if T["wcol"] is not None:
                                sq = sqb[k % 2]
                                act(sq[0:M, :], pb[0:M, :], AF.Square)
                                stt_[k]["sq"] = sq
                            if "after" in T:
                                T["after"]()
                        j = k - 1
                        if 0 <= j < nt:
                            T = tasks[j]
                            M = T["M"]
                            pb = stt_[j]["pb"]
                            qn = qnb[j % 3]
                            if T["wcol"] is not None:
                                pm = banks[3 + j % 2]
                                mm(pm[0:M, :], onesb[0:M, 0:M], stt_[j]["sq"][0:M, :])
                                rs = rstd_t[j % 2]
                                act(rs[0:M, :], pm[0:M, :], AF.Ln, bias=epsb[0:M, :])
                                act(rs[0:M, :], rs[0:M, :], AF.Exp, scale=-0.5)
                                wc = T["wcol"]
                                stt(qn[0:M, :], pb[0:M, :], qkw[0:M, wc:wc + 1], rs[0:M, :], ALU.mult, ALU.mult)
                            else:
                                act(qn[0:M, :], pb[0:M, :], AF.Copy)
                            stt_[j]["qn"] = qn
                        for f_ in fillers.get(k, []):
                            f_()
                        i_ = k - 2
                        if 0 <= i_ < nt:
                            T = tasks[i_]
                            M = T["M"]
                            qn = stt_[i_]["qn"]
                            pr = banks[5 + i_ % 2]
                            mm(pr[0:M, :], rt[0:M, 0:M], qn[0:M, :])
                            a = t1[i_ % 2]
                            b_ = t2[i_ % 2]
                            tt(a[0:M, :], qn[0:M, :], cos[0:M, t0:t0 + CH], ALU.mult)
                            tt(b_[0:M, :], pr[0:M, :], sin[0:M, t0:t0 + CH], ALU.mult)
                            if T.get("post") is not None:
                                tt(a[0:M, :], a[0:M, :], b_[0:M, :], ALU.add)
                            for dst, r0, r1, via in T["outs"]:
                                tgt = uptmp[i_ % 2][r0:r1, :] if via else dst
                                if T.get("post") is not None:
                                    tt(tgt.re("p (s q) -> p s q", s=4), a[r0:r1, :].re("p (s q) -> p s q", s=4),
                                       T["post"][r0:r1], ALU.mult)
                                else:
                                    tt(tgt, a[r0:r1, :], b_[r0:r1, :], ALU.add)
                                if via:
                                    dma("sp", dst, tgt)
                    if stage < 1.5:
                        return
                    if b == 0 and ci == 0:
                        op("dve", [conv_gate], [], lambda h: h.memset(conv_gate.ap, 0.0))
                        for dst, src_piece in deferred_conv:
                            dma("pool", dst, V(src_piece, conv_gate.trs))
                    if stage < 3:
                        return
                    c0 = 32 * ci - 1
                    for kv, (key, srcw, dstT) in enumerate((("CK", kcw, kcmpT), ("CV", vcw, vcmpT))):
                        w1 = wget(wi[key]).re("p (i m) -> p i m", i=16)
                        w2 = wk2 if kv == 0 else wv2
                        if b == 0 and ci == 0:
                            pe_ = pek if kv == 0 else pev
                            pbz = psA()
                            for mh in range(2):
                                for i in range(16):
                                    mm(pbz[:, mh:mh + 1], w1[:, i, mh * 128:(mh + 1) * 128], pe_[:, i:i + 1],
                                       start=(i == 0), stop=(i == 15))
                            vcopy(biasH[:, kv * 2:kv * 2 + 2], pbz[:, 0:2])
                        for mh in range(2):
                            pb = psA()
                            for i in range(16):
                                mm(pb[:, 0:64].re("p (g c) -> p g c", g=2), w1[:, i, mh * 128:(mh + 1) * 128],
                                   srcw[:, :, 2 * i:2 * i + 512:16], start=(i == 0), stop=(i == 15))
                            act(HT[:, mh, :], pb[:, 0:64], AF.Silu, bias=biasH[:, kv * 2 + mh:kv * 2 + mh + 1])
                        for g in range(2):
                            pb = psA()
                            for mh in range(2):
                                mm(pb[0:64, 0:32], w2[:, mh, :], HT[:, mh, g * 32:(g + 1) * 32],
                                   start=(mh == 0), stop=(mh == 1))
                            if ci == 0:
                                vcopy(dstT[:, g, 0:31], pb[0:64, 1:32])
                            else:
                                vcopy(dstT[:, g, c0:c0 + 32], pb[0:64, 0:32])
                    for g in range(2):
                        pb = psA()
                        pbb = pb.bitcast(BF16)
                        transpose(pbb[:, 0:64], vcmpT[:, g, :], ident[0:64, 0:64])
                        vcopy(vcmp_aug[:, g, 0:64], pbb[:, 0:64])

                    if stage < 4:
                        return
                    ret_pieces = []
                    for s_ in range(4):
                        ts_ = slice(s_ * 128, (s_ + 1) * 128)
                        cell = {}

                        def p0(s_=s_, ts_=ts_, cell=cell):
                            pb = psA()
                            pbb = pb.bitcast(BF16)
                            for hr in range(4):
                                transpose(pbb[:, hr * 64:(hr + 1) * 64], rkT[:, hr, ts_], ident[0:64, 0:64])
                            tt(kz, pbb[:, 0:256].re("p (h d) -> p h d", h=4), zeta8.un(2).bc([128, 4, 64]), ALU.mult)

                        def p1(s_=s_, ts_=ts_, cell=cell):
                            for hr in range(4):
                                pi = psA()
                                mm(pi[:, 0:128], rkT[:, hr, ts_], rqxT[:, hr, ts_])
                                tt(innb[hr], pi[:, 0:128], dmatp[:, hr, :], ALU.mult)

                        def p2(s_=s_, ts_=ts_, cell=cell):
                            py = banks[4]
                            for hr in range(4):
                                mm(py[:, hr * 128:(hr + 1) * 128], innb[hr], rv[:, s_, hr * 128:(hr + 1) * 128],
                                   start=(hr == 0), stop=False)
                                mm(py[:, hr * 128:(hr + 1) * 128], rqxT[:, hr, ts_], state_bf[:, hr, :],
                                   start=False, stop=(hr == 3))
                            cell["py"] = py

                        def p2s(s_=s_, cell=cell):
                            py = cell["py"]
                            ysb_s = t2[s_ % 2].re("p (h v) -> p h v", h=4)
                            ysq = rot("t1", t1).re("p (h v) -> p h v", h=4)
                            cell["ysb"] = ysb_s
                            for hr in range(4):
                                act(ysb_s[:, hr, :], py[:, hr * 128:(hr + 1) * 128], AF.Copy,
                                    accum=st1[:, s_ * 8 + hr:s_ * 8 + hr + 1])
                            for hr in range(4):
                                act(ysq[:, hr, :], py[:, hr * 128:(hr + 1) * 128], AF.Square,
                                    accum=st1[:, s_ * 8 + 4 + hr:s_ * 8 + 5 + hr])

                        def p2b(s_=s_, cell=cell):
                            sm = st1[:, s_ * 8:s_ * 8 + 4]
                            sq_ = st1[:, s_ * 8 + 4:s_ * 8 + 8]
                            ts(stm, sm, 1.0 / 128, None, ALU.mult)
                            tt(stv, stm, stm, ALU.mult)
                            stt(stv, sq_, 1.0 / 128, stv, ALU.mult, ALU.subtract)

                        def p3(s_=s_, ts_=ts_, cell=cell):
                            pst = psA()
                            for hr in range(4):
                                mm(pst[0:64, hr * 128:(hr + 1) * 128], kz[:, hr, :], rv[:, s_, hr * 128:(hr + 1) * 128],
                                   start=(hr == 0), stop=(hr == 3))
                            for hr in range(4):
                                stt(state[:, hr, :], state[:, hr, :], gamma_c[hr], pst[0:64, hr * 128:(hr + 1) * 128],
                                    ALU.mult, ALU.add)

                        def p3b(s_=s_, cell=cell):
                            act(stv, stv, AF.Sqrt, bias=epsb)
                            act(state_bf, state, AF.Copy)

                        def p4(s_=s_, cell=cell):
                            recip(stv, stv)
                            stt(st2, stm, -1.0, stv, ALU.mult, ALU.mult)

                        def p4b(s_=s_, cell=cell):
                            ysb_s = cell["ysb"]
                            for hr in range(4):
                                act(ysb_s[:, hr, :], ysb_s[:, hr, :], AF.Identity, bias=st2[:, hr:hr + 1],
                                    scale=stv[:, hr:hr + 1])

                        def p5(s_=s_, cell=cell):
                            ysb_s = cell["ysb"]
                            tt(ysb_s, ysb_s, retw.re("p (h v) -> p h v", h=4), ALU.mult)
                            tt(tokb_t[s_][:, 512:1024].re("p (h v) -> p h v", h=4), ysb_s,
                               rgs[:, s_, :].re("p (h v) -> p h v", h=4), ALU.mult)

                        ret_pieces += [p0, p1, p2, p2s, p2b, p3, p3b, p4, p4b, p5]
                    op("dve", [st1], [], lambda h: h.memset(st1.ap, 0.0))
                    ret_pos = [0]

                    def next_piece():
                        if ret_pos[0] < len(ret_pieces):
                            ret_pieces[ret_pos[0]]()
                            ret_pos[0] += 1

                    def evac_branch(ob, hq, br, first):
                        o3 = ob[:, 0:260].re("p (q e) -> p q e", q=4)
                        ts(rd, o3[:, :, 64], 1e-30, None, ALU.max)
                        recip(rd, rd)
                        tt(ff, rd, gates[:, :, br * 8 + hq], ALU.mult)
                        dst = o_nsa[:, :, hq * 64:(hq + 1) * 64]
                        if first:
                            tt(dst, o3[:, :, 0:64], ff.un(2).bc([128, 4, 64]), ALU.mult)
                        else:
                            tt(otmp, o3[:, :, 0:64], ff.un(2).bc([128, 4, 64]), ALU.mult)
                            tt(dst, dst, otmp, ALU.add)

                    pipe = {"pend": [], "n": 0}

                    def pipe_push(sc_fn, pv_fn, depth=6):
                        sc_fn()
                        pipe["pend"].append(pv_fn)
                        while len(pipe["pend"]) > depth:
                            pipe["pend"].pop(0)()
                        pipe["n"] += 1
                        if pipe["n"] % 2 == 0:
                            next_piece()

                    def pipe_flush():
                        while pipe["pend"]:
                            pipe["pend"].pop(0)()

                    def run_pipe(steps, depth=6, final=None):
                        for i_s, (sc_fn, pv_fn) in enumerate(steps):
                            if final is not None and i_s == len(steps) - 1:
                                def pv_last(pv_fn=pv_fn):
                                    pv_fn()
                                    final()
                                pipe_push(sc_fn, pv_last, depth)
                            else:
                                pipe_push(sc_fn, pv_fn, depth)
                        if VAR == 'flush':
                            pipe_flush()

                    for g in range(2):
                        steps = []
                        for r in range(4):
                            hq = g * 4 + r
                            cell = {}

                            def sc_fn(hq=hq, cell=cell):
                                ps = psA()
                                mm(ps, kcmpT[:, g, :], QT[0:64, hq, :], start=True, stop=False)
                                mm(ps, ident, cmpmask[:, t0:t0 + CH], start=False, stop=True)
                                pt = rot("PT", PT)
                                act(pt, ps, AF.Exp, scale=0.125)
                                cell["pt"] = pt

                            def pv_fn(hq=hq, r=r, cell=cell):
                                pt = cell["pt"]
                                oc = psB()
                                oc3 = oc[:, 0:260].re("p (q e) -> p q e", q=4)
                                for qs in range(4):
                                    mm(oc3[:, qs, :], pt[:, qs * 128:(qs + 1) * 128], vcmp_aug[:, g, 0:65],
                                       start=(qs == 0), stop=(qs == 3))
                                psl = psA()
                                psl3 = psl[:, 0:132].re("p (q e) -> p q e", q=4)
                                for qs in range(4):
                                    mm(psl3[:, qs, :], pt[:, qs * 128:(qs + 1) * 128], ovl, start=(qs == 0),
                                       stop=(qs == 3))
                                evac_branch(oc, hq, 0, True)
                                tt(sctmp, psl3[:, :, 0:32], rd.un(2).bc([128, 4, 32]), ALU.mult)
                                if r == 0:
                                    tt(score, sctmp, selbias[:, ci * 4:ci * 4 + 4, :], ALU.add)
                                else:
                                    tt(score, score, sctmp, ALU.add)
                                next_piece()

                            steps.append((sc_fn, pv_fn))
                        run_pipe(steps)
                        pipe_flush()
                        for qs in range(4):
                            op("dve", [m8], [score], lambda h, qs=qs: h.max(m8.ap[:, qs, :], score.ap[:, qs, :]))
                        ts(thr, m8[:, :, 7], -0.5, None, ALU.max)
                        tt(sctmp, score, thr.un(2).bc([128, 4, 32]), ALU.is_ge)
                        ts(mneg[:, :, 64:96], sctmp, -NEG, NEG, ALU.mult, ALU.add)
                        def emit_mask(g=g):
                            pb = psA()
                            pbb = pb.bitcast(BF16)
                            for qs in range(4):
                                transpose(pbb[0:96, qs * 128:(qs + 1) * 128], mneg[:, qs, :], ident)
                            for r_ in range(4):
                                act(QT[64:96, g * 4 + r_, :], pbb[64:96, 0:512], AF.Copy)

                        for r in range(4):
                            hq = g * 4 + r
                            Qs = QT[:, hq, :]
                            Q = QT[0:64, hq, :]
                            if r == 3:
                                emit_mask()
                            owb = psB()
                            ow3 = owb[:, 0:260].re("p (q e) -> p q e", q=4)
                            ow_state = {"first": True}
                            steps = []
                            for kt in range(max(0, 4 * ci - 2), 4 * ci + 4):
                                cell = {}

                                def sc_fn(kt=kt, cell=cell):
                                    dg = kt - 4 * ci
                                    qa, qb = max(0, dg), min(3, dg + 2)
                                    ps = psA()
                                    mm(ps[:, qa * 128:(qb + 1) * 128], kwT[:, g, kt * 128:(kt + 1) * 128],
                                       Q[:, qa * 128:(qb + 1) * 128], start=True, stop=False)
                                    masks = []
                                    for qs in range(qa, qb + 1):
                                        dd = qs - dg
                                        if dd == 0:
                                            masks.append((qs, trile))
                                        elif dd == 2:
                                            masks.append((qs, trigt))
                                    for mi, (qs, mk_) in enumerate(masks):
                                        mm(ps[:, qs * 128:(qs + 1) * 128], ident, mk_, start=False,
                                           stop=(mi == len(masks) - 1))
                                    pt = rot("PT", PT)
                                    act(pt[:, qa * 128:(qb + 1) * 128], ps[:, qa * 128:(qb + 1) * 128], AF.Exp,
                                        scale=0.125)
                                    cell["pt"] = pt
                                    cell["qa"], cell["qb"] = qa, qb

                                def pv_fn(kt=kt, cell=cell, ow3=ow3, ow_state=ow_state, g=g):
                                    pt = cell["pt"]
                                    for qs in range(cell["qa"], cell["qb"] + 1):
                                        mm(ow3[:, qs, :], pt[:, qs * 128:(qs + 1) * 128], vw_aug[:, kt, g, 0:65],
                                           start=ow_state["first"], stop=(kt == 4 * ci + 3))
                                        ow_state["first"] = False

                                steps.append((sc_fn, pv_fn))
                            def fin_w(owb=owb, hq=hq):
                                evac_branch(owb, hq, 2, False)
                                next_piece()
                            run_pipe(steps, final=fin_w)
                            if VAR == 'flushw':
                                pipe_flush()

                        for r in range(4):
                            hq = g * 4 + r
                            Qs = QT[:, hq, :]
                            Q = QT[0:64, hq, :]
                            osb = psB()
                            os3 = osb[:, 0:260].re("p (q e) -> p q e", q=4)
                            nkt = 4 * ci + 4
                            steps = []
                            for kt in range(nkt):
                                cell = {}

                                def sc_fn(kt=kt, cell=cell):
                                    ps = psA()
                                    dg = kt - 4 * ci
                                    mm(ps, ksT[:, g, kt * 128:(kt + 1) * 128], Qs, start=True, stop=(dg < 0))
                                    qlo = max(0, dg)
                                    if dg >= 0:
                                        mm(ps[:, dg * 128:(dg + 1) * 128], ident, trile, start=False, stop=True)
                                    pt = rot("PT", PT)
                                    act(pt[:, qlo * 128:512], ps[:, qlo * 128:512], AF.Exp, scale=0.125)
                                    cell["pt"] = pt
                                    cell["qlo"] = qlo

                                def pv_fn(kt=kt, cell=cell, os3=os3, g=g, nkt=nkt):
                                    pt = cell["pt"]
                                    for qs in range(cell["qlo"], 4):
                                        mm(os3[:, qs, :], pt[:, qs * 128:(qs + 1) * 128], vs_aug[:, kt, g, 0:65],
                                           start=(kt == 0 and qs == 0), stop=(kt == nkt - 1))

                                steps.append((sc_fn, pv_fn))
                            def fin_s(osb=osb, hq=hq):
                                evac_branch(osb, hq, 1, False)
                                next_piece()
                            run_pipe(steps, final=fin_s)
                            if VAR == 'flushs':
                                pipe_flush()
                    pipe_flush()
                    for t_ in range(4):
                        act(tokb_t[t_][:, 0:512], o_nsa[:, t_, :], AF.Copy)

                    pipe_flush()
                    while ret_pos[0] < len(ret_pieces):
                        next_piece()
                    wo_slots = [wget(wi["O0"]).re("p (kc n) -> p kc n", kc=8), wget(wi["O1"]).re("p (kc n) -> p kc n", kc=8)]
                    for t_ in range(4):
                        T_tile(0, t_)
                    for t_ in range(4):
                        for nh in range(2):
                            pb = psB()
                            for kc in range(8):
                                mm(pb, T8_t[0][t_][:, kc, :], wo_slots[nh][:, kc, :], start=(kc == 0), stop=(kc == 7))
                            dst = xh_t[t_][:, nh * 512:(nh + 1) * 512]
                            tt(dst, dst, pb, ALU.add)
                        ln_tile(t_)
                        if t_ >= 1:
                            T_tile(1, t_ - 1, lnw2)
                    T_tile(1, 3, lnw2)

                    if b == 0 and ci == NCH - 1:
                        dump("mix", tokb, [128, 4, D])
                        dump("gates", gates, [128, 4, 24])
                    if stage < 6:
                        return
                    hT = T8[1]
                    def ffn_up(fc):
                        U = wget(wi[("U", fc)]).re("p (kc n) -> p kc n", kc=8)
                        at = actT[fc % 2]
                        for fs in range(4):
                            pb = psA()
                            for kc in range(8):
                                mm(pb, U[:, kc, fs * 128:(fs + 1) * 128], hT[:, kc, :], start=(kc == 0), stop=(kc == 7))
                            rl = rot("relu_t", relu_t)
                            act(rl, pb, AF.Relu)
                            act(at[:, fs, :], rl, AF.Square)

                    def ffn_down(fc):
                        Dn = wget(wi[("D", fc)]).re("p (fs n) -> p fs n", fs=4)
                        at = actT[fc % 2]
                        for t_ in range(4):
                            for nh in range(2):
                                pd = psB()
                                for fs in range(4):
                                    mm(pd, at[:, fs, t_ * 128:(t_ + 1) * 128], Dn[:, fs, nh * 512:(nh + 1) * 512],
                                       start=(fs == 0), stop=(fs == 3))
                                dst = xh_t[t_][:, nh * 512:(nh + 1) * 512]
                                tt(dst, dst, pd, ALU.add)

                    nxt = None
                    if ci + 1 < NCH:
                        nxt = (b, ci + 1)
                    elif b + 1 < NB:
                        nxt = (b + 1, 0)

                    def pfA(t_):
                        nb_, nci = nxt
                        r0 = nci * CH + t_ * 128
                        dma("sp", xs, V(x_d.ap[nb_, r0:r0 + 128, :], ()))

                    def pfB(t_):
                        ln_tile(t_, src=xs)

                    def pfC(t_):
                        T_tile(0, t_, lnw1)

                    ffn_up(0)
                    for fc in range(8):
                        if fc + 1 < 8:
                            wget(wi[("D", fc)])
                            ffn_up(fc + 1)
                        if nxt is not None and stage >= 99:
                            if 2 <= fc <= 5:
                                pfC(fc - 2)
                            if 1 <= fc <= 4:
                                pfB(fc - 1)
                            if fc <= 3:
                                pfA(fc)
                        ffn_down(fc)
                    prefetched[0] = nxt is not None and stage >= 99

                chunk_body()
                dma("sp", V(out_d.ap[b, t0:t0 + CH, :].rearrange("(t p) d -> p t d", p=128), ()), xh)
        fw.finish()
        build.stats = {e: fw.E[e].n for e in fw.E}
        build.stats["waits"] = fw.nwaits
    return nc


_CONSTS = None


def _host_inputs(inputs, NB, cores):
    global _CONSTS
    if _CONSTS is None:
        _CONSTS = _consts()
    c = _CONSTS
    f = lambda a: np.ascontiguousarray(np.asarray(a, dtype=np.float32))
    shared = {
        "w_in": f(inputs["w_in"][0]), "w_out": f(inputs["w_out"][0]), "w_up": f(inputs["w_up"][0]),
        "w_down": f(inputs["w_down"][0]),
        "cmp_wk1": f(inputs["cmp_wk1"][0]), "cmp_wv1": f(inputs["cmp_wv1"][0]),
        "cmp_wk2": f(inputs["cmp_wk2"][0]), "cmp_wv2": f(inputs["cmp_wv2"][0]),
        "ln1_t": f(np.asarray(inputs["ln1_w"][0]).reshape(8, 128).T),
        "ln2_t": f(np.asarray(inputs["ln2_w"][0]).reshape(8, 128).T),
        "pek_t": f(np.asarray(inputs["cmp_pe_k"][0]).reshape(16, 128).T),
        "pev_t": f(np.asarray(inputs["cmp_pe_v"][0]).reshape(16, 128).T),
        "retw": f(np.asarray(inputs["ret_norm_w"][0]).reshape(1, 512)),
    }
    qn = np.asarray(inputs["q_norm_w"][0], dtype=np.float32)
    kn = np.asarray(inputs["k_norm_w"][0], dtype=np.float32)
    qkw = np.stack([qn, kn[0], kn[1], kn[2]], axis=1)
    shared["qkw_t"] = f(np.concatenate([qkw, qkw], 0))
    for k, v in c.items():
        if not k.startswith("_"):
            shared[k] = v
    x = np.asarray(inputs["x"], dtype=np.float32)
    maps = []
    for ci in range(cores):
        m = dict(shared)
        m["x"] = np.ascontiguousarray(x[ci * NB:(ci + 1) * NB])
        maps.append(m)
    return maps


def kernel(**inputs):
    global _CONSTS
    if _CONSTS is None:
        _CONSTS = _consts()
    n = 8
    NB = 2
    nc = build(NB=NB, NCH=NCH_FULL, consts=_CONSTS)
    maps = _host_inputs(inputs, NB, n)
    res = run_bass_kernel_spmd(nc, maps, core_ids=list(range(n)))
    out = np.concatenate([r["out"] for r in res.results], axis=0)
    return out.astype(np.float32)
```
